# Optimizing a Trainium2 kernel written in Bass

```python
import jax, jax.numpy as jnp
from jax import lax
import numpy as np

D_MODEL = 2048
BATCH = 1
SEQ = 8192
DEPTH = 1

A_HEAD_DIM = 128
A_WIDTH = D_MODEL // 2
A_HEADS = A_WIDTH // A_HEAD_DIM
A_CHUNK = 64
B_HEAD_DIM = 128
B_WIDTH = D_MODEL // 2
B_HEADS = B_WIDTH // B_HEAD_DIM
Q_BLOCK = 128
D_FF = -(-(8 * D_MODEL) // (3 * 256)) * 256
N_IN = 4 * A_WIDTH + 3 * B_WIDTH + B_HEADS + 2 * D_MODEL
RMS_EPS = 1e-6

kernel_name = "hgrn2_fox_gated_parallel_sandwich_block"


def _in_split_points():
    sizes = [A_WIDTH] * 4 + [B_WIDTH] * 3 + [B_HEADS, D_MODEL, D_MODEL]
    return [int(v) for v in np.cumsum(sizes)[:-1]]


def rms_norm(x, w):
    xf = x.astype(jnp.float32)
    y = xf * lax.rsqrt(jnp.mean(xf * xf, axis=-1, keepdims=True) + RMS_EPS)
    return (y * w.astype(jnp.float32)).astype(x.dtype)


def hgrn2_mixer(q, f_logit, i, g, lb, norm_w):
    B, S, _ = q.shape
    H, D, C = A_HEADS, A_HEAD_DIM, A_CHUNK
    z = f_logit.astype(jnp.float32)
    lb = lb.astype(jnp.float32)
    log_f = jnp.log(lb + (1.0 - lb) * jax.nn.sigmoid(z))
    k = (1.0 - lb) * jax.nn.sigmoid(-z)

    def to_chunks(t):
        return t.astype(jnp.float32).reshape(B, S // C, C, H, D).transpose(1, 0, 3, 2, 4)

    causal = jnp.tril(jnp.ones((C, C), dtype=bool))[None, None, :, :, None]

    def step(state, inp):
        qc, kc, vc, gc = inp
        b = jnp.cumsum(gc, axis=2)
        o_inter = jnp.einsum('bhtk,bhkv->bhtv', qc * jnp.exp(b), state)
        diff = jnp.where(causal, b[:, :, :, None, :] - b[:, :, None, :, :], -jnp.inf)
        scores = jnp.einsum('bhtsk,bhsk->bhts', qc[:, :, :, None, :] * jnp.exp(diff), kc)
        o_intra = jnp.einsum('bhts,bhsv->bhtv', scores, vc)
        b_last = b[:, :, -1:, :]
        k_dec = kc * jnp.exp(b_last - b)
        new_state = jnp.exp(b_last[:, :, 0, :])[..., None] * state + jnp.einsum('bhsk,bhsv->bhkv', k_dec, vc)
        return new_state, o_inter + o_intra

    state0 = jnp.zeros((B, H, D, D), jnp.float32)
    _, o = lax.scan(step, state0, (to_chunks(q), to_chunks(k), to_chunks(i), to_chunks(log_f)))
    o = o.transpose(1, 0, 3, 2, 4).reshape(B, S, H, D)
    o = rms_norm(o, norm_w) * jax.nn.silu(g.astype(jnp.float32).reshape(B, S, H, D))
    return o.reshape(B, S, A_WIDTH).astype(q.dtype)


def fox_mixer(q, k, v, f_logit):
    B, S, _ = q.shape
    H, D, Qb = B_HEADS, B_HEAD_DIM, Q_BLOCK
    n_blk = S // Qb
    qh = q.reshape(B, S, H, D).transpose(0, 2, 1, 3) * (D ** -0.5)
    kh = k.reshape(B, S, H, D).transpose(0, 2, 1, 3)
    vh = v.reshape(B, S, H, D).transpose(0, 2, 1, 3)
    cum = jnp.cumsum(jax.nn.log_sigmoid(f_logit.astype(jnp.float32)), axis=1).transpose(0, 2, 1)
    q_blocks = qh.reshape(B, H, n_blk, Qb, D).transpose(2, 0, 1, 3, 4)
    c_blocks = cum.reshape(B, H, n_blk, Qb).transpose(2, 0, 1, 3)
    pos_k = jnp.arange(S)

    def block(args):
        idx, q_blk, c_blk = args
        pos_q = idx * Qb + jnp.arange(Qb)
        s = jnp.einsum('bhqd,bhkd->bhqk', q_blk, kh).astype(jnp.float32)
        s = s + c_blk[..., None] - cum[:, :, None, :]
        s = jnp.where((pos_q[:, None] >= pos_k[None, :])[None, None], s, -jnp.inf)
        p = jax.nn.softmax(s, axis=-1)
        return jnp.einsum('bhqk,bhkd->bhqd', p.astype(vh.dtype), vh)

    out = lax.map(block, (jnp.arange(n_blk), q_blocks, c_blocks))
    return out.transpose(1, 0, 3, 2, 4).reshape(B, S, B_WIDTH)


def setup_inputs(seed: int = 0) -> dict:
    key = jax.random.key(seed)
    ks = jax.random.split(key, 14)
    f32 = jnp.float32

    def dense(k, fan_in, fan_out):
        return jax.random.normal(k, (DEPTH, fan_in, fan_out), f32) * fan_in ** -0.5

    def gain(k, n):
        return 1.0 + 0.02 * jax.random.normal(k, (DEPTH, n), f32)

    return {
        "x": jax.random.normal(ks[0], (BATCH, SEQ, D_MODEL), f32),
        "w_in": dense(ks[1], D_MODEL, N_IN),
        "b_fox_f": 0.1 * jax.random.normal(ks[2], (DEPTH, B_HEADS), f32),
        "hgrn_lb_logits": jax.random.normal(ks[3], (DEPTH + 1, A_WIDTH), f32),
        "hgrn_norm_w": gain(ks[4], A_HEAD_DIM),
        "w_up_a": dense(ks[5], A_WIDTH, D_MODEL),
        "w_up_b": dense(ks[6], B_WIDTH, D_MODEL),
        "w_o": dense(ks[7], D_MODEL, D_MODEL),
        "norm_mix_pre": gain(ks[8], D_MODEL),
        "norm_mix_post": gain(ks[9], D_MODEL),
        "norm_ffn_pre": gain(ks[10], D_MODEL),
        "norm_ffn_post": gain(ks[11], D_MODEL),
        "w_ffn_in": dense(ks[12], D_MODEL, 2 * D_FF),
        "w_ffn_down": dense(ks[13], D_FF, D_MODEL),
    }


def reference(x, w_in, b_fox_f, hgrn_lb_logits, hgrn_norm_w, w_up_a, w_up_b, w_o,
              norm_mix_pre, norm_mix_post, norm_ffn_pre, norm_ffn_post, w_ffn_in, w_ffn_down):
    split_points = _in_split_points()
    lb_table = jnp.cumsum(jax.nn.softmax(hgrn_lb_logits.astype(jnp.float32), axis=0), axis=0)
    for l in range(DEPTH):
        h = rms_norm(x, norm_mix_pre[l])
        proj = h @ w_in[l]
        a_q, a_f, a_i, a_g, b_q, b_k, b_v, b_f, g_a, g_b = jnp.split(proj, split_points, axis=-1)
        y_a = hgrn2_mixer(a_q, a_f, a_i, a_g, lb_table[l], hgrn_norm_w[l]) @ w_up_a[l]
        y_b = fox_mixer(b_q, b_k, b_v, b_f + b_fox_f[l]) @ w_up_b[l]
        merged = jax.nn.sigmoid(g_a) * y_a + jax.nn.sigmoid(g_b) * y_b
        x = x + rms_norm(merged @ w_o[l], norm_mix_post[l])
        h = rms_norm(x, norm_ffn_pre[l])
        gate, up = jnp.split(h @ w_ffn_in[l], 2, axis=-1)
        x = x + rms_norm((jax.nn.silu(gate) * up) @ w_ffn_down[l], norm_ffn_post[l])
    return x
```

```python
import contextlib
import numpy as np
import ml_dtypes
import concourse.bass as bass
import concourse.mybir as mybir
from concourse.bass_utils import run_bass_kernel_spmd

F32 = mybir.dt.float32
BF16 = mybir.dt.bfloat16
ALU = mybir.AluOpType
AF = mybir.ActivationFunctionType

D_MODEL = 2048
SEQ = 8192
D_FF = 5632
EPS = 1e-6
NCORES = 8

ENGS = ("pe", "act", "dve", "pool", "sp")


class Sched:
    def __init__(self):
        self.ops = {e: [] for e in ENGS}
        self.lastw = {}
        self.readers = {}
        self.dma_cnt = {}

    def _tok(self, eng, rec, idx):
        if rec["dma"] is not None:
            return ("dma", rec["dma"], rec["dma_val"])
        return ("eng", eng, idx)

    def op(self, eng, fn, reads=(), writes=(), dma=None):
        idx = len(self.ops[eng])
        rec = dict(fn=fn, dma=dma, needed=False, deps={})
        if dma is not None:
            self.dma_cnt[dma] = self.dma_cnt.get(dma, 0) + 1
            rec["dma_val"] = 16 * self.dma_cnt[dma]
        tok = self._tok(eng, rec, idx)
        deps = rec["deps"]

        def add(t):
            if t is None:
                return
            if t[0] == "dma":
                cur = self.dma_cnt[t[1]] - (1 if dma == t[1] else 0)
                t = ("dma", t[1], 16 * cur)
            if t[0] == "eng" and t[1] == "pe" and eng == "pe" and dma is None:
                return
            k = (t[0], t[1])
            if k not in deps or deps[k] < t[2]:
                deps[k] = t[2]

        for key in reads:
            add(self.lastw.get(key))
        for key in writes:
            add(self.lastw.get(key))
            for t in self.readers.get(key, {}).values():
                add(t)
        self.ops[eng].append(rec)
        for key in writes:
            self.lastw[key] = tok
            self.readers[key] = {}
        for key in reads:
            r = self.readers.setdefault(key, {})
            k = (tok[0], tok[1])
            if k not in r or r[k][2] < tok[2]:
                r[k] = tok
        return tok

    def barrier(self):
        toks = []
        for e in ENGS:
            for i in range(len(self.ops[e]) - 1, -1, -1):
                if self.ops[e][i]["dma"] is None and self.ops[e][i]["fn"] is not None:
                    toks.append(("eng", e, i))
                    break
        for k, c in self.dma_cnt.items():
            toks.append(("dma", k, 16 * c))
        for e in ENGS:
            rec = dict(fn=None, dma=None, needed=False, deps={})
            for t in toks:
                rec["deps"][(t[0], t[1])] = t[2]
            self.ops[e].append(rec)
        self.lastw = {}
        self.readers = {}

    def emit(self, nc, block, stack):
        sems = {}
        for e in ENGS:
            sems[("eng", e)] = stack.enter_context(nc.semaphore("s_" + e))
        for k in self.dma_cnt:
            sems[("dma", k)] = stack.enter_context(nc.semaphore("d_" + str(k)))
        for e in ENGS:
            for rec in self.ops[e]:
                for (kind, name), val in rec["deps"].items():
                    if kind == "eng":
                        self.ops[name][val]["needed"] = True
        vals = {}
        for e in ENGS:
            cnt = 0
            for i, rec in enumerate(self.ops[e]):
                if rec["needed"]:
                    cnt += 1
                    vals[(e, i)] = cnt
            assert cnt < 60000, (e, cnt)
        self.n_emitted = {e: len(self.ops[e]) for e in ENGS}

        def run(e, engine):
            waited = {}
            for i, rec in enumerate(self.ops[e]):
                for (kind, name), val in rec["deps"].items():
                    v = vals[(name, val)] if kind == "eng" else val
                    sk = (kind, name)
                    if waited.get(sk, 0) >= v:
                        continue
                    engine.wait_ge(sems[sk], v)
                    waited[sk] = v
                if rec["fn"] is None:
                    continue
                ins = rec["fn"](engine)
                if rec["dma"] is not None:
                    ins.then_inc(sems[("dma", rec["dma"])], 16)
                elif rec["needed"]:
                    ins.then_inc(sems[("eng", e)], 1)

        @block.tensor
        def _(eng):
            run("pe", eng)

        @block.scalar
        def _(eng):
            run("act", eng)

        @block.vector
        def _(eng):
            run("dve", eng)

        @block.gpsimd
        def _(eng):
            run("pool", eng)

        @block.sync
        def _(eng):
            run("sp", eng)


def _v3(ap, k):
    return ap.rearrange("p (k n) -> p k n", k=k)


def build_l2(NT=1024):
    nc = bass.Bass("TRN2", target_bir_lowering=False)
    S = Sched()
    NTT = NT // 512
    dt = nc.dram_tensor
    xT = dt("xT", [2048, NT], F32, kind="ExternalInput").ap()
    oT = dt("oT", [2048, NT], F32, kind="ExternalInput").ap()
    wg = dt("wg", [2048, 4096], F32, kind="ExternalInput").ap()
    wua = dt("wua", [1024, 2048], F32, kind="ExternalInput").ap()
    wub = dt("wub", [1024, 2048], F32, kind="ExternalInput").ap()
    wo = dt("wo", [2048, 2048], F32, kind="ExternalInput").ap()
    wfi = dt("wfi", [2048, 2 * D_FF], F32, kind="ExternalInput").ap()
    wfd = dt("wfd", [D_FF, 2048], F32, kind="ExternalInput").ap()
    nv = dt("nv", [128, 64], F32, kind="ExternalInput").ap()
    cst = dt("cst", [128, 128], BF16, kind="ExternalInput").ap()
    outT = dt("outT", [2048, NT], F32, kind="ExternalOutput").ap()
    zd = dt("zd", [2048, NT], F32).ap()
    x1d = dt("x1d", [2048, NT], F32).ap()

    def cm(ap):
        return ap.rearrange("(k p) n -> p k n", p=128)

    xTv, oTv, wgv, wuav, wubv, wov, wfiv, wfdv = map(cm, (xT, oT, wg, wua, wub, wo, wfi, wfd))
    outTv, zdv, x1dv = cm(outT), cm(zd), cm(x1d)

    with contextlib.ExitStack() as st:
        sb = lambda name, shape, d: st.enter_context(nc.sbuf_tensor(name, shape, d))
        A = sb("A", [128, 16 * NT * 2], BF16)
        B = sb("B", [128, 16 * NT], BF16)
        W = sb("W", [128, 24576], BF16)
        T = sb("T", [128, 8, 512], F32)
        XS = sb("XS", [128, 16, 256], F32)
        SQ = sb("SQ", [128, 16, 256], BF16)
        SQT = sb("SQT", [128, 4, 512], BF16)
        R = sb("R", [128, 2, NT], F32)
        NV = sb("NV", [128, 64], F32)
        ONES = sb("ONES", [128, 128], BF16)
        PS = [st.enter_context(nc.psum_tensor("ps%d" % i, [128, 512], F32)) for i in range(8)]
        block = st.enter_context(nc.Block())

        xb = _v3(A[:, 0:16 * NT], 16)
        ob = _v3(A[:, 16 * NT:32 * NT], 16)
        hid = _v3(A[:, 0:44 * 512], 44)
        mb = _v3(B[:, :], 16)
        x1b = mb

        S.op("sp", lambda e: e.dma_start(out=NV[:, :], in_=nv[:, :]), writes=["NV"], dma="c0")
        S.op("sp", lambda e: e.dma_start(out=ONES[:, :], in_=cst[:, :]), writes=["ONES"], dma="c1")

        def rstd_from(ps_ap, r_ap, keyp, keyr):
            S.op("dve", lambda e: e.tensor_scalar(out=r_ap, in0=ps_ap, scalar1=1.0 / D_MODEL, scalar2=EPS,
                                                  op0=ALU.mult, op1=ALU.add), reads=[keyp], writes=[keyr])
            S.op("act", lambda e: e.activation(out=r_ap, in_=r_ap, func=AF.Sqrt), reads=[keyr], writes=[keyr])
            S.op("dve", lambda e: e.reciprocal(out=r_ap, in_=r_ap), reads=[keyr], writes=[keyr])

        for s in range(NT // 256):
            cs = slice(s * 256, (s + 1) * 256)
            S.op("sp", lambda e, cs=cs: e.dma_start(out=XS[:, :, :], in_=xTv[:, :, cs]), writes=["XS"], dma="xs")
            S.op("act", lambda e: e.activation(out=SQ[:, :, :], in_=XS[:, :, :], func=AF.Square), reads=["XS"], writes=["SQ"])
            for k in range(16):
                S.op("dve", lambda e, k=k, cs=cs: e.tensor_scalar(out=xb[:, k, cs], in0=XS[:, k, :], scalar1=NV[:, k:k + 1],
                                                                  scalar2=None, op0=ALU.mult),
                     reads=["XS", "NV"], writes=[("A0", k, s)])
            for k in range(16):
                S.op("pe", lambda e, k=k: e.matmul(PS[0][:, 0:256], lhsT=ONES[:, :], rhs=SQ[:, k, :], start=(k == 0), stop=(k == 15)),
                     reads=["SQ", "ONES"], writes=[("P", 0)])
            rstd_from(PS[0][:, 0:256], R[:, 0, cs], ("P", 0), ("R0", s))
        for h in range(2):
            S.op("pool", lambda e, h=h: e.dma_start(out=ob[:, 8 * h:8 * h + 8, :], in_=oTv[:, 8 * h:8 * h + 8, :]),
                 writes=[("A1", h)], dma="ob%d" % h)
        xb_keys = [("A0", k, s) for k in range(16) for s in range(NT // 256)]
        r0_keys = [("R0", s) for s in range(NT // 256)]

        for g in range(8):
            s_ = g % 2
            base = s_ * 12288
            wga = _v3(W[:, base:base + 4096], 16)
            wgb = _v3(W[:, base + 4096:base + 8192], 16)
            wa = _v3(W[:, base + 8192:base + 10240], 8)
            wb = _v3(W[:, base + 10240:base + 12288], 8)
            gs = slice(g * 256, (g + 1) * 256)
            gs2 = slice(2048 + g * 256, 2048 + (g + 1) * 256)
            wk = ("W", s_)
            dk = "w%d" % s_
            S.op("pool", lambda e, wga=wga, gs=gs: e.dma_start(out=wga[:, :, :], in_=wgv[:, :, gs]), writes=[wk + (0,)], dma=dk + "a")
            S.op("pool", lambda e, wgb=wgb, gs2=gs2: e.dma_start(out=wgb[:, :, :], in_=wgv[:, :, gs2]), writes=[wk + (1,)], dma=dk + "b")
            S.op("pool", lambda e, wa=wa, gs=gs: e.dma_start(out=wa[:, :, :], in_=wuav[:, :, gs]), writes=[wk + (2,)], dma=dk + "c")
            S.op("pool", lambda e, wb=wb, gs=gs: e.dma_start(out=wb[:, :, :], in_=wubv[:, :, gs]), writes=[wk + (3,)], dma=dk + "d")
            for nb in range(2):
                n = g * 2 + nb
                ns = slice(nb * 128, (nb + 1) * 128)
                for t in range(NTT):
                    ts_ = slice(t * 512, (t + 1) * 512)
                    par = (n * NTT + t) % 2
                    pga, pgb, pya, pyb = (PS[4 * par + i] for i in range(4))
                    kga, kgb, kya, kyb = (("P", 4 * par + i) for i in range(4))
                    t0, t1 = T[:, 2 * par, :], T[:, 2 * par + 1, :]
                    k0, k1 = ("T", 2 * par), ("T", 2 * par + 1)
                    for k in range(16):
                        S.op("pe", lambda e, k=k, pga=pga, wga=wga, ns=ns, ts_=ts_: e.matmul(pga[:, :], lhsT=wga[:, k, ns], rhs=xb[:, k, ts_], start=(k == 0), stop=(k == 15)),
                             reads=[wk + (0,)] + xb_keys, writes=[kga])
                    for k in range(16):
                        S.op("pe", lambda e, k=k, pgb=pgb, wgb=wgb, ns=ns, ts_=ts_: e.matmul(pgb[:, :], lhsT=wgb[:, k, ns], rhs=xb[:, k, ts_], start=(k == 0), stop=(k == 15)),
                             reads=[wk + (1,)] + xb_keys, writes=[kgb])
                    for k in range(8):
                        S.op("pe", lambda e, k=k, pya=pya, wa=wa, ns=ns, ts_=ts_: e.matmul(pya[:, :], lhsT=wa[:, k, ns], rhs=ob[:, k, ts_], start=(k == 0), stop=(k == 7)),
                             reads=[wk + (2,), ("A1", 0)], writes=[kya])
                    for k in range(8):
                        S.op("pe", lambda e, k=k, pyb=pyb, wb=wb, ns=ns, ts_=ts_: e.matmul(pyb[:, :], lhsT=wb[:, k, ns], rhs=ob[:, 8 + k, ts_], start=(k == 0), stop=(k == 7)),
                             reads=[wk + (3,), ("A1", 1)], writes=[kyb])
                    S.op("dve", lambda e, t0=t0, pga=pga, ts_=ts_: e.tensor_tensor(out=t0, in0=pga[:, :], in1=R[:, 0, ts_], op=ALU.mult),
                         reads=[kga] + r0_keys, writes=[k0])
                    S.op("act", lambda e, t0=t0: e.activation(out=t0, in_=t0, func=AF.Sigmoid), reads=[k0], writes=[k0])
                    S.op("dve", lambda e, t1=t1, pgb=pgb, ts_=ts_: e.tensor_tensor(out=t1, in0=pgb[:, :], in1=R[:, 0, ts_], op=ALU.mult),
                         reads=[kgb] + r0_keys, writes=[k1])
                    S.op("act", lambda e, t1=t1: e.activation(out=t1, in_=t1, func=AF.Sigmoid), reads=[k1], writes=[k1])
                    S.op("dve", lambda e, t0=t0, pya=pya: e.tensor_tensor(out=t0, in0=pya[:, :], in1=t0, op=ALU.mult),
                         reads=[kya, k0], writes=[k0])
                    S.op("dve", lambda e, t1=t1, pyb=pyb: e.tensor_tensor(out=t1, in0=pyb[:, :], in1=t1, op=ALU.mult),
                         reads=[kyb, k1], writes=[k1])
                    S.op("dve", lambda e, t0=t0, t1=t1, n=n, ts_=ts_: e.tensor_tensor(out=mb[:, n, ts_], in0=t0, in1=t1, op=ALU.add),
                         reads=[k0, k1], writes=[("B", n, t)])

        S.barrier()
        def evac_stats(ps, pkey, nchunk, t, ssb, first, last, dram_v, par):
            zt = T[:, 4 + par, :]
            zk = ("T", 4 + par)
            sq = SQT[:, par, :]
            sk = ("SQT", par)
            S.op("act", lambda e: e.activation(out=zt, in_=ps[:, :], func=AF.Copy), reads=[pkey], writes=[zk])
            S.op("act", lambda e: e.activation(out=sq, in_=ps[:, :], func=AF.Square), reads=[pkey], writes=[sk])
            S.op("pe", lambda e: e.matmul(PS[ssb][:, :], lhsT=ONES[:, :], rhs=sq, start=first, stop=last),
                 reads=[sk, "ONES"], writes=[("P", ssb)])
            S.op("sp", lambda e: e.dma_start(out=dram_v[:, nchunk, t * 512:(t + 1) * 512], in_=zt), reads=[zk],
                 writes=[("zd", nchunk, t)], dma="zst%d" % par)

        for g in range(8):
            s_ = g % 2
            base = s_ * 12288
            wos = _v3(W[:, base:base + 4096], 16)
            wk = ("W", s_)
            gs = slice(g * 256, (g + 1) * 256)
            S.op("pool", lambda e, wos=wos, gs=gs: e.dma_start(out=wos[:, :, :], in_=wov[:, :, gs]), writes=[wk], dma="w%d" % s_)
            for nb in range(2):
                n = g * 2 + nb
                ns = slice(nb * 128, (nb + 1) * 128)
                for t in range(NTT):
                    ts_ = slice(t * 512, (t + 1) * 512)
                    pb = (n * NTT + t) % 4
                    for k in range(16):
                        S.op("pe", lambda e, k=k, pb=pb, wos=wos, ns=ns, ts_=ts_: e.matmul(PS[pb][:, :], lhsT=wos[:, k, ns], rhs=mb[:, k, ts_], start=(k == 0), stop=(k == 15)),
                             reads=[wk] + [("B", k, t) for k in range(16)], writes=[("P", pb)])
                    evac_stats(PS[pb], ("P", pb), n, t, 6 + t, n == 0, n == 15, zdv, (n * NTT + t) % 2)
        for t in range(NTT):
            rstd_from(PS[6 + t][:, :], R[:, 1, t * 512:(t + 1) * 512], ("P", 6 + t), ("R1", t))

        def resid_pass(t, src_v, res_v, resk, rrow, rkey, gain_off, dst_v, dstk, dst_dma, make_next):
            ts_ = slice(t * 512, (t + 1) * 512)
            for n in range(16):
                par = n % 2
                ta, tb = T[:, par, :], T[:, 2 + par, :]
                ka, kb = ("T", par), ("T", 2 + par)
                S.op("sp", lambda e, ta=ta, n=n: e.dma_start(out=ta, in_=res_v[:, n, ts_]), reads=[(resk, n, t)], writes=[ka], dma="ra%d" % par)
                S.op("sp", lambda e, tb=tb, n=n: e.dma_start(out=tb, in_=src_v[:, n, ts_]), reads=[("zd", n, t)], writes=[kb], dma="rb%d" % par)
                S.op("dve", lambda e, tb=tb, n=n: e.scalar_tensor_tensor(out=tb, in0=tb, scalar=NV[:, gain_off + n:gain_off + n + 1], in1=R[:, rrow, ts_],
                                                                           op0=ALU.mult, op1=ALU.mult), reads=[kb, "NV", (rkey, t)], writes=[kb])
                S.op("dve", lambda e, ta=ta, tb=tb: e.tensor_tensor(out=ta, in0=ta, in1=tb, op=ALU.add), reads=[ka, kb], writes=[ka])
                S.op("sp", lambda e, ta=ta, n=n: e.dma_start(out=dst_v[:, n, ts_], in_=ta), reads=[ka], writes=[(dstk, n, t)], dma=dst_dma)
                if make_next:
                    sq = SQT[:, 2 + par, :]
                    sk = ("SQT", 2 + par)
                    S.op("dve", lambda e, ta=ta, n=n: e.tensor_scalar(out=x1b[:, n, ts_], in0=ta, scalar1=NV[:, 32 + n:33 + n], scalar2=None, op0=ALU.mult),
                         reads=[ka, "NV"], writes=[("B", n, t)])
                    S.op("act", lambda e, ta=ta, sq=sq: e.activation(out=sq, in_=ta, func=AF.Square), reads=[ka], writes=[sk])
                    S.op("pe", lambda e, sq=sq, n=n: e.matmul(PS[4 + t][:, :], lhsT=ONES[:, :], rhs=sq, start=(n == 0), stop=(n == 15)),
                         reads=[sk, "ONES"], writes=[("P", 4 + t)])

        S.barrier()
        for t in range(NTT):
            resid_pass(t, zdv, xTv, "xin", 1, "R1", 16, x1dv, "x1d", "x1st", True)
        for t in range(NTT):
            rstd_from(PS[4 + t][:, :], R[:, 0, t * 512:(t + 1) * 512], ("P", 4 + t), ("R2", t))

        S.barrier()

        def ffn_tile(t):
            ts_ = slice(t * 512, (t + 1) * 512)
            for g in range(22):
                s_ = g % 2
                base = s_ * 12288
                wfg = _v3(W[:, base:base + 4096], 16)
                wfu = _v3(W[:, base + 4096:base + 8192], 16)
                wk = ("W", s_)
                gs = slice(g * 256, (g + 1) * 256)
                gs2 = slice(D_FF + g * 256, D_FF + (g + 1) * 256)
                S.op("pool", lambda e, wfg=wfg, gs=gs: e.dma_start(out=wfg[:, :, :], in_=wfiv[:, :, gs]), writes=[wk + (0,)], dma="w%da" % s_)
                S.op("pool", lambda e, wfu=wfu, gs2=gs2: e.dma_start(out=wfu[:, :, :], in_=wfiv[:, :, gs2]), writes=[wk + (1,)], dma="w%db" % s_)
                for nb in range(2):
                    f = g * 2 + nb
                    ns = slice(nb * 128, (nb + 1) * 128)
                    par = f % 2
                    pg_, pu_ = PS[2 * par], PS[2 * par + 1]
                    kg_, ku_ = ("P", 2 * par), ("P", 2 * par + 1)
                    t0, t1 = T[:, 2 * par, :], T[:, 2 * par + 1, :]
                    k0, k1 = ("T", 2 * par), ("T", 2 * par + 1)
                    for k in range(16):
                        S.op("pe", lambda e, k=k, pg_=pg_, wfg=wfg, ns=ns: e.matmul(pg_[:, :], lhsT=wfg[:, k, ns], rhs=x1b[:, k, ts_], start=(k == 0), stop=(k == 15)),
                             reads=[wk + (0,)] + [("B", k, t) for k in range(16)], writes=[kg_])
                    for k in range(16):
                        S.op("pe", lambda e, k=k, pu_=pu_, wfu=wfu, ns=ns: e.matmul(pu_[:, :], lhsT=wfu[:, k, ns], rhs=x1b[:, k, ts_], start=(k == 0), stop=(k == 15)),
                             reads=[wk + (1,)] + [("B", k, t) for k in range(16)], writes=[ku_])
                    S.op("dve", lambda e, t0=t0, pg_=pg_: e.tensor_tensor(out=t0, in0=pg_[:, :], in1=R[:, 0, ts_], op=ALU.mult),
                         reads=[kg_, ("R2", t)], writes=[k0])
                    S.op("act", lambda e, t0=t0: e.activation(out=t0, in_=t0, func=AF.Silu), reads=[k0], writes=[k0])
                    S.op("dve", lambda e, t1=t1, pu_=pu_: e.tensor_tensor(out=t1, in0=pu_[:, :], in1=R[:, 0, ts_], op=ALU.mult),
                         reads=[ku_, ("R2", t)], writes=[k1])
                    S.op("dve", lambda e, t0=t0, t1=t1, f=f: e.tensor_tensor(out=hid[:, f, :], in0=t0, in1=t1, op=ALU.mult),
                         reads=[k0, k1], writes=[("hid", f)])
            hid_keys = [("hid", f) for f in range(44)]
            S.barrier()
            for g in range(8):
                s_ = g % 2
                base = s_ * 12288
                wd = _v3(W[:, base:base + 44 * 256], 44)
                wk = ("W", s_)
                gs = slice(g * 256, (g + 1) * 256)
                for hh in range(2):
                    S.op("pool", lambda e, wd=wd, gs=gs, hh=hh: e.dma_start(out=wd[:, 22 * hh:22 * hh + 22, :], in_=wfdv[:, 22 * hh:22 * hh + 22, gs]),
                         writes=[wk + (hh,)], dma="w%d%s" % (s_, "ab"[hh]))
                for nb in range(2):
                    n = g * 2 + nb
                    ns = slice(nb * 128, (nb + 1) * 128)
                    pb = 4 + n % 2
                    for k in range(44):
                        S.op("pe", lambda e, k=k, pb=pb, wd=wd, ns=ns: e.matmul(PS[pb][:, :], lhsT=wd[:, k, ns], rhs=hid[:, k, :], start=(k == 0), stop=(k == 43)),
                             reads=[wk + (0,), wk + (1,)] + hid_keys, writes=[("P", pb)])
                    evac_stats(PS[pb], ("P", pb), n, t, 6 + t, n == 0, n == 15, zdv, n % 2)
            rstd_from(PS[6 + t][:, :], R[:, 1, ts_], ("P", 6 + t), ("R3", t))
            S.barrier()
            resid_pass(t, zdv, x1dv, "x1d", 1, "R3", 48, outTv, "out", "ost", False)

        for t in range(NTT):
            ffn_tile(t)
        S.barrier()
        S.emit(nc, block, st)
    return nc


def _colvec(v):
    return np.ascontiguousarray(np.asarray(v, np.float32).reshape(16, 128).T)


def prep_l2(inp, oA, oB):
    x = np.asarray(inp["x"], np.float32)[0]
    w_in = np.asarray(inp["w_in"], np.float32)[0]
    wg = np.ascontiguousarray(w_in[:, 7176:7176 + 4096])
    nv = np.ascontiguousarray(np.concatenate([_colvec(inp["norm_mix_pre"][0]), _colvec(inp["norm_mix_post"][0]),
                                              _colvec(inp["norm_ffn_pre"][0]), _colvec(inp["norm_ffn_post"][0])], axis=1))
    ones = np.ones((128, 128), ml_dtypes.bfloat16)
    common = dict(wg=wg, wua=np.ascontiguousarray(inp["w_up_a"][0], dtype=np.float32), wub=np.ascontiguousarray(inp["w_up_b"][0], dtype=np.float32),
                  wo=np.ascontiguousarray(inp["w_o"][0], dtype=np.float32), wfi=np.ascontiguousarray(inp["w_ffn_in"][0], dtype=np.float32),
                  wfd=np.ascontiguousarray(inp["w_ffn_down"][0], dtype=np.float32), nv=nv, cst=ones)
    maps = []
    for c in range(NCORES):
        sl = slice(c * 1024, (c + 1) * 1024)
        oT = np.ascontiguousarray(np.concatenate([oA[sl].T, oB[sl].T], axis=0), dtype=np.float32)
        maps.append(dict(common, xT=np.ascontiguousarray(x[sl].T), oT=oT))
    return maps


def post_l2(results):
    return np.concatenate([np.ascontiguousarray(r["outT"].T) for r in results], axis=0)[None]


def build_l1(NTILES=16, stage=9):
    nc = bass.Bass("TRN2", target_bir_lowering=False)
    S = Sched()
    NTOK = NTILES * 512
    dt = nc.dram_tensor
    xT = dt("xT", [2048, NTOK], F32, kind="ExternalInput").ap()
    w1 = dt("w1", [2048, 896], F32, kind="ExternalInput").ap()
    wf = dt("wf", [2048, 2], F32, kind="ExternalInput").ap()
    nv1 = dt("nv1", [128, 16], F32, kind="ExternalInput").ap()
    lbl = dt("lbl", [128, 2], F32, kind="ExternalInput").ap()
    hnw = dt("hnw", [128, 128], F32, kind="ExternalInput").ap()
    bfx = dt("bfx", [128, 1], F32, kind="ExternalInput").ap()
    cb = dt("cb", [128, 512], BF16, kind="ExternalInput").ap()
    cf = dt("cf", [128, 1024], F32, kind="ExternalInput").ap()
    oA = dt("oA", [NTOK, 128], F32, kind="ExternalOutput").ap()
    oBT = dt("oBT", [128, NTOK], F32, kind="ExternalOutput").ap()
    xTv = xT.rearrange("(k p) n -> p k n", p=128)
    w1v = w1.rearrange("(k p) n -> p k n", p=128)
    wfv = wf.rearrange("(k p) n -> p k n", p=128)
    oAv = oA.rearrange("(b p) v -> p b v", p=128)

    with contextlib.ExitStack() as st:
        sb = lambda name, shape, d: st.enter_context(nc.sbuf_tensor(name, shape, d))
        XS = sb("XS", [128, 2, 8, 512], F32)
        XB = sb("XB", [128, 16, 512], BF16)
        SQ = sb("SQ", [128, 16, 512], BF16)
        W1 = sb("W1", [128, 16, 896], BF16)
        WF = sb("WF", [128, 16, 2], BF16)
        NV = sb("NV", [128, 16], F32)
        LBL = sb("LBL", [128, 2], F32)
        LB = sb("LB", [128, 2], F32)
        HNW = sb("HNW", [128, 128], F32)
        BFX = sb("BFX", [128, 1], F32)
        CB = sb("CB", [128, 512], BF16)
        CF = sb("CF", [128, 1024], F32)
        RT = sb("RT", [128, 512], F32)
        RC = sb("RC", [128, 4], F32)
        KT = sb("KT", [128, NTOK], BF16)
        V = sb("V", [128, NTILES * 4, 128], BF16)
        QT = sb("QT", [128, 512], BF16)
        TQ = sb("TQ", [128, 512], F32)
        TZ = sb("TZ", [128, 512], F32)
        TE = sb("TE", [128, 512], F32)
        TS = sb("TS", [128, 512], F32)
        TF = sb("TF", [128, 512], F32)
        TK = sb("TK", [128, 512], F32)
        TB = sb("TB", [128, 512], F32)
        TD = sb("TD", [128, 512], F32)
        TE1 = sb("TE1", [128, 512], F32)
        TE2 = sb("TE2", [128, 512], F32)
        QTB = sb("QTB", [128, 512], BF16)
        KTB = sb("KTB", [128, 512], BF16)
        KTOK = sb("KTOK", [128, 4, 128], BF16)
        VTOK = sb("VTOK", [128, 4, 128], BF16)
        GTOK = sb("GTOK", [128, 4, 128], F32)
        EBM = sb("EBM", [128, 8], F32)
        EBL = sb("EBL", [128, 8], F32)
        EBLM = sb("EBLM", [128, 8], F32)
        BDM = sb("BDM", [128, 8], F32)
        ST = sb("ST", [128, 128], F32)
        SM = sb("SM", [128, 2, 128], BF16)
        SCB = sb("SCB", [128, 128], BF16)
        TKV = sb("TKV", [128, 128], F32)
        T1 = sb("T1", [128, 128], F32)
        TSC = sb("TSC", [128, 128], F32)
        EG = sb("EG", [128, 128], F32)
        GG = sb("GG", [128, 128], F32)
        OAT = sb("OAT", [128, 2, 128], F32)
        SSO = sb("SSO", [128, 1], F32)
        RO = sb("RO", [128, 1], F32)
        PT = sb("PT", [128, 2, 512], BF16)
        CROW = sb("CROW", [1, 512], BF16)
        LROW = sb("LROW", [1, 512], F32)
        ZF = sb("ZF", [1, 512], F32)
        POSC = sb("POSC", [128, NTILES * 4], F32)
        RP = sb("RP", [128, NTILES + 1], F32)
        BIAS = sb("BIAS", [128, NTILES * 4], F32)
        ODEN = sb("ODEN", [128, 512], F32)
        OOUT = sb("OOUT", [128, 512], F32)
        PS = [st.enter_context(nc.psum_tensor("ps%d" % i, [128, 512], F32)) for i in range(8)]
        block = st.enter_context(nc.Block())

        ONESB, IDENT, MASKNEG, BDMASK = CB[:, 0:128], CB[:, 128:256], CB[:, 256:384], CB[:, 384:512]
        RESET, ONESF = CF[:, 0:512], CF[:, 512:1024]

        def ld(eng, dst, src, key, dk):
            S.op(eng, lambda e: e.dma_start(out=dst, in_=src), writes=[key], dma=dk)

        ld("sp", NV[:, :], nv1[:, :], "NV", "c0")
        ld("sp", LBL[:, :], lbl[:, :], "LBL", "c1")
        ld("sp", HNW[:, :], hnw[:, :], "HNW", "c2")
        ld("sp", BFX[:, :], bfx[:, :], "BFX", "c3")
        ld("sp", CB[:, :], cb[:, :], "CB", "c4")
        ld("sp", CF[:, :], cf[:, :], "CF", "c5")
        ld("pool", W1[:, 0:8, :], w1v[:, 0:8, :], ("W1", 0), "w1a")
        ld("pool", W1[:, 8:16, :], w1v[:, 8:16, :], ("W1", 1), "w1b")
        ld("pool", WF[:, :, :], wfv[:, :, :], "WF", "wf")
        W1K = [("W1", 0), ("W1", 1)]

        S.op("dve", lambda e: e.tensor_tensor(out=LB[:, 0:1], in0=LBL[:, 0:1], in1=LBL[:, 1:2], op=ALU.subtract), reads=["LBL"], writes=["LB"])
        S.op("act", lambda e: e.activation(out=LB[:, 1:2], in_=LB[:, 0:1], func=AF.Exp, scale=-1.0), reads=["LB"], writes=["LB"])
        S.op("dve", lambda e: e.tensor_scalar(out=LB[:, 0:1], in0=LB[:, 1:2], scalar1=1.0, scalar2=None, op0=ALU.add), reads=["LB"], writes=["LB"])
        S.op("dve", lambda e: e.reciprocal(out=LB[:, 0:1], in_=LB[:, 0:1]), reads=["LB"], writes=["LB"])
        S.op("dve", lambda e: e.tensor_tensor(out=LB[:, 1:2], in0=LB[:, 1:2], in1=LB[:, 0:1], op=ALU.mult), reads=["LB"], writes=["LB"])
        S.op("dve", lambda e: e.memset(ST[:, :], 0.0), writes=["ST"])
        S.op("dve", lambda e: e.memset(RP[:, :], 0.0), writes=["RP"])

        def rstd(ps_ap, r_ap, pkey, rkey, n):
            S.op("dve", lambda e: e.tensor_scalar(out=r_ap, in0=ps_ap, scalar1=1.0 / n, scalar2=EPS, op0=ALU.mult, op1=ALU.add),
                 reads=[pkey], writes=[rkey])
            S.op("act", lambda e: e.activation(out=r_ap, in_=r_ap, func=AF.Sqrt), reads=[rkey], writes=[rkey])
            S.op("dve", lambda e: e.reciprocal(out=r_ap, in_=r_ap), reads=[rkey], writes=[rkey])

        def tile_body(T):
            tsl = slice(T * 512, (T + 1) * 512)
            for h in range(2):
                S.op("sp", lambda e, h=h: e.dma_start(out=XS[:, h, :, :], in_=xTv[:, 8 * h:8 * h + 8, tsl]), writes=[("XS", h)], dma="xs%d" % h)
            for h in range(2):
                S.op("act", lambda e, h=h: e.activation(out=SQ[:, 8 * h:8 * h + 8, :], in_=XS[:, h, :, :], func=AF.Square),
                     reads=[("XS", h)], writes=[("SQ", h)])
                for k in range(8):
                    S.op("pool", lambda e, h=h, k=k: e.tensor_scalar(out=XB[:, 8 * h + k, :], in0=XS[:, h, k, :], scalar1=NV[:, 8 * h + k:8 * h + k + 1],
                                                                      scalar2=None, op0=ALU.mult), reads=[("XS", h), "NV"], writes=[("XB", h)])
            SQK = [("SQ", 0), ("SQ", 1)]
            XBK = [("XB", 0), ("XB", 1)]
            for k in range(16):
                S.op("pe", lambda e, k=k: e.matmul(PS[0][:, :], lhsT=ONESB, rhs=SQ[:, k, :], start=(k == 0), stop=(k == 15)),
                     reads=SQK + ["CB"], writes=[("P", 0)])
            rstd(PS[0][:, :], RT[:, :], ("P", 0), "RT", D_MODEL)
            for b in range(4):
                for k in range(16):
                    S.op("pe", lambda e, k=k, b=b: e.matmul(PS[1][:, 400 + b:401 + b], lhsT=SQ[:, k, b * 128:(b + 1) * 128], rhs=ONESB[:, 0:1],
                                                            start=(k == 0), stop=(k == 15)), reads=SQK + ["CB"], writes=[("P", 1)])
            rstd(PS[1][:, 400:404], RC[:, :], ("P", 1), "RC", D_MODEL)
            if stage < 1:
                return
            for j in range(4):
                pb = j % 2
                for k in range(16):
                    S.op("pe", lambda e, k=k, j=j, pb=pb: e.matmul(PS[pb][:, :], lhsT=W1[:, k, j * 128:(j + 1) * 128], rhs=XB[:, k, :],
                                                                   start=(k == 0), stop=(k == 15)), reads=W1K + XBK, writes=[("P", pb)])
                if j == 0:
                    S.op("dve", lambda e, pb=pb: e.tensor_tensor(out=TQ[:, :], in0=PS[pb][:, :], in1=RT[:, :], op=ALU.mult), reads=[("P", pb), "RT"], writes=["TQ"])
                elif j == 1:
                    S.op("dve", lambda e, pb=pb: e.tensor_tensor(out=TZ[:, :], in0=PS[pb][:, :], in1=RT[:, :], op=ALU.mult), reads=[("P", pb), "RT"], writes=["TZ"])
                elif j == 2:
                    S.op("dve", lambda e, pb=pb: e.scalar_tensor_tensor(out=QT[:, :], in0=PS[pb][:, :], scalar=128.0 ** -0.5, in1=RT[:, :], op0=ALU.mult, op1=ALU.mult),
                         reads=[("P", pb), "RT"], writes=["QT"])
                else:
                    S.op("dve", lambda e, pb=pb: e.tensor_tensor(out=KT[:, tsl], in0=PS[pb][:, :], in1=RT[:, :], op=ALU.mult), reads=[("P", pb), "RT"], writes=[("KT", T)])
            if stage < 2:
                return
            for b in range(4):
                pb = b % 2
                for k in range(16):
                    S.op("pe", lambda e, k=k, b=b, pb=pb: e.matmul(PS[pb][:, 0:384], lhsT=XB[:, k, b * 128:(b + 1) * 128], rhs=W1[:, k, 512:896],
                                                                   start=(k == 0), stop=(k == 15)), reads=W1K + XBK, writes=[("P", pb)])
                S.op("act", lambda e, b=b, pb=pb: e.activation(out=VTOK[:, b, :], in_=PS[pb][:, 0:128], func=AF.Copy, scale=RC[:, b:b + 1]),
                     reads=[("P", pb), "RC"], writes=[("VTOK", b)])
                S.op("act", lambda e, b=b, pb=pb: e.activation(out=GTOK[:, b, :], in_=PS[pb][:, 128:256], func=AF.Copy, scale=RC[:, b:b + 1]),
                     reads=[("P", pb), "RC"], writes=[("GTOK", b)])
                S.op("act", lambda e, b=b, pb=pb: e.activation(out=V[:, 4 * T + b, :], in_=PS[pb][:, 256:384], func=AF.Copy, scale=RC[:, b:b + 1]),
                     reads=[("P", pb), "RC"], writes=[("V", 4 * T + b)])
            if stage < 3:
                return
            for k in range(16):
                S.op("pe", lambda e, k=k: e.matmul(PS[1][0:1, :], lhsT=WF[:, k, 0:1], rhs=XB[:, k, :], start=(k == 0), stop=(k == 15)),
                     reads=["WF"] + XBK, writes=[("P", 1)])
            S.op("dve", lambda e: e.tensor_tensor(out=ZF[0:1, :], in0=PS[1][0:1, :], in1=RT[0:1, :], op=ALU.mult), reads=[("P", 1), "RT"], writes=["ZF"])
            S.op("dve", lambda e: e.tensor_scalar(out=ZF[0:1, :], in0=ZF[0:1, :], scalar1=BFX[0:1, 0:1], scalar2=None, op0=ALU.add), reads=["ZF", "BFX"], writes=["ZF"])
            S.op("act", lambda e: e.activation(out=ZF[0:1, :], in_=ZF[0:1, :], func=AF.Exp, scale=-1.0), reads=["ZF"], writes=["ZF"])
            S.op("act", lambda e: e.activation(out=ZF[0:1, :], in_=ZF[0:1, :], func=AF.Ln, bias=1.0), reads=["ZF"], writes=["ZF"])
            S.op("dve", lambda e: e.tensor_tensor_scan(out=LROW[0:1, :], data0=ONESF[0:1, :], data1=ZF[0:1, :], initial=0.0, op0=ALU.mult, op1=ALU.add),
                 reads=["ZF", "CF"], writes=["LROW"])
            S.op("dve", lambda e: e.tensor_scalar(out=CROW[0:1, :], in0=LROW[0:1, :], scalar1=-1.0, scalar2=None, op0=ALU.mult), reads=["LROW"], writes=["CROW"])
            for b in range(4):
                S.op("pe", lambda e, b=b: e.matmul(PS[1][:, 384 + b:385 + b], lhsT=LROW[0:1, b * 128:(b + 1) * 128], rhs=ONESF[0:1, 0:1], start=True, stop=True),
                     reads=["LROW", "CF"], writes=[("P", 1)])
            S.op("pe", lambda e: e.matmul(PS[1][:, 388:389], lhsT=ONESF[0:1, 0:128], rhs=LROW[0:1, 511:512], start=True, stop=True),
                 reads=["LROW", "CF"], writes=[("P", 1)])
            S.op("dve", lambda e: e.tensor_scalar(out=POSC[:, 4 * T:4 * T + 4], in0=PS[1][:, 384:388], scalar1=RP[:, T:T + 1], scalar2=None, op0=ALU.add),
                 reads=[("P", 1), "RP"], writes=["POSC"])
            S.op("dve", lambda e: e.tensor_tensor(out=RP[:, T + 1:T + 2], in0=PS[1][:, 388:389], in1=RP[:, T:T + 1], op=ALU.add),
                 reads=[("P", 1), "RP"], writes=["RP"])
            S.op("dve", lambda e: e.tensor_scalar(out=BIAS[:, 0:4 * T + 4], in0=POSC[:, 0:4 * T + 4], scalar1=RP[:, T:T + 1], scalar2=None, op0=ALU.subtract),
                 reads=["POSC", "RP"], writes=["BIAS"])

            if stage < 4:
                return
            S.op("act", lambda e: e.activation(out=TE[:, :], in_=TZ[:, :], func=AF.Exp, scale=-1.0), reads=["TZ"], writes=["TE"])
            S.op("pool", lambda e: e.tensor_scalar(out=TS[:, :], in0=TE[:, :], scalar1=1.0, scalar2=None, op0=ALU.add), reads=["TE"], writes=["TS"])
            S.op("dve", lambda e: e.reciprocal(out=TS[:, :], in_=TS[:, :]), reads=["TS"], writes=["TS"])
            S.op("dve", lambda e: e.tensor_scalar(out=TF[:, :], in0=TS[:, :], scalar1=LB[:, 1:2], scalar2=LB[:, 0:1], op0=ALU.mult, op1=ALU.add),
                 reads=["TS", "LB"], writes=["TF"])
            if stage < 4.1:
                return
            S.op("act", lambda e: e.activation(out=TF[:, :], in_=TF[:, :], func=AF.Ln), reads=["TF"], writes=["TF"])
            S.op("dve", lambda e: e.scalar_tensor_tensor(out=TK[:, :], in0=TE[:, :], scalar=LB[:, 1:2], in1=TS[:, :], op0=ALU.mult, op1=ALU.mult),
                 reads=["TE", "TS", "LB"], writes=["TK"])
            S.op("dve", lambda e: e.tensor_tensor_scan(out=TB[:, :], data0=RESET, data1=TF[:, :], initial=0.0, op0=ALU.mult, op1=ALU.add),
                 reads=["TF", "CF"], writes=["TB"])
            if stage < 4.2:
                return
            TBv = TB[:, :].rearrange("p (c n) -> p c n", n=64)
            for c in range(8):
                S.op("dve", lambda e, c=c: e.tensor_scalar(out=TD[:, c * 64:(c + 1) * 64], in0=TB[:, c * 64:(c + 1) * 64], scalar1=TB[:, c * 64 + 31:c * 64 + 32],
                                                           scalar2=None, op0=ALU.subtract), reads=["TB"], writes=["TD"])
            if stage < 4.3:
                return
            S.op("act", lambda e: e.activation(out=TE1[:, :], in_=TD[:, :], func=AF.Exp), reads=["TD"], writes=["TE1"])
            S.op("act", lambda e: e.activation(out=TE2[:, :], in_=TD[:, :], func=AF.Exp, scale=-1.0), reads=["TD"], writes=["TE2"])
            S.op("dve", lambda e: e.tensor_tensor(out=QTB[:, :], in0=TQ[:, :], in1=TE1[:, :], op=ALU.mult), reads=["TQ", "TE1"], writes=["QTB"])
            S.op("pool", lambda e: e.tensor_tensor(out=KTB[:, :], in0=TK[:, :], in1=TE2[:, :], op=ALU.mult), reads=["TK", "TE2"], writes=["KTB"])
            if stage < 4.4:
                return
            S.op("act", lambda e: e.activation(out=EBM[:, :], in_=TBv[:, :, 31], func=AF.Exp), reads=["TB"], writes=["EBM"])
            S.op("act", lambda e: e.activation(out=EBL[:, :], in_=TBv[:, :, 63], func=AF.Exp), reads=["TB"], writes=["EBL"])
            if stage < 4.5:
                return
            S.op("dve", lambda e: e.tensor_tensor(out=BDM[:, :], in0=TBv[:, :, 63], in1=TBv[:, :, 31], op=ALU.subtract), reads=["TB"], writes=["BDM"])
            S.op("act", lambda e: e.activation(out=EBLM[:, :], in_=BDM[:, :], func=AF.Exp), reads=["BDM"], writes=["EBLM"])
            ESC = ["EBM", "EBL", "EBLM"]

            if stage < 4.65:
                return
            for b in range(4):
                bs = slice(b * 128, (b + 1) * 128)
                trp = PS[2][:, 448:512].bitcast(BF16)
                S.op("pe", lambda e, bs=bs, trp=trp: e.transpose(out=trp, in_=KTB[:, bs], identity=IDENT), reads=["KTB", "CB"], writes=[("P", 2)])
                S.op("act", lambda e, b=b, trp=trp: e.activation(out=KTOK[:, b, :], in_=trp, func=AF.Copy), reads=[("P", 2)], writes=[("KTOK", b)])
                if stage < 4.7:
                    return
                S.op("pe", lambda e, bs=bs: e.matmul(PS[2][:, 0:128], lhsT=KTB[:, bs], rhs=QTB[:, bs], start=True, stop=True),
                     reads=["KTB", "QTB"], writes=[("P", 2)])
                if stage == 4.75:
                    return
                if stage == 4.76:
                    S.op("dve", lambda e: e.tensor_copy(out=T1[:, :], in_=PS[2][:, 0:128]), reads=[("P", 2)], writes=["T1"])
                    return
                if stage == 4.77:
                    S.op("act", lambda e: e.activation(out=T1[:, :], in_=PS[2][:, 0:128], func=AF.Copy), reads=[("P", 2)], writes=["T1"])
                    S.op("dve", lambda e: e.tensor_tensor(out=SCB[:, :], in0=T1[:, :], in1=BDMASK, op=ALU.mult), reads=["T1", "CB"], writes=["SCB"])
                    return
                if stage == 4.78:
                    S.op("dve", lambda e: e.tensor_tensor(out=T1[:, :], in0=PS[2][:, 0:128], in1=HNW[:, :], op=ALU.mult), reads=[("P", 2), "HNW"], writes=["T1"])
                    return
                S.op("act", lambda e: e.activation(out=TSC[:, :], in_=PS[2][:, 0:128], func=AF.Copy), reads=[("P", 2)], writes=["TSC"])
                S.op("dve", lambda e: e.tensor_tensor(out=SCB[:, :], in0=TSC[:, :], in1=BDMASK, op=ALU.mult), reads=["TSC", "CB"], writes=["SCB"])
                if stage < 4.8:
                    return
                for c2 in range(2):
                    rs = slice(c2 * 64, (c2 + 1) * 64)
                    S.op("pe", lambda e, b=b, c2=c2, rs=rs: e.matmul(PS[3 - 2 * c2][:, 0:128], lhsT=KTOK[rs, b, :], rhs=VTOK[rs, b, :], start=True, stop=True),
                         reads=[("KTOK", b), ("VTOK", b)], writes=[("P", 3 - 2 * c2)])
                if stage < 4.9:
                    return
                for c2 in range(2):
                    c = 2 * b + c2
                    S.op("dve", lambda e, c=c, c2=c2: e.tensor_scalar(out=SM[:, c2, :], in0=ST[:, :], scalar1=EBM[:, c:c + 1], scalar2=None, op0=ALU.mult),
                         reads=["ST"] + ESC, writes=[("SM", c2)])
                    S.op("dve", lambda e, c=c, c2=c2: e.tensor_scalar(out=TKV[:, :], in0=PS[3 - 2 * c2][:, 0:128], scalar1=EBLM[:, c:c + 1], scalar2=None, op0=ALU.mult),
                         reads=[("P", 3 - 2 * c2)] + ESC, writes=["TKV"])
                    S.op("dve", lambda e, c=c: e.scalar_tensor_tensor(out=ST[:, :], in0=ST[:, :], scalar=EBL[:, c:c + 1], in1=TKV[:, :], op0=ALU.mult, op1=ALU.add),
                         reads=["ST", "TKV"] + ESC, writes=["ST"])
                if stage < 4.92:
                    return
                S.op("pe", lambda e, b=b: e.matmul(PS[0][:, 0:128], lhsT=SCB[:, :], rhs=VTOK[:, b, :], start=True, stop=False),
                     reads=["SCB", ("VTOK", b)], writes=[("P", 0)])
                for c2 in range(2):
                    S.op("pe", lambda e, b=b, c2=c2: e.matmul(PS[0][c2 * 64:(c2 + 1) * 64, 0:128], lhsT=QTB[:, b * 128 + c2 * 64:b * 128 + (c2 + 1) * 64], rhs=SM[:, c2, :],
                                                              start=False, stop=True), reads=["QTB", ("SM", c2)], writes=[("P", 0)])
                if stage < 4.94:
                    return
                ob_ = (4 * T + b) % 2
                S.op("act", lambda e: e.activation(out=T1[:, :], in_=PS[0][:, 0:128], func=AF.Square, accum_out=SSO[:, 0:1]), reads=[("P", 0)], writes=["T1", "SSO"])
                rstd(SSO[:, 0:1], RO[:, 0:1], "SSO", "RO", 128)
                if stage < 4.96:
                    return
                S.op("act", lambda e, b=b: e.activation(out=EG[:, :], in_=GTOK[:, b, :], func=AF.Exp, scale=-1.0), reads=[("GTOK", b)], writes=["EG"])
                S.op("pool", lambda e: e.tensor_scalar(out=EG[:, :], in0=EG[:, :], scalar1=1.0, scalar2=None, op0=ALU.add), reads=["EG"], writes=["EG"])
                S.op("dve", lambda e: e.reciprocal(out=EG[:, :], in_=EG[:, :]), reads=["EG"], writes=["EG"])
                S.op("pool", lambda e, b=b: e.tensor_tensor(out=GG[:, :], in0=GTOK[:, b, :], in1=EG[:, :], op=ALU.mult), reads=["EG", ("GTOK", b)], writes=["GG"])
                S.op("dve", lambda e: e.scalar_tensor_tensor(out=T1[:, :], in0=PS[0][:, 0:128], scalar=RO[:, 0:1], in1=HNW[:, :], op0=ALU.mult, op1=ALU.mult),
                     reads=[("P", 0), "RO", "HNW"], writes=["T1"])
                S.op("pool", lambda e, ob_=ob_: e.tensor_tensor(out=OAT[:, ob_, :], in0=T1[:, :], in1=GG[:, :], op=ALU.mult), reads=["T1", "GG"], writes=[("OAT", ob_)])
                S.op("sp", lambda e, ob_=ob_, b=b: e.dma_start(out=oAv[:, 4 * T + b, :], in_=OAT[:, ob_, :]), reads=[("OAT", ob_)], dma="oa%d" % ob_)

            if stage < 6:
                return
            nkb = 4 * T + 4
            for j in range(nkb):
                lo = 0 if j < 4 * T else 128 * (j - 4 * T)
                sp_ = j % 2
                sps = PS[4 + sp_]
                S.op("pe", lambda e, j=j, lo=lo, sps=sps: e.matmul(sps[:, lo:512], lhsT=KT[:, j * 128:(j + 1) * 128], rhs=QT[:, lo:512], start=True, stop=False),
                     reads=[("KT", j // 4), "QT"], writes=[("P", 4 + sp_)])
                diag = j >= 4 * T
                S.op("pe", lambda e, lo=lo, sps=sps, diag=diag: e.matmul(sps[:, lo:512], lhsT=ONESB[0:1, :], rhs=CROW[0:1, lo:512], start=False, stop=(not diag)),
                     reads=["CROW", "CB"], writes=[("P", 4 + sp_)])
                if diag:
                    S.op("pe", lambda e, lo=lo, sps=sps: e.matmul(sps[:, lo:lo + 128], lhsT=IDENT, rhs=MASKNEG, start=False, stop=True),
                         reads=["CB"], writes=[("P", 4 + sp_)])
                S.op("act", lambda e, j=j, lo=lo, sps=sps, sp_=sp_: e.activation(out=PT[:, sp_, lo:512], in_=sps[:, lo:512], func=AF.Exp, bias=BIAS[:, j:j + 1]),
                     reads=[("P", 4 + sp_), "BIAS"], writes=[("PT", sp_)])
                S.op("pe", lambda e, j=j, lo=lo, sp_=sp_: e.matmul(PS[6][:, lo:512], lhsT=V[:, j, :], rhs=PT[:, sp_, lo:512], start=(j == 0), stop=(j == nkb - 1)),
                     reads=[("V", j), ("PT", sp_)], writes=[("P", 6)])
                S.op("pe", lambda e, j=j, lo=lo, sp_=sp_: e.matmul(PS[7][:, lo:512], lhsT=ONESB, rhs=PT[:, sp_, lo:512], start=(j == 0), stop=(j == nkb - 1)),
                     reads=["CB", ("PT", sp_)], writes=[("P", 7)])
            S.op("dve", lambda e: e.reciprocal(out=ODEN[:, :], in_=PS[7][:, :]), reads=[("P", 7)], writes=["ODEN"])
            S.op("dve", lambda e: e.tensor_tensor(out=OOUT[:, :], in0=PS[6][:, :], in1=ODEN[:, :], op=ALU.mult), reads=[("P", 6), "ODEN"], writes=["OOUT"])
            S.op("sp", lambda e: e.dma_start(out=oBT[:, tsl], in_=OOUT[:, :]), reads=["OOUT"], dma="ob")

        for T in range(NTILES):
            tile_body(T)
        S.barrier()
        S.emit(nc, block, st)
    return nc


def prep_l1(inp, ntok=SEQ):
    x = np.asarray(inp["x"], np.float32)[0]
    xT = np.ascontiguousarray(x[:ntok].T)
    w_in = np.asarray(inp["w_in"], np.float32)[0]
    nv1 = _colvec(inp["norm_mix_pre"][0])
    p = np.arange(128)
    ones = np.ones((128, 128), np.float32)
    ident = np.eye(128, dtype=np.float32)
    maskneg = np.where(p[:, None] > p[None, :], -30000.0, 0.0).astype(np.float32)
    bdmask = ((p[:, None] // 64 == p[None, :] // 64) & (p[:, None] <= p[None, :])).astype(np.float32)
    cb = np.concatenate([ones, ident, maskneg, bdmask], axis=1).astype(ml_dtypes.bfloat16)
    reset = np.tile((np.arange(512) % 64 != 0).astype(np.float32)[None, :], (128, 1))
    cf = np.ascontiguousarray(np.concatenate([reset, np.ones((128, 512), np.float32)], axis=1))
    lbl_all = np.asarray(inp["hgrn_lb_logits"], np.float32)
    hnw = np.ascontiguousarray(np.tile(np.asarray(inp["hgrn_norm_w"], np.float32)[0][None, :], (128, 1)))
    maps = []
    for c in range(NCORES):
        cs = lambda base: w_in[:, base + c * 128: base + (c + 1) * 128]
        w1 = np.ascontiguousarray(np.concatenate([cs(0), cs(1024), cs(4096), cs(5120), cs(2048), cs(3072), cs(6144)], axis=1))
        wf = np.ascontiguousarray(np.repeat(w_in[:, 7168 + c: 7169 + c], 2, axis=1))
        lbl = np.ascontiguousarray(lbl_all[:, c * 128:(c + 1) * 128].T)
        bfx = np.full((128, 1), np.asarray(inp["b_fox_f"], np.float32)[0, c], np.float32)
        maps.append(dict(xT=xT, w1=w1, wf=wf, nv1=nv1, lbl=lbl, hnw=hnw, bfx=bfx, cb=cb, cf=cf))
    return maps


def post_l1(results):
    oA = np.concatenate([r["oA"] for r in results], axis=1)
    oB = np.concatenate([np.ascontiguousarray(r["oBT"].T) for r in results], axis=1)
    return oA, oB


_CACHE = {}


def kernel(**inputs):
    inputs = {k: np.asarray(v) for k, v in inputs.items()}
    if "l1" not in _CACHE:
        _CACHE["l1"] = build_l1(SEQ // 512)
        _CACHE["l2"] = build_l2()
    cores = list(range(NCORES))
    r1 = run_bass_kernel_spmd(_CACHE["l1"], prep_l1(inputs), core_ids=cores)
    oA, oB = post_l1(r1.results)
    r2 = run_bass_kernel_spmd(_CACHE["l2"], prep_l2(inputs, oA, oB), core_ids=cores)
    return post_l2(r2.results).astype(np.float32)
```

```python
import contextlib
import numpy as np
import ml_dtypes
import concourse.bass as bass
import concourse.mybir as mybir
from concourse.bass_utils import run_bass_kernel_spmd

F32 = mybir.dt.float32
BF16 = mybir.dt.bfloat16
ALU = mybir.AluOpType
AF = mybir.ActivationFunctionType

D_MODEL = 2048
SEQ = 8192
D_FF = 5632
EPS = 1e-6
NCORES = 8

ENGS = ("pe", "act", "dve", "pool", "sp")


class Sched:
    def __init__(self):
        self.ops = {e: [] for e in ENGS}
        self.lastw = {}
        self.readers = {}
        self.dma_cnt = {}

    def _tok(self, eng, rec, idx):
        if rec["dma"] is not None:
            return ("dma", rec["dma"], rec["dma_val"])
        return ("eng", eng, idx)

    def op(self, eng, fn, reads=(), writes=(), dma=None):
        idx = len(self.ops[eng])
        rec = dict(fn=fn, dma=dma, needed=False, deps={})
        if dma is not None:
            self.dma_cnt[dma] = self.dma_cnt.get(dma, 0) + 1
            rec["dma_val"] = 16 * self.dma_cnt[dma]
        tok = self._tok(eng, rec, idx)
        deps = rec["deps"]

        def add(t):
            if t is None:
                return
            if t[0] == "dma":
                cur = self.dma_cnt[t[1]] - (1 if dma == t[1] else 0)
                t = ("dma", t[1], 16 * cur)
            if t[0] == "eng" and t[1] == "pe" and eng == "pe" and dma is None:
                return
            k = (t[0], t[1])
            if k not in deps or deps[k] < t[2]:
                deps[k] = t[2]

        for key in reads:
            add(self.lastw.get(key))
        for key in writes:
            add(self.lastw.get(key))
            for t in self.readers.get(key, {}).values():
                add(t)
        self.ops[eng].append(rec)
        for key in writes:
            self.lastw[key] = tok
            self.readers[key] = {}
        for key in reads:
            r = self.readers.setdefault(key, {})
            k = (tok[0], tok[1])
            if k not in r or r[k][2] < tok[2]:
                r[k] = tok
        return tok

    def barrier(self):
        toks = []
        for e in ENGS:
            for i in range(len(self.ops[e]) - 1, -1, -1):
                if self.ops[e][i]["dma"] is None and self.ops[e][i]["fn"] is not None:
                    toks.append(("eng", e, i))
                    break
        for k, c in self.dma_cnt.items():
            toks.append(("dma", k, 16 * c))
        for e in ENGS:
            rec = dict(fn=None, dma=None, needed=False, deps={})
            for t in toks:
                rec["deps"][(t[0], t[1])] = t[2]
            self.ops[e].append(rec)
        self.lastw = {}
        self.readers = {}

    def emit(self, nc, block, stack):
        sems = {}
        for e in ENGS:
            sems[("eng", e)] = stack.enter_context(nc.semaphore("s_" + e))
        for k in self.dma_cnt:
            sems[("dma", k)] = stack.enter_context(nc.semaphore("d_" + str(k)))
        for e in ENGS:
            for rec in self.ops[e]:
                for (kind, name), val in rec["deps"].items():
                    if kind == "eng":
                        self.ops[name][val]["needed"] = True
        vals = {}
        for e in ENGS:
            cnt = 0
            for i, rec in enumerate(self.ops[e]):
                if rec["needed"]:
                    cnt += 1
                    vals[(e, i)] = cnt
            assert cnt < 60000, (e, cnt)
        self.n_emitted = {e: len(self.ops[e]) for e in ENGS}

        def run(e, engine):
            waited = {}
            for i, rec in enumerate(self.ops[e]):
                for (kind, name), val in rec["deps"].items():
                    v = vals[(name, val)] if kind == "eng" else val
                    sk = (kind, name)
                    if waited.get(sk, 0) >= v:
                        continue
                    engine.wait_ge(sems[sk], v)
                    waited[sk] = v
                if rec["fn"] is None:
                    continue
                ins = rec["fn"](engine)
                if rec["dma"] is not None:
                    ins.then_inc(sems[("dma", rec["dma"])], 16)
                elif rec["needed"]:
                    ins.then_inc(sems[("eng", e)], 1)

        @block.tensor
        def _(eng):
            run("pe", eng)

        @block.scalar
        def _(eng):
            run("act", eng)

        @block.vector
        def _(eng):
            run("dve", eng)

        @block.gpsimd
        def _(eng):
            run("pool", eng)

        @block.sync
        def _(eng):
            run("sp", eng)


def _v3(ap, k):
    return ap.rearrange("p (k n) -> p k n", k=k)


def build_l2(NT=1024):
    nc = bass.Bass("TRN2", target_bir_lowering=False)
    S = Sched()
    NTT = NT // 512
    dt = nc.dram_tensor
    xT = dt("xT", [2048, NT], F32, kind="ExternalInput").ap()
    oT = dt("oT", [2048, NT], F32, kind="ExternalInput").ap()
    wg = dt("wg", [2048, 4096], F32, kind="ExternalInput").ap()
    wua = dt("wua", [1024, 2048], F32, kind="ExternalInput").ap()
    wub = dt("wub", [1024, 2048], F32, kind="ExternalInput").ap()
    wo = dt("wo", [2048, 2048], F32, kind="ExternalInput").ap()
    wfi = dt("wfi", [2048, 2 * D_FF], F32, kind="ExternalInput").ap()
    wfd = dt("wfd", [D_FF, 2048], F32, kind="ExternalInput").ap()
    nv = dt("nv", [128, 64], F32, kind="ExternalInput").ap()
    cst = dt("cst", [128, 128], BF16, kind="ExternalInput").ap()
    outT = dt("outT", [2048, NT], F32, kind="ExternalOutput").ap()
    zd = dt("zd", [2048, NT], F32).ap()
    x1d = dt("x1d", [2048, NT], F32).ap()

    def cm(ap):
        return ap.rearrange("(k p) n -> p k n", p=128)

    xTv, oTv, wgv, wuav, wubv, wov, wfiv, wfdv = map(cm, (xT, oT, wg, wua, wub, wo, wfi, wfd))
    outTv, zdv, x1dv = cm(outT), cm(zd), cm(x1d)

    with contextlib.ExitStack() as st:
        sb = lambda name, shape, d: st.enter_context(nc.sbuf_tensor(name, shape, d))
        A = sb("A", [128, 16 * NT * 2], BF16)
        B = sb("B", [128, 16 * NT], BF16)
        W = sb("W", [128, 24576], BF16)
        T = sb("T", [128, 8, 512], F32)
        XS = sb("XS", [128, 16, 256], F32)
        SQ = sb("SQ", [128, 16, 256], BF16)
        SQT = sb("SQT", [128, 4, 512], BF16)
        R = sb("R", [128, 2, NT], F32)
        NV = sb("NV", [128, 64], F32)
        ONES = sb("ONES", [128, 128], BF16)
        PS = [st.enter_context(nc.psum_tensor("ps%d" % i, [128, 512], F32)) for i in range(8)]
        block = st.enter_context(nc.Block())

        xb = _v3(A[:, 0:16 * NT], 16)
        ob = _v3(A[:, 16 * NT:32 * NT], 16)
        hid = _v3(A[:, 0:44 * 512], 44)
        mb = _v3(B[:, :], 16)
        x1b = mb

        S.op("sp", lambda e: e.dma_start(out=NV[:, :], in_=nv[:, :]), writes=["NV"], dma="c0")
        S.op("sp", lambda e: e.dma_start(out=ONES[:, :], in_=cst[:, :]), writes=["ONES"], dma="c1")

        def rstd_from(ps_ap, r_ap, keyp, keyr):
            S.op("dve", lambda e: e.tensor_scalar(out=r_ap, in0=ps_ap, scalar1=1.0 / D_MODEL, scalar2=EPS,
                                                  op0=ALU.mult, op1=ALU.add), reads=[keyp], writes=[keyr])
            S.op("act", lambda e: e.activation(out=r_ap, in_=r_ap, func=AF.Sqrt), reads=[keyr], writes=[keyr])
            S.op("dve", lambda e: e.reciprocal(out=r_ap, in_=r_ap), reads=[keyr], writes=[keyr])

        for s in range(NT // 256):
            cs = slice(s * 256, (s + 1) * 256)
            S.op("sp", lambda e, cs=cs: e.dma_start(out=XS[:, :, :], in_=xTv[:, :, cs]), writes=["XS"], dma="xs")
            S.op("act", lambda e: e.activation(out=SQ[:, :, :], in_=XS[:, :, :], func=AF.Square), reads=["XS"], writes=["SQ"])
            for k in range(16):
                S.op("dve", lambda e, k=k, cs=cs: e.tensor_scalar(out=xb[:, k, cs], in0=XS[:, k, :], scalar1=NV[:, k:k + 1],
                                                                  scalar2=None, op0=ALU.mult),
                     reads=["XS", "NV"], writes=[("A0", k, s)])
            for k in range(16):
                S.op("pe", lambda e, k=k: e.matmul(PS[0][:, 0:256], lhsT=ONES[:, :], rhs=SQ[:, k, :], start=(k == 0), stop=(k == 15)),
                     reads=["SQ", "ONES"], writes=[("P", 0)])
            rstd_from(PS[0][:, 0:256], R[:, 0, cs], ("P", 0), ("R0", s))
        for h in range(2):
            S.op("pool", lambda e, h=h: e.dma_start(out=ob[:, 8 * h:8 * h + 8, :], in_=oTv[:, 8 * h:8 * h + 8, :]),
                 writes=[("A1", h)], dma="ob%d" % h)
        xb_keys = [("A0", k, s) for k in range(16) for s in range(NT // 256)]
        r0_keys = [("R0", s) for s in range(NT // 256)]

        for g in range(8):
            s_ = g % 2
            base = s_ * 12288
            wga = _v3(W[:, base:base + 4096], 16)
            wgb = _v3(W[:, base + 4096:base + 8192], 16)
            wa = _v3(W[:, base + 8192:base + 10240], 8)
            wb = _v3(W[:, base + 10240:base + 12288], 8)
            gs = slice(g * 256, (g + 1) * 256)
            gs2 = slice(2048 + g * 256, 2048 + (g + 1) * 256)
            wk = ("W", s_)
            dk = "w%d" % s_
            S.op("pool", lambda e, wga=wga, gs=gs: e.dma_start(out=wga[:, :, :], in_=wgv[:, :, gs]), writes=[wk + (0,)], dma=dk + "a")
            S.op("pool", lambda e, wgb=wgb, gs2=gs2: e.dma_start(out=wgb[:, :, :], in_=wgv[:, :, gs2]), writes=[wk + (1,)], dma=dk + "b")
            S.op("pool", lambda e, wa=wa, gs=gs: e.dma_start(out=wa[:, :, :], in_=wuav[:, :, gs]), writes=[wk + (2,)], dma=dk + "c")
            S.op("pool", lambda e, wb=wb, gs=gs: e.dma_start(out=wb[:, :, :], in_=wubv[:, :, gs]), writes=[wk + (3,)], dma=dk + "d")
            for nb in range(2):
                n = g * 2 + nb
                ns = slice(nb * 128, (nb + 1) * 128)
                for t in range(NTT):
                    ts_ = slice(t * 512, (t + 1) * 512)
                    par = (n * NTT + t) % 2
                    pga, pgb, pya, pyb = (PS[4 * par + i] for i in range(4))
                    kga, kgb, kya, kyb = (("P", 4 * par + i) for i in range(4))
                    t0, t1 = T[:, 2 * par, :], T[:, 2 * par + 1, :]
                    k0, k1 = ("T", 2 * par), ("T", 2 * par + 1)
                    for k in range(16):
                        S.op("pe", lambda e, k=k, pga=pga, wga=wga, ns=ns, ts_=ts_: e.matmul(pga[:, :], lhsT=wga[:, k, ns], rhs=xb[:, k, ts_], start=(k == 0), stop=(k == 15)),
                             reads=[wk + (0,)] + xb_keys, writes=[kga])
                    for k in range(16):
                        S.op("pe", lambda e, k=k, pgb=pgb, wgb=wgb, ns=ns, ts_=ts_: e.matmul(pgb[:, :], lhsT=wgb[:, k, ns], rhs=xb[:, k, ts_], start=(k == 0), stop=(k == 15)),
                             reads=[wk + (1,)] + xb_keys, writes=[kgb])
                    for k in range(8):
                        S.op("pe", lambda e, k=k, pya=pya, wa=wa, ns=ns, ts_=ts_: e.matmul(pya[:, :], lhsT=wa[:, k, ns], rhs=ob[:, k, ts_], start=(k == 0), stop=(k == 7)),
                             reads=[wk + (2,), ("A1", 0)], writes=[kya])
                    for k in range(8):
                        S.op("pe", lambda e, k=k, pyb=pyb, wb=wb, ns=ns, ts_=ts_: e.matmul(pyb[:, :], lhsT=wb[:, k, ns], rhs=ob[:, 8 + k, ts_], start=(k == 0), stop=(k == 7)),
                             reads=[wk + (3,), ("A1", 1)], writes=[kyb])
                    S.op("dve", lambda e, t0=t0, pga=pga, ts_=ts_: e.tensor_tensor(out=t0, in0=pga[:, :], in1=R[:, 0, ts_], op=ALU.mult),
                         reads=[kga] + r0_keys, writes=[k0])
                    S.op("act", lambda e, t0=t0: e.activation(out=t0, in_=t0, func=AF.Sigmoid), reads=[k0], writes=[k0])
                    S.op("dve", lambda e, t1=t1, pgb=pgb, ts_=ts_: e.tensor_tensor(out=t1, in0=pgb[:, :], in1=R[:, 0, ts_], op=ALU.mult),
                         reads=[kgb] + r0_keys, writes=[k1])
                    S.op("act", lambda e, t1=t1: e.activation(out=t1, in_=t1, func=AF.Sigmoid), reads=[k1], writes=[k1])
                    S.op("dve", lambda e, t0=t0, pya=pya: e.tensor_tensor(out=t0, in0=pya[:, :], in1=t0, op=ALU.mult),
                         reads=[kya, k0], writes=[k0])
                    S.op("dve", lambda e, t1=t1, pyb=pyb: e.tensor_tensor(out=t1, in0=pyb[:, :], in1=t1, op=ALU.mult),
                         reads=[kyb, k1], writes=[k1])
                    S.op("dve", lambda e, t0=t0, t1=t1, n=n, ts_=ts_: e.tensor_tensor(out=mb[:, n, ts_], in0=t0, in1=t1, op=ALU.add),
                         reads=[k0, k1], writes=[("B", n, t)])

        S.barrier()
        def evac_stats(ps, pkey, nchunk, t, ssb, first, last, dram_v, par):
            zt = T[:, 4 + par, :]
            zk = ("T", 4 + par)
            sq = SQT[:, par, :]
            sk = ("SQT", par)
            S.op("act", lambda e: e.activation(out=zt, in_=ps[:, :], func=AF.Copy), reads=[pkey], writes=[zk])
            S.op("act", lambda e: e.activation(out=sq, in_=ps[:, :], func=AF.Square), reads=[pkey], writes=[sk])
            S.op("pe", lambda e: e.matmul(PS[ssb][:, :], lhsT=ONES[:, :], rhs=sq, start=first, stop=last),
                 reads=[sk, "ONES"], writes=[("P", ssb)])
            S.op("sp", lambda e: e.dma_start(out=dram_v[:, nchunk, t * 512:(t + 1) * 512], in_=zt), reads=[zk],
                 writes=[("zd", nchunk, t)], dma="zst%d" % par)

        for g in range(8):
            s_ = g % 2
            base = s_ * 12288
            wos = _v3(W[:, base:base + 4096], 16)
            wk = ("W", s_)
            gs = slice(g * 256, (g + 1) * 256)
            S.op("pool", lambda e, wos=wos, gs=gs: e.dma_start(out=wos[:, :, :], in_=wov[:, :, gs]), writes=[wk], dma="w%d" % s_)
            for nb in range(2):
                n = g * 2 + nb
                ns = slice(nb * 128, (nb + 1) * 128)
                for t in range(NTT):
                    ts_ = slice(t * 512, (t + 1) * 512)
                    pb = (n * NTT + t) % 4
                    for k in range(16):
                        S.op("pe", lambda e, k=k, pb=pb, wos=wos, ns=ns, ts_=ts_: e.matmul(PS[pb][:, :], lhsT=wos[:, k, ns], rhs=mb[:, k, ts_], start=(k == 0), stop=(k == 15)),
                             reads=[wk] + [("B", k, t) for k in range(16)], writes=[("P", pb)])
                    evac_stats(PS[pb], ("P", pb), n, t, 6 + t, n == 0, n == 15, zdv, (n * NTT + t) % 2)
        for t in range(NTT):
            rstd_from(PS[6 + t][:, :], R[:, 1, t * 512:(t + 1) * 512], ("P", 6 + t), ("R1", t))

        def resid_pass(t, src_v, res_v, resk, rrow, rkey, gain_off, dst_v, dstk, dst_dma, make_next):
            ts_ = slice(t * 512, (t + 1) * 512)
            for n in range(16):
                par = n % 2
                ta, tb = T[:, par, :], T[:, 2 + par, :]
                ka, kb = ("T", par), ("T", 2 + par)
                S.op("sp", lambda e, ta=ta, n=n: e.dma_start(out=ta, in_=res_v[:, n, ts_]), reads=[(resk, n, t)], writes=[ka], dma="ra%d" % par)
                S.op("sp", lambda e, tb=tb, n=n: e.dma_start(out=tb, in_=src_v[:, n, ts_]), reads=[("zd", n, t)], writes=[kb], dma="rb%d" % par)
                S.op("dve", lambda e, tb=tb, n=n: e.scalar_tensor_tensor(out=tb, in0=tb, scalar=NV[:, gain_off + n:gain_off + n + 1], in1=R[:, rrow, ts_],
                                                                           op0=ALU.mult, op1=ALU.mult), reads=[kb, "NV", (rkey, t)], writes=[kb])
                S.op("dve", lambda e, ta=ta, tb=tb: e.tensor_tensor(out=ta, in0=ta, in1=tb, op=ALU.add), reads=[ka, kb], writes=[ka])
                S.op("sp", lambda e, ta=ta, n=n: e.dma_start(out=dst_v[:, n, ts_], in_=ta), reads=[ka], writes=[(dstk, n, t)], dma=dst_dma)
                if make_next:
                    sq = SQT[:, 2 + par, :]
                    sk = ("SQT", 2 + par)
                    S.op("dve", lambda e, ta=ta, n=n: e.tensor_scalar(out=x1b[:, n, ts_], in0=ta, scalar1=NV[:, 32 + n:33 + n], scalar2=None, op0=ALU.mult),
                         reads=[ka, "NV"], writes=[("B", n, t)])
                    S.op("act", lambda e, ta=ta, sq=sq: e.activation(out=sq, in_=ta, func=AF.Square), reads=[ka], writes=[sk])
                    S.op("pe", lambda e, sq=sq, n=n: e.matmul(PS[4 + t][:, :], lhsT=ONES[:, :], rhs=sq, start=(n == 0), stop=(n == 15)),
                         reads=[sk, "ONES"], writes=[("P", 4 + t)])

        S.barrier()
        for t in range(NTT):
            resid_pass(t, zdv, xTv, "xin", 1, "R1", 16, x1dv, "x1d", "x1st", True)
        for t in range(NTT):
            rstd_from(PS[4 + t][:, :], R[:, 0, t * 512:(t + 1) * 512], ("P", 4 + t), ("R2", t))

        S.barrier()

        def ffn_tile(t):
            ts_ = slice(t * 512, (t + 1) * 512)
            for g in range(22):
                s_ = g % 2
                base = s_ * 12288
                wfg = _v3(W[:, base:base + 4096], 16)
                wfu = _v3(W[:, base + 4096:base + 8192], 16)
                wk = ("W", s_)
                gs = slice(g * 256, (g + 1) * 256)
                gs2 = slice(D_FF + g * 256, D_FF + (g + 1) * 256)
                S.op("pool", lambda e, wfg=wfg, gs=gs: e.dma_start(out=wfg[:, :, :], in_=wfiv[:, :, gs]), writes=[wk + (0,)], dma="w%da" % s_)
                S.op("pool", lambda e, wfu=wfu, gs2=gs2: e.dma_start(out=wfu[:, :, :], in_=wfiv[:, :, gs2]), writes=[wk + (1,)], dma="w%db" % s_)
                for nb in range(2):
                    f = g * 2 + nb
                    ns = slice(nb * 128, (nb + 1) * 128)
                    par = f % 2
                    pg_, pu_ = PS[2 * par], PS[2 * par + 1]
                    kg_, ku_ = ("P", 2 * par), ("P", 2 * par + 1)
                    t0, t1 = T[:, 2 * par, :], T[:, 2 * par + 1, :]
                    k0, k1 = ("T", 2 * par), ("T", 2 * par + 1)
                    for k in range(16):
                        S.op("pe", lambda e, k=k, pg_=pg_, wfg=wfg, ns=ns: e.matmul(pg_[:, :], lhsT=wfg[:, k, ns], rhs=x1b[:, k, ts_], start=(k == 0), stop=(k == 15)),
                             reads=[wk + (0,)] + [("B", k, t) for k in range(16)], writes=[kg_])
                    for k in range(16):
                        S.op("pe", lambda e, k=k, pu_=pu_, wfu=wfu, ns=ns: e.matmul(pu_[:, :], lhsT=wfu[:, k, ns], rhs=x1b[:, k, ts_], start=(k == 0), stop=(k == 15)),
                             reads=[wk + (1,)] + [("B", k, t) for k in range(16)], writes=[ku_])
                    S.op("dve", lambda e, t0=t0, pg_=pg_: e.tensor_tensor(out=t0, in0=pg_[:, :], in1=R[:, 0, ts_], op=ALU.mult),
                         reads=[kg_, ("R2", t)], writes=[k0])
                    S.op("act", lambda e, t0=t0: e.activation(out=t0, in_=t0, func=AF.Silu), reads=[k0], writes=[k0])
                    S.op("dve", lambda e, t1=t1, pu_=pu_: e.tensor_tensor(out=t1, in0=pu_[:, :], in1=R[:, 0, ts_], op=ALU.mult),
                         reads=[ku_, ("R2", t)], writes=[k1])
                    S.op("dve", lambda e, t0=t0, t1=t1, f=f: e.tensor_tensor(out=hid[:, f, :], in0=t0, in1=t1, op=ALU.mult),
                         reads=[k0, k1], writes=[("hid", f)])
            hid_keys = [("hid", f) for f in range(44)]
            S.barrier()
            for g in range(8):
                s_ = g % 2
                base = s_ * 12288
                wd = _v3(W[:, base:base + 44 * 256], 44)
                wk = ("W", s_)
                gs = slice(g * 256, (g + 1) * 256)
                for hh in range(2):
                    S.op("pool", lambda e, wd=wd, gs=gs, hh=hh: e.dma_start(out=wd[:, 22 * hh:22 * hh + 22, :], in_=wfdv[:, 22 * hh:22 * hh + 22, gs]),
                         writes=[wk + (hh,)], dma="w%d%s" % (s_, "ab"[hh]))
                for nb in range(2):
                    n = g * 2 + nb
                    ns = slice(nb * 128, (nb + 1) * 128)
                    pb = 4 + n % 2
                    for k in range(44):
                        S.op("pe", lambda e, k=k, pb=pb, wd=wd, ns=ns: e.matmul(PS[pb][:, :], lhsT=wd[:, k, ns], rhs=hid[:, k, :], start=(k == 0), stop=(k == 43)),
                             reads=[wk + (0,), wk + (1,)] + hid_keys, writes=[("P", pb)])
                    evac_stats(PS[pb], ("P", pb), n, t, 6 + t, n == 0, n == 15, zdv, n % 2)
            rstd_from(PS[6 + t][:, :], R[:, 1, ts_], ("P", 6 + t), ("R3", t))
            S.barrier()
            resid_pass(t, zdv, x1dv, "x1d", 1, "R3", 48, outTv, "out", "ost", False)

        for t in range(NTT):
            ffn_tile(t)
        S.barrier()
        S.emit(nc, block, st)
    return nc


def _colvec(v):
    return np.ascontiguousarray(np.asarray(v, np.float32).reshape(16, 128).T)


def prep_l2(inp, oA, oB):
    x = np.asarray(inp["x"], np.float32)[0]
    w_in = np.asarray(inp["w_in"], np.float32)[0]
    wg = np.ascontiguousarray(w_in[:, 7176:7176 + 4096])
    nv = np.ascontiguousarray(np.concatenate([_colvec(inp["norm_mix_pre"][0]), _colvec(inp["norm_mix_post"][0]),
                                              _colvec(inp["norm_ffn_pre"][0]), _colvec(inp["norm_ffn_post"][0])], axis=1))
    ones = np.ones((128, 128), ml_dtypes.bfloat16)
    common = dict(wg=wg, wua=np.ascontiguousarray(inp["w_up_a"][0], dtype=np.float32), wub=np.ascontiguousarray(inp["w_up_b"][0], dtype=np.float32),
                  wo=np.ascontiguousarray(inp["w_o"][0], dtype=np.float32), wfi=np.ascontiguousarray(inp["w_ffn_in"][0], dtype=np.float32),
                  wfd=np.ascontiguousarray(inp["w_ffn_down"][0], dtype=np.float32), nv=nv, cst=ones)
    maps = []
    for c in range(NCORES):
        sl = slice(c * 1024, (c + 1) * 1024)
        oT = np.ascontiguousarray(np.concatenate([oA[sl].T, oB[sl].T], axis=0), dtype=np.float32)
        maps.append(dict(common, xT=np.ascontiguousarray(x[sl].T), oT=oT))
    return maps


def post_l2(results):
    return np.concatenate([np.ascontiguousarray(r["outT"].T) for r in results], axis=0)[None]


def build_l1(NTILES=16, stage=9):
    nc = bass.Bass("TRN2", target_bir_lowering=False)
    S = Sched()
    NTOK = NTILES * 512
    dt = nc.dram_tensor
    xT = dt("xT", [2048, NTOK], F32, kind="ExternalInput").ap()
    w1 = dt("w1", [2048, 896], F32, kind="ExternalInput").ap()
    wf = dt("wf", [2048, 2], F32, kind="ExternalInput").ap()
    nv1 = dt("nv1", [128, 16], F32, kind="ExternalInput").ap()
    lbl = dt("lbl", [128, 2], F32, kind="ExternalInput").ap()
    hnw = dt("hnw", [128, 128], F32, kind="ExternalInput").ap()
    bfx = dt("bfx", [128, 1], F32, kind="ExternalInput").ap()
    cb = dt("cb", [128, 512], BF16, kind="ExternalInput").ap()
    cf = dt("cf", [128, 1024], F32, kind="ExternalInput").ap()
    oA = dt("oA", [NTOK, 128], F32, kind="ExternalOutput").ap()
    oBT = dt("oBT", [128, NTOK], F32, kind="ExternalOutput").ap()
    xTv = xT.rearrange("(k p) n -> p k n", p=128)
    w1v = w1.rearrange("(k p) n -> p k n", p=128)
    wfv = wf.rearrange("(k p) n -> p k n", p=128)
    oAv = oA.rearrange("(b p) v -> p b v", p=128)

    with contextlib.ExitStack() as st:
        sb = lambda name, shape, d: st.enter_context(nc.sbuf_tensor(name, shape, d))
        XS = sb("XS", [128, 2, 8, 512], F32)
        XB = sb("XB", [128, 16, 512], BF16)
        SQ = sb("SQ", [128, 16, 512], BF16)
        W1 = sb("W1", [128, 16, 896], BF16)
        WF = sb("WF", [128, 16, 2], BF16)
        NV = sb("NV", [128, 16], F32)
        LBL = sb("LBL", [128, 2], F32)
        LB = sb("LB", [128, 2], F32)
        HNW = sb("HNW", [128, 128], F32)
        BFX = sb("BFX", [128, 1], F32)
        CB = sb("CB", [128, 512], BF16)
        CF = sb("CF", [128, 1024], F32)
        RT = sb("RT", [128, 512], F32)
        RC = sb("RC", [128, 4], F32)
        KT = sb("KT", [128, NTOK], BF16)
        V = sb("V", [128, NTILES * 4, 128], BF16)
        QT = sb("QT", [128, 2, 512], BF16)
        TQ = sb("TQ", [128, 512], F32)
        TZ = sb("TZ", [128, 512], F32)
        TE = sb("TE", [128, 512], F32)
        TS = sb("TS", [128, 512], F32)
        TF = sb("TF", [128, 512], F32)
        TK = sb("TK", [128, 512], F32)
        TB = sb("TB", [128, 512], F32)
        TD = sb("TD", [128, 512], F32)
        TE1 = sb("TE1", [128, 512], F32)
        TE2 = sb("TE2", [128, 512], F32)
        QTB = sb("QTB", [128, 512], BF16)
        KTB = sb("KTB", [128, 512], BF16)
        KTOK = sb("KTOK", [128, 4, 128], BF16)
        VTOK = sb("VTOK", [128, 2, 4, 128], BF16)
        GTOK = sb("GTOK", [128, 2, 4, 128], F32)
        EBM = sb("EBM", [128, 8], F32)
        EBL = sb("EBL", [128, 8], F32)
        EBLM = sb("EBLM", [128, 8], F32)
        BDM = sb("BDM", [128, 8], F32)
        ST = sb("ST", [128, 128], F32)
        SM = sb("SM", [128, 2, 128], BF16)
        SCB = sb("SCB", [128, 128], BF16)
        TKV = sb("TKV", [128, 128], F32)
        T1 = sb("T1", [128, 128], F32)
        TSC = sb("TSC", [128, 128], F32)
        EG = sb("EG", [128, 128], F32)
        GG = sb("GG", [128, 128], F32)
        OAT = sb("OAT", [128, 2, 128], F32)
        SSO = sb("SSO", [128, 1], F32)
        RO = sb("RO", [128, 1], F32)
        PT = sb("PT", [128, 2, 512], BF16)
        CROW = sb("CROW", [1, 2, 512], BF16)
        LROW = sb("LROW", [1, 512], F32)
        ZF = sb("ZF", [1, 512], F32)
        POSC = sb("POSC", [128, NTILES * 4], F32)
        RP = sb("RP", [128, NTILES + 1], F32)
        BIAS = sb("BIAS", [128, 2, NTILES * 4], F32)
        ODEN = sb("ODEN", [128, 512], F32)
        OOUT = sb("OOUT", [128, 512], F32)
        PS = [st.enter_context(nc.psum_tensor("ps%d" % i, [128, 512], F32)) for i in range(8)]
        block = st.enter_context(nc.Block())

        ONESB, IDENT, MASKNEG, BDMASK = CB[:, 0:128], CB[:, 128:256], CB[:, 256:384], CB[:, 384:512]
        RESET, ONESF = CF[:, 0:512], CF[:, 512:1024]

        def ld(eng, dst, src, key, dk):
            S.op(eng, lambda e: e.dma_start(out=dst, in_=src), writes=[key], dma=dk)

        ld("sp", NV[:, :], nv1[:, :], "NV", "c0")
        ld("sp", LBL[:, :], lbl[:, :], "LBL", "c1")
        ld("sp", HNW[:, :], hnw[:, :], "HNW", "c2")
        ld("sp", BFX[:, :], bfx[:, :], "BFX", "c3")
        ld("sp", CB[:, :], cb[:, :], "CB", "c4")
        ld("sp", CF[:, :], cf[:, :], "CF", "c5")
        ld("pool", W1[:, 0:8, :], w1v[:, 0:8, :], ("W1", 0), "w1a")
        ld("pool", W1[:, 8:16, :], w1v[:, 8:16, :], ("W1", 1), "w1b")
        ld("pool", WF[:, :, :], wfv[:, :, :], "WF", "wf")
        W1K = [("W1", 0), ("W1", 1)]

        S.op("dve", lambda e: e.tensor_tensor(out=LB[:, 0:1], in0=LBL[:, 0:1], in1=LBL[:, 1:2], op=ALU.subtract), reads=["LBL"], writes=["LB"])
        S.op("act", lambda e: e.activation(out=LB[:, 1:2], in_=LB[:, 0:1], func=AF.Exp, scale=-1.0), reads=["LB"], writes=["LB"])
        S.op("dve", lambda e: e.tensor_scalar(out=LB[:, 0:1], in0=LB[:, 1:2], scalar1=1.0, scalar2=None, op0=ALU.add), reads=["LB"], writes=["LB"])
        S.op("dve", lambda e: e.reciprocal(out=LB[:, 0:1], in_=LB[:, 0:1]), reads=["LB"], writes=["LB"])
        S.op("dve", lambda e: e.tensor_tensor(out=LB[:, 1:2], in0=LB[:, 1:2], in1=LB[:, 0:1], op=ALU.mult), reads=["LB"], writes=["LB"])
        S.op("dve", lambda e: e.memset(ST[:, :], 0.0), writes=["ST"])
        S.op("dve", lambda e: e.memset(RP[:, :], 0.0), writes=["RP"])

        def rstd(ps_ap, r_ap, pkey, rkey, n):
            S.op("dve", lambda e: e.tensor_scalar(out=r_ap, in0=ps_ap, scalar1=1.0 / n, scalar2=EPS, op0=ALU.mult, op1=ALU.add),
                 reads=[pkey], writes=[rkey])
            S.op("act", lambda e: e.activation(out=r_ap, in_=r_ap, func=AF.Ln), reads=[rkey], writes=[rkey])
            S.op("act", lambda e: e.activation(out=r_ap, in_=r_ap, func=AF.Exp, scale=-0.5), reads=[rkey], writes=[rkey])

        def gen_A(T):
            par = T % 2
            tsl = slice(T * 512, (T + 1) * 512)
            for h in range(2):
                S.op("sp", lambda e, h=h: e.dma_start(out=XS[:, h, :, :], in_=xTv[:, 8 * h:8 * h + 8, tsl]), writes=[("XS", h)], dma="xs%d" % h)
            yield
            for h in range(2):
                S.op("act", lambda e, h=h: e.activation(out=SQ[:, 8 * h:8 * h + 8, :], in_=XS[:, h, :, :], func=AF.Square),
                     reads=[("XS", h)], writes=[("SQ", h)])
                for k in range(8):
                    S.op("pool", lambda e, h=h, k=k: e.tensor_scalar(out=XB[:, 8 * h + k, :], in0=XS[:, h, k, :], scalar1=NV[:, 8 * h + k:8 * h + k + 1],
                                                                      scalar2=None, op0=ALU.mult), reads=[("XS", h), "NV"], writes=[("XB", h)])
                yield
            SQK = [("SQ", 0), ("SQ", 1)]
            XBK = [("XB", 0), ("XB", 1)]
            for k in range(16):
                S.op("pe", lambda e, k=k: e.matmul(PS[0][:, :], lhsT=ONESB, rhs=SQ[:, k, :], start=(k == 0), stop=(k == 15)),
                     reads=SQK + ["CB"], writes=[("P", 0)])
            rstd(PS[0][:, :], RT[:, :], ("P", 0), "RT", D_MODEL)
            yield
            for b in range(4):
                for k in range(16):
                    S.op("pe", lambda e, k=k, b=b: e.matmul(PS[0][:, 400 + b:401 + b], lhsT=SQ[:, k, b * 128:(b + 1) * 128], rhs=ONESB[:, 0:1],
                                                            start=(k == 0), stop=(k == 15)), reads=SQK + ["CB"], writes=[("P", 0)])
            rstd(PS[0][:, 400:404], RC[:, :], ("P", 0), "RC", D_MODEL)
            yield
            for j in range(4):
                for k in range(16):
                    S.op("pe", lambda e, k=k, j=j: e.matmul(PS[0][:, :], lhsT=W1[:, k, j * 128:(j + 1) * 128], rhs=XB[:, k, :],
                                                            start=(k == 0), stop=(k == 15)), reads=W1K + XBK, writes=[("P", 0)])
                if j == 0:
                    S.op("dve", lambda e: e.tensor_tensor(out=TQ[:, :], in0=PS[0][:, :], in1=RT[:, :], op=ALU.mult), reads=[("P", 0), "RT"], writes=["TQ"])
                elif j == 1:
                    S.op("dve", lambda e: e.tensor_tensor(out=TZ[:, :], in0=PS[0][:, :], in1=RT[:, :], op=ALU.mult), reads=[("P", 0), "RT"], writes=["TZ"])
                elif j == 2:
                    S.op("dve", lambda e: e.scalar_tensor_tensor(out=QT[:, par, :], in0=PS[0][:, :], scalar=128.0 ** -0.5, in1=RT[:, :], op0=ALU.mult, op1=ALU.mult),
                         reads=[("P", 0), "RT"], writes=[("QT", par)])
                else:
                    S.op("dve", lambda e: e.tensor_tensor(out=KT[:, tsl], in0=PS[0][:, :], in1=RT[:, :], op=ALU.mult), reads=[("P", 0), "RT"], writes=[("KT", T)])
                yield
            for b in range(4):
                for k in range(16):
                    S.op("pe", lambda e, k=k, b=b: e.matmul(PS[0][:, 0:384], lhsT=XB[:, k, b * 128:(b + 1) * 128], rhs=W1[:, k, 512:896],
                                                            start=(k == 0), stop=(k == 15)), reads=W1K + XBK, writes=[("P", 0)])
                S.op("act", lambda e, b=b: e.activation(out=VTOK[:, par, b, :], in_=PS[0][:, 0:128], func=AF.Copy, scale=RC[:, b:b + 1]),
                     reads=[("P", 0), "RC"], writes=[("VTOK", par, b)])
                S.op("act", lambda e, b=b: e.activation(out=GTOK[:, par, b, :], in_=PS[0][:, 128:256], func=AF.Copy, scale=RC[:, b:b + 1]),
                     reads=[("P", 0), "RC"], writes=[("GTOK", par, b)])
                S.op("act", lambda e, b=b: e.activation(out=V[:, 4 * T + b, :], in_=PS[0][:, 256:384], func=AF.Copy, scale=RC[:, b:b + 1]),
                     reads=[("P", 0), "RC"], writes=[("V", 4 * T + b)])
                yield
            for k in range(16):
                S.op("pe", lambda e, k=k: e.matmul(PS[0][0:1, :], lhsT=WF[:, k, 0:1], rhs=XB[:, k, :], start=(k == 0), stop=(k == 15)),
                     reads=["WF"] + XBK, writes=[("P", 0)])
            S.op("dve", lambda e: e.tensor_tensor(out=ZF[0:1, :], in0=PS[0][0:1, :], in1=RT[0:1, :], op=ALU.mult), reads=[("P", 0), "RT"], writes=["ZF"])
            S.op("dve", lambda e: e.tensor_scalar(out=ZF[0:1, :], in0=ZF[0:1, :], scalar1=BFX[0:1, 0:1], scalar2=None, op0=ALU.add), reads=["ZF", "BFX"], writes=["ZF"])
            S.op("act", lambda e: e.activation(out=ZF[0:1, :], in_=ZF[0:1, :], func=AF.Exp, scale=-1.0), reads=["ZF"], writes=["ZF"])
            S.op("act", lambda e: e.activation(out=ZF[0:1, :], in_=ZF[0:1, :], func=AF.Ln, bias=1.0), reads=["ZF"], writes=["ZF"])
            S.op("dve", lambda e: e.tensor_tensor_scan(out=LROW[0:1, :], data0=ONESF[0:1, :], data1=ZF[0:1, :], initial=0.0, op0=ALU.mult, op1=ALU.add),
                 reads=["ZF", "CF"], writes=["LROW"])
            S.op("dve", lambda e: e.tensor_scalar(out=CROW[0:1, par, :], in0=LROW[0:1, :], scalar1=-1.0, scalar2=None, op0=ALU.mult), reads=["LROW"], writes=[("CROW", par)])
            yield
            for b in range(4):
                S.op("pe", lambda e, b=b: e.matmul(PS[0][:, 384 + b:385 + b], lhsT=LROW[0:1, b * 128:(b + 1) * 128], rhs=ONESF[0:1, 0:1], start=True, stop=True),
                     reads=["LROW", "CF"], writes=[("P", 0)])
            S.op("pe", lambda e: e.matmul(PS[0][:, 388:389], lhsT=ONESF[0:1, 0:128], rhs=LROW[0:1, 511:512], start=True, stop=True),
                 reads=["LROW", "CF"], writes=[("P", 0)])
            S.op("dve", lambda e: e.tensor_scalar(out=POSC[:, 4 * T:4 * T + 4], in0=PS[0][:, 384:388], scalar1=RP[:, T:T + 1], scalar2=None, op0=ALU.add),
                 reads=[("P", 0), "RP"], writes=["POSC"])
            S.op("dve", lambda e: e.tensor_tensor(out=RP[:, T + 1:T + 2], in0=PS[0][:, 388:389], in1=RP[:, T:T + 1], op=ALU.add),
                 reads=[("P", 0), "RP"], writes=["RP"])
            S.op("dve", lambda e: e.tensor_scalar(out=BIAS[:, par, 0:4 * T + 4], in0=POSC[:, 0:4 * T + 4], scalar1=RP[:, T:T + 1], scalar2=None, op0=ALU.subtract),
                 reads=["POSC", "RP"], writes=[("BIAS", par)])
            yield

        def gen_H(T):
            par = T % 2
            S.op("act", lambda e: e.activation(out=TE[:, :], in_=TZ[:, :], func=AF.Exp, scale=-1.0), reads=["TZ"], writes=["TE"])
            S.op("pool", lambda e: e.tensor_scalar(out=TS[:, :], in0=TE[:, :], scalar1=1.0, scalar2=None, op0=ALU.add), reads=["TE"], writes=["TS"])
            S.op("dve", lambda e: e.reciprocal(out=TS[:, :], in_=TS[:, :]), reads=["TS"], writes=["TS"])
            S.op("dve", lambda e: e.tensor_scalar(out=TF[:, :], in0=TS[:, :], scalar1=LB[:, 1:2], scalar2=LB[:, 0:1], op0=ALU.mult, op1=ALU.add),
                 reads=["TS", "LB"], writes=["TF"])
            yield
            S.op("act", lambda e: e.activation(out=TF[:, :], in_=TF[:, :], func=AF.Ln), reads=["TF"], writes=["TF"])
            S.op("dve", lambda e: e.scalar_tensor_tensor(out=TK[:, :], in0=TE[:, :], scalar=LB[:, 1:2], in1=TS[:, :], op0=ALU.mult, op1=ALU.mult),
                 reads=["TE", "TS", "LB"], writes=["TK"])
            S.op("dve", lambda e: e.tensor_tensor_scan(out=TB[:, :], data0=RESET, data1=TF[:, :], initial=0.0, op0=ALU.mult, op1=ALU.add),
                 reads=["TF", "CF"], writes=["TB"])
            yield
            TBv = TB[:, :].rearrange("p (c n) -> p c n", n=64)
            for c in range(8):
                S.op("dve", lambda e, c=c: e.tensor_scalar(out=TD[:, c * 64:(c + 1) * 64], in0=TB[:, c * 64:(c + 1) * 64], scalar1=TB[:, c * 64 + 31:c * 64 + 32],
                                                           scalar2=None, op0=ALU.subtract), reads=["TB"], writes=["TD"])
            S.op("act", lambda e: e.activation(out=TE1[:, :], in_=TD[:, :], func=AF.Exp), reads=["TD"], writes=["TE1"])
            S.op("act", lambda e: e.activation(out=TE2[:, :], in_=TD[:, :], func=AF.Exp, scale=-1.0), reads=["TD"], writes=["TE2"])
            yield
            S.op("dve", lambda e: e.tensor_tensor(out=QTB[:, :], in0=TQ[:, :], in1=TE1[:, :], op=ALU.mult), reads=["TQ", "TE1"], writes=["QTB"])
            S.op("pool", lambda e: e.tensor_tensor(out=KTB[:, :], in0=TK[:, :], in1=TE2[:, :], op=ALU.mult), reads=["TK", "TE2"], writes=["KTB"])
            S.op("act", lambda e: e.activation(out=EBM[:, :], in_=TBv[:, :, 31], func=AF.Exp), reads=["TB"], writes=["EBM"])
            S.op("act", lambda e: e.activation(out=EBL[:, :], in_=TBv[:, :, 63], func=AF.Exp), reads=["TB"], writes=["EBL"])
            S.op("dve", lambda e: e.tensor_tensor(out=BDM[:, :], in0=TBv[:, :, 63], in1=TBv[:, :, 31], op=ALU.subtract), reads=["TB"], writes=["BDM"])
            S.op("act", lambda e: e.activation(out=EBLM[:, :], in_=BDM[:, :], func=AF.Exp), reads=["BDM"], writes=["EBLM"])
            ESC = ["EBM", "EBL", "EBLM"]
            yield
            for b in range(4):
                bs = slice(b * 128, (b + 1) * 128)
                trp = PS[1][:, 448:512].bitcast(BF16)
                S.op("pe", lambda e, bs=bs, trp=trp: e.transpose(out=trp, in_=KTB[:, bs], identity=IDENT), reads=["KTB", "CB"], writes=[("P", 1)])
                S.op("act", lambda e, b=b, trp=trp: e.activation(out=KTOK[:, b, :], in_=trp, func=AF.Copy), reads=[("P", 1)], writes=[("KTOK", b)])
                yield
                S.op("pe", lambda e, bs=bs: e.matmul(PS[1][:, 0:128], lhsT=KTB[:, bs], rhs=QTB[:, bs], start=True, stop=True),
                     reads=["KTB", "QTB"], writes=[("P", 1)])
                S.op("act", lambda e: e.activation(out=TSC[:, :], in_=PS[1][:, 0:128], func=AF.Copy), reads=[("P", 1)], writes=["TSC"])
                S.op("dve", lambda e: e.tensor_tensor(out=SCB[:, :], in0=TSC[:, :], in1=BDMASK, op=ALU.mult), reads=["TSC", "CB"], writes=["SCB"])
                yield
                for c2 in range(2):
                    rs = slice(c2 * 64, (c2 + 1) * 64)
                    S.op("pe", lambda e, b=b, c2=c2, rs=rs: e.matmul(PS[2 + c2][:, 0:128], lhsT=KTOK[rs, b, :], rhs=VTOK[rs, par, b, :], start=True, stop=True),
                         reads=[("KTOK", b), ("VTOK", par, b)], writes=[("P", 2 + c2)])
                yield
                for c2 in range(2):
                    c = 2 * b + c2
                    S.op("dve", lambda e, c=c, c2=c2: e.tensor_scalar(out=SM[:, c2, :], in0=ST[:, :], scalar1=EBM[:, c:c + 1], scalar2=None, op0=ALU.mult),
                         reads=["ST"] + ESC, writes=[("SM", c2)])
                    S.op("dve", lambda e, c=c, c2=c2: e.tensor_scalar(out=TKV[:, :], in0=PS[2 + c2][:, 0:128], scalar1=EBLM[:, c:c + 1], scalar2=None, op0=ALU.mult),
                         reads=[("P", 2 + c2)] + ESC, writes=["TKV"])
                    S.op("dve", lambda e, c=c: e.scalar_tensor_tensor(out=ST[:, :], in0=ST[:, :], scalar=EBL[:, c:c + 1], in1=TKV[:, :], op0=ALU.mult, op1=ALU.add),
                         reads=["ST", "TKV"] + ESC, writes=["ST"])
                yield
                S.op("pe", lambda e, b=b: e.matmul(PS[1][:, 0:128], lhsT=SCB[:, :], rhs=VTOK[:, par, b, :], start=True, stop=False),
                     reads=["SCB", ("VTOK", par, b)], writes=[("P", 1)])
                for c2 in range(2):
                    S.op("pe", lambda e, b=b, c2=c2: e.matmul(PS[1][c2 * 64:(c2 + 1) * 64, 0:128], lhsT=QTB[:, b * 128 + c2 * 64:b * 128 + (c2 + 1) * 64], rhs=SM[:, c2, :],
                                                              start=False, stop=True), reads=["QTB", ("SM", c2)], writes=[("P", 1)])
                ob_ = (4 * T + b) % 2
                S.op("act", lambda e: e.activation(out=T1[:, :], in_=PS[1][:, 0:128], func=AF.Square, accum_out=SSO[:, 0:1]), reads=[("P", 1)], writes=["T1", "SSO"])
                rstd(SSO[:, 0:1], RO[:, 0:1], "SSO", "RO", 128)
                S.op("dve", lambda e: e.scalar_tensor_tensor(out=T1[:, :], in0=PS[1][:, 0:128], scalar=RO[:, 0:1], in1=HNW[:, :], op0=ALU.mult, op1=ALU.mult),
                     reads=[("P", 1), "RO", "HNW"], writes=["T1"])
                yield
                S.op("act", lambda e, b=b: e.activation(out=EG[:, :], in_=GTOK[:, par, b, :], func=AF.Exp, scale=-1.0), reads=[("GTOK", par, b)], writes=["EG"])
                S.op("pool", lambda e: e.tensor_scalar(out=EG[:, :], in0=EG[:, :], scalar1=1.0, scalar2=None, op0=ALU.add), reads=["EG"], writes=["EG"])
                S.op("dve", lambda e: e.reciprocal(out=EG[:, :], in_=EG[:, :]), reads=["EG"], writes=["EG"])
                S.op("pool", lambda e, b=b: e.tensor_tensor(out=GG[:, :], in0=GTOK[:, par, b, :], in1=EG[:, :], op=ALU.mult), reads=["EG", ("GTOK", par, b)], writes=["GG"])
                S.op("pool", lambda e, ob_=ob_: e.tensor_tensor(out=OAT[:, ob_, :], in0=T1[:, :], in1=GG[:, :], op=ALU.mult), reads=["T1", "GG"], writes=[("OAT", ob_)])
                S.op("sp", lambda e, ob_=ob_, b=b: e.dma_start(out=oAv[:, 4 * T + b, :], in_=OAT[:, ob_, :]), reads=[("OAT", ob_)], dma="oa%d" % ob_)
                yield

        def gen_F(T):
            par = T % 2
            tsl = slice(T * 512, (T + 1) * 512)
            nkb = 4 * T + 4

            def s_stage(j):
                lo = 0 if j < 4 * T else 128 * (j - 4 * T)
                sp_ = j % 2
                sps = PS[4 + sp_]
                S.op("pe", lambda e: e.matmul(sps[:, lo:512], lhsT=KT[:, j * 128:(j + 1) * 128], rhs=QT[:, par, lo:512], start=True, stop=False),
                     reads=[("KT", j // 4), ("QT", par)], writes=[("P", 4 + sp_)])
                diag = j >= 4 * T
                S.op("pe", lambda e: e.matmul(sps[:, lo:512], lhsT=ONESB[0:1, :], rhs=CROW[0:1, par, lo:512], start=False, stop=(not diag)),
                     reads=[("CROW", par), "CB"], writes=[("P", 4 + sp_)])
                if diag:
                    S.op("pe", lambda e: e.matmul(sps[:, lo:lo + 128], lhsT=IDENT, rhs=MASKNEG, start=False, stop=True),
                         reads=["CB"], writes=[("P", 4 + sp_)])
                S.op("act", lambda e: e.activation(out=PT[:, sp_, lo:512], in_=sps[:, lo:512], func=AF.Exp, bias=BIAS[:, par, j:j + 1]),
                     reads=[("P", 4 + sp_), ("BIAS", par)], writes=[("PT", sp_)])

            def pv_stage(j):
                lo = 0 if j < 4 * T else 128 * (j - 4 * T)
                sp_ = j % 2
                S.op("pe", lambda e: e.matmul(PS[6][:, lo:512], lhsT=V[:, j, :], rhs=PT[:, sp_, lo:512], start=(j == 0), stop=(j == nkb - 1)),
                     reads=[("V", j), ("PT", sp_)], writes=[("P", 6)])
                S.op("pe", lambda e: e.matmul(PS[7][:, lo:512], lhsT=ONESB, rhs=PT[:, sp_, lo:512], start=(j == 0), stop=(j == nkb - 1)),
                     reads=["CB", ("PT", sp_)], writes=[("P", 7)])

            s_stage(0)
            yield
            for j in range(nkb):
                if j + 1 < nkb:
                    s_stage(j + 1)
                pv_stage(j)
                yield
            S.op("dve", lambda e: e.reciprocal(out=ODEN[:, :], in_=PS[7][:, :]), reads=[("P", 7)], writes=["ODEN"])
            S.op("dve", lambda e: e.tensor_tensor(out=OOUT[:, :], in0=PS[6][:, :], in1=ODEN[:, :], op=ALU.mult), reads=[("P", 6), "ODEN"], writes=["OOUT"])
            S.op("sp", lambda e: e.dma_start(out=oBT[:, tsl], in_=OOUT[:, :]), reads=["OOUT"], dma="ob")
            yield

        for _ in gen_A(0):
            pass
        for T in range(NTILES):
            gens = [gen_H(T), gen_F(T)]
            if T + 1 < NTILES:
                gens.append(gen_A(T + 1))
            while gens:
                for g in list(gens):
                    try:
                        next(g)
                    except StopIteration:
                        gens.remove(g)
        S.barrier()
        S.emit(nc, block, st)
    return nc


def prep_l1(inp, ntok=SEQ):
    x = np.asarray(inp["x"], np.float32)[0]
    xT = np.ascontiguousarray(x[:ntok].T)
    w_in = np.asarray(inp["w_in"], np.float32)[0]
    nv1 = _colvec(inp["norm_mix_pre"][0])
    p = np.arange(128)
    ones = np.ones((128, 128), np.float32)
    ident = np.eye(128, dtype=np.float32)
    maskneg = np.where(p[:, None] > p[None, :], -30000.0, 0.0).astype(np.float32)
    bdmask = ((p[:, None] // 64 == p[None, :] // 64) & (p[:, None] <= p[None, :])).astype(np.float32)
    cb = np.concatenate([ones, ident, maskneg, bdmask], axis=1).astype(ml_dtypes.bfloat16)
    reset = np.tile((np.arange(512) % 64 != 0).astype(np.float32)[None, :], (128, 1))
    cf = np.ascontiguousarray(np.concatenate([reset, np.ones((128, 512), np.float32)], axis=1))
    lbl_all = np.asarray(inp["hgrn_lb_logits"], np.float32)
    hnw = np.ascontiguousarray(np.tile(np.asarray(inp["hgrn_norm_w"], np.float32)[0][None, :], (128, 1)))
    maps = []
    for c in range(NCORES):
        cs = lambda base: w_in[:, base + c * 128: base + (c + 1) * 128]
        w1 = np.ascontiguousarray(np.concatenate([cs(0), cs(1024), cs(4096), cs(5120), cs(2048), cs(3072), cs(6144)], axis=1))
        wf = np.ascontiguousarray(np.repeat(w_in[:, 7168 + c: 7169 + c], 2, axis=1))
        lbl = np.ascontiguousarray(lbl_all[:, c * 128:(c + 1) * 128].T)
        bfx = np.full((128, 1), np.asarray(inp["b_fox_f"], np.float32)[0, c], np.float32)
        maps.append(dict(xT=xT, w1=w1, wf=wf, nv1=nv1, lbl=lbl, hnw=hnw, bfx=bfx, cb=cb, cf=cf))
    return maps


def post_l1(results):
    oA = np.concatenate([r["oA"] for r in results], axis=1)
    oB = np.concatenate([np.ascontiguousarray(r["oBT"].T) for r in results], axis=1)
    return oA, oB


_CACHE = {}


def kernel(**inputs):
    inputs = {k: np.asarray(v) for k, v in inputs.items()}
    if "l1" not in _CACHE:
        _CACHE["l1"] = build_l1(SEQ // 512)
        _CACHE["l2"] = build_l2()
    cores = list(range(NCORES))
    r1 = run_bass_kernel_spmd(_CACHE["l1"], prep_l1(inputs), core_ids=cores)
    oA, oB = post_l1(r1.results)
    r2 = run_bass_kernel_spmd(_CACHE["l2"], prep_l2(inputs, oA, oB), core_ids=cores)
    return post_l2(r2.results).astype(np.float32)
```

```python
import contextlib
import numpy as np
import ml_dtypes
import concourse.bass as bass
import concourse.mybir as mybir
from concourse.bass_utils import run_bass_kernel_spmd

F32 = mybir.dt.float32
BF16 = mybir.dt.bfloat16
ALU = mybir.AluOpType
AF = mybir.ActivationFunctionType

D_MODEL = 2048
SEQ = 8192
D_FF = 5632
EPS = 1e-6
NCORES = 8

ENGS = ("pe", "act", "dve", "pool", "sp")


class Sched:
    def __init__(self):
        self.ops = {e: [] for e in ENGS}
        self.lastw = {}
        self.readers = {}
        self.dma_cnt = {}

    def _tok(self, eng, rec, idx):
        if rec["dma"] is not None:
            return ("dma", rec["dma"], rec["dma_val"])
        return ("eng", eng, idx)

    def op(self, eng, fn, reads=(), writes=(), dma=None):
        idx = len(self.ops[eng])
        rec = dict(fn=fn, dma=dma, needed=False, deps={})
        if dma is not None:
            self.dma_cnt[dma] = self.dma_cnt.get(dma, 0) + 1
            rec["dma_val"] = 16 * self.dma_cnt[dma]
        tok = self._tok(eng, rec, idx)
        deps = rec["deps"]

        def add(t):
            if t is None:
                return
            if t[0] == "dma":
                cur = self.dma_cnt[t[1]] - (1 if dma == t[1] else 0)
                t = ("dma", t[1], 16 * cur)
            if t[0] == "eng" and t[1] == "pe" and eng == "pe" and dma is None:
                return
            k = (t[0], t[1])
            if k not in deps or deps[k] < t[2]:
                deps[k] = t[2]

        for key in reads:
            add(self.lastw.get(key))
        for key in writes:
            add(self.lastw.get(key))
            for t in self.readers.get(key, {}).values():
                add(t)
        self.ops[eng].append(rec)
        for key in writes:
            self.lastw[key] = tok
            self.readers[key] = {}
        for key in reads:
            r = self.readers.setdefault(key, {})
            k = (tok[0], tok[1])
            if k not in r or r[k][2] < tok[2]:
                r[k] = tok
        return tok

    def barrier(self):
        toks = []
        for e in ENGS:
            for i in range(len(self.ops[e]) - 1, -1, -1):
                if self.ops[e][i]["dma"] is None and self.ops[e][i]["fn"] is not None:
                    toks.append(("eng", e, i))
                    break
        for k, c in self.dma_cnt.items():
            toks.append(("dma", k, 16 * c))
        for e in ENGS:
            rec = dict(fn=None, dma=None, needed=False, deps={})
            for t in toks:
                rec["deps"][(t[0], t[1])] = t[2]
            self.ops[e].append(rec)
        self.lastw = {}
        self.readers = {}

    def emit(self, nc, block, stack):
        sems = {}
        for e in ENGS:
            sems[("eng", e)] = stack.enter_context(nc.semaphore("s_" + e))
        for k in self.dma_cnt:
            sems[("dma", k)] = stack.enter_context(nc.semaphore("d_" + str(k)))
        for e in ENGS:
            for rec in self.ops[e]:
                for (kind, name), val in rec["deps"].items():
                    if kind == "eng":
                        self.ops[name][val]["needed"] = True
        vals = {}
        for e in ENGS:
            cnt = 0
            for i, rec in enumerate(self.ops[e]):
                if rec["needed"]:
                    cnt += 1
                    vals[(e, i)] = cnt
            assert cnt < 60000, (e, cnt)
        self.n_emitted = {e: len(self.ops[e]) for e in ENGS}

        def run(e, engine):
            waited = {}
            for i, rec in enumerate(self.ops[e]):
                for (kind, name), val in rec["deps"].items():
                    v = vals[(name, val)] if kind == "eng" else val
                    sk = (kind, name)
                    if waited.get(sk, 0) >= v:
                        continue
                    engine.wait_ge(sems[sk], v)
                    waited[sk] = v
                if rec["fn"] is None:
                    continue
                ins = rec["fn"](engine)
                if rec["dma"] is not None:
                    ins.then_inc(sems[("dma", rec["dma"])], 16)
                elif rec["needed"]:
                    ins.then_inc(sems[("eng", e)], 1)

        @block.tensor
        def _(eng):
            run("pe", eng)

        @block.scalar
        def _(eng):
            run("act", eng)

        @block.vector
        def _(eng):
            run("dve", eng)

        @block.gpsimd
        def _(eng):
            run("pool", eng)

        @block.sync
        def _(eng):
            run("sp", eng)


def _v3(ap, k):
    return ap.rearrange("p (k n) -> p k n", k=k)


def build_l2(NT=1024):
    nc = bass.Bass("TRN2", target_bir_lowering=False)
    S = Sched()
    NTT = NT // 512
    dt = nc.dram_tensor
    xT = dt("xT", [2048, NT], F32, kind="ExternalInput").ap()
    oT = dt("oT", [2048, NT], F32, kind="ExternalInput").ap()
    wg = dt("wg", [2048, 4096], F32, kind="ExternalInput").ap()
    wua = dt("wua", [1024, 2048], F32, kind="ExternalInput").ap()
    wub = dt("wub", [1024, 2048], F32, kind="ExternalInput").ap()
    wo = dt("wo", [2048, 2048], F32, kind="ExternalInput").ap()
    wfi = dt("wfi", [2048, 2 * D_FF], F32, kind="ExternalInput").ap()
    wfd = dt("wfd", [D_FF, 2048], F32, kind="ExternalInput").ap()
    nv = dt("nv", [128, 64], F32, kind="ExternalInput").ap()
    cst = dt("cst", [128, 128], BF16, kind="ExternalInput").ap()
    outT = dt("outT", [2048, NT], F32, kind="ExternalOutput").ap()
    zd = dt("zd", [2048, NT], F32).ap()
    x1d = dt("x1d", [2048, NT], F32).ap()

    def cm(ap):
        return ap.rearrange("(k p) n -> p k n", p=128)

    xTv, oTv, wgv, wuav, wubv, wov, wfiv, wfdv = map(cm, (xT, oT, wg, wua, wub, wo, wfi, wfd))
    outTv, zdv, x1dv = cm(outT), cm(zd), cm(x1d)

    with contextlib.ExitStack() as st:
        sb = lambda name, shape, d: st.enter_context(nc.sbuf_tensor(name, shape, d))
        A = sb("A", [128, 44 * NT], BF16)
        B = sb("B", [128, 16 * NT], BF16)
        W = sb("W", [128, 16384], BF16)
        T = sb("T", [128, 8, 512], F32)
        SQT = sb("SQT", [128, 4, 512], BF16)
        R = sb("R", [128, 2, NT], F32)
        NV = sb("NV", [128, 64], F32)
        ONES = sb("ONES", [128, 128], BF16)
        PS = [st.enter_context(nc.psum_tensor("ps%d" % i, [128, 512], F32)) for i in range(8)]
        block = st.enter_context(nc.Block())

        xb = _v3(A[:, 0:16 * NT], 16)
        ob = _v3(A[:, 16 * NT:32 * NT], 16)
        SP0 = 32 * NT
        SQ = _v3(A[:, SP0 + 8192:SP0 + 12288], 16)
        XS = _v3(W[:, 8192:16384].bitcast(F32), 16)
        hid = _v3(A[:, 0:44 * NT], 44)
        mb = _v3(B[:, :], 16)
        x1b = mb

        S.op("sp", lambda e: e.dma_start(out=NV[:, :], in_=nv[:, :]), writes=["NV"], dma="c0")
        S.op("sp", lambda e: e.dma_start(out=ONES[:, :], in_=cst[:, :]), writes=["ONES"], dma="c1")

        def rstd_from(ps_ap, r_ap, keyp, keyr):
            S.op("dve", lambda e: e.tensor_scalar(out=r_ap, in0=ps_ap, scalar1=1.0 / D_MODEL, scalar2=EPS,
                                                  op0=ALU.mult, op1=ALU.add), reads=[keyp], writes=[keyr])
            S.op("act", lambda e: e.activation(out=r_ap, in_=r_ap, func=AF.Sqrt), reads=[keyr], writes=[keyr])
            S.op("dve", lambda e: e.reciprocal(out=r_ap, in_=r_ap), reads=[keyr], writes=[keyr])

        XSK = [("W", 1, 0), ("W", 1, 1)]
        for s in range(NT // 256):
            cs = slice(s * 256, (s + 1) * 256)
            S.op("sp", lambda e, cs=cs: e.dma_start(out=XS[:, :, :], in_=xTv[:, :, cs]), writes=XSK, dma="xs")
            S.op("act", lambda e: e.activation(out=SQ[:, :, :], in_=XS[:, :, :], func=AF.Square), reads=XSK, writes=["SQ"])
            for k in range(16):
                S.op("dve", lambda e, k=k, cs=cs: e.tensor_scalar(out=xb[:, k, cs], in0=XS[:, k, :], scalar1=NV[:, k:k + 1],
                                                                  scalar2=None, op0=ALU.mult),
                     reads=XSK + ["NV"], writes=[("A0", k, s)])
            for k in range(16):
                S.op("pe", lambda e, k=k: e.matmul(PS[0][:, 0:256], lhsT=ONES[:, :], rhs=SQ[:, k, :], start=(k == 0), stop=(k == 15)),
                     reads=["SQ", "ONES"], writes=[("P", 0)])
            rstd_from(PS[0][:, 0:256], R[:, 0, cs], ("P", 0), ("R0", s))
        for h in range(2):
            S.op("pool", lambda e, h=h: e.dma_start(out=ob[:, 8 * h:8 * h + 8, :], in_=oTv[:, 8 * h:8 * h + 8, :]),
                 writes=[("A1", h)], dma="ob%d" % h)
        xb_keys = [("A0", k, s) for k in range(16) for s in range(NT // 256)]
        r0_keys = [("R0", s) for s in range(NT // 256)]

        for g in range(8):
            s_ = g % 2
            base = s_ * 8192
            wga = _v3(W[:, base:base + 4096], 16)
            wgb = _v3(W[:, base + 4096:base + 8192], 16)
            wa = _v3(A[:, SP0 + s_ * 4096:SP0 + s_ * 4096 + 2048], 8)
            wb = _v3(A[:, SP0 + s_ * 4096 + 2048:SP0 + s_ * 4096 + 4096], 8)
            gs = slice(g * 256, (g + 1) * 256)
            gs2 = slice(2048 + g * 256, 2048 + (g + 1) * 256)
            wk = ("W", s_)
            dk = "w%d" % s_
            S.op("pool", lambda e, wga=wga, gs=gs: e.dma_start(out=wga[:, :, :], in_=wgv[:, :, gs]), writes=[wk + (0,)], dma=dk + "a")
            S.op("pool", lambda e, wgb=wgb, gs2=gs2: e.dma_start(out=wgb[:, :, :], in_=wgv[:, :, gs2]), writes=[wk + (1,)], dma=dk + "b")
            S.op("pool", lambda e, wa=wa, gs=gs: e.dma_start(out=wa[:, :, :], in_=wuav[:, :, gs]), writes=[wk + (2,)], dma=dk + "c")
            S.op("pool", lambda e, wb=wb, gs=gs: e.dma_start(out=wb[:, :, :], in_=wubv[:, :, gs]), writes=[wk + (3,)], dma=dk + "d")
            for nb in range(2):
                n = g * 2 + nb
                ns = slice(nb * 128, (nb + 1) * 128)
                for t in range(NTT):
                    ts_ = slice(t * 512, (t + 1) * 512)
                    par = (n * NTT + t) % 2
                    pga, pgb, pya, pyb = (PS[4 * par + i] for i in range(4))
                    kga, kgb, kya, kyb = (("P", 4 * par + i) for i in range(4))
                    t0, t1 = T[:, 2 * par, :], T[:, 2 * par + 1, :]
                    k0, k1 = ("T", 2 * par), ("T", 2 * par + 1)
                    for k in range(16):
                        S.op("pe", lambda e, k=k, pga=pga, wga=wga, ns=ns, ts_=ts_: e.matmul(pga[:, :], lhsT=wga[:, k, ns], rhs=xb[:, k, ts_], start=(k == 0), stop=(k == 15)),
                             reads=[wk + (0,)] + xb_keys, writes=[kga])
                    for k in range(16):
                        S.op("pe", lambda e, k=k, pgb=pgb, wgb=wgb, ns=ns, ts_=ts_: e.matmul(pgb[:, :], lhsT=wgb[:, k, ns], rhs=xb[:, k, ts_], start=(k == 0), stop=(k == 15)),
                             reads=[wk + (1,)] + xb_keys, writes=[kgb])
                    for k in range(8):
                        S.op("pe", lambda e, k=k, pya=pya, wa=wa, ns=ns, ts_=ts_: e.matmul(pya[:, :], lhsT=wa[:, k, ns], rhs=ob[:, k, ts_], start=(k == 0), stop=(k == 7)),
                             reads=[wk + (2,), ("A1", 0)], writes=[kya])
                    for k in range(8):
                        S.op("pe", lambda e, k=k, pyb=pyb, wb=wb, ns=ns, ts_=ts_: e.matmul(pyb[:, :], lhsT=wb[:, k, ns], rhs=ob[:, 8 + k, ts_], start=(k == 0), stop=(k == 7)),
                             reads=[wk + (3,), ("A1", 1)], writes=[kyb])
                    S.op("dve", lambda e, t0=t0, pga=pga, ts_=ts_: e.tensor_tensor(out=t0, in0=pga[:, :], in1=R[:, 0, ts_], op=ALU.mult),
                         reads=[kga] + r0_keys, writes=[k0])
                    S.op("act", lambda e, t0=t0: e.activation(out=t0, in_=t0, func=AF.Sigmoid), reads=[k0], writes=[k0])
                    S.op("dve", lambda e, t1=t1, pgb=pgb, ts_=ts_: e.tensor_tensor(out=t1, in0=pgb[:, :], in1=R[:, 0, ts_], op=ALU.mult),
                         reads=[kgb] + r0_keys, writes=[k1])
                    S.op("act", lambda e, t1=t1: e.activation(out=t1, in_=t1, func=AF.Sigmoid), reads=[k1], writes=[k1])
                    S.op("dve", lambda e, t0=t0, pya=pya: e.tensor_tensor(out=t0, in0=pya[:, :], in1=t0, op=ALU.mult),
                         reads=[kya, k0], writes=[k0])
                    S.op("dve", lambda e, t1=t1, pyb=pyb: e.tensor_tensor(out=t1, in0=pyb[:, :], in1=t1, op=ALU.mult),
                         reads=[kyb, k1], writes=[k1])
                    S.op("dve", lambda e, t0=t0, t1=t1, n=n, ts_=ts_: e.tensor_tensor(out=mb[:, n, ts_], in0=t0, in1=t1, op=ALU.add),
                         reads=[k0, k1], writes=[("B", n, t)])

        S.barrier()
        def evac_stats(ps, pkey, nchunk, t, ssb, first, last, dram_v, par):
            zt = T[:, 4 + par, :]
            zk = ("T", 4 + par)
            sq = SQT[:, par, :]
            sk = ("SQT", par)
            S.op("act", lambda e: e.activation(out=zt, in_=ps[:, :], func=AF.Copy), reads=[pkey], writes=[zk])
            S.op("act", lambda e: e.activation(out=sq, in_=ps[:, :], func=AF.Square), reads=[pkey], writes=[sk])
            S.op("pe", lambda e: e.matmul(PS[ssb][:, :], lhsT=ONES[:, :], rhs=sq, start=first, stop=last),
                 reads=[sk, "ONES"], writes=[("P", ssb)])
            S.op("sp", lambda e: e.dma_start(out=dram_v[:, nchunk, t * 512:(t + 1) * 512], in_=zt), reads=[zk],
                 writes=[("zd", nchunk, t)], dma="zst%d" % par)

        for g in range(8):
            s_ = g % 2
            base = s_ * 8192
            wos = _v3(W[:, base:base + 4096], 16)
            wk = ("W", s_)
            gs = slice(g * 256, (g + 1) * 256)
            S.op("pool", lambda e, wos=wos, gs=gs: e.dma_start(out=wos[:, :, :], in_=wov[:, :, gs]), writes=[wk], dma="w%d" % s_)
            for nb in range(2):
                n = g * 2 + nb
                ns = slice(nb * 128, (nb + 1) * 128)
                for t in range(NTT):
                    ts_ = slice(t * 512, (t + 1) * 512)
                    pb = (n * NTT + t) % 4
                    for k in range(16):
                        S.op("pe", lambda e, k=k, pb=pb, wos=wos, ns=ns, ts_=ts_: e.matmul(PS[pb][:, :], lhsT=wos[:, k, ns], rhs=mb[:, k, ts_], start=(k == 0), stop=(k == 15)),
                             reads=[wk] + [("B", k, t) for k in range(16)], writes=[("P", pb)])
                    evac_stats(PS[pb], ("P", pb), n, t, 6 + t, n == 0, n == 15, zdv, (n * NTT + t) % 2)
        for t in range(NTT):
            rstd_from(PS[6 + t][:, :], R[:, 1, t * 512:(t + 1) * 512], ("P", 6 + t), ("R1", t))

        def resid_pass(t, src_v, res_v, resk, rrow, rkey, gain_off, dst_v, dstk, dst_dma, make_next):
            ts_ = slice(t * 512, (t + 1) * 512)
            for n in range(16):
                par = n % 2
                ta, tb = T[:, par, :], T[:, 2 + par, :]
                ka, kb = ("T", par), ("T", 2 + par)
                S.op("sp", lambda e, ta=ta, n=n: e.dma_start(out=ta, in_=res_v[:, n, ts_]), reads=[(resk, n, t)], writes=[ka], dma="ra%d" % par)
                S.op("sp", lambda e, tb=tb, n=n: e.dma_start(out=tb, in_=src_v[:, n, ts_]), reads=[("zd", n, t)], writes=[kb], dma="rb%d" % par)
                S.op("dve", lambda e, tb=tb, n=n: e.scalar_tensor_tensor(out=tb, in0=tb, scalar=NV[:, gain_off + n:gain_off + n + 1], in1=R[:, rrow, ts_],
                                                                           op0=ALU.mult, op1=ALU.mult), reads=[kb, "NV", (rkey, t)], writes=[kb])
                S.op("dve", lambda e, ta=ta, tb=tb: e.tensor_tensor(out=ta, in0=ta, in1=tb, op=ALU.add), reads=[ka, kb], writes=[ka])
                S.op("sp", lambda e, ta=ta, n=n: e.dma_start(out=dst_v[:, n, ts_], in_=ta), reads=[ka], writes=[(dstk, n, t)], dma=dst_dma)
                if make_next:
                    sq = SQT[:, 2 + par, :]
                    sk = ("SQT", 2 + par)
                    S.op("dve", lambda e, ta=ta, n=n: e.tensor_scalar(out=x1b[:, n, ts_], in0=ta, scalar1=NV[:, 32 + n:33 + n], scalar2=None, op0=ALU.mult),
                         reads=[ka, "NV"], writes=[("B", n, t)])
                    S.op("act", lambda e, ta=ta, sq=sq: e.activation(out=sq, in_=ta, func=AF.Square), reads=[ka], writes=[sk])
                    S.op("pe", lambda e, sq=sq, n=n: e.matmul(PS[4 + t][:, :], lhsT=ONES[:, :], rhs=sq, start=(n == 0), stop=(n == 15)),
                         reads=[sk, "ONES"], writes=[("P", 4 + t)])

        S.barrier()
        for t in range(NTT):
            resid_pass(t, zdv, xTv, "xin", 1, "R1", 16, x1dv, "x1d", "x1st", True)
        for t in range(NTT):
            rstd_from(PS[4 + t][:, :], R[:, 0, t * 512:(t + 1) * 512], ("P", 4 + t), ("R2", t))

        xk = {t: [("B", k, t) for k in range(16)] for t in range(NTT)}
        for g in range(22):
            s_ = g % 2
            base = s_ * 8192
            wfg = _v3(W[:, base:base + 4096], 16)
            wfu = _v3(W[:, base + 4096:base + 8192], 16)
            wk = ("W", s_)
            gs = slice(g * 256, (g + 1) * 256)
            gs2 = slice(D_FF + g * 256, D_FF + (g + 1) * 256)
            S.op("pool", lambda e, wfg=wfg, gs=gs: e.dma_start(out=wfg[:, :, :], in_=wfiv[:, :, gs]), writes=[wk + (0,)], dma="w%da" % s_)
            S.op("pool", lambda e, wfu=wfu, gs2=gs2: e.dma_start(out=wfu[:, :, :], in_=wfiv[:, :, gs2]), writes=[wk + (1,)], dma="w%db" % s_)
            for nb in range(2):
                f = g * 2 + nb
                ns = slice(nb * 128, (nb + 1) * 128)
                for t in range(NTT):
                    ts_ = slice(t * 512, (t + 1) * 512)
                    par = (f * NTT + t) % 2
                    pg_, pu_ = PS[2 * par], PS[2 * par + 1]
                    kg_, ku_ = ("P", 2 * par), ("P", 2 * par + 1)
                    t0, t1 = T[:, 4 + 2 * par, :], T[:, 5 + 2 * par, :]
                    k0, k1 = ("T", 4 + 2 * par), ("T", 5 + 2 * par)
                    for k in range(16):
                        S.op("pe", lambda e, k=k, pg_=pg_, wfg=wfg, ns=ns, ts_=ts_: e.matmul(pg_[:, :], lhsT=wfg[:, k, ns], rhs=x1b[:, k, ts_], start=(k == 0), stop=(k == 15)),
                             reads=[wk + (0,)] + xk[t], writes=[kg_])
                    for k in range(16):
                        S.op("pe", lambda e, k=k, pu_=pu_, wfu=wfu, ns=ns, ts_=ts_: e.matmul(pu_[:, :], lhsT=wfu[:, k, ns], rhs=x1b[:, k, ts_], start=(k == 0), stop=(k == 15)),
                             reads=[wk + (1,)] + xk[t], writes=[ku_])
                    S.op("dve", lambda e, t0=t0, pg_=pg_, ts_=ts_: e.tensor_tensor(out=t0, in0=pg_[:, :], in1=R[:, 0, ts_], op=ALU.mult),
                         reads=[kg_, ("R2", t)], writes=[k0])
                    S.op("act", lambda e, t0=t0: e.activation(out=t0, in_=t0, func=AF.Silu), reads=[k0], writes=[k0])
                    S.op("dve", lambda e, t1=t1, pu_=pu_, ts_=ts_: e.tensor_tensor(out=t1, in0=pu_[:, :], in1=R[:, 0, ts_], op=ALU.mult),
                         reads=[ku_, ("R2", t)], writes=[k1])
                    S.op("dve", lambda e, t0=t0, t1=t1, f=f, ts_=ts_: e.tensor_tensor(out=hid[:, f, ts_], in0=t0, in1=t1, op=ALU.mult),
                         reads=[k0, k1], writes=[("hid", f, t)])

        S.barrier()
        for n in range(16):
            s_ = n % 2
            base = s_ * 5632
            wd = _v3(W[:, base:base + 5632], 44)
            wk = ("W", s_)
            ns = slice(n * 128, (n + 1) * 128)
            for hh in range(2):
                S.op("pool", lambda e, wd=wd, ns=ns, hh=hh: e.dma_start(out=wd[:, 22 * hh:22 * hh + 22, :], in_=wfdv[:, 22 * hh:22 * hh + 22, ns]),
                     writes=[wk + (hh,)], dma="w%d%s" % (s_, "ab"[hh]))
            for t in range(NTT):
                ts_ = slice(t * 512, (t + 1) * 512)
                pb = (n * NTT + t) % 4
                for k in range(44):
                    S.op("pe", lambda e, k=k, pb=pb, wd=wd, ts_=ts_: e.matmul(PS[pb][:, :], lhsT=wd[:, k, :], rhs=hid[:, k, ts_], start=(k == 0), stop=(k == 43)),
                         reads=[wk + (0,), wk + (1,)] + [("hid", f, t) for f in range(44)], writes=[("P", pb)])
                evac_stats(PS[pb], ("P", pb), n, t, 6 + t, n == 0, n == 15, zdv, (n * NTT + t) % 2)
        for t in range(NTT):
            rstd_from(PS[6 + t][:, :], R[:, 1, t * 512:(t + 1) * 512], ("P", 6 + t), ("R3", t))
        S.barrier()
        for t in range(NTT):
            resid_pass(t, zdv, x1dv, "x1d", 1, "R3", 48, outTv, "out", "ost", False)

        S.barrier()
        S.emit(nc, block, st)
    return nc


def _colvec(v):
    return np.ascontiguousarray(np.asarray(v, np.float32).reshape(16, 128).T)


def prep_l2(inp, oA, oB):
    x = np.asarray(inp["x"], np.float32)[0]
    w_in = np.asarray(inp["w_in"], np.float32)[0]
    wg = np.ascontiguousarray(w_in[:, 7176:7176 + 4096])
    nv = np.ascontiguousarray(np.concatenate([_colvec(inp["norm_mix_pre"][0]), _colvec(inp["norm_mix_post"][0]),
                                              _colvec(inp["norm_ffn_pre"][0]), _colvec(inp["norm_ffn_post"][0])], axis=1))
    ones = np.ones((128, 128), ml_dtypes.bfloat16)
    common = dict(wg=wg, wua=np.ascontiguousarray(inp["w_up_a"][0], dtype=np.float32), wub=np.ascontiguousarray(inp["w_up_b"][0], dtype=np.float32),
                  wo=np.ascontiguousarray(inp["w_o"][0], dtype=np.float32), wfi=np.ascontiguousarray(inp["w_ffn_in"][0], dtype=np.float32),
                  wfd=np.ascontiguousarray(inp["w_ffn_down"][0], dtype=np.float32), nv=nv, cst=ones)
    maps = []
    for c in range(NCORES):
        sl = slice(c * 1024, (c + 1) * 1024)
        oT = np.ascontiguousarray(np.concatenate([oA[sl].T, oB[sl].T], axis=0), dtype=np.float32)
        maps.append(dict(common, xT=np.ascontiguousarray(x[sl].T), oT=oT))
    return maps


def post_l2(results):
    return np.concatenate([np.ascontiguousarray(r["outT"].T) for r in results], axis=0)[None]


def build_l1(NTILES=16, stage=9):
    nc = bass.Bass("TRN2", target_bir_lowering=False)
    S = Sched()
    NTOK = NTILES * 512
    dt = nc.dram_tensor
    xT = dt("xT", [2048, NTOK], F32, kind="ExternalInput").ap()
    w1 = dt("w1", [2048, 896], F32, kind="ExternalInput").ap()
    wf = dt("wf", [2048, 2], F32, kind="ExternalInput").ap()
    nv1 = dt("nv1", [128, 16], F32, kind="ExternalInput").ap()
    lbl = dt("lbl", [128, 2], F32, kind="ExternalInput").ap()
    hnw = dt("hnw", [128, 128], F32, kind="ExternalInput").ap()
    bfx = dt("bfx", [128, 1], F32, kind="ExternalInput").ap()
    cb = dt("cb", [128, 512], BF16, kind="ExternalInput").ap()
    cf = dt("cf", [128, 1024], F32, kind="ExternalInput").ap()
    oA = dt("oA", [NTOK, 128], F32, kind="ExternalOutput").ap()
    oBT = dt("oBT", [128, NTOK], F32, kind="ExternalOutput").ap()
    xTv = xT.rearrange("(k p) n -> p k n", p=128)
    w1v = w1.rearrange("(k p) n -> p k n", p=128)
    wfv = wf.rearrange("(k p) n -> p k n", p=128)
    oAv = oA.rearrange("(b p) v -> p b v", p=128)

    with contextlib.ExitStack() as st:
        sb = lambda name, shape, d: st.enter_context(nc.sbuf_tensor(name, shape, d))
        XS = sb("XS", [128, 2, 8, 512], F32)
        XB = sb("XB", [128, 16, 512], BF16)
        SQ = sb("SQ", [128, 16, 512], BF16)
        W1 = sb("W1", [128, 16, 896], BF16)
        WF = sb("WF", [128, 16, 2], BF16)
        NV = sb("NV", [128, 16], F32)
        LBL = sb("LBL", [128, 2], F32)
        LB = sb("LB", [128, 2], F32)
        HNW = sb("HNW", [128, 128], F32)
        BFX = sb("BFX", [128, 1], F32)
        CB = sb("CB", [128, 512], BF16)
        CF = sb("CF", [128, 1024], F32)
        RT = sb("RT", [128, 512], F32)
        RC = sb("RC", [128, 4], F32)
        KT = sb("KT", [128, NTOK], BF16)
        V = sb("V", [128, NTILES * 4, 128], BF16)
        QT = sb("QT", [128, 2, 512], BF16)
        TQ = sb("TQ", [128, 512], F32)
        TZ = sb("TZ", [128, 512], F32)
        TE = sb("TE", [128, 512], F32)
        TS = sb("TS", [128, 512], F32)
        TF = sb("TF", [128, 512], F32)
        TK = sb("TK", [128, 512], F32)
        TB = sb("TB", [128, 512], F32)
        TD = sb("TD", [128, 512], F32)
        TE1 = sb("TE1", [128, 512], F32)
        TE2 = sb("TE2", [128, 512], F32)
        QTB = sb("QTB", [128, 512], BF16)
        KTB = sb("KTB", [128, 512], BF16)
        KTOK = sb("KTOK", [128, 4, 128], BF16)
        VTOK = sb("VTOK", [128, 2, 4, 128], BF16)
        GTOK = sb("GTOK", [128, 2, 4, 128], F32)
        EBM = sb("EBM", [128, 8], F32)
        EBL = sb("EBL", [128, 8], F32)
        EBLM = sb("EBLM", [128, 8], F32)
        BDM = sb("BDM", [128, 8], F32)
        ST = sb("ST", [128, 128], F32)
        SM = sb("SM", [128, 2, 128], BF16)
        SCB = sb("SCB", [128, 128], BF16)
        TKV = sb("TKV", [128, 128], F32)
        T1 = sb("T1", [128, 128], F32)
        TSC = sb("TSC", [128, 128], F32)
        EG = sb("EG", [128, 128], F32)
        GG = sb("GG", [128, 128], F32)
        OAT = sb("OAT", [128, 2, 128], F32)
        SSO = sb("SSO", [128, 1], F32)
        RO = sb("RO", [128, 1], F32)
        PT = sb("PT", [128, 2, 512], BF16)
        CROW = sb("CROW", [1, 2, 512], BF16)
        LROW = sb("LROW", [1, 512], F32)
        ZF = sb("ZF", [1, 512], F32)
        POSC = sb("POSC", [128, NTILES * 4], F32)
        RP = sb("RP", [128, NTILES + 1], F32)
        BIAS = sb("BIAS", [128, 2, NTILES * 4], F32)
        ODEN = sb("ODEN", [128, 512], F32)
        OOUT = sb("OOUT", [128, 512], F32)
        PS = [st.enter_context(nc.psum_tensor("ps%d" % i, [128, 512], F32)) for i in range(8)]
        block = st.enter_context(nc.Block())

        ONESB, IDENT, MASKNEG, BDMASK = CB[:, 0:128], CB[:, 128:256], CB[:, 256:384], CB[:, 384:512]
        RESET, ONESF = CF[:, 0:512], CF[:, 512:1024]

        def ld(eng, dst, src, key, dk):
            S.op(eng, lambda e: e.dma_start(out=dst, in_=src), writes=[key], dma=dk)

        ld("sp", NV[:, :], nv1[:, :], "NV", "c0")
        ld("sp", LBL[:, :], lbl[:, :], "LBL", "c1")
        ld("sp", HNW[:, :], hnw[:, :], "HNW", "c2")
        ld("sp", BFX[:, :], bfx[:, :], "BFX", "c3")
        ld("sp", CB[:, :], cb[:, :], "CB", "c4")
        ld("sp", CF[:, :], cf[:, :], "CF", "c5")
        ld("pool", W1[:, 0:8, :], w1v[:, 0:8, :], ("W1", 0), "w1a")
        ld("pool", W1[:, 8:16, :], w1v[:, 8:16, :], ("W1", 1), "w1b")
        ld("pool", WF[:, :, :], wfv[:, :, :], "WF", "wf")
        W1K = [("W1", 0), ("W1", 1)]

        S.op("dve", lambda e: e.tensor_tensor(out=LB[:, 0:1], in0=LBL[:, 0:1], in1=LBL[:, 1:2], op=ALU.subtract), reads=["LBL"], writes=["LB"])
        S.op("act", lambda e: e.activation(out=LB[:, 1:2], in_=LB[:, 0:1], func=AF.Exp, scale=-1.0), reads=["LB"], writes=["LB"])
        S.op("dve", lambda e: e.tensor_scalar(out=LB[:, 0:1], in0=LB[:, 1:2], scalar1=1.0, scalar2=None, op0=ALU.add), reads=["LB"], writes=["LB"])
        S.op("dve", lambda e: e.reciprocal(out=LB[:, 0:1], in_=LB[:, 0:1]), reads=["LB"], writes=["LB"])
        S.op("dve", lambda e: e.tensor_tensor(out=LB[:, 1:2], in0=LB[:, 1:2], in1=LB[:, 0:1], op=ALU.mult), reads=["LB"], writes=["LB"])
        S.op("dve", lambda e: e.memset(ST[:, :], 0.0), writes=["ST"])
        S.op("dve", lambda e: e.memset(RP[:, :], 0.0), writes=["RP"])

        def rstd(ps_ap, r_ap, pkey, rkey, n):
            S.op("dve", lambda e: e.tensor_scalar(out=r_ap, in0=ps_ap, scalar1=1.0 / n, scalar2=EPS, op0=ALU.mult, op1=ALU.add),
                 reads=[pkey], writes=[rkey])
            S.op("act", lambda e: e.activation(out=r_ap, in_=r_ap, func=AF.Ln), reads=[rkey], writes=[rkey])
            S.op("act", lambda e: e.activation(out=r_ap, in_=r_ap, func=AF.Exp, scale=-0.5), reads=[rkey], writes=[rkey])

        def gen_A(T):
            par = T % 2
            tsl = slice(T * 512, (T + 1) * 512)
            for h in range(2):
                S.op("sp", lambda e, h=h: e.dma_start(out=XS[:, h, :, :], in_=xTv[:, 8 * h:8 * h + 8, tsl]), writes=[("XS", h)], dma="xs%d" % h)
            yield
            for h in range(2):
                S.op("act", lambda e, h=h: e.activation(out=SQ[:, 8 * h:8 * h + 8, :], in_=XS[:, h, :, :], func=AF.Square),
                     reads=[("XS", h)], writes=[("SQ", h)])
                for k in range(8):
                    if k % 2 == 0:
                        S.op("dve", lambda e, h=h, k=k: e.tensor_scalar(out=XB[:, 8 * h + k, :], in0=XS[:, h, k, :], scalar1=NV[:, 8 * h + k:8 * h + k + 1],
                                                                         scalar2=None, op0=ALU.mult), reads=[("XS", h), "NV"], writes=[("XB", h, k)])
                    else:
                        S.op("act", lambda e, h=h, k=k: e.activation(out=XB[:, 8 * h + k, :], in_=XS[:, h, k, :], func=AF.Copy, scale=NV[:, 8 * h + k:8 * h + k + 1]),
                             reads=[("XS", h), "NV"], writes=[("XB", h, k)])
                yield
            SQK = [("SQ", 0), ("SQ", 1)]
            XBK = [("XB", h, k) for h in range(2) for k in range(8)]
            for k in range(16):
                S.op("pe", lambda e, k=k: e.matmul(PS[0][:, :], lhsT=ONESB, rhs=SQ[:, k, :], start=(k == 0), stop=(k == 15)),
                     reads=SQK + ["CB"], writes=[("P", 0)])
            rstd(PS[0][:, :], RT[:, :], ("P", 0), "RT", D_MODEL)
            yield
            for b in range(4):
                for k in range(16):
                    S.op("pe", lambda e, k=k, b=b: e.matmul(PS[0][:, 400 + b:401 + b], lhsT=SQ[:, k, b * 128:(b + 1) * 128], rhs=ONESB[:, 0:1],
                                                            start=(k == 0), stop=(k == 15)), reads=SQK + ["CB"], writes=[("P", 0)])
            rstd(PS[0][:, 400:404], RC[:, :], ("P", 0), "RC", D_MODEL)
            yield
            for j in range(4):
                for k in range(16):
                    S.op("pe", lambda e, k=k, j=j: e.matmul(PS[0][:, :], lhsT=W1[:, k, j * 128:(j + 1) * 128], rhs=XB[:, k, :],
                                                            start=(k == 0), stop=(k == 15)), reads=W1K + XBK, writes=[("P", 0)])
                if j == 0:
                    S.op("dve", lambda e: e.tensor_tensor(out=TQ[:, :], in0=PS[0][:, :], in1=RT[:, :], op=ALU.mult), reads=[("P", 0), "RT"], writes=["TQ"])
                elif j == 1:
                    S.op("dve", lambda e: e.tensor_tensor(out=TZ[:, :], in0=PS[0][:, :], in1=RT[:, :], op=ALU.mult), reads=[("P", 0), "RT"], writes=["TZ"])
                elif j == 2:
                    S.op("dve", lambda e: e.scalar_tensor_tensor(out=QT[:, par, :], in0=PS[0][:, :], scalar=128.0 ** -0.5, in1=RT[:, :], op0=ALU.mult, op1=ALU.mult),
                         reads=[("P", 0), "RT"], writes=[("QT", par)])
                else:
                    S.op("dve", lambda e: e.tensor_tensor(out=KT[:, tsl], in0=PS[0][:, :], in1=RT[:, :], op=ALU.mult), reads=[("P", 0), "RT"], writes=[("KT", T)])
                yield
            for b in range(4):
                for k in range(16):
                    S.op("pe", lambda e, k=k, b=b: e.matmul(PS[0][:, 0:384], lhsT=XB[:, k, b * 128:(b + 1) * 128], rhs=W1[:, k, 512:896],
                                                            start=(k == 0), stop=(k == 15)), reads=W1K + XBK, writes=[("P", 0)])
                S.op("act", lambda e, b=b: e.activation(out=VTOK[:, par, b, :], in_=PS[0][:, 0:128], func=AF.Copy, scale=RC[:, b:b + 1]),
                     reads=[("P", 0), "RC"], writes=[("VTOK", par, b)])
                S.op("act", lambda e, b=b: e.activation(out=GTOK[:, par, b, :], in_=PS[0][:, 128:256], func=AF.Copy, scale=RC[:, b:b + 1]),
                     reads=[("P", 0), "RC"], writes=[("GTOK", par, b)])
                S.op("act", lambda e, b=b: e.activation(out=V[:, 4 * T + b, :], in_=PS[0][:, 256:384], func=AF.Copy, scale=RC[:, b:b + 1]),
                     reads=[("P", 0), "RC"], writes=[("V", 4 * T + b)])
                yield
            for k in range(16):
                S.op("pe", lambda e, k=k: e.matmul(PS[0][0:1, :], lhsT=WF[:, k, 0:1], rhs=XB[:, k, :], start=(k == 0), stop=(k == 15)),
                     reads=["WF"] + XBK, writes=[("P", 0)])
            S.op("dve", lambda e: e.tensor_tensor(out=ZF[0:1, :], in0=PS[0][0:1, :], in1=RT[0:1, :], op=ALU.mult), reads=[("P", 0), "RT"], writes=["ZF"])
            S.op("dve", lambda e: e.tensor_scalar(out=ZF[0:1, :], in0=ZF[0:1, :], scalar1=BFX[0:1, 0:1], scalar2=None, op0=ALU.add), reads=["ZF", "BFX"], writes=["ZF"])
            S.op("act", lambda e: e.activation(out=ZF[0:1, :], in_=ZF[0:1, :], func=AF.Exp, scale=-1.0), reads=["ZF"], writes=["ZF"])
            S.op("act", lambda e: e.activation(out=ZF[0:1, :], in_=ZF[0:1, :], func=AF.Ln, bias=1.0), reads=["ZF"], writes=["ZF"])
            S.op("dve", lambda e: e.tensor_tensor_scan(out=LROW[0:1, :], data0=ONESF[0:1, :], data1=ZF[0:1, :], initial=0.0, op0=ALU.mult, op1=ALU.add),
                 reads=["ZF", "CF"], writes=["LROW"])
            S.op("dve", lambda e: e.tensor_scalar(out=CROW[0:1, par, :], in0=LROW[0:1, :], scalar1=-1.0, scalar2=None, op0=ALU.mult), reads=["LROW"], writes=[("CROW", par)])
            yield
            for b in range(4):
                S.op("pe", lambda e, b=b: e.matmul(PS[0][:, 384 + b:385 + b], lhsT=LROW[0:1, b * 128:(b + 1) * 128], rhs=ONESF[0:1, 0:1], start=True, stop=True),
                     reads=["LROW", "CF"], writes=[("P", 0)])
            S.op("pe", lambda e: e.matmul(PS[0][:, 388:389], lhsT=ONESF[0:1, 0:128], rhs=LROW[0:1, 511:512], start=True, stop=True),
                 reads=["LROW", "CF"], writes=[("P", 0)])
            S.op("dve", lambda e: e.tensor_scalar(out=POSC[:, 4 * T:4 * T + 4], in0=PS[0][:, 384:388], scalar1=RP[:, T:T + 1], scalar2=None, op0=ALU.add),
                 reads=[("P", 0), "RP"], writes=["POSC"])
            S.op("dve", lambda e: e.tensor_tensor(out=RP[:, T + 1:T + 2], in0=PS[0][:, 388:389], in1=RP[:, T:T + 1], op=ALU.add),
                 reads=[("P", 0), "RP"], writes=["RP"])
            S.op("dve", lambda e: e.tensor_scalar(out=BIAS[:, par, 0:4 * T + 4], in0=POSC[:, 0:4 * T + 4], scalar1=RP[:, T:T + 1], scalar2=None, op0=ALU.subtract),
                 reads=["POSC", "RP"], writes=[("BIAS", par)])
            yield

        def gen_H(T):
            par = T % 2
            S.op("act", lambda e: e.activation(out=TE[:, :], in_=TZ[:, :], func=AF.Exp, scale=-1.0), reads=["TZ"], writes=["TE"])
            S.op("dve", lambda e: e.tensor_scalar(out=TS[:, :], in0=TE[:, :], scalar1=1.0, scalar2=None, op0=ALU.add), reads=["TE"], writes=["TS"])
            S.op("dve", lambda e: e.reciprocal(out=TS[:, :], in_=TS[:, :]), reads=["TS"], writes=["TS"])
            S.op("dve", lambda e: e.tensor_scalar(out=TF[:, :], in0=TS[:, :], scalar1=LB[:, 1:2], scalar2=LB[:, 0:1], op0=ALU.mult, op1=ALU.add),
                 reads=["TS", "LB"], writes=["TF"])
            yield
            S.op("act", lambda e: e.activation(out=TF[:, :], in_=TF[:, :], func=AF.Ln), reads=["TF"], writes=["TF"])
            S.op("dve", lambda e: e.scalar_tensor_tensor(out=TK[:, :], in0=TE[:, :], scalar=LB[:, 1:2], in1=TS[:, :], op0=ALU.mult, op1=ALU.mult),
                 reads=["TE", "TS", "LB"], writes=["TK"])
            S.op("dve", lambda e: e.tensor_tensor_scan(out=TB[:, :], data0=RESET, data1=TF[:, :], initial=0.0, op0=ALU.mult, op1=ALU.add),
                 reads=["TF", "CF"], writes=["TB"])
            yield
            TBv = TB[:, :].rearrange("p (c n) -> p c n", n=64)
            TDv = TD[:, :].rearrange("p (c n) -> p c n", n=64)
            S.op("dve", lambda e: e.tensor_tensor(out=TDv, in0=TBv, in1=TBv[:, :, 31:32].to_broadcast([128, 8, 64]), op=ALU.subtract), reads=["TB"], writes=["TD"])
            S.op("act", lambda e: e.activation(out=TE1[:, :], in_=TD[:, :], func=AF.Exp), reads=["TD"], writes=["TE1"])
            S.op("act", lambda e: e.activation(out=TE2[:, :], in_=TD[:, :], func=AF.Exp, scale=-1.0), reads=["TD"], writes=["TE2"])
            yield
            S.op("dve", lambda e: e.tensor_tensor(out=QTB[:, :], in0=TQ[:, :], in1=TE1[:, :], op=ALU.mult), reads=["TQ", "TE1"], writes=["QTB"])
            S.op("dve", lambda e: e.tensor_tensor(out=KTB[:, :], in0=TK[:, :], in1=TE2[:, :], op=ALU.mult), reads=["TK", "TE2"], writes=["KTB"])
            S.op("act", lambda e: e.activation(out=EBM[:, :], in_=TBv[:, :, 31], func=AF.Exp), reads=["TB"], writes=["EBM"])
            S.op("act", lambda e: e.activation(out=EBL[:, :], in_=TBv[:, :, 63], func=AF.Exp), reads=["TB"], writes=["EBL"])
            S.op("dve", lambda e: e.tensor_tensor(out=BDM[:, :], in0=TBv[:, :, 63], in1=TBv[:, :, 31], op=ALU.subtract), reads=["TB"], writes=["BDM"])
            S.op("act", lambda e: e.activation(out=EBLM[:, :], in_=BDM[:, :], func=AF.Exp), reads=["BDM"], writes=["EBLM"])
            ESC = ["EBM", "EBL", "EBLM"]
            yield
            for b in range(4):
                bs = slice(b * 128, (b + 1) * 128)
                trp = PS[1][:, 448:512].bitcast(BF16)
                S.op("pe", lambda e, bs=bs, trp=trp: e.transpose(out=trp, in_=KTB[:, bs], identity=IDENT), reads=["KTB", "CB"], writes=[("P", 1)])
                S.op("act", lambda e, b=b, trp=trp: e.activation(out=KTOK[:, b, :], in_=trp, func=AF.Copy), reads=[("P", 1)], writes=[("KTOK", b)])
                yield
                S.op("pe", lambda e, bs=bs: e.matmul(PS[1][:, 0:128], lhsT=KTB[:, bs], rhs=QTB[:, bs], start=True, stop=True),
                     reads=["KTB", "QTB"], writes=[("P", 1)])
                S.op("act", lambda e: e.activation(out=TSC[:, :], in_=PS[1][:, 0:128], func=AF.Copy), reads=[("P", 1)], writes=["TSC"])
                S.op("dve", lambda e: e.tensor_tensor(out=SCB[:, :], in0=TSC[:, :], in1=BDMASK, op=ALU.mult), reads=["TSC", "CB"], writes=["SCB"])
                yield
                for c2 in range(2):
                    rs = slice(c2 * 64, (c2 + 1) * 64)
                    S.op("pe", lambda e, b=b, c2=c2, rs=rs: e.matmul(PS[2 + c2][:, 0:128], lhsT=KTOK[rs, b, :], rhs=VTOK[rs, par, b, :], start=True, stop=True),
                         reads=[("KTOK", b), ("VTOK", par, b)], writes=[("P", 2 + c2)])
                yield
                for c2 in range(2):
                    c = 2 * b + c2
                    S.op("dve", lambda e, c=c, c2=c2: e.tensor_scalar(out=SM[:, c2, :], in0=ST[:, :], scalar1=EBM[:, c:c + 1], scalar2=None, op0=ALU.mult),
                         reads=["ST"] + ESC, writes=[("SM", c2)])
                    S.op("dve", lambda e, c=c, c2=c2: e.tensor_scalar(out=TKV[:, :], in0=PS[2 + c2][:, 0:128], scalar1=EBLM[:, c:c + 1], scalar2=None, op0=ALU.mult),
                         reads=[("P", 2 + c2)] + ESC, writes=["TKV"])
                    S.op("dve", lambda e, c=c: e.scalar_tensor_tensor(out=ST[:, :], in0=ST[:, :], scalar=EBL[:, c:c + 1], in1=TKV[:, :], op0=ALU.mult, op1=ALU.add),
                         reads=["ST", "TKV"] + ESC, writes=["ST"])
                yield
                S.op("pe", lambda e, b=b: e.matmul(PS[1][:, 0:128], lhsT=SCB[:, :], rhs=VTOK[:, par, b, :], start=True, stop=False),
                     reads=["SCB", ("VTOK", par, b)], writes=[("P", 1)])
                for c2 in range(2):
                    S.op("pe", lambda e, b=b, c2=c2: e.matmul(PS[1][c2 * 64:(c2 + 1) * 64, 0:128], lhsT=QTB[:, b * 128 + c2 * 64:b * 128 + (c2 + 1) * 64], rhs=SM[:, c2, :],
                                                              start=False, stop=True), reads=["QTB", ("SM", c2)], writes=[("P", 1)])
                ob_ = (4 * T + b) % 2
                S.op("act", lambda e: e.activation(out=T1[:, :], in_=PS[1][:, 0:128], func=AF.Square, accum_out=SSO[:, 0:1]), reads=[("P", 1)], writes=["T1", "SSO"])
                rstd(SSO[:, 0:1], RO[:, 0:1], "SSO", "RO", 128)
                S.op("dve", lambda e: e.scalar_tensor_tensor(out=T1[:, :], in0=PS[1][:, 0:128], scalar=RO[:, 0:1], in1=HNW[:, :], op0=ALU.mult, op1=ALU.mult),
                     reads=[("P", 1), "RO", "HNW"], writes=["T1"])
                yield
                S.op("act", lambda e, b=b: e.activation(out=EG[:, :], in_=GTOK[:, par, b, :], func=AF.Exp, scale=-1.0), reads=[("GTOK", par, b)], writes=["EG"])
                S.op("dve", lambda e: e.tensor_scalar(out=EG[:, :], in0=EG[:, :], scalar1=1.0, scalar2=None, op0=ALU.add), reads=["EG"], writes=["EG"])
                S.op("dve", lambda e: e.reciprocal(out=EG[:, :], in_=EG[:, :]), reads=["EG"], writes=["EG"])
                S.op("dve", lambda e, b=b: e.tensor_tensor(out=GG[:, :], in0=GTOK[:, par, b, :], in1=EG[:, :], op=ALU.mult), reads=["EG", ("GTOK", par, b)], writes=["GG"])
                S.op("dve", lambda e, ob_=ob_: e.tensor_tensor(out=OAT[:, ob_, :], in0=T1[:, :], in1=GG[:, :], op=ALU.mult), reads=["T1", "GG"], writes=[("OAT", ob_)])
                S.op("sp", lambda e, ob_=ob_, b=b: e.dma_start(out=oAv[:, 4 * T + b, :], in_=OAT[:, ob_, :]), reads=[("OAT", ob_)], dma="oa%d" % ob_)
                yield

        def gen_F(T):
            par = T % 2
            tsl = slice(T * 512, (T + 1) * 512)
            nkb = 4 * T + 4

            def s_stage(j):
                lo = 0 if j < 4 * T else 128 * (j - 4 * T)
                sp_ = j % 2
                sps = PS[4 + sp_]
                S.op("pe", lambda e: e.matmul(sps[:, lo:512], lhsT=KT[:, j * 128:(j + 1) * 128], rhs=QT[:, par, lo:512], start=True, stop=False),
                     reads=[("KT", j // 4), ("QT", par)], writes=[("P", 4 + sp_)])
                diag = j >= 4 * T
                S.op("pe", lambda e: e.matmul(sps[:, lo:512], lhsT=ONESB[0:1, :], rhs=CROW[0:1, par, lo:512], start=False, stop=(not diag)),
                     reads=[("CROW", par), "CB"], writes=[("P", 4 + sp_)])
                if diag:
                    S.op("pe", lambda e: e.matmul(sps[:, lo:lo + 128], lhsT=IDENT, rhs=MASKNEG, start=False, stop=True),
                         reads=["CB"], writes=[("P", 4 + sp_)])
                S.op("act", lambda e: e.activation(out=PT[:, sp_, lo:512], in_=sps[:, lo:512], func=AF.Exp, bias=BIAS[:, par, j:j + 1]),
                     reads=[("P", 4 + sp_), ("BIAS", par)], writes=[("PT", sp_)])

            def pv_stage(j):
                lo = 0 if j < 4 * T else 128 * (j - 4 * T)
                sp_ = j % 2
                S.op("pe", lambda e: e.matmul(PS[6][:, lo:512], lhsT=V[:, j, :], rhs=PT[:, sp_, lo:512], start=(j == 0), stop=(j == nkb - 1)),
                     reads=[("V", j), ("PT", sp_)], writes=[("P", 6)])
                S.op("pe", lambda e: e.matmul(PS[7][:, lo:512], lhsT=ONESB, rhs=PT[:, sp_, lo:512], start=(j == 0), stop=(j == nkb - 1)),
                     reads=["CB", ("PT", sp_)], writes=[("P", 7)])

            s_stage(0)
            yield
            for j in range(nkb):
                if j + 1 < nkb:
                    s_stage(j + 1)
                pv_stage(j)
                yield
            S.op("dve", lambda e: e.reciprocal(out=ODEN[:, :], in_=PS[7][:, :]), reads=[("P", 7)], writes=["ODEN"])
            S.op("dve", lambda e: e.tensor_tensor(out=OOUT[:, :], in0=PS[6][:, :], in1=ODEN[:, :], op=ALU.mult), reads=[("P", 6), "ODEN"], writes=["OOUT"])
            S.op("sp", lambda e: e.dma_start(out=oBT[:, tsl], in_=OOUT[:, :]), reads=["OOUT"], dma="ob")
            yield

        for _ in gen_A(0):
            pass
        for T in range(NTILES):
            gens = [gen_H(T), gen_F(T)]
            if T + 1 < NTILES:
                gens.append(gen_A(T + 1))
            while gens:
                for g in list(gens):
                    try:
                        next(g)
                    except StopIteration:
                        gens.remove(g)
        S.barrier()
        S.emit(nc, block, st)
    return nc


def prep_l1(inp, ntok=SEQ):
    x = np.asarray(inp["x"], np.float32)[0]
    xT = np.ascontiguousarray(x[:ntok].T)
    w_in = np.asarray(inp["w_in"], np.float32)[0]
    nv1 = _colvec(inp["norm_mix_pre"][0])
    p = np.arange(128)
    ones = np.ones((128, 128), np.float32)
    ident = np.eye(128, dtype=np.float32)
    maskneg = np.where(p[:, None] > p[None, :], -30000.0, 0.0).astype(np.float32)
    bdmask = ((p[:, None] // 64 == p[None, :] // 64) & (p[:, None] <= p[None, :])).astype(np.float32)
    cb = np.concatenate([ones, ident, maskneg, bdmask], axis=1).astype(ml_dtypes.bfloat16)
    reset = np.tile((np.arange(512) % 64 != 0).astype(np.float32)[None, :], (128, 1))
    cf = np.ascontiguousarray(np.concatenate([reset, np.ones((128, 512), np.float32)], axis=1))
    lbl_all = np.asarray(inp["hgrn_lb_logits"], np.float32)
    hnw = np.ascontiguousarray(np.tile(np.asarray(inp["hgrn_norm_w"], np.float32)[0][None, :], (128, 1)))
    maps = []
    for c in range(NCORES):
        cs = lambda base: w_in[:, base + c * 128: base + (c + 1) * 128]
        w1 = np.ascontiguousarray(np.concatenate([cs(0), cs(1024), cs(4096), cs(5120), cs(2048), cs(3072), cs(6144)], axis=1))
        wf = np.ascontiguousarray(np.repeat(w_in[:, 7168 + c: 7169 + c], 2, axis=1))
        lbl = np.ascontiguousarray(lbl_all[:, c * 128:(c + 1) * 128].T)
        bfx = np.full((128, 1), np.asarray(inp["b_fox_f"], np.float32)[0, c], np.float32)
        maps.append(dict(xT=xT, w1=w1, wf=wf, nv1=nv1, lbl=lbl, hnw=hnw, bfx=bfx, cb=cb, cf=cf))
    return maps


def post_l1(results):
    oA = np.concatenate([r["oA"] for r in results], axis=1)
    oB = np.concatenate([np.ascontiguousarray(r["oBT"].T) for r in results], axis=1)
    return oA, oB


_CACHE = {}


def kernel(**inputs):
    inputs = {k: np.asarray(v) for k, v in inputs.items()}
    if "l1" not in _CACHE:
        _CACHE["l1"] = build_l1(SEQ // 512)
        _CACHE["l2"] = build_l2()
    cores = list(range(NCORES))
    r1 = run_bass_kernel_spmd(_CACHE["l1"], prep_l1(inputs), core_ids=cores)
    oA, oB = post_l1(r1.results)
    import os
    if os.environ.get("L1ONLY"):
        return np.zeros((1, SEQ, D_MODEL), np.float32)
    r2 = run_bass_kernel_spmd(_CACHE["l2"], prep_l2(inputs, oA, oB), core_ids=cores)
    return post_l2(r2.results).astype(np.float32)
```

```python
import contextlib
import numpy as np
import ml_dtypes
import concourse.bass as bass
import concourse.mybir as mybir
from concourse.bass_utils import run_bass_kernel_spmd

F32 = mybir.dt.float32
BF16 = mybir.dt.bfloat16
ALU = mybir.AluOpType
AF = mybir.ActivationFunctionType

D_MODEL = 2048
SEQ = 8192
D_FF = 5632
EPS = 1e-6
NCORES = 8

ENGS = ("pe", "act", "dve", "pool", "sp")


class Sched:
    def __init__(self):
        self.ops = {e: [] for e in ENGS}
        self.lastw = {}
        self.readers = {}
        self.dma_cnt = {}

    def _tok(self, eng, rec, idx):
        if rec["dma"] is not None:
            return ("dma", rec["dma"], rec["dma_val"])
        return ("eng", eng, idx)

    def op(self, eng, fn, reads=(), writes=(), dma=None):
        idx = len(self.ops[eng])
        rec = dict(fn=fn, dma=dma, needed=False, deps={})
        if dma is not None:
            self.dma_cnt[dma] = self.dma_cnt.get(dma, 0) + 1
            rec["dma_val"] = 16 * self.dma_cnt[dma]
        tok = self._tok(eng, rec, idx)
        deps = rec["deps"]

        def add(t):
            if t is None:
                return
            if t[0] == "dma":
                cur = self.dma_cnt[t[1]] - (1 if dma == t[1] else 0)
                t = ("dma", t[1], 16 * cur)
            if t[0] == "eng" and t[1] == "pe" and eng == "pe" and dma is None:
                return
            k = (t[0], t[1])
            if k not in deps or deps[k] < t[2]:
                deps[k] = t[2]

        for key in reads:
            add(self.lastw.get(key))
        for key in writes:
            add(self.lastw.get(key))
            for t in self.readers.get(key, {}).values():
                add(t)
        self.ops[eng].append(rec)
        for key in writes:
            self.lastw[key] = tok
            self.readers[key] = {}
        for key in reads:
            r = self.readers.setdefault(key, {})
            k = (tok[0], tok[1])
            if k not in r or r[k][2] < tok[2]:
                r[k] = tok
        return tok

    def barrier(self):
        toks = []
        for e in ENGS:
            for i in range(len(self.ops[e]) - 1, -1, -1):
                if self.ops[e][i]["dma"] is None and self.ops[e][i]["fn"] is not None:
                    toks.append(("eng", e, i))
                    break
        for k, c in self.dma_cnt.items():
            toks.append(("dma", k, 16 * c))
        for e in ENGS:
            rec = dict(fn=None, dma=None, needed=False, deps={})
            for t in toks:
                rec["deps"][(t[0], t[1])] = t[2]
            self.ops[e].append(rec)
        self.lastw = {}
        self.readers = {}

    def emit(self, nc, block, stack):
        sems = {}
        for e in ENGS:
            sems[("eng", e)] = stack.enter_context(nc.semaphore("s_" + e))
        for k in self.dma_cnt:
            sems[("dma", k)] = stack.enter_context(nc.semaphore("d_" + str(k)))
        for e in ENGS:
            for rec in self.ops[e]:
                for (kind, name), val in rec["deps"].items():
                    if kind == "eng":
                        self.ops[name][val]["needed"] = True
        vals = {}
        for e in ENGS:
            cnt = 0
            for i, rec in enumerate(self.ops[e]):
                if rec["needed"]:
                    cnt += 1
                    vals[(e, i)] = cnt
            assert cnt < 60000, (e, cnt)
        self.n_emitted = {e: len(self.ops[e]) for e in ENGS}

        def run(e, engine):
            waited = {}
            for i, rec in enumerate(self.ops[e]):
                for (kind, name), val in rec["deps"].items():
                    v = vals[(name, val)] if kind == "eng" else val
                    sk = (kind, name)
                    if waited.get(sk, 0) >= v:
                        continue
                    engine.wait_ge(sems[sk], v)
                    waited[sk] = v
                if rec["fn"] is None:
                    continue
                ins = rec["fn"](engine)
                if rec["dma"] is not None:
                    ins.then_inc(sems[("dma", rec["dma"])], 16)
                elif rec["needed"]:
                    ins.then_inc(sems[("eng", e)], 1)

        @block.tensor
        def _(eng):
            run("pe", eng)

        @block.scalar
        def _(eng):
            run("act", eng)

        @block.vector
        def _(eng):
            run("dve", eng)

        @block.gpsimd
        def _(eng):
            run("pool", eng)

        @block.sync
        def _(eng):
            run("sp", eng)


def _v3(ap, k):
    return ap.rearrange("p (k n) -> p k n", k=k)


def build_l2(NT=1024):
    nc = bass.Bass("TRN2", target_bir_lowering=False)
    S = Sched()
    NTT = NT // 512
    dt = nc.dram_tensor
    xT = dt("xT", [2048, NT], F32, kind="ExternalInput").ap()
    oT = dt("oT", [2048, NT], F32, kind="ExternalInput").ap()
    wg = dt("wg", [2048, 4096], F32, kind="ExternalInput").ap()
    wua = dt("wua", [1024, 2048], F32, kind="ExternalInput").ap()
    wub = dt("wub", [1024, 2048], F32, kind="ExternalInput").ap()
    wo = dt("wo", [2048, 2048], F32, kind="ExternalInput").ap()
    wfi = dt("wfi", [2048, 2 * D_FF], F32, kind="ExternalInput").ap()
    wfd = dt("wfd", [D_FF, 2048], F32, kind="ExternalInput").ap()
    nv = dt("nv", [128, 64], F32, kind="ExternalInput").ap()
    cst = dt("cst", [128, 128], BF16, kind="ExternalInput").ap()
    outT = dt("outT", [2048, NT], F32, kind="ExternalOutput").ap()
    zd = dt("zd", [2048, NT], F32).ap()
    x1d = dt("x1d", [2048, NT], F32).ap()

    def cm(ap):
        return ap.rearrange("(k p) n -> p k n", p=128)

    xTv, oTv, wgv, wuav, wubv, wov, wfiv, wfdv = map(cm, (xT, oT, wg, wua, wub, wo, wfi, wfd))
    outTv, zdv, x1dv = cm(outT), cm(zd), cm(x1d)

    with contextlib.ExitStack() as st:
        sb = lambda name, shape, d: st.enter_context(nc.sbuf_tensor(name, shape, d))
        A = sb("A", [128, 44 * NT], BF16)
        B = sb("B", [128, 16 * NT], BF16)
        W = sb("W", [128, 16384], BF16)
        T = sb("T", [128, 8, 512], F32)
        SQT = sb("SQT", [128, 4, 512], BF16)
        R = sb("R", [128, 2, NT], F32)
        NV = sb("NV", [128, 64], F32)
        ONES = sb("ONES", [128, 128], BF16)
        PS = [st.enter_context(nc.psum_tensor("ps%d" % i, [128, 512], F32)) for i in range(8)]
        block = st.enter_context(nc.Block())

        xb = _v3(A[:, 0:16 * NT], 16)
        ob = _v3(A[:, 16 * NT:32 * NT], 16)
        SP0 = 32 * NT
        SQ = _v3(A[:, SP0 + 8192:SP0 + 12288], 16)
        XS = _v3(W[:, 8192:16384].bitcast(F32), 16)
        hid = _v3(A[:, 0:44 * NT], 44)
        mb = _v3(B[:, :], 16)
        x1b = mb

        S.op("sp", lambda e: e.dma_start(out=NV[:, :], in_=nv[:, :]), writes=["NV"], dma="c0")
        S.op("sp", lambda e: e.dma_start(out=ONES[:, :], in_=cst[:, :]), writes=["ONES"], dma="c1")

        def rstd_from(ps_ap, r_ap, keyp, keyr):
            S.op("dve", lambda e: e.tensor_scalar(out=r_ap, in0=ps_ap, scalar1=1.0 / D_MODEL, scalar2=EPS,
                                                  op0=ALU.mult, op1=ALU.add), reads=[keyp], writes=[keyr])
            S.op("act", lambda e: e.activation(out=r_ap, in_=r_ap, func=AF.Sqrt), reads=[keyr], writes=[keyr])
            S.op("dve", lambda e: e.reciprocal(out=r_ap, in_=r_ap), reads=[keyr], writes=[keyr])

        XSK = [("W", 1, 0), ("W", 1, 1)]
        for s in range(NT // 256):
            cs = slice(s * 256, (s + 1) * 256)
            S.op("sp", lambda e, cs=cs: e.dma_start(out=XS[:, :, :], in_=xTv[:, :, cs]), writes=XSK, dma="xs")
            S.op("act", lambda e: e.activation(out=SQ[:, :, :], in_=XS[:, :, :], func=AF.Square), reads=XSK, writes=["SQ"])
            for k in range(16):
                S.op("dve", lambda e, k=k, cs=cs: e.tensor_scalar(out=xb[:, k, cs], in0=XS[:, k, :], scalar1=NV[:, k:k + 1],
                                                                  scalar2=None, op0=ALU.mult),
                     reads=XSK + ["NV"], writes=[("A0", k, s)])
            for k in range(16):
                S.op("pe", lambda e, k=k: e.matmul(PS[0][:, 0:256], lhsT=ONES[:, :], rhs=SQ[:, k, :], start=(k == 0), stop=(k == 15)),
                     reads=["SQ", "ONES"], writes=[("P", 0)])
            rstd_from(PS[0][:, 0:256], R[:, 0, cs], ("P", 0), ("R0", s))
        for h in range(2):
            S.op("pool", lambda e, h=h: e.dma_start(out=ob[:, 8 * h:8 * h + 8, :], in_=oTv[:, 8 * h:8 * h + 8, :]),
                 writes=[("A1", h)], dma="ob%d" % h)
        xb_keys = [("A0", k, s) for k in range(16) for s in range(NT // 256)]
        r0_keys = [("R0", s) for s in range(NT // 256)]

        for g in range(8):
            s_ = g % 2
            base = s_ * 8192
            wga = _v3(W[:, base:base + 4096], 16)
            wgb = _v3(W[:, base + 4096:base + 8192], 16)
            wa = _v3(A[:, SP0 + s_ * 4096:SP0 + s_ * 4096 + 2048], 8)
            wb = _v3(A[:, SP0 + s_ * 4096 + 2048:SP0 + s_ * 4096 + 4096], 8)
            gs = slice(g * 256, (g + 1) * 256)
            gs2 = slice(2048 + g * 256, 2048 + (g + 1) * 256)
            wk = ("W", s_)
            dk = "w%d" % s_
            S.op("pool", lambda e, wga=wga, gs=gs: e.dma_start(out=wga[:, :, :], in_=wgv[:, :, gs]), writes=[wk + (0,)], dma=dk + "a")
            S.op("pool", lambda e, wgb=wgb, gs2=gs2: e.dma_start(out=wgb[:, :, :], in_=wgv[:, :, gs2]), writes=[wk + (1,)], dma=dk + "b")
            S.op("pool", lambda e, wa=wa, gs=gs: e.dma_start(out=wa[:, :, :], in_=wuav[:, :, gs]), writes=[wk + (2,)], dma=dk + "c")
            S.op("pool", lambda e, wb=wb, gs=gs: e.dma_start(out=wb[:, :, :], in_=wubv[:, :, gs]), writes=[wk + (3,)], dma=dk + "d")
            for nb in range(2):
                n = g * 2 + nb
                ns = slice(nb * 128, (nb + 1) * 128)
                for t in range(NTT):
                    ts_ = slice(t * 512, (t + 1) * 512)
                    par = (n * NTT + t) % 2
                    pga, pgb, pya, pyb = (PS[4 * par + i] for i in range(4))
                    kga, kgb, kya, kyb = (("P", 4 * par + i) for i in range(4))
                    t0, t1 = T[:, 2 * par, :], T[:, 2 * par + 1, :]
                    k0, k1 = ("T", 2 * par), ("T", 2 * par + 1)
                    for k in range(16):
                        S.op("pe", lambda e, k=k, pga=pga, wga=wga, ns=ns, ts_=ts_: e.matmul(pga[:, :], lhsT=wga[:, k, ns], rhs=xb[:, k, ts_], start=(k == 0), stop=(k == 15)),
                             reads=[wk + (0,)] + xb_keys, writes=[kga])
                    for k in range(16):
                        S.op("pe", lambda e, k=k, pgb=pgb, wgb=wgb, ns=ns, ts_=ts_: e.matmul(pgb[:, :], lhsT=wgb[:, k, ns], rhs=xb[:, k, ts_], start=(k == 0), stop=(k == 15)),
                             reads=[wk + (1,)] + xb_keys, writes=[kgb])
                    for k in range(8):
                        S.op("pe", lambda e, k=k, pya=pya, wa=wa, ns=ns, ts_=ts_: e.matmul(pya[:, :], lhsT=wa[:, k, ns], rhs=ob[:, k, ts_], start=(k == 0), stop=(k == 7)),
                             reads=[wk + (2,), ("A1", 0)], writes=[kya])
                    for k in range(8):
                        S.op("pe", lambda e, k=k, pyb=pyb, wb=wb, ns=ns, ts_=ts_: e.matmul(pyb[:, :], lhsT=wb[:, k, ns], rhs=ob[:, 8 + k, ts_], start=(k == 0), stop=(k == 7)),
                             reads=[wk + (3,), ("A1", 1)], writes=[kyb])
                    S.op("dve", lambda e, t0=t0, pga=pga, ts_=ts_: e.tensor_tensor(out=t0, in0=pga[:, :], in1=R[:, 0, ts_], op=ALU.mult),
                         reads=[kga] + r0_keys, writes=[k0])
                    S.op("act", lambda e, t0=t0: e.activation(out=t0, in_=t0, func=AF.Sigmoid), reads=[k0], writes=[k0])
                    S.op("dve", lambda e, t1=t1, pgb=pgb, ts_=ts_: e.tensor_tensor(out=t1, in0=pgb[:, :], in1=R[:, 0, ts_], op=ALU.mult),
                         reads=[kgb] + r0_keys, writes=[k1])
                    S.op("act", lambda e, t1=t1: e.activation(out=t1, in_=t1, func=AF.Sigmoid), reads=[k1], writes=[k1])
                    S.op("dve", lambda e, t0=t0, pya=pya: e.tensor_tensor(out=t0, in0=pya[:, :], in1=t0, op=ALU.mult),
                         reads=[kya, k0], writes=[k0])
                    S.op("dve", lambda e, t1=t1, pyb=pyb: e.tensor_tensor(out=t1, in0=pyb[:, :], in1=t1, op=ALU.mult),
                         reads=[kyb, k1], writes=[k1])
                    S.op("dve", lambda e, t0=t0, t1=t1, n=n, ts_=ts_: e.tensor_tensor(out=mb[:, n, ts_], in0=t0, in1=t1, op=ALU.add),
                         reads=[k0, k1], writes=[("B", n, t)])

        S.barrier()
        ZS = _v3(A[:, 0:32 * NT].bitcast(F32), 16)

        def evac_stats(ps, pkey, nchunk, t, ssb, first, last, dram_v, par):
            zt = T[:, 4 + par, :]
            zk = ("T", 4 + par)
            sq = SQT[:, par, :]
            sk = ("SQT", par)
            S.op("act", lambda e: e.activation(out=zt, in_=ps[:, :], func=AF.Copy), reads=[pkey], writes=[zk])
            S.op("act", lambda e: e.activation(out=sq, in_=ps[:, :], func=AF.Square), reads=[pkey], writes=[sk])
            S.op("sp", lambda e: e.dma_start(out=dram_v[:, nchunk, t * 512:(t + 1) * 512], in_=zt), reads=[zk],
                 writes=[("zd", nchunk, t)], dma="zst%d" % par)
            return lambda: S.op("pe", lambda e: e.matmul(PS[ssb][:, :], lhsT=ONES[:, :], rhs=sq, start=first, stop=last),
                                reads=[sk, "ONES"], writes=[("P", ssb)])

        pend = []
        for g in range(8):
            s_ = g % 2
            base = s_ * 8192
            wos = _v3(W[:, base:base + 4096], 16)
            wk = ("W", s_)
            gs = slice(g * 256, (g + 1) * 256)
            S.op("pool", lambda e, wos=wos, gs=gs: e.dma_start(out=wos[:, :, :], in_=wov[:, :, gs]), writes=[wk], dma="w%d" % s_)
            for nb in range(2):
                n = g * 2 + nb
                ns = slice(nb * 128, (nb + 1) * 128)
                for t in range(NTT):
                    ts_ = slice(t * 512, (t + 1) * 512)
                    pb = (n * NTT + t) % 4
                    par = (n * NTT + t) % 2
                    for k in range(16):
                        S.op("pe", lambda e, k=k, pb=pb, wos=wos, ns=ns, ts_=ts_: e.matmul(PS[pb][:, :], lhsT=wos[:, k, ns], rhs=mb[:, k, ts_], start=(k == 0), stop=(k == 15)),
                             reads=[wk] + [("B", k, t) for k in range(16)], writes=[("P", pb)])
                    for f_ in pend:
                        f_()
                    pend.clear()
                    sq = SQT[:, par, :]
                    S.op("act", lambda e, pb=pb, n=n, ts_=ts_: e.activation(out=ZS[:, n, ts_], in_=PS[pb][:, :], func=AF.Copy), reads=[("P", pb)], writes=[("ZS", n, t)])
                    S.op("act", lambda e, pb=pb, sq=sq: e.activation(out=sq, in_=PS[pb][:, :], func=AF.Square), reads=[("P", pb)], writes=[("SQT", par)])
                    pend.append(lambda sq=sq, t=t, n=n, par=par: S.op("pe", lambda e: e.matmul(PS[6 + t][:, :], lhsT=ONES[:, :], rhs=sq, start=(n == 0), stop=(n == 15)),
                                                                       reads=[("SQT", par), "ONES"], writes=[("P", 6 + t)]))
        for f_ in pend:
            f_()
        pend.clear()
        for t in range(NTT):
            rstd_from(PS[6 + t][:, :], R[:, 1, t * 512:(t + 1) * 512], ("P", 6 + t), ("R1", t))

        for t in range(NTT):
            ts_ = slice(t * 512, (t + 1) * 512)
            for n in range(16):
                sl4 = n % 4
                par = n % 2
                ta = T[:, sl4, :]
                ka = ("T", sl4)
                S.op("sp", lambda e, ta=ta, n=n, ts_=ts_: e.dma_start(out=ta, in_=xTv[:, n, ts_]), writes=[ka], dma="ra%d" % sl4)
                S.op("dve", lambda e, n=n, ts_=ts_: e.scalar_tensor_tensor(out=ZS[:, n, ts_], in0=ZS[:, n, ts_], scalar=NV[:, 16 + n:17 + n], in1=R[:, 1, ts_],
                                                                            op0=ALU.mult, op1=ALU.mult), reads=[("ZS", n, t), "NV", ("R1", t)], writes=[("ZS", n, t)])
                S.op("dve", lambda e, ta=ta, n=n, ts_=ts_: e.tensor_tensor(out=ta, in0=ta, in1=ZS[:, n, ts_], op=ALU.add), reads=[ka, ("ZS", n, t)], writes=[ka])
                S.op("sp", lambda e, ta=ta, n=n, ts_=ts_: e.dma_start(out=x1dv[:, n, ts_], in_=ta), reads=[ka], writes=[("x1d", n, t)], dma="x1st")
                sq = SQT[:, 2 + par, :]
                sk = ("SQT", 2 + par)
                S.op("dve", lambda e, ta=ta, n=n, ts_=ts_: e.tensor_scalar(out=x1b[:, n, ts_], in0=ta, scalar1=NV[:, 32 + n:33 + n], scalar2=None, op0=ALU.mult),
                     reads=[ka, "NV"], writes=[("B", n, t)])
                S.op("act", lambda e, ta=ta, sq=sq: e.activation(out=sq, in_=ta, func=AF.Square), reads=[ka], writes=[sk])
                S.op("pe", lambda e, sq=sq, n=n, t=t: e.matmul(PS[4 + t][:, :], lhsT=ONES[:, :], rhs=sq, start=(n == 0), stop=(n == 15)),
                     reads=[sk, "ONES"], writes=[("P", 4 + t)])
        for t in range(NTT):
            rstd_from(PS[4 + t][:, :], R[:, 0, t * 512:(t + 1) * 512], ("P", 4 + t), ("R2", t))

        S.barrier()
        xk = {t: [("B", k, t) for k in range(16)] for t in range(NTT)}
        for g in range(22):
            s_ = g % 2
            base = s_ * 8192
            wfg = _v3(W[:, base:base + 4096], 16)
            wfu = _v3(W[:, base + 4096:base + 8192], 16)
            wk = ("W", s_)
            gs = slice(g * 256, (g + 1) * 256)
            gs2 = slice(D_FF + g * 256, D_FF + (g + 1) * 256)
            S.op("pool", lambda e, wfg=wfg, gs=gs: e.dma_start(out=wfg[:, :, :], in_=wfiv[:, :, gs]), writes=[wk + (0,)], dma="w%da" % s_)
            S.op("pool", lambda e, wfu=wfu, gs2=gs2: e.dma_start(out=wfu[:, :, :], in_=wfiv[:, :, gs2]), writes=[wk + (1,)], dma="w%db" % s_)
            for nb in range(2):
                f = g * 2 + nb
                ns = slice(nb * 128, (nb + 1) * 128)
                for t in range(NTT):
                    ts_ = slice(t * 512, (t + 1) * 512)
                    par = (f * NTT + t) % 2
                    pg_, pu_ = PS[2 * par], PS[2 * par + 1]
                    kg_, ku_ = ("P", 2 * par), ("P", 2 * par + 1)
                    t0, t1 = T[:, 4 + 2 * par, :], T[:, 5 + 2 * par, :]
                    k0, k1 = ("T", 4 + 2 * par), ("T", 5 + 2 * par)
                    for k in range(16):
                        S.op("pe", lambda e, k=k, pg_=pg_, wfg=wfg, ns=ns, ts_=ts_: e.matmul(pg_[:, :], lhsT=wfg[:, k, ns], rhs=x1b[:, k, ts_], start=(k == 0), stop=(k == 15)),
                             reads=[wk + (0,)] + xk[t], writes=[kg_])
                    for k in range(16):
                        S.op("pe", lambda e, k=k, pu_=pu_, wfu=wfu, ns=ns, ts_=ts_: e.matmul(pu_[:, :], lhsT=wfu[:, k, ns], rhs=x1b[:, k, ts_], start=(k == 0), stop=(k == 15)),
                             reads=[wk + (1,)] + xk[t], writes=[ku_])
                    S.op("dve", lambda e, t0=t0, pg_=pg_, ts_=ts_: e.tensor_tensor(out=t0, in0=pg_[:, :], in1=R[:, 0, ts_], op=ALU.mult),
                         reads=[kg_, ("R2", t)], writes=[k0])
                    S.op("act", lambda e, t0=t0: e.activation(out=t0, in_=t0, func=AF.Silu), reads=[k0], writes=[k0])
                    S.op("dve", lambda e, t1=t1, pu_=pu_, ts_=ts_: e.tensor_tensor(out=t1, in0=pu_[:, :], in1=R[:, 0, ts_], op=ALU.mult),
                         reads=[ku_, ("R2", t)], writes=[k1])
                    S.op("dve", lambda e, t0=t0, t1=t1, f=f, ts_=ts_: e.tensor_tensor(out=hid[:, f, ts_], in0=t0, in1=t1, op=ALU.mult),
                         reads=[k0, k1], writes=[("hid", f, t)])

        S.barrier()
        for g in range(8):
            s_ = g % 2
            wd = _v3((W if s_ == 0 else B)[:, 0:11264], 44)
            wk = ("W", s_)
            gs = slice(g * 256, (g + 1) * 256)
            for hh in range(2):
                S.op("pool", lambda e, wd=wd, gs=gs, hh=hh: e.dma_start(out=wd[:, 22 * hh:22 * hh + 22, :], in_=wfdv[:, 22 * hh:22 * hh + 22, gs]),
                     writes=[wk + (hh,)], dma="w%d%s" % (s_, "ab"[hh]))
            for nb in range(2):
                n = g * 2 + nb
                ns = slice(nb * 128, (nb + 1) * 128)
                for t in range(NTT):
                    ts_ = slice(t * 512, (t + 1) * 512)
                    pb = (n * NTT + t) % 4
                    for k in range(44):
                        S.op("pe", lambda e, k=k, pb=pb, wd=wd, ns=ns, ts_=ts_: e.matmul(PS[pb][:, :], lhsT=wd[:, k, ns], rhs=hid[:, k, ts_], start=(k == 0), stop=(k == 43)),
                             reads=[wk + (0,), wk + (1,)] + [("hid", f, t) for f in range(44)], writes=[("P", pb)])
                    for f_ in pend:
                        f_()
                    pend.clear()
                    pend.append(evac_stats(PS[pb], ("P", pb), n, t, 6 + t, n == 0, n == 15, zdv, (n * NTT + t) % 2))
        for f_ in pend:
            f_()
        pend.clear()
        for t in range(NTT):
            rstd_from(PS[6 + t][:, :], R[:, 1, t * 512:(t + 1) * 512], ("P", 6 + t), ("R3", t))
        S.barrier()
        for t in range(NTT):
            ts_ = slice(t * 512, (t + 1) * 512)
            for n in range(16):
                sl4 = n % 4
                ta, tb = T[:, sl4, :], T[:, 4 + sl4, :]
                ka, kb = ("T", sl4), ("T", 4 + sl4)
                S.op("sp", lambda e, ta=ta, n=n, ts_=ts_: e.dma_start(out=ta, in_=x1dv[:, n, ts_]), reads=[("x1d", n, t)], writes=[ka], dma="ra%d" % sl4)
                S.op("act", lambda e, tb=tb, n=n, ts_=ts_: e.dma_start(out=tb, in_=zdv[:, n, ts_]), reads=[("zd", n, t)], writes=[kb], dma="rb%d" % sl4)
                S.op("dve", lambda e, tb=tb, n=n, ts_=ts_: e.scalar_tensor_tensor(out=tb, in0=tb, scalar=NV[:, 48 + n:49 + n], in1=R[:, 1, ts_],
                                                                                   op0=ALU.mult, op1=ALU.mult), reads=[kb, "NV", ("R3", t)], writes=[kb])
                S.op("dve", lambda e, ta=ta, tb=tb: e.tensor_tensor(out=ta, in0=ta, in1=tb, op=ALU.add), reads=[ka, kb], writes=[ka])
                S.op("sp", lambda e, ta=ta, n=n, ts_=ts_: e.dma_start(out=outTv[:, n, ts_], in_=ta), reads=[ka], writes=[("out", n, t)], dma="ost")

        S.barrier()
        S.emit(nc, block, st)
    return nc


def _colvec(v):
    return np.ascontiguousarray(np.asarray(v, np.float32).reshape(16, 128).T)


def prep_l2(inp, oA, oB):
    x = np.asarray(inp["x"], np.float32)[0]
    w_in = np.asarray(inp["w_in"], np.float32)[0]
    wg = np.ascontiguousarray(w_in[:, 7176:7176 + 4096])
    nv = np.ascontiguousarray(np.concatenate([_colvec(inp["norm_mix_pre"][0]), _colvec(inp["norm_mix_post"][0]),
                                              _colvec(inp["norm_ffn_pre"][0]), _colvec(inp["norm_ffn_post"][0])], axis=1))
    ones = np.ones((128, 128), ml_dtypes.bfloat16)
    common = dict(wg=wg, wua=np.ascontiguousarray(inp["w_up_a"][0], dtype=np.float32), wub=np.ascontiguousarray(inp["w_up_b"][0], dtype=np.float32),
                  wo=np.ascontiguousarray(inp["w_o"][0], dtype=np.float32), wfi=np.ascontiguousarray(inp["w_ffn_in"][0], dtype=np.float32),
                  wfd=np.ascontiguousarray(inp["w_ffn_down"][0], dtype=np.float32), nv=nv, cst=ones)
    maps = []
    for c in range(NCORES):
        sl = slice(c * 1024, (c + 1) * 1024)
        oT = np.ascontiguousarray(np.concatenate([oA[sl].T, oB[sl].T], axis=0), dtype=np.float32)
        maps.append(dict(common, xT=np.ascontiguousarray(x[sl].T), oT=oT))
    return maps


def post_l2(results):
    return np.concatenate([np.ascontiguousarray(r["outT"].T) for r in results], axis=0)[None]


def build_l1(NTILES=16, stage=9):
    nc = bass.Bass("TRN2", target_bir_lowering=False)
    S = Sched()
    NTOK = NTILES * 512
    dt = nc.dram_tensor
    xT = dt("xT", [2048, NTOK], F32, kind="ExternalInput").ap()
    w1 = dt("w1", [2048, 896], F32, kind="ExternalInput").ap()
    wf = dt("wf", [2048, 2], F32, kind="ExternalInput").ap()
    nv1 = dt("nv1", [128, 16], F32, kind="ExternalInput").ap()
    lbl = dt("lbl", [128, 2], F32, kind="ExternalInput").ap()
    hnw = dt("hnw", [128, 128], F32, kind="ExternalInput").ap()
    bfx = dt("bfx", [128, 1], F32, kind="ExternalInput").ap()
    cb = dt("cb", [128, 640], BF16, kind="ExternalInput").ap()
    cf = dt("cf", [128, 1024], F32, kind="ExternalInput").ap()
    oA = dt("oA", [NTOK, 128], F32, kind="ExternalOutput").ap()
    oBT = dt("oBT", [128, NTOK], F32, kind="ExternalOutput").ap()
    xTv = xT.rearrange("(k p) n -> p k n", p=128)
    w1v = w1.rearrange("(k p) n -> p k n", p=128)
    wfv = wf.rearrange("(k p) n -> p k n", p=128)
    oAv = oA.rearrange("(b p) v -> p b v", p=128)

    with contextlib.ExitStack() as st:
        sb = lambda name, shape, d: st.enter_context(nc.sbuf_tensor(name, shape, d))
        XS = sb("XS", [128, 2, 8, 512], F32)
        XB = sb("XB", [128, 16, 512], BF16)
        SQ = sb("SQ", [128, 16, 512], BF16)
        W1 = sb("W1", [128, 16, 896], BF16)
        WF = sb("WF", [128, 16, 2], BF16)
        NV = sb("NV", [128, 16], F32)
        LBL = sb("LBL", [128, 2], F32)
        LB = sb("LB", [128, 2], F32)
        HNW = sb("HNW", [128, 128], F32)
        BFX = sb("BFX", [128, 1], F32)
        CB = sb("CB", [128, 640], BF16)
        CF = sb("CF", [128, 1024], F32)
        RT = sb("RT", [128, 512], F32)
        RC = sb("RC", [128, 4], F32)
        KT = sb("KT", [128, NTOK], BF16)
        V = sb("V", [128, NTILES * 4, 128], BF16)
        QT = sb("QT", [128, 2, 512], BF16)
        TQ = sb("TQ", [128, 512], F32)
        TZ = sb("TZ", [128, 512], F32)
        TE = sb("TE", [128, 512], F32)
        TS = sb("TS", [128, 512], F32)
        TF = sb("TF", [128, 512], F32)
        TK = sb("TK", [128, 512], F32)
        TB = sb("TB", [128, 512], F32)
        TD = sb("TD", [128, 512], F32)
        TE1 = sb("TE1", [128, 512], F32)
        TE2 = sb("TE2", [128, 512], F32)
        QTB = sb("QTB", [128, 512], BF16)
        KTB = sb("KTB", [128, 512], BF16)
        KTOK = sb("KTOK", [128, 4, 128], BF16)
        VTOK = sb("VTOK", [128, 2, 4, 128], BF16)
        GTOK = sb("GTOK", [128, 2, 4, 128], F32)
        EBM = sb("EBM", [128, 8], F32)
        EBL = sb("EBL", [128, 8], F32)
        EBLM = sb("EBLM", [128, 8], F32)
        BDM = sb("BDM", [128, 8], F32)
        ST = sb("ST", [128, 128], F32)
        SM = sb("SM", [128, 2, 128], BF16)
        SCB = sb("SCB", [128, 128], BF16)
        TKV = sb("TKV", [128, 128], F32)
        T1 = sb("T1", [128, 128], F32)
        TSC = sb("TSC", [128, 128], F32)
        EG = sb("EG", [128, 128], F32)
        GG = sb("GG", [128, 128], F32)
        OAT = sb("OAT", [128, 2, 128], F32)
        SSO = sb("SSO", [128, 1], F32)
        RO = sb("RO", [128, 1], F32)
        PT = sb("PT", [128, 2, 512], BF16)
        CROW = sb("CROW", [128, 2, 512], BF16)
        LROW = sb("LROW", [1, 512], F32)
        ZF = sb("ZF", [1, 512], F32)
        POSC = sb("POSC", [128, NTILES * 4], F32)
        RP = sb("RP", [128, NTILES + 1], F32)
        BIAS = sb("BIAS", [128, 2, NTILES * 4], F32)
        ODEN = sb("ODEN", [128, 512], F32)
        OOUT = sb("OOUT", [128, 512], F32)
        PS = [st.enter_context(nc.psum_tensor("ps%d" % i, [128, 512], F32)) for i in range(8)]
        block = st.enter_context(nc.Block())

        ONESB, IDENT, MASKNEG, BDMASK, E0 = CB[:, 0:128], CB[:, 128:256], CB[:, 256:384], CB[:, 384:512], CB[:, 512:640]
        RESET, ONESF = CF[:, 0:512], CF[:, 512:1024]

        def ld(eng, dst, src, key, dk):
            S.op(eng, lambda e: e.dma_start(out=dst, in_=src), writes=[key], dma=dk)

        ld("sp", NV[:, :], nv1[:, :], "NV", "c0")
        ld("sp", LBL[:, :], lbl[:, :], "LBL", "c1")
        ld("sp", HNW[:, :], hnw[:, :], "HNW", "c2")
        ld("sp", BFX[:, :], bfx[:, :], "BFX", "c3")
        ld("sp", CB[:, :], cb[:, :], "CB", "c4")
        ld("sp", CF[:, :], cf[:, :], "CF", "c5")
        ld("pool", W1[:, 0:8, :], w1v[:, 0:8, :], ("W1", 0), "w1a")
        ld("pool", W1[:, 8:16, :], w1v[:, 8:16, :], ("W1", 1), "w1b")
        ld("pool", WF[:, :, :], wfv[:, :, :], "WF", "wf")
        W1K = [("W1", 0), ("W1", 1)]

        S.op("dve", lambda e: e.tensor_tensor(out=LB[:, 0:1], in0=LBL[:, 0:1], in1=LBL[:, 1:2], op=ALU.subtract), reads=["LBL"], writes=["LB"])
        S.op("act", lambda e: e.activation(out=LB[:, 1:2], in_=LB[:, 0:1], func=AF.Exp, scale=-1.0), reads=["LB"], writes=["LB"])
        S.op("dve", lambda e: e.tensor_scalar(out=LB[:, 0:1], in0=LB[:, 1:2], scalar1=1.0, scalar2=None, op0=ALU.add), reads=["LB"], writes=["LB"])
        S.op("dve", lambda e: e.reciprocal(out=LB[:, 0:1], in_=LB[:, 0:1]), reads=["LB"], writes=["LB"])
        S.op("dve", lambda e: e.tensor_tensor(out=LB[:, 1:2], in0=LB[:, 1:2], in1=LB[:, 0:1], op=ALU.mult), reads=["LB"], writes=["LB"])
        S.op("dve", lambda e: e.memset(ST[:, :], 0.0), writes=["ST"])
        S.op("dve", lambda e: e.memset(RP[:, :], 0.0), writes=["RP"])
        S.op("dve", lambda e: e.memset(CROW[:, :, :], 0.0), writes=[("CROW", 0), ("CROW", 1)])

        def rstd(ps_ap, r_ap, pkey, rkey, n):
            S.op("dve", lambda e: e.tensor_scalar(out=r_ap, in0=ps_ap, scalar1=1.0 / n, scalar2=EPS, op0=ALU.mult, op1=ALU.add),
                 reads=[pkey], writes=[rkey])
            S.op("act", lambda e: e.activation(out=r_ap, in_=r_ap, func=AF.Ln), reads=[rkey], writes=[rkey])
            S.op("act", lambda e: e.activation(out=r_ap, in_=r_ap, func=AF.Exp, scale=-0.5), reads=[rkey], writes=[rkey])

        def gen_A(T):
            par = T % 2
            tsl = slice(T * 512, (T + 1) * 512)
            for h in range(2):
                S.op("sp", lambda e, h=h: e.dma_start(out=XS[:, h, :, :], in_=xTv[:, 8 * h:8 * h + 8, tsl]), writes=[("XS", h)], dma="xs%d" % h)
            yield
            for h in range(2):
                S.op("act", lambda e, h=h: e.activation(out=SQ[:, 8 * h:8 * h + 8, :], in_=XS[:, h, :, :], func=AF.Square),
                     reads=[("XS", h)], writes=[("SQ", h)])
                for k in range(8):
                    if k % 2 == 0:
                        S.op("dve", lambda e, h=h, k=k: e.tensor_scalar(out=XB[:, 8 * h + k, :], in0=XS[:, h, k, :], scalar1=NV[:, 8 * h + k:8 * h + k + 1],
                                                                         scalar2=None, op0=ALU.mult), reads=[("XS", h), "NV"], writes=[("XB", h, k)])
                    else:
                        S.op("act", lambda e, h=h, k=k: e.activation(out=XB[:, 8 * h + k, :], in_=XS[:, h, k, :], func=AF.Copy, scale=NV[:, 8 * h + k:8 * h + k + 1]),
                             reads=[("XS", h), "NV"], writes=[("XB", h, k)])
                yield
            SQK = [("SQ", 0), ("SQ", 1)]
            XBK = [("XB", h, k) for h in range(2) for k in range(8)]
            for k in range(16):
                S.op("pe", lambda e, k=k: e.matmul(PS[0][:, :], lhsT=ONESB, rhs=SQ[:, k, :], start=(k == 0), stop=(k == 15)),
                     reads=SQK + ["CB"], writes=[("P", 0)])
            rstd(PS[0][:, :], RT[:, :], ("P", 0), "RT", D_MODEL)
            yield
            for b in range(4):
                for k in range(16):
                    S.op("pe", lambda e, k=k, b=b: e.matmul(PS[0][:, 400 + b:401 + b], lhsT=SQ[:, k, b * 128:(b + 1) * 128], rhs=ONESB[:, 0:1],
                                                            start=(k == 0), stop=(k == 15)), reads=SQK + ["CB"], writes=[("P", 0)])
            rstd(PS[0][:, 400:404], RC[:, :], ("P", 0), "RC", D_MODEL)
            yield
            for j in range(4):
                for k in range(16):
                    S.op("pe", lambda e, k=k, j=j: e.matmul(PS[0][:, :], lhsT=W1[:, k, j * 128:(j + 1) * 128], rhs=XB[:, k, :],
                                                            start=(k == 0), stop=(k == 15)), reads=W1K + XBK, writes=[("P", 0)])
                if j == 0:
                    S.op("dve", lambda e: e.tensor_tensor(out=TQ[:, :], in0=PS[0][:, :], in1=RT[:, :], op=ALU.mult), reads=[("P", 0), "RT"], writes=["TQ"])
                elif j == 1:
                    S.op("dve", lambda e: e.tensor_tensor(out=TZ[:, :], in0=PS[0][:, :], in1=RT[:, :], op=ALU.mult), reads=[("P", 0), "RT"], writes=["TZ"])
                elif j == 2:
                    S.op("dve", lambda e: e.scalar_tensor_tensor(out=QT[:, par, :], in0=PS[0][:, :], scalar=128.0 ** -0.5, in1=RT[:, :], op0=ALU.mult, op1=ALU.mult),
                         reads=[("P", 0), "RT"], writes=[("QT", par)])
                else:
                    S.op("dve", lambda e: e.tensor_tensor(out=KT[:, tsl], in0=PS[0][:, :], in1=RT[:, :], op=ALU.mult), reads=[("P", 0), "RT"], writes=[("KT", T)])
                yield
            for b in range(4):
                for k in range(16):
                    S.op("pe", lambda e, k=k, b=b: e.matmul(PS[0][:, 0:384], lhsT=XB[:, k, b * 128:(b + 1) * 128], rhs=W1[:, k, 512:896],
                                                            start=(k == 0), stop=(k == 15)), reads=W1K + XBK, writes=[("P", 0)])
                S.op("act", lambda e, b=b: e.activation(out=VTOK[:, par, b, :], in_=PS[0][:, 0:128], func=AF.Copy, scale=RC[:, b:b + 1]),
                     reads=[("P", 0), "RC"], writes=[("VTOK", par, b)])
                S.op("act", lambda e, b=b: e.activation(out=GTOK[:, par, b, :], in_=PS[0][:, 128:256], func=AF.Copy, scale=RC[:, b:b + 1]),
                     reads=[("P", 0), "RC"], writes=[("GTOK", par, b)])
                S.op("act", lambda e, b=b: e.activation(out=V[:, 4 * T + b, :], in_=PS[0][:, 256:384], func=AF.Copy, scale=RC[:, b:b + 1]),
                     reads=[("P", 0), "RC"], writes=[("V", 4 * T + b)])
                yield
            for k in range(16):
                S.op("pe", lambda e, k=k: e.matmul(PS[0][0:1, :], lhsT=WF[:, k, 0:1], rhs=XB[:, k, :], start=(k == 0), stop=(k == 15)),
                     reads=["WF"] + XBK, writes=[("P", 0)])
            S.op("dve", lambda e: e.tensor_tensor(out=ZF[0:1, :], in0=PS[0][0:1, :], in1=RT[0:1, :], op=ALU.mult), reads=[("P", 0), "RT"], writes=["ZF"])
            S.op("dve", lambda e: e.tensor_scalar(out=ZF[0:1, :], in0=ZF[0:1, :], scalar1=BFX[0:1, 0:1], scalar2=None, op0=ALU.add), reads=["ZF", "BFX"], writes=["ZF"])
            S.op("act", lambda e: e.activation(out=ZF[0:1, :], in_=ZF[0:1, :], func=AF.Exp, scale=-1.0), reads=["ZF"], writes=["ZF"])
            S.op("act", lambda e: e.activation(out=ZF[0:1, :], in_=ZF[0:1, :], func=AF.Ln, bias=1.0), reads=["ZF"], writes=["ZF"])
            S.op("dve", lambda e: e.tensor_tensor_scan(out=LROW[0:1, :], data0=ONESF[0:1, :], data1=ZF[0:1, :], initial=0.0, op0=ALU.mult, op1=ALU.add),
                 reads=["ZF", "CF"], writes=["LROW"])
            S.op("dve", lambda e: e.tensor_scalar(out=CROW[0:1, par, :], in0=LROW[0:1, :], scalar1=-1.0, scalar2=None, op0=ALU.mult), reads=["LROW"], writes=[("CROW", par)])
            yield
            for b in range(4):
                S.op("pe", lambda e, b=b: e.matmul(PS[0][:, 384 + b:385 + b], lhsT=LROW[0:1, b * 128:(b + 1) * 128], rhs=ONESF[0:1, 0:1], start=True, stop=True),
                     reads=["LROW", "CF"], writes=[("P", 0)])
            S.op("pe", lambda e: e.matmul(PS[0][:, 388:389], lhsT=ONESF[0:1, 0:128], rhs=LROW[0:1, 511:512], start=True, stop=True),
                 reads=["LROW", "CF"], writes=[("P", 0)])
            S.op("dve", lambda e: e.tensor_scalar(out=POSC[:, 4 * T:4 * T + 4], in0=PS[0][:, 384:388], scalar1=RP[:, T:T + 1], scalar2=None, op0=ALU.add),
                 reads=[("P", 0), "RP"], writes=["POSC"])
            S.op("dve", lambda e: e.tensor_tensor(out=RP[:, T + 1:T + 2], in0=PS[0][:, 388:389], in1=RP[:, T:T + 1], op=ALU.add),
                 reads=[("P", 0), "RP"], writes=["RP"])
            S.op("dve", lambda e: e.tensor_scalar(out=BIAS[:, par, 0:4 * T + 4], in0=POSC[:, 0:4 * T + 4], scalar1=RP[:, T:T + 1], scalar2=None, op0=ALU.subtract),
                 reads=["POSC", "RP"], writes=[("BIAS", par)])
            yield

        def gen_H(T):
            par = T % 2
            S.op("act", lambda e: e.activation(out=TE[:, :], in_=TZ[:, :], func=AF.Exp, scale=-1.0), reads=["TZ"], writes=["TE"])
            S.op("dve", lambda e: e.tensor_scalar(out=TS[:, :], in0=TE[:, :], scalar1=1.0, scalar2=None, op0=ALU.add), reads=["TE"], writes=["TS"])
            S.op("dve", lambda e: e.reciprocal(out=TS[:, :], in_=TS[:, :]), reads=["TS"], writes=["TS"])
            S.op("dve", lambda e: e.tensor_scalar(out=TF[:, :], in0=TS[:, :], scalar1=LB[:, 1:2], scalar2=LB[:, 0:1], op0=ALU.mult, op1=ALU.add),
                 reads=["TS", "LB"], writes=["TF"])
            yield
            S.op("act", lambda e: e.activation(out=TF[:, :], in_=TF[:, :], func=AF.Ln), reads=["TF"], writes=["TF"])
            S.op("dve", lambda e: e.scalar_tensor_tensor(out=TK[:, :], in0=TE[:, :], scalar=LB[:, 1:2], in1=TS[:, :], op0=ALU.mult, op1=ALU.mult),
                 reads=["TE", "TS", "LB"], writes=["TK"])
            S.op("dve", lambda e: e.tensor_tensor_scan(out=TB[:, :], data0=RESET, data1=TF[:, :], initial=0.0, op0=ALU.mult, op1=ALU.add),
                 reads=["TF", "CF"], writes=["TB"])
            yield
            TBv = TB[:, :].rearrange("p (c n) -> p c n", n=64)
            TDv = TD[:, :].rearrange("p (c n) -> p c n", n=64)
            S.op("dve", lambda e: e.tensor_tensor(out=TDv, in0=TBv, in1=TBv[:, :, 31:32].to_broadcast([128, 8, 64]), op=ALU.subtract), reads=["TB"], writes=["TD"])
            S.op("act", lambda e: e.activation(out=TE1[:, :], in_=TD[:, :], func=AF.Exp), reads=["TD"], writes=["TE1"])
            S.op("act", lambda e: e.activation(out=TE2[:, :], in_=TD[:, :], func=AF.Exp, scale=-1.0), reads=["TD"], writes=["TE2"])
            yield
            S.op("dve", lambda e: e.tensor_tensor(out=QTB[:, :], in0=TQ[:, :], in1=TE1[:, :], op=ALU.mult), reads=["TQ", "TE1"], writes=["QTB"])
            S.op("dve", lambda e: e.tensor_tensor(out=KTB[:, :], in0=TK[:, :], in1=TE2[:, :], op=ALU.mult), reads=["TK", "TE2"], writes=["KTB"])
            S.op("act", lambda e: e.activation(out=EBM[:, :], in_=TBv[:, :, 31], func=AF.Exp), reads=["TB"], writes=["EBM"])
            S.op("act", lambda e: e.activation(out=EBL[:, :], in_=TBv[:, :, 63], func=AF.Exp), reads=["TB"], writes=["EBL"])
            S.op("dve", lambda e: e.tensor_tensor(out=BDM[:, :], in0=TBv[:, :, 63], in1=TBv[:, :, 31], op=ALU.subtract), reads=["TB"], writes=["BDM"])
            S.op("act", lambda e: e.activation(out=EBLM[:, :], in_=BDM[:, :], func=AF.Exp), reads=["BDM"], writes=["EBLM"])
            ESC = ["EBM", "EBL", "EBLM"]
            yield
            for b in range(4):
                bs = slice(b * 128, (b + 1) * 128)
                trp = PS[1][:, 448:512].bitcast(BF16)
                S.op("pe", lambda e, bs=bs, trp=trp: e.transpose(out=trp, in_=KTB[:, bs], identity=IDENT), reads=["KTB", "CB"], writes=[("P", 1)])
                S.op("act", lambda e, b=b, trp=trp: e.activation(out=KTOK[:, b, :], in_=trp, func=AF.Copy), reads=[("P", 1)], writes=[("KTOK", b)])
                yield
                S.op("pe", lambda e, bs=bs: e.matmul(PS[1][:, 0:128], lhsT=KTB[:, bs], rhs=QTB[:, bs], start=True, stop=True),
                     reads=["KTB", "QTB"], writes=[("P", 1)])
                S.op("act", lambda e: e.activation(out=TSC[:, :], in_=PS[1][:, 0:128], func=AF.Copy), reads=[("P", 1)], writes=["TSC"])
                S.op("dve", lambda e: e.tensor_tensor(out=SCB[:, :], in0=TSC[:, :], in1=BDMASK, op=ALU.mult), reads=["TSC", "CB"], writes=["SCB"])
                yield
                for c2 in range(2):
                    rs = slice(c2 * 64, (c2 + 1) * 64)
                    S.op("pe", lambda e, b=b, c2=c2, rs=rs: e.matmul(PS[2 + c2][:, 0:128], lhsT=KTOK[rs, b, :], rhs=VTOK[rs, par, b, :], start=True, stop=True),
                         reads=[("KTOK", b), ("VTOK", par, b)], writes=[("P", 2 + c2)])
                yield
                for c2 in range(2):
                    c = 2 * b + c2
                    S.op("dve", lambda e, c=c, c2=c2: e.tensor_scalar(out=SM[:, c2, :], in0=ST[:, :], scalar1=EBM[:, c:c + 1], scalar2=None, op0=ALU.mult),
                         reads=["ST"] + ESC, writes=[("SM", c2)])
                    S.op("dve", lambda e, c=c, c2=c2: e.tensor_scalar(out=TKV[:, :], in0=PS[2 + c2][:, 0:128], scalar1=EBLM[:, c:c + 1], scalar2=None, op0=ALU.mult),
                         reads=[("P", 2 + c2)] + ESC, writes=["TKV"])
                    S.op("dve", lambda e, c=c: e.scalar_tensor_tensor(out=ST[:, :], in0=ST[:, :], scalar=EBL[:, c:c + 1], in1=TKV[:, :], op0=ALU.mult, op1=ALU.add),
                         reads=["ST", "TKV"] + ESC, writes=["ST"])
                yield
                S.op("pe", lambda e, b=b: e.matmul(PS[1][:, 0:128], lhsT=SCB[:, :], rhs=VTOK[:, par, b, :], start=True, stop=False),
                     reads=["SCB", ("VTOK", par, b)], writes=[("P", 1)])
                for c2 in range(2):
                    S.op("pe", lambda e, b=b, c2=c2: e.matmul(PS[1][c2 * 64:(c2 + 1) * 64, 0:128], lhsT=QTB[:, b * 128 + c2 * 64:b * 128 + (c2 + 1) * 64], rhs=SM[:, c2, :],
                                                              start=False, stop=True), reads=["QTB", ("SM", c2)], writes=[("P", 1)])
                ob_ = (4 * T + b) % 2
                S.op("act", lambda e: e.activation(out=T1[:, :], in_=PS[1][:, 0:128], func=AF.Square, accum_out=SSO[:, 0:1]), reads=[("P", 1)], writes=["T1", "SSO"])
                rstd(SSO[:, 0:1], RO[:, 0:1], "SSO", "RO", 128)
                S.op("dve", lambda e: e.scalar_tensor_tensor(out=T1[:, :], in0=PS[1][:, 0:128], scalar=RO[:, 0:1], in1=HNW[:, :], op0=ALU.mult, op1=ALU.mult),
                     reads=[("P", 1), "RO", "HNW"], writes=["T1"])
                yield
                S.op("act", lambda e, b=b: e.activation(out=EG[:, :], in_=GTOK[:, par, b, :], func=AF.Exp, scale=-1.0), reads=[("GTOK", par, b)], writes=["EG"])
                S.op("dve", lambda e: e.tensor_scalar(out=EG[:, :], in0=EG[:, :], scalar1=1.0, scalar2=None, op0=ALU.add), reads=["EG"], writes=["EG"])
                S.op("dve", lambda e: e.reciprocal(out=EG[:, :], in_=EG[:, :]), reads=["EG"], writes=["EG"])
                S.op("dve", lambda e, b=b: e.tensor_tensor(out=GG[:, :], in0=GTOK[:, par, b, :], in1=EG[:, :], op=ALU.mult), reads=["EG", ("GTOK", par, b)], writes=["GG"])
                S.op("dve", lambda e, ob_=ob_: e.tensor_tensor(out=OAT[:, ob_, :], in0=T1[:, :], in1=GG[:, :], op=ALU.mult), reads=["T1", "GG"], writes=[("OAT", ob_)])
                S.op("sp", lambda e, ob_=ob_, b=b: e.dma_start(out=oAv[:, 4 * T + b, :], in_=OAT[:, ob_, :]), reads=[("OAT", ob_)], dma="oa%d" % ob_)
                yield

        def gen_F(T):
            par = T % 2
            tsl = slice(T * 512, (T + 1) * 512)
            nkb = 4 * T + 4

            def s_stage(j):
                lo = 0 if j < 4 * T else 128 * (j - 4 * T)
                sp_ = j % 2
                sps = PS[4 + sp_]
                S.op("pe", lambda e: e.matmul(sps[:, lo:512], lhsT=KT[:, j * 128:(j + 1) * 128], rhs=QT[:, par, lo:512], start=True, stop=False),
                     reads=[("KT", j // 4), ("QT", par)], writes=[("P", 4 + sp_)])
                diag = j >= 4 * T
                S.op("pe", lambda e: e.matmul(sps[:, lo:512], lhsT=E0, rhs=CROW[:, par, lo:512], start=False, stop=(not diag)),
                     reads=[("CROW", par), "CB"], writes=[("P", 4 + sp_)])
                if diag:
                    S.op("pe", lambda e: e.matmul(sps[:, lo:lo + 128], lhsT=IDENT, rhs=MASKNEG, start=False, stop=True),
                         reads=["CB"], writes=[("P", 4 + sp_)])
                S.op("act", lambda e: e.activation(out=PT[:, sp_, lo:512], in_=sps[:, lo:512], func=AF.Exp, bias=BIAS[:, par, j:j + 1]),
                     reads=[("P", 4 + sp_), ("BIAS", par)], writes=[("PT", sp_)])

            def pv_stage(j):
                lo = 0 if j < 4 * T else 128 * (j - 4 * T)
                sp_ = j % 2
                S.op("pe", lambda e: e.matmul(PS[6][:, lo:512], lhsT=V[:, j, :], rhs=PT[:, sp_, lo:512], start=(j == 0), stop=(j == nkb - 1)),
                     reads=[("V", j), ("PT", sp_)], writes=[("P", 6)])
                S.op("pe", lambda e: e.matmul(PS[7][:, lo:512], lhsT=ONESB, rhs=PT[:, sp_, lo:512], start=(j == 0), stop=(j == nkb - 1)),
                     reads=["CB", ("PT", sp_)], writes=[("P", 7)])

            s_stage(0)
            yield
            for j in range(nkb):
                if j + 1 < nkb:
                    s_stage(j + 1)
                pv_stage(j)
                yield
            S.op("dve", lambda e: e.reciprocal(out=ODEN[:, :], in_=PS[7][:, :]), reads=[("P", 7)], writes=["ODEN"])
            S.op("dve", lambda e: e.tensor_tensor(out=OOUT[:, :], in0=PS[6][:, :], in1=ODEN[:, :], op=ALU.mult), reads=[("P", 6), "ODEN"], writes=["OOUT"])
            S.op("sp", lambda e: e.dma_start(out=oBT[:, tsl], in_=OOUT[:, :]), reads=["OOUT"], dma="ob")
            yield

        for _ in gen_A(0):
            pass
        for T in range(NTILES):
            gens = [gen_H(T), gen_F(T)]
            if T + 1 < NTILES:
                gens.append(gen_A(T + 1))
            while gens:
                for g in list(gens):
                    try:
                        next(g)
                    except StopIteration:
                        gens.remove(g)
        S.barrier()
        S.emit(nc, block, st)
    return nc


def prep_l1(inp, ntok=SEQ):
    x = np.asarray(inp["x"], np.float32)[0]
    xT = np.ascontiguousarray(x[:ntok].T)
    w_in = np.asarray(inp["w_in"], np.float32)[0]
    nv1 = _colvec(inp["norm_mix_pre"][0])
    p = np.arange(128)
    ones = np.ones((128, 128), np.float32)
    ident = np.eye(128, dtype=np.float32)
    maskneg = np.where(p[:, None] > p[None, :], -30000.0, 0.0).astype(np.float32)
    bdmask = ((p[:, None] // 64 == p[None, :] // 64) & (p[:, None] <= p[None, :])).astype(np.float32)
    e0 = np.zeros((128, 128), np.float32)
    e0[0, :] = 1.0
    cb = np.concatenate([ones, ident, maskneg, bdmask, e0], axis=1).astype(ml_dtypes.bfloat16)
    reset = np.tile((np.arange(512) % 64 != 0).astype(np.float32)[None, :], (128, 1))
    cf = np.ascontiguousarray(np.concatenate([reset, np.ones((128, 512), np.float32)], axis=1))
    lbl_all = np.asarray(inp["hgrn_lb_logits"], np.float32)
    hnw = np.ascontiguousarray(np.tile(np.asarray(inp["hgrn_norm_w"], np.float32)[0][None, :], (128, 1)))
    maps = []
    for c in range(NCORES):
        cs = lambda base: w_in[:, base + c * 128: base + (c + 1) * 128]
        w1 = np.ascontiguousarray(np.concatenate([cs(0), cs(1024), cs(4096), cs(5120), cs(2048), cs(3072), cs(6144)], axis=1))
        wf = np.ascontiguousarray(np.repeat(w_in[:, 7168 + c: 7169 + c], 2, axis=1))
        lbl = np.ascontiguousarray(lbl_all[:, c * 128:(c + 1) * 128].T)
        bfx = np.full((128, 1), np.asarray(inp["b_fox_f"], np.float32)[0, c], np.float32)
        maps.append(dict(xT=xT, w1=w1, wf=wf, nv1=nv1, lbl=lbl, hnw=hnw, bfx=bfx, cb=cb, cf=cf))
    return maps


def post_l1(results):
    oA = np.concatenate([r["oA"] for r in results], axis=1)
    oB = np.concatenate([np.ascontiguousarray(r["oBT"].T) for r in results], axis=1)
    return oA, oB


_CACHE = {}


def kernel(**inputs):
    inputs = {k: np.asarray(v) for k, v in inputs.items()}
    if "l1" not in _CACHE:
        _CACHE["l1"] = build_l1(SEQ // 512)
        _CACHE["l2"] = build_l2()
    cores = list(range(NCORES))
    r1 = run_bass_kernel_spmd(_CACHE["l1"], prep_l1(inputs), core_ids=cores)
    oA, oB = post_l1(r1.results)
    import os
    if os.environ.get("L1ONLY"):
        return np.zeros((1, SEQ, D_MODEL), np.float32)
    r2 = run_bass_kernel_spmd(_CACHE["l2"], prep_l2(inputs, oA, oB), core_ids=cores)
    return post_l2(r2.results).astype(np.float32)
```

```python
import contextlib
import numpy as np
import ml_dtypes
import concourse.bass as bass
import concourse.mybir as mybir
from concourse.bass_utils import run_bass_kernel_spmd

F32 = mybir.dt.float32
BF16 = mybir.dt.bfloat16
ALU = mybir.AluOpType
AF = mybir.ActivationFunctionType

D_MODEL = 2048
SEQ = 8192
D_FF = 5632
EPS = 1e-6
NCORES = 8

ENGS = ("pe", "act", "dve", "pool", "sp")


class Sched:
    def __init__(self):
        self.ops = {e: [] for e in ENGS}
        self.lastw = {}
        self.readers = {}
        self.dma_cnt = {}

    def _tok(self, eng, rec, idx):
        if rec["dma"] is not None:
            return ("dma", rec["dma"], rec["dma_val"])
        return ("eng", eng, idx)

    def op(self, eng, fn, reads=(), writes=(), dma=None):
        idx = len(self.ops[eng])
        rec = dict(fn=fn, dma=dma, needed=False, deps={})
        if dma is not None:
            self.dma_cnt[dma] = self.dma_cnt.get(dma, 0) + 1
            rec["dma_val"] = 16 * self.dma_cnt[dma]
        tok = self._tok(eng, rec, idx)
        deps = rec["deps"]

        def add(t):
            if t is None:
                return
            if t[0] == "dma":
                cur = self.dma_cnt[t[1]] - (1 if dma == t[1] else 0)
                t = ("dma", t[1], 16 * cur)
            if t[0] == "eng" and t[1] == "pe" and eng == "pe" and dma is None:
                return
            k = (t[0], t[1])
            if k not in deps or deps[k] < t[2]:
                deps[k] = t[2]

        for key in reads:
            add(self.lastw.get(key))
        for key in writes:
            add(self.lastw.get(key))
            for t in self.readers.get(key, {}).values():
                add(t)
        self.ops[eng].append(rec)
        for key in writes:
            self.lastw[key] = tok
            self.readers[key] = {}
        for key in reads:
            r = self.readers.setdefault(key, {})
            k = (tok[0], tok[1])
            if k not in r or r[k][2] < tok[2]:
                r[k] = tok
        return tok

    def barrier(self):
        toks = []
        for e in ENGS:
            for i in range(len(self.ops[e]) - 1, -1, -1):
                if self.ops[e][i]["dma"] is None and self.ops[e][i]["fn"] is not None:
                    toks.append(("eng", e, i))
                    break
        for k, c in self.dma_cnt.items():
            toks.append(("dma", k, 16 * c))
        for e in ENGS:
            rec = dict(fn=None, dma=None, needed=False, deps={})
            for t in toks:
                rec["deps"][(t[0], t[1])] = t[2]
            self.ops[e].append(rec)
        self.lastw = {}
        self.readers = {}

    def emit(self, nc, block, stack):
        sems = {}
        for e in ENGS:
            sems[("eng", e)] = stack.enter_context(nc.semaphore("s_" + e))
        for k in self.dma_cnt:
            sems[("dma", k)] = stack.enter_context(nc.semaphore("d_" + str(k)))
        for e in ENGS:
            for rec in self.ops[e]:
                for (kind, name), val in rec["deps"].items():
                    if kind == "eng":
                        self.ops[name][val]["needed"] = True
        vals = {}
        for e in ENGS:
            cnt = 0
            for i, rec in enumerate(self.ops[e]):
                if rec["needed"]:
                    cnt += 1
                    vals[(e, i)] = cnt
            assert cnt < 60000, (e, cnt)
        self.n_emitted = {e: len(self.ops[e]) for e in ENGS}

        def run(e, engine):
            waited = {}
            for i, rec in enumerate(self.ops[e]):
                for (kind, name), val in rec["deps"].items():
                    v = vals[(name, val)] if kind == "eng" else val
                    sk = (kind, name)
                    if waited.get(sk, 0) >= v:
                        continue
                    engine.wait_ge(sems[sk], v)
                    waited[sk] = v
                if rec["fn"] is None:
                    continue
                ins = rec["fn"](engine)
                if rec["dma"] is not None:
                    ins.then_inc(sems[("dma", rec["dma"])], 16)
                elif rec["needed"]:
                    ins.then_inc(sems[("eng", e)], 1)

        @block.tensor
        def _(eng):
            run("pe", eng)

        @block.scalar
        def _(eng):
            run("act", eng)

        @block.vector
        def _(eng):
            run("dve", eng)

        @block.gpsimd
        def _(eng):
            run("pool", eng)

        @block.sync
        def _(eng):
            run("sp", eng)


def _v3(ap, k):
    return ap.rearrange("p (k n) -> p k n", k=k)


def build_l2(NT=1024):
    nc = bass.Bass("TRN2", target_bir_lowering=False)
    S = Sched()
    NTT = NT // 512
    dt = nc.dram_tensor
    xT = dt("xT", [2048, NT], F32, kind="ExternalInput").ap()
    oT = dt("oT", [2048, NT], F32, kind="ExternalInput").ap()
    wg = dt("wg", [2048, 4096], F32, kind="ExternalInput").ap()
    wua = dt("wua", [1024, 2048], F32, kind="ExternalInput").ap()
    wub = dt("wub", [1024, 2048], F32, kind="ExternalInput").ap()
    wo = dt("wo", [2048, 2048], F32, kind="ExternalInput").ap()
    wfi = dt("wfi", [2048, 2 * D_FF], F32, kind="ExternalInput").ap()
    wfd = dt("wfd", [D_FF, 2048], F32, kind="ExternalInput").ap()
    nv = dt("nv", [128, 64], F32, kind="ExternalInput").ap()
    cst = dt("cst", [128, 128], BF16, kind="ExternalInput").ap()
    outT = dt("outT", [2048, NT], F32, kind="ExternalOutput").ap()
    zd = dt("zd", [2048, NT], F32).ap()
    x1d = dt("x1d", [2048, NT], F32).ap()

    def cm(ap):
        return ap.rearrange("(k p) n -> p k n", p=128)

    xTv, oTv, wgv, wuav, wubv, wov, wfiv, wfdv = map(cm, (xT, oT, wg, wua, wub, wo, wfi, wfd))
    outTv, zdv, x1dv = cm(outT), cm(zd), cm(x1d)

    with contextlib.ExitStack() as st:
        sb = lambda name, shape, d: st.enter_context(nc.sbuf_tensor(name, shape, d))
        A = sb("A", [128, 44 * NT], BF16)
        B = sb("B", [128, 16 * NT], BF16)
        W = sb("W", [128, 16384], BF16)
        T = sb("T", [128, 8, 512], F32)
        SQT = sb("SQT", [128, 4, 512], BF16)
        R = sb("R", [128, 2, NT], F32)
        NV = sb("NV", [128, 64], F32)
        ONES = sb("ONES", [128, 128], BF16)
        PS = [st.enter_context(nc.psum_tensor("ps%d" % i, [128, 512], F32)) for i in range(8)]
        block = st.enter_context(nc.Block())

        xb = _v3(A[:, 0:16 * NT], 16)
        ob = _v3(A[:, 16 * NT:32 * NT], 16)
        SP0 = 32 * NT
        SQ = _v3(A[:, SP0 + 8192:SP0 + 12288], 16)
        XS = _v3(W[:, 8192:16384].bitcast(F32), 16)
        hid = _v3(A[:, 0:44 * NT], 44)
        mb = _v3(B[:, :], 16)
        x1b = mb

        S.op("sp", lambda e: e.dma_start(out=NV[:, :], in_=nv[:, :]), writes=["NV"], dma="c0")
        S.op("sp", lambda e: e.dma_start(out=ONES[:, :], in_=cst[:, :]), writes=["ONES"], dma="c1")

        def rstd_from(ps_ap, r_ap, keyp, keyr):
            S.op("dve", lambda e: e.tensor_scalar(out=r_ap, in0=ps_ap, scalar1=1.0 / D_MODEL, scalar2=EPS,
                                                  op0=ALU.mult, op1=ALU.add), reads=[keyp], writes=[keyr])
            S.op("act", lambda e: e.activation(out=r_ap, in_=r_ap, func=AF.Sqrt), reads=[keyr], writes=[keyr])
            S.op("dve", lambda e: e.reciprocal(out=r_ap, in_=r_ap), reads=[keyr], writes=[keyr])

        XSK = [("W", 1, 0), ("W", 1, 1)]
        for s in range(NT // 256):
            cs = slice(s * 256, (s + 1) * 256)
            S.op("sp", lambda e, cs=cs: e.dma_start(out=XS[:, :, :], in_=xTv[:, :, cs]), writes=XSK, dma="xs")
            S.op("act", lambda e: e.activation(out=SQ[:, :, :], in_=XS[:, :, :], func=AF.Square), reads=XSK, writes=["SQ"])
            for k in range(16):
                S.op("dve", lambda e, k=k, cs=cs: e.tensor_scalar(out=xb[:, k, cs], in0=XS[:, k, :], scalar1=NV[:, k:k + 1],
                                                                  scalar2=None, op0=ALU.mult),
                     reads=XSK + ["NV"], writes=[("A0", k, s)])
            for k in range(16):
                S.op("pe", lambda e, k=k: e.matmul(PS[0][:, 0:256], lhsT=ONES[:, :], rhs=SQ[:, k, :], start=(k == 0), stop=(k == 15)),
                     reads=["SQ", "ONES"], writes=[("P", 0)])
            rstd_from(PS[0][:, 0:256], R[:, 0, cs], ("P", 0), ("R0", s))
        for h in range(2):
            S.op("pool", lambda e, h=h: e.dma_start(out=ob[:, 8 * h:8 * h + 8, :], in_=oTv[:, 8 * h:8 * h + 8, :]),
                 writes=[("A1", h)], dma="ob%d" % h)
        xb_keys = [("A0", k, s) for k in range(16) for s in range(NT // 256)]
        r0_keys = [("R0", s) for s in range(NT // 256)]

        for g in range(8):
            s_ = g % 2
            base = s_ * 8192
            wga = _v3(W[:, base:base + 4096], 16)
            wgb = _v3(W[:, base + 4096:base + 8192], 16)
            wa = _v3(A[:, SP0 + s_ * 4096:SP0 + s_ * 4096 + 2048], 8)
            wb = _v3(A[:, SP0 + s_ * 4096 + 2048:SP0 + s_ * 4096 + 4096], 8)
            gs = slice(g * 256, (g + 1) * 256)
            gs2 = slice(2048 + g * 256, 2048 + (g + 1) * 256)
            wk = ("W", s_)
            dk = "w%d" % s_
            S.op("pool", lambda e, wga=wga, gs=gs: e.dma_start(out=wga[:, :, :], in_=wgv[:, :, gs]), writes=[wk + (0,)], dma=dk + "a")
            S.op("pool", lambda e, wgb=wgb, gs2=gs2: e.dma_start(out=wgb[:, :, :], in_=wgv[:, :, gs2]), writes=[wk + (1,)], dma=dk + "b")
            S.op("pool", lambda e, wa=wa, gs=gs: e.dma_start(out=wa[:, :, :], in_=wuav[:, :, gs]), writes=[wk + (2,)], dma=dk + "c")
            S.op("pool", lambda e, wb=wb, gs=gs: e.dma_start(out=wb[:, :, :], in_=wubv[:, :, gs]), writes=[wk + (3,)], dma=dk + "d")
            for nb in range(2):
                n = g * 2 + nb
                ns = slice(nb * 128, (nb + 1) * 128)
                for t in range(NTT):
                    ts_ = slice(t * 512, (t + 1) * 512)
                    par = (n * NTT + t) % 2
                    pga, pgb, pya, pyb = (PS[4 * par + i] for i in range(4))
                    kga, kgb, kya, kyb = (("P", 4 * par + i) for i in range(4))
                    t0, t1 = T[:, 2 * par, :], T[:, 2 * par + 1, :]
                    k0, k1 = ("T", 2 * par), ("T", 2 * par + 1)
                    for k in range(16):
                        S.op("pe", lambda e, k=k, pga=pga, wga=wga, ns=ns, ts_=ts_: e.matmul(pga[:, :], lhsT=wga[:, k, ns], rhs=xb[:, k, ts_], start=(k == 0), stop=(k == 15)),
                             reads=[wk + (0,)] + xb_keys, writes=[kga])
                    for k in range(16):
                        S.op("pe", lambda e, k=k, pgb=pgb, wgb=wgb, ns=ns, ts_=ts_: e.matmul(pgb[:, :], lhsT=wgb[:, k, ns], rhs=xb[:, k, ts_], start=(k == 0), stop=(k == 15)),
                             reads=[wk + (1,)] + xb_keys, writes=[kgb])
                    for k in range(8):
                        S.op("pe", lambda e, k=k, pya=pya, wa=wa, ns=ns, ts_=ts_: e.matmul(pya[:, :], lhsT=wa[:, k, ns], rhs=ob[:, k, ts_], start=(k == 0), stop=(k == 7)),
                             reads=[wk + (2,), ("A1", 0)], writes=[kya])
                    for k in range(8):
                        S.op("pe", lambda e, k=k, pyb=pyb, wb=wb, ns=ns, ts_=ts_: e.matmul(pyb[:, :], lhsT=wb[:, k, ns], rhs=ob[:, 8 + k, ts_], start=(k == 0), stop=(k == 7)),
                             reads=[wk + (3,), ("A1", 1)], writes=[kyb])
                    S.op("dve", lambda e, t0=t0, pga=pga, ts_=ts_: e.tensor_tensor(out=t0, in0=pga[:, :], in1=R[:, 0, ts_], op=ALU.mult),
                         reads=[kga] + r0_keys, writes=[k0])
                    S.op("act", lambda e, t0=t0: e.activation(out=t0, in_=t0, func=AF.Sigmoid), reads=[k0], writes=[k0])
                    S.op("dve", lambda e, t1=t1, pgb=pgb, ts_=ts_: e.tensor_tensor(out=t1, in0=pgb[:, :], in1=R[:, 0, ts_], op=ALU.mult),
                         reads=[kgb] + r0_keys, writes=[k1])
                    S.op("act", lambda e, t1=t1: e.activation(out=t1, in_=t1, func=AF.Sigmoid), reads=[k1], writes=[k1])
                    S.op("dve", lambda e, t0=t0, pya=pya: e.tensor_tensor(out=t0, in0=pya[:, :], in1=t0, op=ALU.mult),
                         reads=[kya, k0], writes=[k0])
                    S.op("dve", lambda e, t1=t1, pyb=pyb: e.tensor_tensor(out=t1, in0=pyb[:, :], in1=t1, op=ALU.mult),
                         reads=[kyb, k1], writes=[k1])
                    S.op("dve", lambda e, t0=t0, t1=t1, n=n, ts_=ts_: e.tensor_tensor(out=mb[:, n, ts_], in0=t0, in1=t1, op=ALU.add),
                         reads=[k0, k1], writes=[("B", n, t)])

        S.barrier()
        ZS = _v3(A[:, 0:32 * NT].bitcast(F32), 16)

        def evac_stats(ps, pkey, nchunk, t, ssb, first, last, dram_v, par):
            zt = T[:, 4 + par, :]
            zk = ("T", 4 + par)
            sq = SQT[:, par, :]
            sk = ("SQT", par)
            S.op("act", lambda e: e.activation(out=zt, in_=ps[:, :], func=AF.Copy), reads=[pkey], writes=[zk])
            S.op("act", lambda e: e.activation(out=sq, in_=ps[:, :], func=AF.Square), reads=[pkey], writes=[sk])
            S.op("sp", lambda e: e.dma_start(out=dram_v[:, nchunk, t * 512:(t + 1) * 512], in_=zt), reads=[zk],
                 writes=[("zd", nchunk, t)], dma="zst%d" % par)
            return lambda: S.op("pe", lambda e: e.matmul(PS[ssb][:, :], lhsT=ONES[:, :], rhs=sq, start=first, stop=last),
                                reads=[sk, "ONES"], writes=[("P", ssb)])

        pend = []
        for g in range(8):
            s_ = g % 2
            base = s_ * 8192
            wos = _v3(W[:, base:base + 4096], 16)
            wk = ("W", s_)
            gs = slice(g * 256, (g + 1) * 256)
            S.op("pool", lambda e, wos=wos, gs=gs: e.dma_start(out=wos[:, :, :], in_=wov[:, :, gs]), writes=[wk], dma="w%d" % s_)
            for nb in range(2):
                n = g * 2 + nb
                ns = slice(nb * 128, (nb + 1) * 128)
                for t in range(NTT):
                    ts_ = slice(t * 512, (t + 1) * 512)
                    pb = (n * NTT + t) % 4
                    par = (n * NTT + t) % 2
                    for k in range(16):
                        S.op("pe", lambda e, k=k, pb=pb, wos=wos, ns=ns, ts_=ts_: e.matmul(PS[pb][:, :], lhsT=wos[:, k, ns], rhs=mb[:, k, ts_], start=(k == 0), stop=(k == 15)),
                             reads=[wk] + [("B", k, t) for k in range(16)], writes=[("P", pb)])
                    for f_ in pend:
                        f_()
                    pend.clear()
                    sq = SQT[:, par, :]
                    S.op("act", lambda e, pb=pb, n=n, ts_=ts_: e.activation(out=ZS[:, n, ts_], in_=PS[pb][:, :], func=AF.Copy), reads=[("P", pb)], writes=[("ZS", n, t)])
                    S.op("act", lambda e, pb=pb, sq=sq: e.activation(out=sq, in_=PS[pb][:, :], func=AF.Square), reads=[("P", pb)], writes=[("SQT", par)])
                    pend.append(lambda sq=sq, t=t, n=n, par=par: S.op("pe", lambda e: e.matmul(PS[6 + t][:, :], lhsT=ONES[:, :], rhs=sq, start=(n == 0), stop=(n == 15)),
                                                                       reads=[("SQT", par), "ONES"], writes=[("P", 6 + t)]))
        for f_ in pend:
            f_()
        pend.clear()
        for t in range(NTT):
            rstd_from(PS[6 + t][:, :], R[:, 1, t * 512:(t + 1) * 512], ("P", 6 + t), ("R1", t))

        for t in range(NTT):
            ts_ = slice(t * 512, (t + 1) * 512)
            for n in range(16):
                sl4 = n % 4
                par = n % 2
                ta = T[:, sl4, :]
                ka = ("T", sl4)
                S.op("sp", lambda e, ta=ta, n=n, ts_=ts_: e.dma_start(out=ta, in_=xTv[:, n, ts_]), writes=[ka], dma="ra%d" % sl4)
                S.op("dve", lambda e, n=n, ts_=ts_: e.scalar_tensor_tensor(out=ZS[:, n, ts_], in0=ZS[:, n, ts_], scalar=NV[:, 16 + n:17 + n], in1=R[:, 1, ts_],
                                                                            op0=ALU.mult, op1=ALU.mult), reads=[("ZS", n, t), "NV", ("R1", t)], writes=[("ZS", n, t)])
                S.op("dve", lambda e, ta=ta, n=n, ts_=ts_: e.tensor_tensor(out=ta, in0=ta, in1=ZS[:, n, ts_], op=ALU.add), reads=[ka, ("ZS", n, t)], writes=[ka])
                S.op("pool", lambda e, ta=ta, n=n, ts_=ts_: e.dma_start(out=x1dv[:, n, ts_], in_=ta), reads=[ka], writes=[("x1d", n, t)], dma="x1st")
                sq = SQT[:, 2 + par, :]
                sk = ("SQT", 2 + par)
                S.op("dve", lambda e, ta=ta, n=n, ts_=ts_: e.tensor_scalar(out=x1b[:, n, ts_], in0=ta, scalar1=NV[:, 32 + n:33 + n], scalar2=None, op0=ALU.mult),
                     reads=[ka, "NV"], writes=[("B", n, t)])
                S.op("act", lambda e, ta=ta, sq=sq: e.activation(out=sq, in_=ta, func=AF.Square), reads=[ka], writes=[sk])
                S.op("pe", lambda e, sq=sq, n=n, t=t: e.matmul(PS[4 + t][:, :], lhsT=ONES[:, :], rhs=sq, start=(n == 0), stop=(n == 15)),
                     reads=[sk, "ONES"], writes=[("P", 4 + t)])
        for t in range(NTT):
            rstd_from(PS[4 + t][:, :], R[:, 0, t * 512:(t + 1) * 512], ("P", 4 + t), ("R2", t))

        S.barrier()
        xk = {t: [("B", k, t) for k in range(16)] for t in range(NTT)}
        for g in range(22):
            s_ = g % 2
            base = s_ * 8192
            wfg = _v3(W[:, base:base + 4096], 16)
            wfu = _v3(W[:, base + 4096:base + 8192], 16)
            wk = ("W", s_)
            gs = slice(g * 256, (g + 1) * 256)
            gs2 = slice(D_FF + g * 256, D_FF + (g + 1) * 256)
            S.op("pool", lambda e, wfg=wfg, gs=gs: e.dma_start(out=wfg[:, :, :], in_=wfiv[:, :, gs]), writes=[wk + (0,)], dma="w%da" % s_)
            S.op("pool", lambda e, wfu=wfu, gs2=gs2: e.dma_start(out=wfu[:, :, :], in_=wfiv[:, :, gs2]), writes=[wk + (1,)], dma="w%db" % s_)
            for nb in range(2):
                f = g * 2 + nb
                ns = slice(nb * 128, (nb + 1) * 128)
                for t in range(NTT):
                    ts_ = slice(t * 512, (t + 1) * 512)
                    par = (f * NTT + t) % 2
                    pg_, pu_ = PS[2 * par], PS[2 * par + 1]
                    kg_, ku_ = ("P", 2 * par), ("P", 2 * par + 1)
                    t0, t1 = T[:, 4 + 2 * par, :], T[:, 5 + 2 * par, :]
                    k0, k1 = ("T", 4 + 2 * par), ("T", 5 + 2 * par)
                    for k in range(16):
                        S.op("pe", lambda e, k=k, pg_=pg_, wfg=wfg, ns=ns, ts_=ts_: e.matmul(pg_[:, :], lhsT=wfg[:, k, ns], rhs=x1b[:, k, ts_], start=(k == 0), stop=(k == 15)),
                             reads=[wk + (0,)] + xk[t], writes=[kg_])
                    for k in range(16):
                        S.op("pe", lambda e, k=k, pu_=pu_, wfu=wfu, ns=ns, ts_=ts_: e.matmul(pu_[:, :], lhsT=wfu[:, k, ns], rhs=x1b[:, k, ts_], start=(k == 0), stop=(k == 15)),
                             reads=[wk + (1,)] + xk[t], writes=[ku_])
                    S.op("dve", lambda e, t0=t0, pg_=pg_, ts_=ts_: e.tensor_tensor(out=t0, in0=pg_[:, :], in1=R[:, 0, ts_], op=ALU.mult),
                         reads=[kg_, ("R2", t)], writes=[k0])
                    S.op("act", lambda e, t0=t0: e.activation(out=t0, in_=t0, func=AF.Silu), reads=[k0], writes=[k0])
                    S.op("dve", lambda e, t1=t1, pu_=pu_, ts_=ts_: e.tensor_tensor(out=t1, in0=pu_[:, :], in1=R[:, 0, ts_], op=ALU.mult),
                         reads=[ku_, ("R2", t)], writes=[k1])
                    S.op("dve", lambda e, t0=t0, t1=t1, f=f, ts_=ts_: e.tensor_tensor(out=hid[:, f, ts_], in0=t0, in1=t1, op=ALU.mult),
                         reads=[k0, k1], writes=[("hid", f, t)])

        S.barrier()
        for g in range(8):
            s_ = g % 2
            wd = _v3((W if s_ == 0 else B)[:, 0:11264], 44)
            wk = ("W", s_)
            gs = slice(g * 256, (g + 1) * 256)
            for hh in range(2):
                S.op("pool", lambda e, wd=wd, gs=gs, hh=hh: e.dma_start(out=wd[:, 22 * hh:22 * hh + 22, :], in_=wfdv[:, 22 * hh:22 * hh + 22, gs]),
                     writes=[wk + (hh,)], dma="w%d%s" % (s_, "ab"[hh]))
            for nb in range(2):
                n = g * 2 + nb
                ns = slice(nb * 128, (nb + 1) * 128)
                for t in range(NTT):
                    ts_ = slice(t * 512, (t + 1) * 512)
                    pb = (n * NTT + t) % 4
                    for k in range(44):
                        S.op("pe", lambda e, k=k, pb=pb, wd=wd, ns=ns, ts_=ts_: e.matmul(PS[pb][:, :], lhsT=wd[:, k, ns], rhs=hid[:, k, ts_], start=(k == 0), stop=(k == 43)),
                             reads=[wk + (0,), wk + (1,)] + [("hid", f, t) for f in range(44)], writes=[("P", pb)])
                    for f_ in pend:
                        f_()
                    pend.clear()
                    pend.append(evac_stats(PS[pb], ("P", pb), n, t, 6 + t, n == 0, n == 15, zdv, (n * NTT + t) % 2))
        for f_ in pend:
            f_()
        pend.clear()
        for t in range(NTT):
            rstd_from(PS[6 + t][:, :], R[:, 1, t * 512:(t + 1) * 512], ("P", 6 + t), ("R3", t))
        S.barrier()
        for t in range(NTT):
            ts_ = slice(t * 512, (t + 1) * 512)
            for n in range(16):
                sl4 = n % 4
                ta, tb = T[:, sl4, :], T[:, 4 + sl4, :]
                ka, kb = ("T", sl4), ("T", 4 + sl4)
                S.op("sp", lambda e, ta=ta, n=n, ts_=ts_: e.dma_start(out=ta, in_=x1dv[:, n, ts_]), reads=[("x1d", n, t)], writes=[ka], dma="ra%d" % sl4)
                S.op("act", lambda e, tb=tb, n=n, ts_=ts_: e.dma_start(out=tb, in_=zdv[:, n, ts_]), reads=[("zd", n, t)], writes=[kb], dma="rb%d" % sl4)
                S.op("dve", lambda e, tb=tb, n=n, ts_=ts_: e.scalar_tensor_tensor(out=tb, in0=tb, scalar=NV[:, 48 + n:49 + n], in1=R[:, 1, ts_],
                                                                                   op0=ALU.mult, op1=ALU.mult), reads=[kb, "NV", ("R3", t)], writes=[kb])
                S.op("dve", lambda e, ta=ta, tb=tb: e.tensor_tensor(out=ta, in0=ta, in1=tb, op=ALU.add), reads=[ka, kb], writes=[ka])
                S.op("pool", lambda e, ta=ta, n=n, ts_=ts_: e.dma_start(out=outTv[:, n, ts_], in_=ta), reads=[ka], writes=[("out", n, t)], dma="ost")

        S.barrier()
        S.emit(nc, block, st)
    return nc


def _colvec(v):
    return np.ascontiguousarray(np.asarray(v, np.float32).reshape(16, 128).T)


def prep_l2(inp, oA, oB):
    x = np.asarray(inp["x"], np.float32)[0]
    w_in = np.asarray(inp["w_in"], np.float32)[0]
    wg = np.ascontiguousarray(w_in[:, 7176:7176 + 4096])
    nv = np.ascontiguousarray(np.concatenate([_colvec(inp["norm_mix_pre"][0]), _colvec(inp["norm_mix_post"][0]),
                                              _colvec(inp["norm_ffn_pre"][0]), _colvec(inp["norm_ffn_post"][0])], axis=1))
    ones = np.ones((128, 128), ml_dtypes.bfloat16)
    common = dict(wg=wg, wua=np.ascontiguousarray(inp["w_up_a"][0], dtype=np.float32), wub=np.ascontiguousarray(inp["w_up_b"][0], dtype=np.float32),
                  wo=np.ascontiguousarray(inp["w_o"][0], dtype=np.float32), wfi=np.ascontiguousarray(inp["w_ffn_in"][0], dtype=np.float32),
                  wfd=np.ascontiguousarray(inp["w_ffn_down"][0], dtype=np.float32), nv=nv, cst=ones)
    maps = []
    for c in range(NCORES):
        sl = slice(c * 1024, (c + 1) * 1024)
        oT = np.ascontiguousarray(np.concatenate([oA[sl].T, oB[sl].T], axis=0), dtype=np.float32)
        maps.append(dict(common, xT=np.ascontiguousarray(x[sl].T), oT=oT))
    return maps


def post_l2(results):
    return np.concatenate([np.ascontiguousarray(r["outT"].T) for r in results], axis=0)[None]


def build_l1(NTILES=16, stage=9):
    nc = bass.Bass("TRN2", target_bir_lowering=False)
    S = Sched()
    NTOK = NTILES * 512
    dt = nc.dram_tensor
    xT = dt("xT", [2048, NTOK], F32, kind="ExternalInput").ap()
    w1 = dt("w1", [2048, 896], F32, kind="ExternalInput").ap()
    wf = dt("wf", [2048, 2], F32, kind="ExternalInput").ap()
    nv1 = dt("nv1", [128, 16], F32, kind="ExternalInput").ap()
    lbl = dt("lbl", [128, 2], F32, kind="ExternalInput").ap()
    hnw = dt("hnw", [128, 128], F32, kind="ExternalInput").ap()
    bfx = dt("bfx", [128, 1], F32, kind="ExternalInput").ap()
    cb = dt("cb", [128, 640], BF16, kind="ExternalInput").ap()
    cf = dt("cf", [128, 1024], F32, kind="ExternalInput").ap()
    oA = dt("oA", [NTOK, 128], F32, kind="ExternalOutput").ap()
    oBT = dt("oBT", [128, NTOK], F32, kind="ExternalOutput").ap()
    xTv = xT.rearrange("(k p) n -> p k n", p=128)
    w1v = w1.rearrange("(k p) n -> p k n", p=128)
    wfv = wf.rearrange("(k p) n -> p k n", p=128)
    oAv = oA.rearrange("(b p) v -> p b v", p=128)

    with contextlib.ExitStack() as st:
        sb = lambda name, shape, d: st.enter_context(nc.sbuf_tensor(name, shape, d))
        XS = sb("XS", [128, 2, 8, 512], F32)
        XB = sb("XB", [128, 16, 512], BF16)
        SQ = sb("SQ", [128, 16, 512], BF16)
        W1 = sb("W1", [128, 16, 896], BF16)
        WF = sb("WF", [128, 16, 2], BF16)
        NV = sb("NV", [128, 16], F32)
        LBL = sb("LBL", [128, 2], F32)
        LB = sb("LB", [128, 2], F32)
        HNW = sb("HNW", [128, 128], F32)
        BFX = sb("BFX", [128, 1], F32)
        CB = sb("CB", [128, 640], BF16)
        CF = sb("CF", [128, 1024], F32)
        RT = sb("RT", [128, 512], F32)
        RC = sb("RC", [128, 4], F32)
        KT = sb("KT", [128, NTOK], BF16)
        V = sb("V", [128, NTILES * 4, 128], BF16)
        QT = sb("QT", [128, 2, 512], BF16)
        TQ = sb("TQ", [128, 512], F32)
        TZ = sb("TZ", [128, 512], F32)
        TE = sb("TE", [128, 512], F32)
        TS = sb("TS", [128, 512], F32)
        TF = sb("TF", [128, 512], F32)
        TK = sb("TK", [128, 512], F32)
        TB = sb("TB", [128, 512], F32)
        TD = sb("TD", [128, 512], F32)
        TE1 = sb("TE1", [128, 512], F32)
        TE2 = sb("TE2", [128, 512], F32)
        QTB = sb("QTB", [128, 512], BF16)
        KTB = sb("KTB", [128, 512], BF16)
        KTOK = sb("KTOK", [128, 4, 128], BF16)
        VTOK = sb("VTOK", [128, 2, 4, 128], BF16)
        GTOK = sb("GTOK", [128, 2, 4, 128], F32)
        EBM = sb("EBM", [128, 8], F32)
        EBL = sb("EBL", [128, 8], F32)
        EBLM = sb("EBLM", [128, 8], F32)
        BDM = sb("BDM", [128, 8], F32)
        ST = sb("ST", [128, 128], F32)
        SM = sb("SM", [128, 2, 128], BF16)
        SCB = sb("SCB", [128, 128], BF16)
        TKV = sb("TKV", [128, 128], F32)
        T1 = sb("T1", [128, 128], F32)
        TSC = sb("TSC", [128, 128], F32)
        EG = sb("EG", [128, 128], F32)
        GG = sb("GG", [128, 128], F32)
        OAT = sb("OAT", [128, 2, 128], F32)
        SSO = sb("SSO", [128, 1], F32)
        RO = sb("RO", [128, 1], F32)
        PT = sb("PT", [128, 2, 512], BF16)
        CROW = sb("CROW", [128, 2, 512], BF16)
        LROW = sb("LROW", [1, 512], F32)
        ZF = sb("ZF", [1, 512], F32)
        POSC = sb("POSC", [128, NTILES * 4], F32)
        RP = sb("RP", [128, NTILES + 1], F32)
        BIAS = sb("BIAS", [128, 2, NTILES * 4], F32)
        ODEN = sb("ODEN", [128, 512], F32)
        OOUT = sb("OOUT", [128, 512], F32)
        PS = [st.enter_context(nc.psum_tensor("ps%d" % i, [128, 512], F32)) for i in range(8)]
        block = st.enter_context(nc.Block())

        ONESB, IDENT, MASKNEG, BDMASK, E0 = CB[:, 0:128], CB[:, 128:256], CB[:, 256:384], CB[:, 384:512], CB[:, 512:640]
        RESET, ONESF = CF[:, 0:512], CF[:, 512:1024]

        def ld(eng, dst, src, key, dk):
            S.op(eng, lambda e: e.dma_start(out=dst, in_=src), writes=[key], dma=dk)

        ld("sp", NV[:, :], nv1[:, :], "NV", "c0")
        ld("sp", LBL[:, :], lbl[:, :], "LBL", "c1")
        ld("sp", HNW[:, :], hnw[:, :], "HNW", "c2")
        ld("sp", BFX[:, :], bfx[:, :], "BFX", "c3")
        ld("sp", CB[:, :], cb[:, :], "CB", "c4")
        ld("sp", CF[:, :], cf[:, :], "CF", "c5")
        ld("pool", W1[:, 0:8, :], w1v[:, 0:8, :], ("W1", 0), "w1a")
        ld("pool", W1[:, 8:16, :], w1v[:, 8:16, :], ("W1", 1), "w1b")
        ld("pool", WF[:, :, :], wfv[:, :, :], "WF", "wf")
        W1K = [("W1", 0), ("W1", 1)]

        S.op("dve", lambda e: e.tensor_tensor(out=LB[:, 0:1], in0=LBL[:, 0:1], in1=LBL[:, 1:2], op=ALU.subtract), reads=["LBL"], writes=["LB"])
        S.op("act", lambda e: e.activation(out=LB[:, 1:2], in_=LB[:, 0:1], func=AF.Exp, scale=-1.0), reads=["LB"], writes=["LB"])
        S.op("dve", lambda e: e.tensor_scalar(out=LB[:, 0:1], in0=LB[:, 1:2], scalar1=1.0, scalar2=None, op0=ALU.add), reads=["LB"], writes=["LB"])
        S.op("dve", lambda e: e.reciprocal(out=LB[:, 0:1], in_=LB[:, 0:1]), reads=["LB"], writes=["LB"])
        S.op("dve", lambda e: e.tensor_tensor(out=LB[:, 1:2], in0=LB[:, 1:2], in1=LB[:, 0:1], op=ALU.mult), reads=["LB"], writes=["LB"])
        S.op("dve", lambda e: e.memset(ST[:, :], 0.0), writes=["ST"])
        S.op("dve", lambda e: e.memset(RP[:, :], 0.0), writes=["RP"])
        S.op("dve", lambda e: e.memset(CROW[:, :, :], 0.0), writes=[("CROW", 0), ("CROW", 1)])

        def rstd(ps_ap, r_ap, pkey, rkey, n):
            S.op("dve", lambda e: e.tensor_scalar(out=r_ap, in0=ps_ap, scalar1=1.0 / n, scalar2=EPS, op0=ALU.mult, op1=ALU.add),
                 reads=[pkey], writes=[rkey])
            S.op("act", lambda e: e.activation(out=r_ap, in_=r_ap, func=AF.Ln), reads=[rkey], writes=[rkey])
            S.op("act", lambda e: e.activation(out=r_ap, in_=r_ap, func=AF.Exp, scale=-0.5), reads=[rkey], writes=[rkey])

        def gen_A(T):
            par = T % 2
            tsl = slice(T * 512, (T + 1) * 512)
            for h in range(2):
                S.op("sp", lambda e, h=h: e.dma_start(out=XS[:, h, :, :], in_=xTv[:, 8 * h:8 * h + 8, tsl]), writes=[("XS", h)], dma="xs%d" % h)
            yield
            for h in range(2):
                S.op("act", lambda e, h=h: e.activation(out=SQ[:, 8 * h:8 * h + 8, :], in_=XS[:, h, :, :], func=AF.Square),
                     reads=[("XS", h)], writes=[("SQ", h)])
                for k in range(8):
                    if k % 2 == 0:
                        S.op("dve", lambda e, h=h, k=k: e.tensor_scalar(out=XB[:, 8 * h + k, :], in0=XS[:, h, k, :], scalar1=NV[:, 8 * h + k:8 * h + k + 1],
                                                                         scalar2=None, op0=ALU.mult), reads=[("XS", h), "NV"], writes=[("XB", h, k)])
                    else:
                        S.op("act", lambda e, h=h, k=k: e.activation(out=XB[:, 8 * h + k, :], in_=XS[:, h, k, :], func=AF.Copy, scale=NV[:, 8 * h + k:8 * h + k + 1]),
                             reads=[("XS", h), "NV"], writes=[("XB", h, k)])
                yield
            SQK = [("SQ", 0), ("SQ", 1)]
            XBK = [("XB", h, k) for h in range(2) for k in range(8)]
            for k in range(16):
                S.op("pe", lambda e, k=k: e.matmul(PS[0][:, :], lhsT=ONESB, rhs=SQ[:, k, :], start=(k == 0), stop=(k == 15)),
                     reads=SQK + ["CB"], writes=[("P", 0)])
            rstd(PS[0][:, :], RT[:, :], ("P", 0), "RT", D_MODEL)
            yield
            for b in range(4):
                for k in range(16):
                    S.op("pe", lambda e, k=k, b=b: e.matmul(PS[3][:, 400 + b:401 + b], lhsT=SQ[:, k, b * 128:(b + 1) * 128], rhs=ONESB[:, 0:1],
                                                            start=(k == 0), stop=(k == 15)), reads=SQK + ["CB"], writes=[("P", 3)])
            rstd(PS[3][:, 400:404], RC[:, :], ("P", 3), "RC", D_MODEL)
            yield
            for j in range(4):
                pa = 0 if j % 2 == 0 else 3
                for k in range(16):
                    S.op("pe", lambda e, k=k, j=j, pa=pa: e.matmul(PS[pa][:, :], lhsT=W1[:, k, j * 128:(j + 1) * 128], rhs=XB[:, k, :],
                                                            start=(k == 0), stop=(k == 15)), reads=W1K + XBK, writes=[("P", pa)])
                if j == 0:
                    S.op("dve", lambda e, pa=pa: e.tensor_tensor(out=TQ[:, :], in0=PS[pa][:, :], in1=RT[:, :], op=ALU.mult), reads=[("P", pa), "RT"], writes=["TQ"])
                elif j == 1:
                    S.op("dve", lambda e, pa=pa: e.tensor_tensor(out=TZ[:, :], in0=PS[pa][:, :], in1=RT[:, :], op=ALU.mult), reads=[("P", pa), "RT"], writes=["TZ"])
                elif j == 2:
                    S.op("dve", lambda e, pa=pa: e.scalar_tensor_tensor(out=QT[:, par, :], in0=PS[pa][:, :], scalar=128.0 ** -0.5, in1=RT[:, :], op0=ALU.mult, op1=ALU.mult),
                         reads=[("P", pa), "RT"], writes=[("QT", par)])
                else:
                    S.op("dve", lambda e, pa=pa: e.tensor_tensor(out=KT[:, tsl], in0=PS[pa][:, :], in1=RT[:, :], op=ALU.mult), reads=[("P", pa), "RT"], writes=[("KT", T)])
                yield
            for b in range(4):
                pa = 0 if b % 2 == 0 else 3
                for k in range(16):
                    S.op("pe", lambda e, k=k, b=b, pa=pa: e.matmul(PS[pa][:, 0:384], lhsT=XB[:, k, b * 128:(b + 1) * 128], rhs=W1[:, k, 512:896],
                                                            start=(k == 0), stop=(k == 15)), reads=W1K + XBK, writes=[("P", pa)])
                S.op("act", lambda e, b=b, pa=pa: e.activation(out=VTOK[:, par, b, :], in_=PS[pa][:, 0:128], func=AF.Copy, scale=RC[:, b:b + 1]),
                     reads=[("P", pa), "RC"], writes=[("VTOK", par, b)])
                S.op("act", lambda e, b=b, pa=pa: e.activation(out=GTOK[:, par, b, :], in_=PS[pa][:, 128:256], func=AF.Copy, scale=RC[:, b:b + 1]),
                     reads=[("P", pa), "RC"], writes=[("GTOK", par, b)])
                S.op("act", lambda e, b=b, pa=pa: e.activation(out=V[:, 4 * T + b, :], in_=PS[pa][:, 256:384], func=AF.Copy, scale=RC[:, b:b + 1]),
                     reads=[("P", pa), "RC"], writes=[("V", 4 * T + b)])
                yield
            for k in range(16):
                S.op("pe", lambda e, k=k: e.matmul(PS[0][0:1, :], lhsT=WF[:, k, 0:1], rhs=XB[:, k, :], start=(k == 0), stop=(k == 15)),
                     reads=["WF"] + XBK, writes=[("P", 0)])
            S.op("dve", lambda e: e.tensor_tensor(out=ZF[0:1, :], in0=PS[0][0:1, :], in1=RT[0:1, :], op=ALU.mult), reads=[("P", 0), "RT"], writes=["ZF"])
            S.op("dve", lambda e: e.tensor_scalar(out=ZF[0:1, :], in0=ZF[0:1, :], scalar1=BFX[0:1, 0:1], scalar2=None, op0=ALU.add), reads=["ZF", "BFX"], writes=["ZF"])
            S.op("act", lambda e: e.activation(out=ZF[0:1, :], in_=ZF[0:1, :], func=AF.Exp, scale=-1.0), reads=["ZF"], writes=["ZF"])
            S.op("act", lambda e: e.activation(out=ZF[0:1, :], in_=ZF[0:1, :], func=AF.Ln, bias=1.0), reads=["ZF"], writes=["ZF"])
            S.op("dve", lambda e: e.tensor_tensor_scan(out=LROW[0:1, :], data0=ONESF[0:1, :], data1=ZF[0:1, :], initial=0.0, op0=ALU.mult, op1=ALU.add),
                 reads=["ZF", "CF"], writes=["LROW"])
            S.op("dve", lambda e: e.tensor_scalar(out=CROW[0:1, par, :], in0=LROW[0:1, :], scalar1=-1.0, scalar2=None, op0=ALU.mult), reads=["LROW"], writes=[("CROW", par)])
            yield
            for b in range(4):
                S.op("pe", lambda e, b=b: e.matmul(PS[3][:, 384 + b:385 + b], lhsT=LROW[0:1, b * 128:(b + 1) * 128], rhs=ONESF[0:1, 0:1], start=True, stop=True),
                     reads=["LROW", "CF"], writes=[("P", 3)])
            S.op("pe", lambda e: e.matmul(PS[3][:, 388:389], lhsT=ONESF[0:1, 0:128], rhs=LROW[0:1, 511:512], start=True, stop=True),
                 reads=["LROW", "CF"], writes=[("P", 3)])
            S.op("dve", lambda e: e.tensor_scalar(out=POSC[:, 4 * T:4 * T + 4], in0=PS[3][:, 384:388], scalar1=RP[:, T:T + 1], scalar2=None, op0=ALU.add),
                 reads=[("P", 3), "RP"], writes=["POSC"])
            S.op("dve", lambda e: e.tensor_tensor(out=RP[:, T + 1:T + 2], in0=PS[3][:, 388:389], in1=RP[:, T:T + 1], op=ALU.add),
                 reads=[("P", 3), "RP"], writes=["RP"])
            S.op("dve", lambda e: e.tensor_scalar(out=BIAS[:, par, 0:4 * T + 4], in0=POSC[:, 0:4 * T + 4], scalar1=RP[:, T:T + 1], scalar2=None, op0=ALU.subtract),
                 reads=["POSC", "RP"], writes=[("BIAS", par)])
            yield

        def gen_H(T):
            par = T % 2
            S.op("act", lambda e: e.activation(out=TE[:, :], in_=TZ[:, :], func=AF.Exp, scale=-1.0), reads=["TZ"], writes=["TE"])
            S.op("dve", lambda e: e.tensor_scalar(out=TS[:, :], in0=TE[:, :], scalar1=1.0, scalar2=None, op0=ALU.add), reads=["TE"], writes=["TS"])
            S.op("dve", lambda e: e.reciprocal(out=TS[:, :], in_=TS[:, :]), reads=["TS"], writes=["TS"])
            S.op("dve", lambda e: e.tensor_scalar(out=TF[:, :], in0=TS[:, :], scalar1=LB[:, 1:2], scalar2=LB[:, 0:1], op0=ALU.mult, op1=ALU.add),
                 reads=["TS", "LB"], writes=["TF"])
            yield
            S.op("act", lambda e: e.activation(out=TF[:, :], in_=TF[:, :], func=AF.Ln), reads=["TF"], writes=["TF"])
            S.op("dve", lambda e: e.scalar_tensor_tensor(out=TK[:, :], in0=TE[:, :], scalar=LB[:, 1:2], in1=TS[:, :], op0=ALU.mult, op1=ALU.mult),
                 reads=["TE", "TS", "LB"], writes=["TK"])
            S.op("dve", lambda e: e.tensor_tensor_scan(out=TB[:, :], data0=RESET, data1=TF[:, :], initial=0.0, op0=ALU.mult, op1=ALU.add),
                 reads=["TF", "CF"], writes=["TB"])
            yield
            TBv = TB[:, :].rearrange("p (c n) -> p c n", n=64)
            TDv = TD[:, :].rearrange("p (c n) -> p c n", n=64)
            S.op("dve", lambda e: e.tensor_tensor(out=TDv, in0=TBv, in1=TBv[:, :, 31:32].to_broadcast([128, 8, 64]), op=ALU.subtract), reads=["TB"], writes=["TD"])
            S.op("act", lambda e: e.activation(out=TE1[:, :], in_=TD[:, :], func=AF.Exp), reads=["TD"], writes=["TE1"])
            S.op("act", lambda e: e.activation(out=TE2[:, :], in_=TD[:, :], func=AF.Exp, scale=-1.0), reads=["TD"], writes=["TE2"])
            yield
            S.op("dve", lambda e: e.tensor_tensor(out=QTB[:, :], in0=TQ[:, :], in1=TE1[:, :], op=ALU.mult), reads=["TQ", "TE1"], writes=["QTB"])
            S.op("dve", lambda e: e.tensor_tensor(out=KTB[:, :], in0=TK[:, :], in1=TE2[:, :], op=ALU.mult), reads=["TK", "TE2"], writes=["KTB"])
            S.op("act", lambda e: e.activation(out=EBM[:, :], in_=TBv[:, :, 31], func=AF.Exp), reads=["TB"], writes=["EBM"])
            S.op("act", lambda e: e.activation(out=EBL[:, :], in_=TBv[:, :, 63], func=AF.Exp), reads=["TB"], writes=["EBL"])
            S.op("dve", lambda e: e.tensor_tensor(out=BDM[:, :], in0=TBv[:, :, 63], in1=TBv[:, :, 31], op=ALU.subtract), reads=["TB"], writes=["BDM"])
            S.op("act", lambda e: e.activation(out=EBLM[:, :], in_=BDM[:, :], func=AF.Exp), reads=["BDM"], writes=["EBLM"])
            ESC = ["EBM", "EBL", "EBLM"]
            yield
            for b in range(4):
                bs = slice(b * 128, (b + 1) * 128)
                trp = PS[1][:, 448:512].bitcast(BF16)
                S.op("pe", lambda e, bs=bs, trp=trp: e.transpose(out=trp, in_=KTB[:, bs], identity=IDENT), reads=["KTB", "CB"], writes=[("P", 1)])
                S.op("act", lambda e, b=b, trp=trp: e.activation(out=KTOK[:, b, :], in_=trp, func=AF.Copy), reads=[("P", 1)], writes=[("KTOK", b)])
                yield
                S.op("pe", lambda e, bs=bs: e.matmul(PS[1][:, 0:128], lhsT=KTB[:, bs], rhs=QTB[:, bs], start=True, stop=True),
                     reads=["KTB", "QTB"], writes=[("P", 1)])
                S.op("act", lambda e: e.activation(out=TSC[:, :], in_=PS[1][:, 0:128], func=AF.Copy), reads=[("P", 1)], writes=["TSC"])
                S.op("dve", lambda e: e.tensor_tensor(out=SCB[:, :], in0=TSC[:, :], in1=BDMASK, op=ALU.mult), reads=["TSC", "CB"], writes=["SCB"])
                yield
                for c2 in range(2):
                    rs = slice(c2 * 64, (c2 + 1) * 64)
                    S.op("pe", lambda e, b=b, c2=c2, rs=rs: e.matmul(PS[1 + c2][:, 128:256], lhsT=KTOK[rs, b, :], rhs=VTOK[rs, par, b, :], start=True, stop=True),
                         reads=[("KTOK", b), ("VTOK", par, b)], writes=[("P", 1 + c2)])
                yield
                for c2 in range(2):
                    c = 2 * b + c2
                    S.op("dve", lambda e, c=c, c2=c2: e.tensor_scalar(out=SM[:, c2, :], in0=ST[:, :], scalar1=EBM[:, c:c + 1], scalar2=None, op0=ALU.mult),
                         reads=["ST"] + ESC, writes=[("SM", c2)])
                    S.op("dve", lambda e, c=c, c2=c2: e.tensor_scalar(out=TKV[:, :], in0=PS[1 + c2][:, 128:256], scalar1=EBLM[:, c:c + 1], scalar2=None, op0=ALU.mult),
                         reads=[("P", 1 + c2)] + ESC, writes=["TKV"])
                    S.op("dve", lambda e, c=c: e.scalar_tensor_tensor(out=ST[:, :], in0=ST[:, :], scalar=EBL[:, c:c + 1], in1=TKV[:, :], op0=ALU.mult, op1=ALU.add),
                         reads=["ST", "TKV"] + ESC, writes=["ST"])
                yield
                S.op("pe", lambda e, b=b: e.matmul(PS[1][:, 0:128], lhsT=SCB[:, :], rhs=VTOK[:, par, b, :], start=True, stop=False),
                     reads=["SCB", ("VTOK", par, b)], writes=[("P", 1)])
                for c2 in range(2):
                    S.op("pe", lambda e, b=b, c2=c2: e.matmul(PS[1][c2 * 64:(c2 + 1) * 64, 0:128], lhsT=QTB[:, b * 128 + c2 * 64:b * 128 + (c2 + 1) * 64], rhs=SM[:, c2, :],
                                                              start=False, stop=True), reads=["QTB", ("SM", c2)], writes=[("P", 1)])
                ob_ = (4 * T + b) % 2
                S.op("act", lambda e: e.activation(out=T1[:, :], in_=PS[1][:, 0:128], func=AF.Square, accum_out=SSO[:, 0:1]), reads=[("P", 1)], writes=["T1", "SSO"])
                rstd(SSO[:, 0:1], RO[:, 0:1], "SSO", "RO", 128)
                S.op("dve", lambda e: e.scalar_tensor_tensor(out=T1[:, :], in0=PS[1][:, 0:128], scalar=RO[:, 0:1], in1=HNW[:, :], op0=ALU.mult, op1=ALU.mult),
                     reads=[("P", 1), "RO", "HNW"], writes=["T1"])
                yield
                S.op("act", lambda e, b=b: e.activation(out=EG[:, :], in_=GTOK[:, par, b, :], func=AF.Exp, scale=-1.0), reads=[("GTOK", par, b)], writes=["EG"])
                S.op("dve", lambda e: e.tensor_scalar(out=EG[:, :], in0=EG[:, :], scalar1=1.0, scalar2=None, op0=ALU.add), reads=["EG"], writes=["EG"])
                S.op("dve", lambda e: e.reciprocal(out=EG[:, :], in_=EG[:, :]), reads=["EG"], writes=["EG"])
                S.op("dve", lambda e, b=b: e.tensor_tensor(out=GG[:, :], in0=GTOK[:, par, b, :], in1=EG[:, :], op=ALU.mult), reads=["EG", ("GTOK", par, b)], writes=["GG"])
                S.op("dve", lambda e, ob_=ob_: e.tensor_tensor(out=OAT[:, ob_, :], in0=T1[:, :], in1=GG[:, :], op=ALU.mult), reads=["T1", "GG"], writes=[("OAT", ob_)])
                S.op("sp", lambda e, ob_=ob_, b=b: e.dma_start(out=oAv[:, 4 * T + b, :], in_=OAT[:, ob_, :]), reads=[("OAT", ob_)], dma="oa%d" % ob_)
                yield

        def gen_F(T):
            par = T % 2
            tsl = slice(T * 512, (T + 1) * 512)
            nkb = 4 * T + 4

            def s_stage(j):
                lo = 0 if j < 4 * T else 128 * (j - 4 * T)
                sp_ = j % 2
                sps = PS[4 + sp_]
                S.op("pe", lambda e: e.matmul(sps[:, lo:512], lhsT=KT[:, j * 128:(j + 1) * 128], rhs=QT[:, par, lo:512], start=True, stop=False),
                     reads=[("KT", j // 4), ("QT", par)], writes=[("P", 4 + sp_)])
                diag = j >= 4 * T
                S.op("pe", lambda e: e.matmul(sps[:, lo:512], lhsT=E0, rhs=CROW[:, par, lo:512], start=False, stop=(not diag)),
                     reads=[("CROW", par), "CB"], writes=[("P", 4 + sp_)])
                if diag:
                    S.op("pe", lambda e: e.matmul(sps[:, lo:lo + 128], lhsT=IDENT, rhs=MASKNEG, start=False, stop=True),
                         reads=["CB"], writes=[("P", 4 + sp_)])
                S.op("act", lambda e: e.activation(out=PT[:, sp_, lo:512], in_=sps[:, lo:512], func=AF.Exp, bias=BIAS[:, par, j:j + 1]),
                     reads=[("P", 4 + sp_), ("BIAS", par)], writes=[("PT", sp_)])

            def pv_stage(j):
                lo = 0 if j < 4 * T else 128 * (j - 4 * T)
                sp_ = j % 2
                S.op("pe", lambda e: e.matmul(PS[6][:, lo:512], lhsT=V[:, j, :], rhs=PT[:, sp_, lo:512], start=(j == 0), stop=(j == nkb - 1)),
                     reads=[("V", j), ("PT", sp_)], writes=[("P", 6)])
                S.op("pe", lambda e: e.matmul(PS[7][:, lo:512], lhsT=ONESB, rhs=PT[:, sp_, lo:512], start=(j == 0), stop=(j == nkb - 1)),
                     reads=["CB", ("PT", sp_)], writes=[("P", 7)])

            s_stage(0)
            yield
            for j in range(nkb):
                if j + 1 < nkb:
                    s_stage(j + 1)
                pv_stage(j)
                yield
            S.op("dve", lambda e: e.reciprocal(out=ODEN[:, :], in_=PS[7][:, :]), reads=[("P", 7)], writes=["ODEN"])
            S.op("dve", lambda e: e.tensor_tensor(out=OOUT[:, :], in0=PS[6][:, :], in1=ODEN[:, :], op=ALU.mult), reads=[("P", 6), "ODEN"], writes=["OOUT"])
            S.op("sp", lambda e: e.dma_start(out=oBT[:, tsl], in_=OOUT[:, :]), reads=["OOUT"], dma="ob")
            yield

        for _ in gen_A(0):
            pass
        for T in range(NTILES):
            gens = [gen_H(T), gen_F(T)]
            if T + 1 < NTILES:
                gens.append(gen_A(T + 1))
            while gens:
                for g in list(gens):
                    try:
                        next(g)
                    except StopIteration:
                        gens.remove(g)
        S.barrier()
        S.emit(nc, block, st)
    return nc


def prep_l1(inp, ntok=SEQ):
    x = np.asarray(inp["x"], np.float32)[0]
    xT = np.ascontiguousarray(x[:ntok].T)
    w_in = np.asarray(inp["w_in"], np.float32)[0]
    nv1 = _colvec(inp["norm_mix_pre"][0])
    p = np.arange(128)
    ones = np.ones((128, 128), np.float32)
    ident = np.eye(128, dtype=np.float32)
    maskneg = np.where(p[:, None] > p[None, :], -30000.0, 0.0).astype(np.float32)
    bdmask = ((p[:, None] // 64 == p[None, :] // 64) & (p[:, None] <= p[None, :])).astype(np.float32)
    e0 = np.zeros((128, 128), np.float32)
    e0[0, :] = 1.0
    cb = np.concatenate([ones, ident, maskneg, bdmask, e0], axis=1).astype(ml_dtypes.bfloat16)
    reset = np.tile((np.arange(512) % 64 != 0).astype(np.float32)[None, :], (128, 1))
    cf = np.ascontiguousarray(np.concatenate([reset, np.ones((128, 512), np.float32)], axis=1))
    lbl_all = np.asarray(inp["hgrn_lb_logits"], np.float32)
    hnw = np.ascontiguousarray(np.tile(np.asarray(inp["hgrn_norm_w"], np.float32)[0][None, :], (128, 1)))
    maps = []
    for c in range(NCORES):
        cs = lambda base: w_in[:, base + c * 128: base + (c + 1) * 128]
        w1 = np.ascontiguousarray(np.concatenate([cs(0), cs(1024), cs(4096), cs(5120), cs(2048), cs(3072), cs(6144)], axis=1))
        wf = np.ascontiguousarray(np.repeat(w_in[:, 7168 + c: 7169 + c], 2, axis=1))
        lbl = np.ascontiguousarray(lbl_all[:, c * 128:(c + 1) * 128].T)
        bfx = np.full((128, 1), np.asarray(inp["b_fox_f"], np.float32)[0, c], np.float32)
        maps.append(dict(xT=xT, w1=w1, wf=wf, nv1=nv1, lbl=lbl, hnw=hnw, bfx=bfx, cb=cb, cf=cf))
    return maps


def post_l1(results):
    oA = np.concatenate([r["oA"] for r in results], axis=1)
    oB = np.concatenate([np.ascontiguousarray(r["oBT"].T) for r in results], axis=1)
    return oA, oB


_CACHE = {}


def kernel(**inputs):
    inputs = {k: np.asarray(v) for k, v in inputs.items()}
    if "l1" not in _CACHE:
        _CACHE["l1"] = build_l1(SEQ // 512)
        _CACHE["l2"] = build_l2()
    cores = list(range(NCORES))
    r1 = run_bass_kernel_spmd(_CACHE["l1"], prep_l1(inputs), core_ids=cores)
    oA, oB = post_l1(r1.results)
    import os
    if os.environ.get("L1ONLY"):
        return np.zeros((1, SEQ, D_MODEL), np.float32)
    r2 = run_bass_kernel_spmd(_CACHE["l2"], prep_l2(inputs, oA, oB), core_ids=cores)
    return post_l2(r2.results).astype(np.float32)
```

```python
import contextlib
import numpy as np
import ml_dtypes
import concourse.bass as bass
import concourse.mybir as mybir
from concourse.bass_utils import run_bass_kernel_spmd

F32 = mybir.dt.float32
BF16 = mybir.dt.bfloat16
ALU = mybir.AluOpType
AF = mybir.ActivationFunctionType

D_MODEL = 2048
SEQ = 8192
D_FF = 5632
EPS = 1e-6
NCORES = 8

ENGS = ("pe", "act", "dve", "pool", "sp")


class Sched:
    def __init__(self):
        self.ops = {e: [] for e in ENGS}
        self.lastw = {}
        self.readers = {}
        self.dma_cnt = {}

    def _tok(self, eng, rec, idx):
        if rec["dma"] is not None:
            return ("dma", rec["dma"], rec["dma_val"])
        return ("eng", eng, idx)

    def op(self, eng, fn, reads=(), writes=(), dma=None):
        idx = len(self.ops[eng])
        rec = dict(fn=fn, dma=dma, needed=False, deps={})
        if dma is not None:
            self.dma_cnt[dma] = self.dma_cnt.get(dma, 0) + 1
            rec["dma_val"] = 16 * self.dma_cnt[dma]
        tok = self._tok(eng, rec, idx)
        deps = rec["deps"]

        def add(t):
            if t is None:
                return
            if t[0] == "dma":
                cur = self.dma_cnt[t[1]] - (1 if dma == t[1] else 0)
                t = ("dma", t[1], 16 * cur)
            if t[0] == "eng" and t[1] == "pe" and eng == "pe" and dma is None:
                return
            k = (t[0], t[1])
            if k not in deps or deps[k] < t[2]:
                deps[k] = t[2]

        for key in reads:
            add(self.lastw.get(key))
        for key in writes:
            add(self.lastw.get(key))
            for t in self.readers.get(key, {}).values():
                add(t)
        self.ops[eng].append(rec)
        for key in writes:
            self.lastw[key] = tok
            self.readers[key] = {}
        for key in reads:
            r = self.readers.setdefault(key, {})
            k = (tok[0], tok[1])
            if k not in r or r[k][2] < tok[2]:
                r[k] = tok
        return tok

    def barrier(self):
        toks = []
        for e in ENGS:
            for i in range(len(self.ops[e]) - 1, -1, -1):
                if self.ops[e][i]["dma"] is None and self.ops[e][i]["fn"] is not None:
                    toks.append(("eng", e, i))
                    break
        for k, c in self.dma_cnt.items():
            toks.append(("dma", k, 16 * c))
        for e in ENGS:
            rec = dict(fn=None, dma=None, needed=False, deps={})
            for t in toks:
                rec["deps"][(t[0], t[1])] = t[2]
            self.ops[e].append(rec)
        self.lastw = {}
        self.readers = {}

    def emit(self, nc, block, stack):
        sems = {}
        for e in ENGS:
            sems[("eng", e)] = stack.enter_context(nc.semaphore("s_" + e))
        for k in self.dma_cnt:
            sems[("dma", k)] = stack.enter_context(nc.semaphore("d_" + str(k)))
        for e in ENGS:
            for rec in self.ops[e]:
                for (kind, name), val in rec["deps"].items():
                    if kind == "eng":
                        self.ops[name][val]["needed"] = True
        vals = {}
        for e in ENGS:
            cnt = 0
            for i, rec in enumerate(self.ops[e]):
                if rec["needed"]:
                    cnt += 1
                    vals[(e, i)] = cnt
            assert cnt < 60000, (e, cnt)
        self.n_emitted = {e: len(self.ops[e]) for e in ENGS}

        def run(e, engine):
            waited = {}
            for i, rec in enumerate(self.ops[e]):
                for (kind, name), val in rec["deps"].items():
                    v = vals[(name, val)] if kind == "eng" else val
                    sk = (kind, name)
                    if waited.get(sk, 0) >= v:
                        continue
                    engine.wait_ge(sems[sk], v)
                    waited[sk] = v
                if rec["fn"] is None:
                    continue
                ins = rec["fn"](engine)
                if rec["dma"] is not None:
                    ins.then_inc(sems[("dma", rec["dma"])], 16)
                elif rec["needed"]:
                    ins.then_inc(sems[("eng", e)], 1)

        @block.tensor
        def _(eng):
            run("pe", eng)

        @block.scalar
        def _(eng):
            run("act", eng)

        @block.vector
        def _(eng):
            run("dve", eng)

        @block.gpsimd
        def _(eng):
            run("pool", eng)

        @block.sync
        def _(eng):
            run("sp", eng)


def _v3(ap, k):
    return ap.rearrange("p (k n) -> p k n", k=k)


def build_l2(NT=1024):
    nc = bass.Bass("TRN2", target_bir_lowering=False)
    S = Sched()
    NTT = NT // 512
    dt = nc.dram_tensor
    xT = dt("xT", [2048, NT], F32, kind="ExternalInput").ap()
    oT = dt("oT", [2048, NT], F32, kind="ExternalInput").ap()
    wg = dt("wg", [2048, 4096], F32, kind="ExternalInput").ap()
    wua = dt("wua", [1024, 2048], F32, kind="ExternalInput").ap()
    wub = dt("wub", [1024, 2048], F32, kind="ExternalInput").ap()
    wo = dt("wo", [2048, 2048], F32, kind="ExternalInput").ap()
    wfi = dt("wfi", [2048, 2 * D_FF], F32, kind="ExternalInput").ap()
    wfd = dt("wfd", [D_FF, 2048], F32, kind="ExternalInput").ap()
    nv = dt("nv", [128, 64], F32, kind="ExternalInput").ap()
    cst = dt("cst", [128, 128], BF16, kind="ExternalInput").ap()
    outT = dt("outT", [2048, NT], F32, kind="ExternalOutput").ap()
    zd = dt("zd", [2048, NT], F32).ap()
    x1d = dt("x1d", [2048, NT], F32).ap()

    def cm(ap):
        return ap.rearrange("(k p) n -> p k n", p=128)

    xTv, oTv, wgv, wuav, wubv, wov, wfiv, wfdv = map(cm, (xT, oT, wg, wua, wub, wo, wfi, wfd))
    outTv, zdv, x1dv = cm(outT), cm(zd), cm(x1d)

    with contextlib.ExitStack() as st:
        sb = lambda name, shape, d: st.enter_context(nc.sbuf_tensor(name, shape, d))
        A = sb("A", [128, 44 * NT], BF16)
        B = sb("B", [128, 16 * NT], BF16)
        W = sb("W", [128, 16384], BF16)
        T = sb("T", [128, 8, 512], F32)
        SQT = sb("SQT", [128, 4, 512], BF16)
        R = sb("R", [128, 2, NT], F32)
        NV = sb("NV", [128, 64], F32)
        ONES = sb("ONES", [128, 128], BF16)
        PS = [st.enter_context(nc.psum_tensor("ps%d" % i, [128, 512], F32)) for i in range(8)]
        block = st.enter_context(nc.Block())

        xb = _v3(A[:, 0:16 * NT], 16)
        ob = _v3(A[:, 16 * NT:32 * NT], 16)
        SP0 = 32 * NT
        SQ = _v3(A[:, SP0 + 8192:SP0 + 12288], 16)
        XS = _v3(W[:, 8192:16384].bitcast(F32), 16)
        hid = _v3(A[:, 0:44 * NT], 44)
        mb = _v3(B[:, :], 16)
        x1b = mb

        S.op("sp", lambda e: e.dma_start(out=NV[:, :], in_=nv[:, :]), writes=["NV"], dma="c0")
        S.op("sp", lambda e: e.dma_start(out=ONES[:, :], in_=cst[:, :]), writes=["ONES"], dma="c1")

        def rstd_from(ps_ap, r_ap, keyp, keyr):
            S.op("dve", lambda e: e.tensor_scalar(out=r_ap, in0=ps_ap, scalar1=1.0 / D_MODEL, scalar2=EPS,
                                                  op0=ALU.mult, op1=ALU.add), reads=[keyp], writes=[keyr])
            S.op("act", lambda e: e.activation(out=r_ap, in_=r_ap, func=AF.Sqrt), reads=[keyr], writes=[keyr])
            S.op("dve", lambda e: e.reciprocal(out=r_ap, in_=r_ap), reads=[keyr], writes=[keyr])

        XSK = [("W", 1, 0), ("W", 1, 1)]
        for s in range(NT // 256):
            cs = slice(s * 256, (s + 1) * 256)
            S.op("sp", lambda e, cs=cs: e.dma_start(out=XS[:, :, :], in_=xTv[:, :, cs]), writes=XSK, dma="xs")
            S.op("act", lambda e: e.activation(out=SQ[:, :, :], in_=XS[:, :, :], func=AF.Square), reads=XSK, writes=["SQ"])
            for k in range(16):
                S.op("dve", lambda e, k=k, cs=cs: e.tensor_scalar(out=xb[:, k, cs], in0=XS[:, k, :], scalar1=NV[:, k:k + 1],
                                                                  scalar2=None, op0=ALU.mult),
                     reads=XSK + ["NV"], writes=[("A0", k, s)])
            for k in range(16):
                S.op("pe", lambda e, k=k: e.matmul(PS[0][:, 0:256], lhsT=ONES[:, :], rhs=SQ[:, k, :], start=(k == 0), stop=(k == 15)),
                     reads=["SQ", "ONES"], writes=[("P", 0)])
            rstd_from(PS[0][:, 0:256], R[:, 0, cs], ("P", 0), ("R0", s))
        for h in range(2):
            S.op("pool", lambda e, h=h: e.dma_start(out=ob[:, 8 * h:8 * h + 8, :], in_=oTv[:, 8 * h:8 * h + 8, :]),
                 writes=[("A1", h)], dma="ob%d" % h)
        xb_keys = [("A0", k, s) for k in range(16) for s in range(NT // 256)]
        r0_keys = [("R0", s) for s in range(NT // 256)]

        for g in range(8):
            s_ = g % 2
            base = s_ * 8192
            wga = _v3(W[:, base:base + 4096], 16)
            wgb = _v3(W[:, base + 4096:base + 8192], 16)
            wa = _v3(A[:, SP0 + s_ * 4096:SP0 + s_ * 4096 + 2048], 8)
            wb = _v3(A[:, SP0 + s_ * 4096 + 2048:SP0 + s_ * 4096 + 4096], 8)
            gs = slice(g * 256, (g + 1) * 256)
            gs2 = slice(2048 + g * 256, 2048 + (g + 1) * 256)
            wk = ("W", s_)
            dk = "w%d" % s_
            S.op("pool", lambda e, wga=wga, gs=gs: e.dma_start(out=wga[:, :, :], in_=wgv[:, :, gs]), writes=[wk + (0,)], dma=dk + "a")
            S.op("pool", lambda e, wgb=wgb, gs2=gs2: e.dma_start(out=wgb[:, :, :], in_=wgv[:, :, gs2]), writes=[wk + (1,)], dma=dk + "b")
            S.op("pool", lambda e, wa=wa, gs=gs: e.dma_start(out=wa[:, :, :], in_=wuav[:, :, gs]), writes=[wk + (2,)], dma=dk + "c")
            S.op("pool", lambda e, wb=wb, gs=gs: e.dma_start(out=wb[:, :, :], in_=wubv[:, :, gs]), writes=[wk + (3,)], dma=dk + "d")
            for nb in range(2):
                n = g * 2 + nb
                ns = slice(nb * 128, (nb + 1) * 128)
                for t in range(NTT):
                    ts_ = slice(t * 512, (t + 1) * 512)
                    par = (n * NTT + t) % 2
                    pga, pgb, pya, pyb = (PS[4 * par + i] for i in range(4))
                    kga, kgb, kya, kyb = (("P", 4 * par + i) for i in range(4))
                    t0, t1 = T[:, 2 * par, :], T[:, 2 * par + 1, :]
                    k0, k1 = ("T", 2 * par), ("T", 2 * par + 1)
                    for k in range(16):
                        S.op("pe", lambda e, k=k, pga=pga, wga=wga, ns=ns, ts_=ts_: e.matmul(pga[:, :], lhsT=wga[:, k, ns], rhs=xb[:, k, ts_], start=(k == 0), stop=(k == 15)),
                             reads=[wk + (0,)] + xb_keys, writes=[kga])
                    for k in range(16):
                        S.op("pe", lambda e, k=k, pgb=pgb, wgb=wgb, ns=ns, ts_=ts_: e.matmul(pgb[:, :], lhsT=wgb[:, k, ns], rhs=xb[:, k, ts_], start=(k == 0), stop=(k == 15)),
                             reads=[wk + (1,)] + xb_keys, writes=[kgb])
                    for k in range(8):
                        S.op("pe", lambda e, k=k, pya=pya, wa=wa, ns=ns, ts_=ts_: e.matmul(pya[:, :], lhsT=wa[:, k, ns], rhs=ob[:, k, ts_], start=(k == 0), stop=(k == 7)),
                             reads=[wk + (2,), ("A1", 0)], writes=[kya])
                    for k in range(8):
                        S.op("pe", lambda e, k=k, pyb=pyb, wb=wb, ns=ns, ts_=ts_: e.matmul(pyb[:, :], lhsT=wb[:, k, ns], rhs=ob[:, 8 + k, ts_], start=(k == 0), stop=(k == 7)),
                             reads=[wk + (3,), ("A1", 1)], writes=[kyb])
                    S.op("dve", lambda e, t0=t0, pga=pga, ts_=ts_: e.tensor_tensor(out=t0, in0=pga[:, :], in1=R[:, 0, ts_], op=ALU.mult),
                         reads=[kga] + r0_keys, writes=[k0])
                    S.op("act", lambda e, t0=t0: e.activation(out=t0, in_=t0, func=AF.Sigmoid), reads=[k0], writes=[k0])
                    S.op("dve", lambda e, t1=t1, pgb=pgb, ts_=ts_: e.tensor_tensor(out=t1, in0=pgb[:, :], in1=R[:, 0, ts_], op=ALU.mult),
                         reads=[kgb] + r0_keys, writes=[k1])
                    S.op("act", lambda e, t1=t1: e.activation(out=t1, in_=t1, func=AF.Sigmoid), reads=[k1], writes=[k1])
                    S.op("dve", lambda e, t0=t0, pya=pya: e.tensor_tensor(out=t0, in0=pya[:, :], in1=t0, op=ALU.mult),
                         reads=[kya, k0], writes=[k0])
                    S.op("dve", lambda e, t1=t1, pyb=pyb: e.tensor_tensor(out=t1, in0=pyb[:, :], in1=t1, op=ALU.mult),
                         reads=[kyb, k1], writes=[k1])
                    S.op("dve", lambda e, t0=t0, t1=t1, n=n, ts_=ts_: e.tensor_tensor(out=mb[:, n, ts_], in0=t0, in1=t1, op=ALU.add),
                         reads=[k0, k1], writes=[("B", n, t)])

        S.barrier()
        ZS = _v3(A[:, 0:32 * NT].bitcast(F32), 16)

        def evac_stats(ps, pkey, nchunk, t, ssb, first, last, dram_v, par):
            zt = T[:, 4 + par, :]
            zk = ("T", 4 + par)
            sq = SQT[:, par, :]
            sk = ("SQT", par)
            S.op("act", lambda e: e.activation(out=zt, in_=ps[:, :], func=AF.Copy), reads=[pkey], writes=[zk])
            S.op("act", lambda e: e.activation(out=sq, in_=ps[:, :], func=AF.Square), reads=[pkey], writes=[sk])
            S.op("sp", lambda e: e.dma_start(out=dram_v[:, nchunk, t * 512:(t + 1) * 512], in_=zt), reads=[zk],
                 writes=[("zd", nchunk, t)], dma="zst%d" % par)
            return lambda: S.op("pe", lambda e: e.matmul(PS[ssb][:, :], lhsT=ONES[:, :], rhs=sq, start=first, stop=last),
                                reads=[sk, "ONES"], writes=[("P", ssb)])

        pend = []
        for g in range(8):
            s_ = g % 2
            base = s_ * 8192
            wos = _v3(W[:, base:base + 4096], 16)
            wk = ("W", s_)
            gs = slice(g * 256, (g + 1) * 256)
            S.op("pool", lambda e, wos=wos, gs=gs: e.dma_start(out=wos[:, :, :], in_=wov[:, :, gs]), writes=[wk], dma="w%d" % s_)
            for nb in range(2):
                n = g * 2 + nb
                ns = slice(nb * 128, (nb + 1) * 128)
                for t in range(NTT):
                    ts_ = slice(t * 512, (t + 1) * 512)
                    pb = (n * NTT + t) % 4
                    par = (n * NTT + t) % 2
                    for k in range(16):
                        S.op("pe", lambda e, k=k, pb=pb, wos=wos, ns=ns, ts_=ts_: e.matmul(PS[pb][:, :], lhsT=wos[:, k, ns], rhs=mb[:, k, ts_], start=(k == 0), stop=(k == 15)),
                             reads=[wk] + [("B", k, t) for k in range(16)], writes=[("P", pb)])
                    for f_ in pend:
                        f_()
                    pend.clear()
                    sq = SQT[:, par, :]
                    S.op("act", lambda e, pb=pb, n=n, ts_=ts_: e.activation(out=ZS[:, n, ts_], in_=PS[pb][:, :], func=AF.Copy), reads=[("P", pb)], writes=[("ZS", n, t)])
                    S.op("act", lambda e, pb=pb, sq=sq: e.activation(out=sq, in_=PS[pb][:, :], func=AF.Square), reads=[("P", pb)], writes=[("SQT", par)])
                    pend.append(lambda sq=sq, t=t, n=n, par=par: S.op("pe", lambda e: e.matmul(PS[6 + t][:, :], lhsT=ONES[:, :], rhs=sq, start=(n == 0), stop=(n == 15)),
                                                                       reads=[("SQT", par), "ONES"], writes=[("P", 6 + t)]))
        for f_ in pend:
            f_()
        pend.clear()
        for t in range(NTT):
            rstd_from(PS[6 + t][:, :], R[:, 1, t * 512:(t + 1) * 512], ("P", 6 + t), ("R1", t))

        for t in range(NTT):
            ts_ = slice(t * 512, (t + 1) * 512)
            for n in range(16):
                sl4 = n % 4
                par = n % 2
                ta = T[:, sl4, :]
                ka = ("T", sl4)
                S.op("sp", lambda e, ta=ta, n=n, ts_=ts_: e.dma_start(out=ta, in_=xTv[:, n, ts_]), writes=[ka], dma="ra%d" % sl4)
                S.op("dve", lambda e, n=n, ts_=ts_: e.scalar_tensor_tensor(out=ZS[:, n, ts_], in0=ZS[:, n, ts_], scalar=NV[:, 16 + n:17 + n], in1=R[:, 1, ts_],
                                                                            op0=ALU.mult, op1=ALU.mult), reads=[("ZS", n, t), "NV", ("R1", t)], writes=[("ZS", n, t)])
                S.op("dve", lambda e, ta=ta, n=n, ts_=ts_: e.tensor_tensor(out=ta, in0=ta, in1=ZS[:, n, ts_], op=ALU.add), reads=[ka, ("ZS", n, t)], writes=[ka])
                S.op("pool", lambda e, ta=ta, n=n, ts_=ts_: e.dma_start(out=x1dv[:, n, ts_], in_=ta), reads=[ka], writes=[("x1d", n, t)], dma="x1st%d" % sl4)
                sq = SQT[:, 2 + par, :]
                sk = ("SQT", 2 + par)
                S.op("dve", lambda e, ta=ta, n=n, ts_=ts_: e.tensor_scalar(out=x1b[:, n, ts_], in0=ta, scalar1=NV[:, 32 + n:33 + n], scalar2=None, op0=ALU.mult),
                     reads=[ka, "NV"], writes=[("B", n, t)])
                S.op("act", lambda e, ta=ta, sq=sq: e.activation(out=sq, in_=ta, func=AF.Square), reads=[ka], writes=[sk])
                S.op("pe", lambda e, sq=sq, n=n, t=t: e.matmul(PS[4 + t][:, :], lhsT=ONES[:, :], rhs=sq, start=(n == 0), stop=(n == 15)),
                     reads=[sk, "ONES"], writes=[("P", 4 + t)])
        for t in range(NTT):
            rstd_from(PS[4 + t][:, :], R[:, 0, t * 512:(t + 1) * 512], ("P", 4 + t), ("R2", t))

        S.barrier()
        xk = {t: [("B", k, t) for k in range(16)] for t in range(NTT)}
        for g in range(22):
            s_ = g % 2
            base = s_ * 8192
            wfg = _v3(W[:, base:base + 4096], 16)
            wfu = _v3(W[:, base + 4096:base + 8192], 16)
            wk = ("W", s_)
            gs = slice(g * 256, (g + 1) * 256)
            gs2 = slice(D_FF + g * 256, D_FF + (g + 1) * 256)
            S.op("pool", lambda e, wfg=wfg, gs=gs: e.dma_start(out=wfg[:, :, :], in_=wfiv[:, :, gs]), writes=[wk + (0,)], dma="w%da" % s_)
            S.op("pool", lambda e, wfu=wfu, gs2=gs2: e.dma_start(out=wfu[:, :, :], in_=wfiv[:, :, gs2]), writes=[wk + (1,)], dma="w%db" % s_)
            for nb in range(2):
                f = g * 2 + nb
                ns = slice(nb * 128, (nb + 1) * 128)
                for t in range(NTT):
                    ts_ = slice(t * 512, (t + 1) * 512)
                    par = (f * NTT + t) % 2
                    pg_, pu_ = PS[2 * par], PS[2 * par + 1]
                    kg_, ku_ = ("P", 2 * par), ("P", 2 * par + 1)
                    t0, t1 = T[:, 4 + 2 * par, :], T[:, 5 + 2 * par, :]
                    k0, k1 = ("T", 4 + 2 * par), ("T", 5 + 2 * par)
                    for k in range(16):
                        S.op("pe", lambda e, k=k, pg_=pg_, wfg=wfg, ns=ns, ts_=ts_: e.matmul(pg_[:, :], lhsT=wfg[:, k, ns], rhs=x1b[:, k, ts_], start=(k == 0), stop=(k == 15)),
                             reads=[wk + (0,)] + xk[t], writes=[kg_])
                    for k in range(16):
                        S.op("pe", lambda e, k=k, pu_=pu_, wfu=wfu, ns=ns, ts_=ts_: e.matmul(pu_[:, :], lhsT=wfu[:, k, ns], rhs=x1b[:, k, ts_], start=(k == 0), stop=(k == 15)),
                             reads=[wk + (1,)] + xk[t], writes=[ku_])
                    S.op("dve", lambda e, t0=t0, pg_=pg_, ts_=ts_: e.tensor_tensor(out=t0, in0=pg_[:, :], in1=R[:, 0, ts_], op=ALU.mult),
                         reads=[kg_, ("R2", t)], writes=[k0])
                    S.op("act", lambda e, t0=t0: e.activation(out=t0, in_=t0, func=AF.Silu), reads=[k0], writes=[k0])
                    S.op("dve", lambda e, t1=t1, pu_=pu_, ts_=ts_: e.tensor_tensor(out=t1, in0=pu_[:, :], in1=R[:, 0, ts_], op=ALU.mult),
                         reads=[ku_, ("R2", t)], writes=[k1])
                    S.op("dve", lambda e, t0=t0, t1=t1, f=f, ts_=ts_: e.tensor_tensor(out=hid[:, f, ts_], in0=t0, in1=t1, op=ALU.mult),
                         reads=[k0, k1], writes=[("hid", f, t)])

        S.barrier()
        for g in range(8):
            s_ = g % 2
            wd = _v3((W if s_ == 0 else B)[:, 0:11264], 44)
            wk = ("W", s_)
            gs = slice(g * 256, (g + 1) * 256)
            for hh in range(2):
                S.op("pool", lambda e, wd=wd, gs=gs, hh=hh: e.dma_start(out=wd[:, 22 * hh:22 * hh + 22, :], in_=wfdv[:, 22 * hh:22 * hh + 22, gs]),
                     writes=[wk + (hh,)], dma="w%d%s" % (s_, "ab"[hh]))
            for nb in range(2):
                n = g * 2 + nb
                ns = slice(nb * 128, (nb + 1) * 128)
                for t in range(NTT):
                    ts_ = slice(t * 512, (t + 1) * 512)
                    pb = (n * NTT + t) % 4
                    for k in range(44):
                        S.op("pe", lambda e, k=k, pb=pb, wd=wd, ns=ns, ts_=ts_: e.matmul(PS[pb][:, :], lhsT=wd[:, k, ns], rhs=hid[:, k, ts_], start=(k == 0), stop=(k == 43)),
                             reads=[wk + (0,), wk + (1,)] + [("hid", f, t) for f in range(44)], writes=[("P", pb)])
                    for f_ in pend:
                        f_()
                    pend.clear()
                    pend.append(evac_stats(PS[pb], ("P", pb), n, t, 6 + t, n == 0, n == 15, zdv, (n * NTT + t) % 2))
        for f_ in pend:
            f_()
        pend.clear()
        for t in range(NTT):
            rstd_from(PS[6 + t][:, :], R[:, 1, t * 512:(t + 1) * 512], ("P", 6 + t), ("R3", t))
        S.barrier()
        for t in range(NTT):
            ts_ = slice(t * 512, (t + 1) * 512)
            for n in range(16):
                sl4 = n % 4
                ta, tb = T[:, sl4, :], T[:, 4 + sl4, :]
                ka, kb = ("T", sl4), ("T", 4 + sl4)
                S.op("sp", lambda e, ta=ta, n=n, ts_=ts_: e.dma_start(out=ta, in_=x1dv[:, n, ts_]), reads=[("x1d", n, t)], writes=[ka], dma="ra%d" % sl4)
                S.op("act", lambda e, tb=tb, n=n, ts_=ts_: e.dma_start(out=tb, in_=zdv[:, n, ts_]), reads=[("zd", n, t)], writes=[kb], dma="rb%d" % sl4)
                S.op("dve", lambda e, tb=tb, n=n, ts_=ts_: e.scalar_tensor_tensor(out=tb, in0=tb, scalar=NV[:, 48 + n:49 + n], in1=R[:, 1, ts_],
                                                                                   op0=ALU.mult, op1=ALU.mult), reads=[kb, "NV", ("R3", t)], writes=[kb])
                S.op("dve", lambda e, ta=ta, tb=tb: e.tensor_tensor(out=ta, in0=ta, in1=tb, op=ALU.add), reads=[ka, kb], writes=[ka])
                S.op("pool", lambda e, ta=ta, n=n, ts_=ts_: e.dma_start(out=outTv[:, n, ts_], in_=ta), reads=[ka], writes=[("out", n, t)], dma="ost%d" % sl4)

        S.barrier()
        S.emit(nc, block, st)
    return nc


def _colvec(v):
    return np.ascontiguousarray(np.asarray(v, np.float32).reshape(16, 128).T)


def prep_l2(inp, oA, oB):
    x = np.asarray(inp["x"], np.float32)[0]
    w_in = np.asarray(inp["w_in"], np.float32)[0]
    wg = np.ascontiguousarray(w_in[:, 7176:7176 + 4096])
    nv = np.ascontiguousarray(np.concatenate([_colvec(inp["norm_mix_pre"][0]), _colvec(inp["norm_mix_post"][0]),
                                              _colvec(inp["norm_ffn_pre"][0]), _colvec(inp["norm_ffn_post"][0])], axis=1))
    ones = np.ones((128, 128), ml_dtypes.bfloat16)
    common = dict(wg=wg, wua=np.ascontiguousarray(inp["w_up_a"][0], dtype=np.float32), wub=np.ascontiguousarray(inp["w_up_b"][0], dtype=np.float32),
                  wo=np.ascontiguousarray(inp["w_o"][0], dtype=np.float32), wfi=np.ascontiguousarray(inp["w_ffn_in"][0], dtype=np.float32),
                  wfd=np.ascontiguousarray(inp["w_ffn_down"][0], dtype=np.float32), nv=nv, cst=ones)
    maps = []
    for c in range(NCORES):
        sl = slice(c * 1024, (c + 1) * 1024)
        oT = np.ascontiguousarray(np.concatenate([oA[sl].T, oB[sl].T], axis=0), dtype=np.float32)
        maps.append(dict(common, xT=np.ascontiguousarray(x[sl].T), oT=oT))
    return maps


def post_l2(results):
    return np.concatenate([np.ascontiguousarray(r["outT"].T) for r in results], axis=0)[None]


def build_l1(NTILES=16, stage=9):
    nc = bass.Bass("TRN2", target_bir_lowering=False)
    S = Sched()
    NTOK = NTILES * 512
    dt = nc.dram_tensor
    xT = dt("xT", [2048, NTOK], F32, kind="ExternalInput").ap()
    w1 = dt("w1", [2048, 896], F32, kind="ExternalInput").ap()
    wf = dt("wf", [2048, 2], F32, kind="ExternalInput").ap()
    nv1 = dt("nv1", [128, 16], F32, kind="ExternalInput").ap()
    lbl = dt("lbl", [128, 2], F32, kind="ExternalInput").ap()
    hnw = dt("hnw", [128, 128], F32, kind="ExternalInput").ap()
    bfx = dt("bfx", [128, 1], F32, kind="ExternalInput").ap()
    cb = dt("cb", [128, 640], BF16, kind="ExternalInput").ap()
    cf = dt("cf", [128, 1024], F32, kind="ExternalInput").ap()
    oA = dt("oA", [NTOK, 128], F32, kind="ExternalOutput").ap()
    oBT = dt("oBT", [128, NTOK], F32, kind="ExternalOutput").ap()
    xTv = xT.rearrange("(k p) n -> p k n", p=128)
    w1v = w1.rearrange("(k p) n -> p k n", p=128)
    wfv = wf.rearrange("(k p) n -> p k n", p=128)
    oAv = oA.rearrange("(b p) v -> p b v", p=128)

    with contextlib.ExitStack() as st:
        sb = lambda name, shape, d: st.enter_context(nc.sbuf_tensor(name, shape, d))
        XS = sb("XS", [128, 2, 8, 512], F32)
        XB = sb("XB", [128, 16, 512], BF16)
        SQ = sb("SQ", [128, 16, 512], BF16)
        W1 = sb("W1", [128, 16, 896], BF16)
        WF = sb("WF", [128, 16, 2], BF16)
        NV = sb("NV", [128, 16], F32)
        LBL = sb("LBL", [128, 2], F32)
        LB = sb("LB", [128, 2], F32)
        HNW = sb("HNW", [128, 128], F32)
        BFX = sb("BFX", [128, 1], F32)
        CB = sb("CB", [128, 640], BF16)
        CF = sb("CF", [128, 1024], F32)
        RT = sb("RT", [128, 512], F32)
        RC = sb("RC", [128, 4], F32)
        KT = sb("KT", [128, NTOK], BF16)
        V = sb("V", [128, NTILES * 4, 128], BF16)
        QT = sb("QT", [128, 2, 512], BF16)
        TQ = sb("TQ", [128, 512], F32)
        TZ = sb("TZ", [128, 512], F32)
        TE = sb("TE", [128, 512], F32)
        TS = sb("TS", [128, 512], F32)
        TF = sb("TF", [128, 512], F32)
        TK = sb("TK", [128, 512], F32)
        TB = sb("TB", [128, 512], F32)
        TD = sb("TD", [128, 512], F32)
        TE1 = sb("TE1", [128, 512], F32)
        TE2 = sb("TE2", [128, 512], F32)
        QTB = sb("QTB", [128, 512], BF16)
        KTB = sb("KTB", [128, 512], BF16)
        KTOK = sb("KTOK", [128, 4, 128], BF16)
        VTOK = sb("VTOK", [128, 2, 4, 128], BF16)
        GTOK = sb("GTOK", [128, 2, 4, 128], F32)
        EBM = sb("EBM", [128, 8], F32)
        EBL = sb("EBL", [128, 8], F32)
        EBLM = sb("EBLM", [128, 8], F32)
        BDM = sb("BDM", [128, 8], F32)
        ST = sb("ST", [128, 128], F32)
        SM = sb("SM", [128, 2, 128], BF16)
        SCB = sb("SCB", [128, 128], BF16)
        TKV = sb("TKV", [128, 128], F32)
        T1 = sb("T1", [128, 128], F32)
        TSC = sb("TSC", [128, 128], F32)
        EG = sb("EG", [128, 128], F32)
        GG = sb("GG", [128, 128], F32)
        OAT = sb("OAT", [128, 2, 128], F32)
        SSO = sb("SSO", [128, 1], F32)
        RO = sb("RO", [128, 1], F32)
        PT = sb("PT", [128, 2, 512], BF16)
        CROW = sb("CROW", [128, 2, 512], BF16)
        LROW = sb("LROW", [1, 512], F32)
        ZF = sb("ZF", [1, 512], F32)
        POSC = sb("POSC", [128, NTILES * 4], F32)
        RP = sb("RP", [128, NTILES + 1], F32)
        BIAS = sb("BIAS", [128, 2, NTILES * 4], F32)
        ODEN = sb("ODEN", [128, 512], F32)
        OOUT = sb("OOUT", [128, 512], F32)
        PS = [st.enter_context(nc.psum_tensor("ps%d" % i, [128, 512], F32)) for i in range(8)]
        block = st.enter_context(nc.Block())

        ONESB, IDENT, MASKNEG, BDMASK, E0 = CB[:, 0:128], CB[:, 128:256], CB[:, 256:384], CB[:, 384:512], CB[:, 512:640]
        RESET, ONESF = CF[:, 0:512], CF[:, 512:1024]

        def ld(eng, dst, src, key, dk):
            S.op(eng, lambda e: e.dma_start(out=dst, in_=src), writes=[key], dma=dk)

        ld("sp", NV[:, :], nv1[:, :], "NV", "c0")
        ld("sp", LBL[:, :], lbl[:, :], "LBL", "c1")
        ld("sp", HNW[:, :], hnw[:, :], "HNW", "c2")
        ld("sp", BFX[:, :], bfx[:, :], "BFX", "c3")
        ld("sp", CB[:, :], cb[:, :], "CB", "c4")
        ld("sp", CF[:, :], cf[:, :], "CF", "c5")
        ld("pool", W1[:, 0:8, :], w1v[:, 0:8, :], ("W1", 0), "w1a")
        ld("pool", W1[:, 8:16, :], w1v[:, 8:16, :], ("W1", 1), "w1b")
        ld("pool", WF[:, :, :], wfv[:, :, :], "WF", "wf")
        W1K = [("W1", 0), ("W1", 1)]

        S.op("dve", lambda e: e.tensor_tensor(out=LB[:, 0:1], in0=LBL[:, 0:1], in1=LBL[:, 1:2], op=ALU.subtract), reads=["LBL"], writes=["LB"])
        S.op("act", lambda e: e.activation(out=LB[:, 1:2], in_=LB[:, 0:1], func=AF.Exp, scale=-1.0), reads=["LB"], writes=["LB"])
        S.op("dve", lambda e: e.tensor_scalar(out=LB[:, 0:1], in0=LB[:, 1:2], scalar1=1.0, scalar2=None, op0=ALU.add), reads=["LB"], writes=["LB"])
        S.op("dve", lambda e: e.reciprocal(out=LB[:, 0:1], in_=LB[:, 0:1]), reads=["LB"], writes=["LB"])
        S.op("dve", lambda e: e.tensor_tensor(out=LB[:, 1:2], in0=LB[:, 1:2], in1=LB[:, 0:1], op=ALU.mult), reads=["LB"], writes=["LB"])
        S.op("dve", lambda e: e.memset(ST[:, :], 0.0), writes=["ST"])
        S.op("dve", lambda e: e.memset(RP[:, :], 0.0), writes=["RP"])
        S.op("dve", lambda e: e.memset(CROW[:, :, :], 0.0), writes=[("CROW", 0), ("CROW", 1)])

        def rstd(ps_ap, r_ap, pkey, rkey, n):
            S.op("dve", lambda e: e.tensor_scalar(out=r_ap, in0=ps_ap, scalar1=1.0 / n, scalar2=EPS, op0=ALU.mult, op1=ALU.add),
                 reads=[pkey], writes=[rkey])
            S.op("act", lambda e: e.activation(out=r_ap, in_=r_ap, func=AF.Ln), reads=[rkey], writes=[rkey])
            S.op("act", lambda e: e.activation(out=r_ap, in_=r_ap, func=AF.Exp, scale=-0.5), reads=[rkey], writes=[rkey])

        def gen_A(T):
            par = T % 2
            tsl = slice(T * 512, (T + 1) * 512)
            for h in range(2):
                S.op("sp", lambda e, h=h: e.dma_start(out=XS[:, h, :, :], in_=xTv[:, 8 * h:8 * h + 8, tsl]), writes=[("XS", h)], dma="xs%d" % h)
            yield
            for h in range(2):
                S.op("act", lambda e, h=h: e.activation(out=SQ[:, 8 * h:8 * h + 8, :], in_=XS[:, h, :, :], func=AF.Square),
                     reads=[("XS", h)], writes=[("SQ", h)])
                for k in range(8):
                    if k % 2 == 0:
                        S.op("dve", lambda e, h=h, k=k: e.tensor_scalar(out=XB[:, 8 * h + k, :], in0=XS[:, h, k, :], scalar1=NV[:, 8 * h + k:8 * h + k + 1],
                                                                         scalar2=None, op0=ALU.mult), reads=[("XS", h), "NV"], writes=[("XB", h, k)])
                    else:
                        S.op("act", lambda e, h=h, k=k: e.activation(out=XB[:, 8 * h + k, :], in_=XS[:, h, k, :], func=AF.Copy, scale=NV[:, 8 * h + k:8 * h + k + 1]),
                             reads=[("XS", h), "NV"], writes=[("XB", h, k)])
                yield
            SQK = [("SQ", 0), ("SQ", 1)]
            XBK = [("XB", h, k) for h in range(2) for k in range(8)]
            for k in range(16):
                S.op("pe", lambda e, k=k: e.matmul(PS[0][:, :], lhsT=ONESB, rhs=SQ[:, k, :], start=(k == 0), stop=(k == 15)),
                     reads=SQK + ["CB"], writes=[("P", 0)])
            rstd(PS[0][:, :], RT[:, :], ("P", 0), "RT", D_MODEL)
            yield
            for b in range(4):
                for k in range(16):
                    S.op("pe", lambda e, k=k, b=b: e.matmul(PS[3][:, 400 + b:401 + b], lhsT=SQ[:, k, b * 128:(b + 1) * 128], rhs=ONESB[:, 0:1],
                                                            start=(k == 0), stop=(k == 15)), reads=SQK + ["CB"], writes=[("P", 3)])
            rstd(PS[3][:, 400:404], RC[:, :], ("P", 3), "RC", D_MODEL)
            yield
            for j in range(4):
                pa = 0 if j % 2 == 0 else 3
                for k in range(16):
                    S.op("pe", lambda e, k=k, j=j, pa=pa: e.matmul(PS[pa][:, :], lhsT=W1[:, k, j * 128:(j + 1) * 128], rhs=XB[:, k, :],
                                                            start=(k == 0), stop=(k == 15)), reads=W1K + XBK, writes=[("P", pa)])
                if j == 0:
                    S.op("dve", lambda e, pa=pa: e.tensor_tensor(out=TQ[:, :], in0=PS[pa][:, :], in1=RT[:, :], op=ALU.mult), reads=[("P", pa), "RT"], writes=["TQ"])
                elif j == 1:
                    S.op("dve", lambda e, pa=pa: e.tensor_tensor(out=TZ[:, :], in0=PS[pa][:, :], in1=RT[:, :], op=ALU.mult), reads=[("P", pa), "RT"], writes=["TZ"])
                elif j == 2:
                    S.op("dve", lambda e, pa=pa: e.scalar_tensor_tensor(out=QT[:, par, :], in0=PS[pa][:, :], scalar=128.0 ** -0.5, in1=RT[:, :], op0=ALU.mult, op1=ALU.mult),
                         reads=[("P", pa), "RT"], writes=[("QT", par)])
                else:
                    S.op("dve", lambda e, pa=pa: e.tensor_tensor(out=KT[:, tsl], in0=PS[pa][:, :], in1=RT[:, :], op=ALU.mult), reads=[("P", pa), "RT"], writes=[("KT", T)])
                yield
            for b in range(4):
                pa = 0 if b % 2 == 0 else 3
                for k in range(16):
                    S.op("pe", lambda e, k=k, b=b, pa=pa: e.matmul(PS[pa][:, 0:384], lhsT=XB[:, k, b * 128:(b + 1) * 128], rhs=W1[:, k, 512:896],
                                                            start=(k == 0), stop=(k == 15)), reads=W1K + XBK, writes=[("P", pa)])
                S.op("act", lambda e, b=b, pa=pa: e.activation(out=VTOK[:, par, b, :], in_=PS[pa][:, 0:128], func=AF.Copy, scale=RC[:, b:b + 1]),
                     reads=[("P", pa), "RC"], writes=[("VTOK", par, b)])
                S.op("act", lambda e, b=b, pa=pa: e.activation(out=GTOK[:, par, b, :], in_=PS[pa][:, 128:256], func=AF.Copy, scale=RC[:, b:b + 1]),
                     reads=[("P", pa), "RC"], writes=[("GTOK", par, b)])
                S.op("act", lambda e, b=b, pa=pa: e.activation(out=V[:, 4 * T + b, :], in_=PS[pa][:, 256:384], func=AF.Copy, scale=RC[:, b:b + 1]),
                     reads=[("P", pa), "RC"], writes=[("V", 4 * T + b)])
                yield
            for k in range(16):
                S.op("pe", lambda e, k=k: e.matmul(PS[0][0:1, :], lhsT=WF[:, k, 0:1], rhs=XB[:, k, :], start=(k == 0), stop=(k == 15)),
                     reads=["WF"] + XBK, writes=[("P", 0)])
            S.op("dve", lambda e: e.tensor_tensor(out=ZF[0:1, :], in0=PS[0][0:1, :], in1=RT[0:1, :], op=ALU.mult), reads=[("P", 0), "RT"], writes=["ZF"])
            S.op("dve", lambda e: e.tensor_scalar(out=ZF[0:1, :], in0=ZF[0:1, :], scalar1=BFX[0:1, 0:1], scalar2=None, op0=ALU.add), reads=["ZF", "BFX"], writes=["ZF"])
            S.op("act", lambda e: e.activation(out=ZF[0:1, :], in_=ZF[0:1, :], func=AF.Exp, scale=-1.0), reads=["ZF"], writes=["ZF"])
            S.op("act", lambda e: e.activation(out=ZF[0:1, :], in_=ZF[0:1, :], func=AF.Ln, bias=1.0), reads=["ZF"], writes=["ZF"])
            S.op("dve", lambda e: e.tensor_tensor_scan(out=LROW[0:1, :], data0=ONESF[0:1, :], data1=ZF[0:1, :], initial=0.0, op0=ALU.mult, op1=ALU.add),
                 reads=["ZF", "CF"], writes=["LROW"])
            S.op("dve", lambda e: e.tensor_scalar(out=CROW[0:1, par, :], in0=LROW[0:1, :], scalar1=-1.0, scalar2=None, op0=ALU.mult), reads=["LROW"], writes=[("CROW", par)])
            yield
            for b in range(4):
                S.op("pe", lambda e, b=b: e.matmul(PS[3][:, 384 + b:385 + b], lhsT=LROW[0:1, b * 128:(b + 1) * 128], rhs=ONESF[0:1, 0:1], start=True, stop=True),
                     reads=["LROW", "CF"], writes=[("P", 3)])
            S.op("pe", lambda e: e.matmul(PS[3][:, 388:389], lhsT=ONESF[0:1, 0:128], rhs=LROW[0:1, 511:512], start=True, stop=True),
                 reads=["LROW", "CF"], writes=[("P", 3)])
            S.op("dve", lambda e: e.tensor_scalar(out=POSC[:, 4 * T:4 * T + 4], in0=PS[3][:, 384:388], scalar1=RP[:, T:T + 1], scalar2=None, op0=ALU.add),
                 reads=[("P", 3), "RP"], writes=["POSC"])
            S.op("dve", lambda e: e.tensor_tensor(out=RP[:, T + 1:T + 2], in0=PS[3][:, 388:389], in1=RP[:, T:T + 1], op=ALU.add),
                 reads=[("P", 3), "RP"], writes=["RP"])
            S.op("dve", lambda e: e.tensor_scalar(out=BIAS[:, par, 0:4 * T + 4], in0=POSC[:, 0:4 * T + 4], scalar1=RP[:, T:T + 1], scalar2=None, op0=ALU.subtract),
                 reads=["POSC", "RP"], writes=[("BIAS", par)])
            yield

        def gen_H(T):
            par = T % 2
            S.op("act", lambda e: e.activation(out=TE[:, :], in_=TZ[:, :], func=AF.Exp, scale=-1.0), reads=["TZ"], writes=["TE"])
            S.op("dve", lambda e: e.tensor_scalar(out=TS[:, :], in0=TE[:, :], scalar1=1.0, scalar2=None, op0=ALU.add), reads=["TE"], writes=["TS"])
            S.op("dve", lambda e: e.reciprocal(out=TS[:, :], in_=TS[:, :]), reads=["TS"], writes=["TS"])
            S.op("dve", lambda e: e.tensor_scalar(out=TF[:, :], in0=TS[:, :], scalar1=LB[:, 1:2], scalar2=LB[:, 0:1], op0=ALU.mult, op1=ALU.add),
                 reads=["TS", "LB"], writes=["TF"])
            yield
            S.op("act", lambda e: e.activation(out=TF[:, :], in_=TF[:, :], func=AF.Ln), reads=["TF"], writes=["TF"])
            S.op("dve", lambda e: e.scalar_tensor_tensor(out=TK[:, :], in0=TE[:, :], scalar=LB[:, 1:2], in1=TS[:, :], op0=ALU.mult, op1=ALU.mult),
                 reads=["TE", "TS", "LB"], writes=["TK"])
            S.op("dve", lambda e: e.tensor_tensor_scan(out=TB[:, :], data0=RESET, data1=TF[:, :], initial=0.0, op0=ALU.mult, op1=ALU.add),
                 reads=["TF", "CF"], writes=["TB"])
            yield
            TBv = TB[:, :].rearrange("p (c n) -> p c n", n=64)
            TDv = TD[:, :].rearrange("p (c n) -> p c n", n=64)
            S.op("dve", lambda e: e.tensor_tensor(out=TDv, in0=TBv, in1=TBv[:, :, 31:32].to_broadcast([128, 8, 64]), op=ALU.subtract), reads=["TB"], writes=["TD"])
            S.op("act", lambda e: e.activation(out=TE1[:, :], in_=TD[:, :], func=AF.Exp), reads=["TD"], writes=["TE1"])
            S.op("act", lambda e: e.activation(out=TE2[:, :], in_=TD[:, :], func=AF.Exp, scale=-1.0), reads=["TD"], writes=["TE2"])
            yield
            S.op("dve", lambda e: e.tensor_tensor(out=QTB[:, :], in0=TQ[:, :], in1=TE1[:, :], op=ALU.mult), reads=["TQ", "TE1"], writes=["QTB"])
            S.op("dve", lambda e: e.tensor_tensor(out=KTB[:, :], in0=TK[:, :], in1=TE2[:, :], op=ALU.mult), reads=["TK", "TE2"], writes=["KTB"])
            S.op("act", lambda e: e.activation(out=EBM[:, :], in_=TBv[:, :, 31], func=AF.Exp), reads=["TB"], writes=["EBM"])
            S.op("act", lambda e: e.activation(out=EBL[:, :], in_=TBv[:, :, 63], func=AF.Exp), reads=["TB"], writes=["EBL"])
            S.op("dve", lambda e: e.tensor_tensor(out=BDM[:, :], in0=TBv[:, :, 63], in1=TBv[:, :, 31], op=ALU.subtract), reads=["TB"], writes=["BDM"])
            S.op("act", lambda e: e.activation(out=EBLM[:, :], in_=BDM[:, :], func=AF.Exp), reads=["BDM"], writes=["EBLM"])
            ESC = ["EBM", "EBL", "EBLM"]
            yield
            for b in range(4):
                bs = slice(b * 128, (b + 1) * 128)
                trp = PS[1][:, 448:512].bitcast(BF16)
                S.op("pe", lambda e, bs=bs, trp=trp: e.transpose(out=trp, in_=KTB[:, bs], identity=IDENT), reads=["KTB", "CB"], writes=[("P", 1)])
                S.op("act", lambda e, b=b, trp=trp: e.activation(out=KTOK[:, b, :], in_=trp, func=AF.Copy), reads=[("P", 1)], writes=[("KTOK", b)])
                yield
                S.op("pe", lambda e, bs=bs: e.matmul(PS[1][:, 0:128], lhsT=KTB[:, bs], rhs=QTB[:, bs], start=True, stop=True),
                     reads=["KTB", "QTB"], writes=[("P", 1)])
                S.op("act", lambda e: e.activation(out=TSC[:, :], in_=PS[1][:, 0:128], func=AF.Copy), reads=[("P", 1)], writes=["TSC"])
                S.op("dve", lambda e: e.tensor_tensor(out=SCB[:, :], in0=TSC[:, :], in1=BDMASK, op=ALU.mult), reads=["TSC", "CB"], writes=["SCB"])
                yield
                for c2 in range(2):
                    rs = slice(c2 * 64, (c2 + 1) * 64)
                    S.op("pe", lambda e, b=b, c2=c2, rs=rs: e.matmul(PS[1 + c2][:, 128:256], lhsT=KTOK[rs, b, :], rhs=VTOK[rs, par, b, :], start=True, stop=True),
                         reads=[("KTOK", b), ("VTOK", par, b)], writes=[("P", 1 + c2)])
                yield
                for c2 in range(2):
                    c = 2 * b + c2
                    S.op("dve", lambda e, c=c, c2=c2: e.tensor_scalar(out=SM[:, c2, :], in0=ST[:, :], scalar1=EBM[:, c:c + 1], scalar2=None, op0=ALU.mult),
                         reads=["ST"] + ESC, writes=[("SM", c2)])
                    S.op("dve", lambda e, c=c, c2=c2: e.tensor_scalar(out=TKV[:, :], in0=PS[1 + c2][:, 128:256], scalar1=EBLM[:, c:c + 1], scalar2=None, op0=ALU.mult),
                         reads=[("P", 1 + c2)] + ESC, writes=["TKV"])
                    S.op("dve", lambda e, c=c: e.scalar_tensor_tensor(out=ST[:, :], in0=ST[:, :], scalar=EBL[:, c:c + 1], in1=TKV[:, :], op0=ALU.mult, op1=ALU.add),
                         reads=["ST", "TKV"] + ESC, writes=["ST"])
                yield
                S.op("pe", lambda e, b=b: e.matmul(PS[1][:, 0:128], lhsT=SCB[:, :], rhs=VTOK[:, par, b, :], start=True, stop=False),
                     reads=["SCB", ("VTOK", par, b)], writes=[("P", 1)])
                for c2 in range(2):
                    S.op("pe", lambda e, b=b, c2=c2: e.matmul(PS[1][c2 * 64:(c2 + 1) * 64, 0:128], lhsT=QTB[:, b * 128 + c2 * 64:b * 128 + (c2 + 1) * 64], rhs=SM[:, c2, :],
                                                              start=False, stop=True), reads=["QTB", ("SM", c2)], writes=[("P", 1)])
                ob_ = (4 * T + b) % 2
                S.op("act", lambda e: e.activation(out=T1[:, :], in_=PS[1][:, 0:128], func=AF.Square, accum_out=SSO[:, 0:1]), reads=[("P", 1)], writes=["T1", "SSO"])
                rstd(SSO[:, 0:1], RO[:, 0:1], "SSO", "RO", 128)
                S.op("dve", lambda e: e.scalar_tensor_tensor(out=T1[:, :], in0=PS[1][:, 0:128], scalar=RO[:, 0:1], in1=HNW[:, :], op0=ALU.mult, op1=ALU.mult),
                     reads=[("P", 1), "RO", "HNW"], writes=["T1"])
                yield
                S.op("act", lambda e, b=b: e.activation(out=EG[:, :], in_=GTOK[:, par, b, :], func=AF.Exp, scale=-1.0), reads=[("GTOK", par, b)], writes=["EG"])
                S.op("dve", lambda e: e.tensor_scalar(out=EG[:, :], in0=EG[:, :], scalar1=1.0, scalar2=None, op0=ALU.add), reads=["EG"], writes=["EG"])
                S.op("dve", lambda e: e.reciprocal(out=EG[:, :], in_=EG[:, :]), reads=["EG"], writes=["EG"])
                S.op("dve", lambda e, b=b: e.tensor_tensor(out=GG[:, :], in0=GTOK[:, par, b, :], in1=EG[:, :], op=ALU.mult), reads=["EG", ("GTOK", par, b)], writes=["GG"])
                S.op("dve", lambda e, ob_=ob_: e.tensor_tensor(out=OAT[:, ob_, :], in0=T1[:, :], in1=GG[:, :], op=ALU.mult), reads=["T1", "GG"], writes=[("OAT", ob_)])
                S.op("sp", lambda e, ob_=ob_, b=b: e.dma_start(out=oAv[:, 4 * T + b, :], in_=OAT[:, ob_, :]), reads=[("OAT", ob_)], dma="oa%d" % ob_)
                yield

        def gen_F(T):
            par = T % 2
            tsl = slice(T * 512, (T + 1) * 512)
            nkb = 4 * T + 4

            def s_stage(j):
                lo = 0 if j < 4 * T else 128 * (j - 4 * T)
                sp_ = j % 2
                sps = PS[4 + sp_]
                S.op("pe", lambda e: e.matmul(sps[:, lo:512], lhsT=KT[:, j * 128:(j + 1) * 128], rhs=QT[:, par, lo:512], start=True, stop=False),
                     reads=[("KT", j // 4), ("QT", par)], writes=[("P", 4 + sp_)])
                diag = j >= 4 * T
                S.op("pe", lambda e: e.matmul(sps[:, lo:512], lhsT=E0, rhs=CROW[:, par, lo:512], start=False, stop=(not diag)),
                     reads=[("CROW", par), "CB"], writes=[("P", 4 + sp_)])
                if diag:
                    S.op("pe", lambda e: e.matmul(sps[:, lo:lo + 128], lhsT=IDENT, rhs=MASKNEG, start=False, stop=True),
                         reads=["CB"], writes=[("P", 4 + sp_)])
                S.op("act", lambda e: e.activation(out=PT[:, sp_, lo:512], in_=sps[:, lo:512], func=AF.Exp, bias=BIAS[:, par, j:j + 1]),
                     reads=[("P", 4 + sp_), ("BIAS", par)], writes=[("PT", sp_)])

            def pv_stage(j):
                lo = 0 if j < 4 * T else 128 * (j - 4 * T)
                sp_ = j % 2
                S.op("pe", lambda e: e.matmul(PS[6][:, lo:512], lhsT=V[:, j, :], rhs=PT[:, sp_, lo:512], start=(j == 0), stop=(j == nkb - 1)),
                     reads=[("V", j), ("PT", sp_)], writes=[("P", 6)])
                S.op("pe", lambda e: e.matmul(PS[7][:, lo:512], lhsT=ONESB, rhs=PT[:, sp_, lo:512], start=(j == 0), stop=(j == nkb - 1)),
                     reads=["CB", ("PT", sp_)], writes=[("P", 7)])

            s_stage(0)
            yield
            for j in range(nkb):
                if j + 1 < nkb:
                    s_stage(j + 1)
                pv_stage(j)
                yield
            S.op("dve", lambda e: e.reciprocal(out=ODEN[:, :], in_=PS[7][:, :]), reads=[("P", 7)], writes=["ODEN"])
            S.op("dve", lambda e: e.tensor_tensor(out=OOUT[:, :], in0=PS[6][:, :], in1=ODEN[:, :], op=ALU.mult), reads=[("P", 6), "ODEN"], writes=["OOUT"])
            S.op("sp", lambda e: e.dma_start(out=oBT[:, tsl], in_=OOUT[:, :]), reads=["OOUT"], dma="ob")
            yield

        for _ in gen_A(0):
            pass
        for T in range(NTILES):
            gens = [gen_H(T), gen_F(T)]
            if T + 1 < NTILES:
                gens.append(gen_A(T + 1))
            while gens:
                for g in list(gens):
                    try:
                        next(g)
                    except StopIteration:
                        gens.remove(g)
        S.barrier()
        S.emit(nc, block, st)
    return nc


def prep_l1(inp, ntok=SEQ):
    x = np.asarray(inp["x"], np.float32)[0]
    xT = np.ascontiguousarray(x[:ntok].T)
    w_in = np.asarray(inp["w_in"], np.float32)[0]
    nv1 = _colvec(inp["norm_mix_pre"][0])
    p = np.arange(128)
    ones = np.ones((128, 128), np.float32)
    ident = np.eye(128, dtype=np.float32)
    maskneg = np.where(p[:, None] > p[None, :], -30000.0, 0.0).astype(np.float32)
    bdmask = ((p[:, None] // 64 == p[None, :] // 64) & (p[:, None] <= p[None, :])).astype(np.float32)
    e0 = np.zeros((128, 128), np.float32)
    e0[0, :] = 1.0
    cb = np.concatenate([ones, ident, maskneg, bdmask, e0], axis=1).astype(ml_dtypes.bfloat16)
    reset = np.tile((np.arange(512) % 64 != 0).astype(np.float32)[None, :], (128, 1))
    cf = np.ascontiguousarray(np.concatenate([reset, np.ones((128, 512), np.float32)], axis=1))
    lbl_all = np.asarray(inp["hgrn_lb_logits"], np.float32)
    hnw = np.ascontiguousarray(np.tile(np.asarray(inp["hgrn_norm_w"], np.float32)[0][None, :], (128, 1)))
    maps = []
    for c in range(NCORES):
        cs = lambda base: w_in[:, base + c * 128: base + (c + 1) * 128]
        w1 = np.ascontiguousarray(np.concatenate([cs(0), cs(1024), cs(4096), cs(5120), cs(2048), cs(3072), cs(6144)], axis=1))
        wf = np.ascontiguousarray(np.repeat(w_in[:, 7168 + c: 7169 + c], 2, axis=1))
        lbl = np.ascontiguousarray(lbl_all[:, c * 128:(c + 1) * 128].T)
        bfx = np.full((128, 1), np.asarray(inp["b_fox_f"], np.float32)[0, c], np.float32)
        maps.append(dict(xT=xT, w1=w1, wf=wf, nv1=nv1, lbl=lbl, hnw=hnw, bfx=bfx, cb=cb, cf=cf))
    return maps


def post_l1(results):
    oA = np.concatenate([r["oA"] for r in results], axis=1)
    oB = np.concatenate([np.ascontiguousarray(r["oBT"].T) for r in results], axis=1)
    return oA, oB


_CACHE = {}


def kernel(**inputs):
    inputs = {k: np.asarray(v) for k, v in inputs.items()}
    if "l1" not in _CACHE:
        _CACHE["l1"] = build_l1(SEQ // 512)
        _CACHE["l2"] = build_l2()
    cores = list(range(NCORES))
    r1 = run_bass_kernel_spmd(_CACHE["l1"], prep_l1(inputs), core_ids=cores)
    oA, oB = post_l1(r1.results)
    import os
    if os.environ.get("L1ONLY"):
        return np.zeros((1, SEQ, D_MODEL), np.float32)
    r2 = run_bass_kernel_spmd(_CACHE["l2"], prep_l2(inputs, oA, oB), core_ids=cores)
    return post_l2(r2.results).astype(np.float32)
```

```python
import contextlib
import numpy as np
import ml_dtypes
import concourse.bass as bass
import concourse.mybir as mybir
from concourse.bass_utils import run_bass_kernel_spmd

F32 = mybir.dt.float32
BF16 = mybir.dt.bfloat16
ALU = mybir.AluOpType
AF = mybir.ActivationFunctionType

D_MODEL = 2048
SEQ = 8192
D_FF = 5632
EPS = 1e-6
NCORES = 8

ENGS = ("pe", "act", "dve", "pool", "sp")


class Sched:
    def __init__(self):
        self.ops = {e: [] for e in ENGS}
        self.lastw = {}
        self.readers = {}
        self.dma_cnt = {}

    def _tok(self, eng, rec, idx):
        if rec["dma"] is not None:
            return ("dma", rec["dma"], rec["dma_val"])
        return ("eng", eng, idx)

    def op(self, eng, fn, reads=(), writes=(), dma=None):
        idx = len(self.ops[eng])
        rec = dict(fn=fn, dma=dma, needed=False, deps={})
        if dma is not None:
            self.dma_cnt[dma] = self.dma_cnt.get(dma, 0) + 1
            rec["dma_val"] = 16 * self.dma_cnt[dma]
        tok = self._tok(eng, rec, idx)
        deps = rec["deps"]

        def add(t):
            if t is None:
                return
            if t[0] == "dma":
                cur = self.dma_cnt[t[1]] - (1 if dma == t[1] else 0)
                t = ("dma", t[1], 16 * cur)
            if t[0] == "eng" and t[1] == "pe" and eng == "pe" and dma is None:
                return
            k = (t[0], t[1])
            if k not in deps or deps[k] < t[2]:
                deps[k] = t[2]

        for key in reads:
            add(self.lastw.get(key))
        for key in writes:
            add(self.lastw.get(key))
            for t in self.readers.get(key, {}).values():
                add(t)
        self.ops[eng].append(rec)
        for key in writes:
            self.lastw[key] = tok
            self.readers[key] = {}
        for key in reads:
            r = self.readers.setdefault(key, {})
            k = (tok[0], tok[1])
            if k not in r or r[k][2] < tok[2]:
                r[k] = tok
        return tok

    def barrier(self):
        toks = []
        for e in ENGS:
            for i in range(len(self.ops[e]) - 1, -1, -1):
                if self.ops[e][i]["dma"] is None and self.ops[e][i]["fn"] is not None:
                    toks.append(("eng", e, i))
                    break
        for k, c in self.dma_cnt.items():
            toks.append(("dma", k, 16 * c))
        for e in ENGS:
            rec = dict(fn=None, dma=None, needed=False, deps={})
            for t in toks:
                rec["deps"][(t[0], t[1])] = t[2]
            self.ops[e].append(rec)
        self.lastw = {}
        self.readers = {}

    def emit(self, nc, block, stack):
        sems = {}
        for e in ENGS:
            sems[("eng", e)] = stack.enter_context(nc.semaphore("s_" + e))
        for k in self.dma_cnt:
            sems[("dma", k)] = stack.enter_context(nc.semaphore("d_" + str(k)))
        for e in ENGS:
            for rec in self.ops[e]:
                for (kind, name), val in rec["deps"].items():
                    if kind == "eng":
                        self.ops[name][val]["needed"] = True
        vals = {}
        for e in ENGS:
            cnt = 0
            for i, rec in enumerate(self.ops[e]):
                if rec["needed"]:
                    cnt += 1
                    vals[(e, i)] = cnt
            assert cnt < 60000, (e, cnt)
        self.n_emitted = {e: len(self.ops[e]) for e in ENGS}

        def run(e, engine):
            waited = {}
            for i, rec in enumerate(self.ops[e]):
                for (kind, name), val in rec["deps"].items():
                    v = vals[(name, val)] if kind == "eng" else val
                    sk = (kind, name)
                    if waited.get(sk, 0) >= v:
                        continue
                    engine.wait_ge(sems[sk], v)
                    waited[sk] = v
                if rec["fn"] is None:
                    continue
                ins = rec["fn"](engine)
                if rec["dma"] is not None:
                    ins.then_inc(sems[("dma", rec["dma"])], 16)
                elif rec["needed"]:
                    ins.then_inc(sems[("eng", e)], 1)

        @block.tensor
        def _(eng):
            run("pe", eng)

        @block.scalar
        def _(eng):
            run("act", eng)

        @block.vector
        def _(eng):
            run("dve", eng)

        @block.gpsimd
        def _(eng):
            run("pool", eng)

        @block.sync
        def _(eng):
            run("sp", eng)


def _v3(ap, k):
    return ap.rearrange("p (k n) -> p k n", k=k)


def build_l2(NT=1024):
    nc = bass.Bass("TRN2", target_bir_lowering=False)
    S = Sched()
    NTT = NT // 512
    dt = nc.dram_tensor
    xT = dt("xT", [2048, NT], F32, kind="ExternalInput").ap()
    oT = dt("oT", [2048, NT], F32, kind="ExternalInput").ap()
    wg = dt("wg", [2048, 4096], F32, kind="ExternalInput").ap()
    wua = dt("wua", [1024, 2048], F32, kind="ExternalInput").ap()
    wub = dt("wub", [1024, 2048], F32, kind="ExternalInput").ap()
    wo = dt("wo", [2048, 2048], F32, kind="ExternalInput").ap()
    wfi = dt("wfi", [2048, 2 * D_FF], F32, kind="ExternalInput").ap()
    wfd = dt("wfd", [D_FF, 2048], F32, kind="ExternalInput").ap()
    nv = dt("nv", [128, 64], F32, kind="ExternalInput").ap()
    cst = dt("cst", [128, 128], BF16, kind="ExternalInput").ap()
    outT = dt("outT", [2048, NT], F32, kind="ExternalOutput").ap()
    zd = dt("zd", [2048, NT], F32).ap()
    x1d = dt("x1d", [2048, NT], F32).ap()

    def cm(ap):
        return ap.rearrange("(k p) n -> p k n", p=128)

    xTv, oTv, wgv, wuav, wubv, wov, wfiv, wfdv = map(cm, (xT, oT, wg, wua, wub, wo, wfi, wfd))
    outTv, zdv, x1dv = cm(outT), cm(zd), cm(x1d)

    with contextlib.ExitStack() as st:
        sb = lambda name, shape, d: st.enter_context(nc.sbuf_tensor(name, shape, d))
        A = sb("A", [128, 44 * NT], BF16)
        B = sb("B", [128, 16 * NT], BF16)
        W = sb("W", [128, 16384], BF16)
        T = sb("T", [128, 16, 512], F32)
        SQT = sb("SQT", [128, 4, 512], BF16)
        R = sb("R", [128, 2, NT], F32)
        NV = sb("NV", [128, 64], F32)
        ONES = sb("ONES", [128, 128], BF16)
        PS = [st.enter_context(nc.psum_tensor("ps%d" % i, [128, 512], F32)) for i in range(8)]
        block = st.enter_context(nc.Block())

        xb = _v3(A[:, 0:16 * NT], 16)
        ob = _v3(A[:, 16 * NT:32 * NT], 16)
        SP0 = 32 * NT
        SQ = _v3(A[:, SP0 + 8192:SP0 + 12288], 16)
        XS = _v3(W[:, 8192:16384].bitcast(F32), 16)
        hid = _v3(A[:, 0:44 * NT], 44)
        mb = _v3(B[:, :], 16)
        x1b = mb

        S.op("sp", lambda e: e.dma_start(out=NV[:, :], in_=nv[:, :]), writes=["NV"], dma="c0")
        S.op("sp", lambda e: e.dma_start(out=ONES[:, :], in_=cst[:, :]), writes=["ONES"], dma="c1")

        def rstd_from(ps_ap, r_ap, keyp, keyr):
            S.op("dve", lambda e: e.tensor_scalar(out=r_ap, in0=ps_ap, scalar1=1.0 / D_MODEL, scalar2=EPS,
                                                  op0=ALU.mult, op1=ALU.add), reads=[keyp], writes=[keyr])
            S.op("act", lambda e: e.activation(out=r_ap, in_=r_ap, func=AF.Sqrt), reads=[keyr], writes=[keyr])
            S.op("dve", lambda e: e.reciprocal(out=r_ap, in_=r_ap), reads=[keyr], writes=[keyr])

        XS2 = _v3(T[:, 0:8, :].rearrange("p a b -> p (a b)"), 16)
        for s in range(NT // 256):
            cs = slice(s * 256, (s + 1) * 256)
            XS_ = XS if s % 2 == 0 else XS2
            XSK = [("W", 1, 0), ("W", 1, 1)] if s % 2 == 0 else [("T", i) for i in range(8)]
            S.op("sp", lambda e, cs=cs, XS_=XS_: e.dma_start(out=XS_[:, :, :], in_=xTv[:, :, cs]), writes=XSK, dma="xs%d" % (s % 2))
            S.op("act", lambda e, XS_=XS_: e.activation(out=SQ[:, :, :], in_=XS_[:, :, :], func=AF.Square), reads=XSK, writes=["SQ"])
            for k in range(16):
                S.op("dve", lambda e, k=k, cs=cs, XS_=XS_: e.tensor_scalar(out=xb[:, k, cs], in0=XS_[:, k, :], scalar1=NV[:, k:k + 1],
                                                                           scalar2=None, op0=ALU.mult),
                     reads=XSK + ["NV"], writes=[("A0", k, s)])
            for k in range(16):
                S.op("pe", lambda e, k=k: e.matmul(PS[0][:, 0:256], lhsT=ONES[:, :], rhs=SQ[:, k, :], start=(k == 0), stop=(k == 15)),
                     reads=["SQ", "ONES"], writes=[("P", 0)])
            rstd_from(PS[0][:, 0:256], R[:, 0, cs], ("P", 0), ("R0", s))
        for h in range(2):
            S.op("pool", lambda e, h=h: e.dma_start(out=ob[:, 8 * h:8 * h + 8, :], in_=oTv[:, 8 * h:8 * h + 8, :]),
                 writes=[("A1", h)], dma="ob%d" % h)
        xb_keys = [("A0", k, s) for k in range(16) for s in range(NT // 256)]
        r0_keys = [("R0", s) for s in range(NT // 256)]

        for g in range(8):
            s_ = g % 2
            base = s_ * 8192
            wga = _v3(W[:, base:base + 4096], 16)
            wgb = _v3(W[:, base + 4096:base + 8192], 16)
            wa = _v3(A[:, SP0 + s_ * 4096:SP0 + s_ * 4096 + 2048], 8)
            wb = _v3(A[:, SP0 + s_ * 4096 + 2048:SP0 + s_ * 4096 + 4096], 8)
            gs = slice(g * 256, (g + 1) * 256)
            gs2 = slice(2048 + g * 256, 2048 + (g + 1) * 256)
            wk = ("W", s_)
            dk = "w%d" % s_
            S.op("pool", lambda e, wga=wga, gs=gs: e.dma_start(out=wga[:, :, :], in_=wgv[:, :, gs]), writes=[wk + (0,)], dma=dk + "a")
            S.op("pool", lambda e, wgb=wgb, gs2=gs2: e.dma_start(out=wgb[:, :, :], in_=wgv[:, :, gs2]), writes=[wk + (1,)], dma=dk + "b")
            S.op("pool", lambda e, wa=wa, gs=gs: e.dma_start(out=wa[:, :, :], in_=wuav[:, :, gs]), writes=[wk + (2,)], dma=dk + "c")
            S.op("pool", lambda e, wb=wb, gs=gs: e.dma_start(out=wb[:, :, :], in_=wubv[:, :, gs]), writes=[wk + (3,)], dma=dk + "d")
            for nb in range(2):
                n = g * 2 + nb
                ns = slice(nb * 128, (nb + 1) * 128)
                for t in range(NTT):
                    ts_ = slice(t * 512, (t + 1) * 512)
                    par = (n * NTT + t) % 2
                    pga, pgb, pya, pyb = (PS[4 * par + i] for i in range(4))
                    kga, kgb, kya, kyb = (("P", 4 * par + i) for i in range(4))
                    t0, t1 = T[:, 2 * par, :], T[:, 2 * par + 1, :]
                    k0, k1 = ("T", 2 * par), ("T", 2 * par + 1)
                    for k in range(16):
                        S.op("pe", lambda e, k=k, pga=pga, wga=wga, ns=ns, ts_=ts_: e.matmul(pga[:, :], lhsT=wga[:, k, ns], rhs=xb[:, k, ts_], start=(k == 0), stop=(k == 15)),
                             reads=[wk + (0,)] + xb_keys, writes=[kga])
                    for k in range(16):
                        S.op("pe", lambda e, k=k, pgb=pgb, wgb=wgb, ns=ns, ts_=ts_: e.matmul(pgb[:, :], lhsT=wgb[:, k, ns], rhs=xb[:, k, ts_], start=(k == 0), stop=(k == 15)),
                             reads=[wk + (1,)] + xb_keys, writes=[kgb])
                    for k in range(8):
                        S.op("pe", lambda e, k=k, pya=pya, wa=wa, ns=ns, ts_=ts_: e.matmul(pya[:, :], lhsT=wa[:, k, ns], rhs=ob[:, k, ts_], start=(k == 0), stop=(k == 7)),
                             reads=[wk + (2,), ("A1", 0)], writes=[kya])
                    for k in range(8):
                        S.op("pe", lambda e, k=k, pyb=pyb, wb=wb, ns=ns, ts_=ts_: e.matmul(pyb[:, :], lhsT=wb[:, k, ns], rhs=ob[:, 8 + k, ts_], start=(k == 0), stop=(k == 7)),
                             reads=[wk + (3,), ("A1", 1)], writes=[kyb])
                    S.op("dve", lambda e, t0=t0, pga=pga, ts_=ts_: e.tensor_tensor(out=t0, in0=pga[:, :], in1=R[:, 0, ts_], op=ALU.mult),
                         reads=[kga] + r0_keys, writes=[k0])
                    S.op("act", lambda e, t0=t0: e.activation(out=t0, in_=t0, func=AF.Sigmoid), reads=[k0], writes=[k0])
                    S.op("dve", lambda e, t1=t1, pgb=pgb, ts_=ts_: e.tensor_tensor(out=t1, in0=pgb[:, :], in1=R[:, 0, ts_], op=ALU.mult),
                         reads=[kgb] + r0_keys, writes=[k1])
                    S.op("act", lambda e, t1=t1: e.activation(out=t1, in_=t1, func=AF.Sigmoid), reads=[k1], writes=[k1])
                    S.op("dve", lambda e, t0=t0, pya=pya: e.tensor_tensor(out=t0, in0=pya[:, :], in1=t0, op=ALU.mult),
                         reads=[kya, k0], writes=[k0])
                    S.op("dve", lambda e, t1=t1, pyb=pyb: e.tensor_tensor(out=t1, in0=pyb[:, :], in1=t1, op=ALU.mult),
                         reads=[kyb, k1], writes=[k1])
                    S.op("dve", lambda e, t0=t0, t1=t1, n=n, ts_=ts_: e.tensor_tensor(out=mb[:, n, ts_], in0=t0, in1=t1, op=ALU.add),
                         reads=[k0, k1], writes=[("B", n, t)])

        S.barrier()
        ZS = _v3(A[:, 0:32 * NT].bitcast(F32), 16)

        def evac_stats(ps, pkey, nchunk, t, ssb, first, last, dram_v, par):
            zt = T[:, 4 + par, :]
            zk = ("T", 4 + par)
            sq = SQT[:, par, :]
            sk = ("SQT", par)
            S.op("act", lambda e: e.activation(out=zt, in_=ps[:, :], func=AF.Copy), reads=[pkey], writes=[zk])
            S.op("act", lambda e: e.activation(out=sq, in_=ps[:, :], func=AF.Square), reads=[pkey], writes=[sk])
            S.op("sp", lambda e: e.dma_start(out=dram_v[:, nchunk, t * 512:(t + 1) * 512], in_=zt), reads=[zk],
                 writes=[("zd", nchunk, t)], dma="zst%d" % par)
            return lambda: S.op("pe", lambda e: e.matmul(PS[ssb][:, :], lhsT=ONES[:, :], rhs=sq, start=first, stop=last),
                                reads=[sk, "ONES"], writes=[("P", ssb)])

        pend = []
        for g in range(8):
            s_ = g % 2
            base = s_ * 8192
            wos = _v3(W[:, base:base + 4096], 16)
            wk = ("W", s_)
            gs = slice(g * 256, (g + 1) * 256)
            S.op("pool", lambda e, wos=wos, gs=gs: e.dma_start(out=wos[:, :, :], in_=wov[:, :, gs]), writes=[wk], dma="w%d" % s_)
            for nb in range(2):
                n = g * 2 + nb
                ns = slice(nb * 128, (nb + 1) * 128)
                for t in range(NTT):
                    ts_ = slice(t * 512, (t + 1) * 512)
                    pb = (n * NTT + t) % 4
                    par = (n * NTT + t) % 2
                    for k in range(16):
                        S.op("pe", lambda e, k=k, pb=pb, wos=wos, ns=ns, ts_=ts_: e.matmul(PS[pb][:, :], lhsT=wos[:, k, ns], rhs=mb[:, k, ts_], start=(k == 0), stop=(k == 15)),
                             reads=[wk] + [("B", k, t) for k in range(16)], writes=[("P", pb)])
                    for f_ in pend:
                        f_()
                    pend.clear()
                    sq = SQT[:, par, :]
                    S.op("act", lambda e, pb=pb, n=n, ts_=ts_: e.activation(out=ZS[:, n, ts_], in_=PS[pb][:, :], func=AF.Copy), reads=[("P", pb)], writes=[("ZS", n, t)])
                    S.op("act", lambda e, pb=pb, sq=sq: e.activation(out=sq, in_=PS[pb][:, :], func=AF.Square), reads=[("P", pb)], writes=[("SQT", par)])
                    pend.append(lambda sq=sq, t=t, n=n, par=par: S.op("pe", lambda e: e.matmul(PS[6 + t][:, :], lhsT=ONES[:, :], rhs=sq, start=(n == 0), stop=(n == 15)),
                                                                       reads=[("SQT", par), "ONES"], writes=[("P", 6 + t)]))
        for f_ in pend:
            f_()
        pend.clear()
        for t in range(NTT):
            rstd_from(PS[6 + t][:, :], R[:, 1, t * 512:(t + 1) * 512], ("P", 6 + t), ("R1", t))

        for t in range(NTT):
            ts_ = slice(t * 512, (t + 1) * 512)
            for n in range(16):
                sl4 = n % 8
                par = n % 2
                ta = T[:, sl4, :]
                ka = ("T", sl4)
                S.op("sp", lambda e, ta=ta, n=n, ts_=ts_: e.dma_start(out=ta, in_=xTv[:, n, ts_]), writes=[ka], dma="ra%d" % sl4)
                S.op("dve", lambda e, n=n, ts_=ts_: e.scalar_tensor_tensor(out=ZS[:, n, ts_], in0=ZS[:, n, ts_], scalar=NV[:, 16 + n:17 + n], in1=R[:, 1, ts_],
                                                                            op0=ALU.mult, op1=ALU.mult), reads=[("ZS", n, t), "NV", ("R1", t)], writes=[("ZS", n, t)])
                S.op("dve", lambda e, ta=ta, n=n, ts_=ts_: e.tensor_tensor(out=ta, in0=ta, in1=ZS[:, n, ts_], op=ALU.add), reads=[ka, ("ZS", n, t)], writes=[ka])
                S.op("pool", lambda e, ta=ta, n=n, ts_=ts_: e.dma_start(out=x1dv[:, n, ts_], in_=ta), reads=[ka], writes=[("x1d", n, t)], dma="x1st%d" % sl4)
                sq = SQT[:, 2 + par, :]
                sk = ("SQT", 2 + par)
                S.op("dve", lambda e, ta=ta, n=n, ts_=ts_: e.tensor_scalar(out=x1b[:, n, ts_], in0=ta, scalar1=NV[:, 32 + n:33 + n], scalar2=None, op0=ALU.mult),
                     reads=[ka, "NV"], writes=[("B", n, t)])
                S.op("act", lambda e, ta=ta, sq=sq: e.activation(out=sq, in_=ta, func=AF.Square), reads=[ka], writes=[sk])
                S.op("pe", lambda e, sq=sq, n=n, t=t: e.matmul(PS[4 + t][:, :], lhsT=ONES[:, :], rhs=sq, start=(n == 0), stop=(n == 15)),
                     reads=[sk, "ONES"], writes=[("P", 4 + t)])
        for t in range(NTT):
            rstd_from(PS[4 + t][:, :], R[:, 0, t * 512:(t + 1) * 512], ("P", 4 + t), ("R2", t))

        S.barrier()
        xk = {t: [("B", k, t) for k in range(16)] for t in range(NTT)}
        for g in range(22):
            s_ = g % 2
            base = s_ * 8192
            wfg = _v3(W[:, base:base + 4096], 16)
            wfu = _v3(W[:, base + 4096:base + 8192], 16)
            wk = ("W", s_)
            gs = slice(g * 256, (g + 1) * 256)
            gs2 = slice(D_FF + g * 256, D_FF + (g + 1) * 256)
            S.op("pool", lambda e, wfg=wfg, gs=gs: e.dma_start(out=wfg[:, :, :], in_=wfiv[:, :, gs]), writes=[wk + (0,)], dma="w%da" % s_)
            S.op("pool", lambda e, wfu=wfu, gs2=gs2: e.dma_start(out=wfu[:, :, :], in_=wfiv[:, :, gs2]), writes=[wk + (1,)], dma="w%db" % s_)
            for nb in range(2):
                f = g * 2 + nb
                ns = slice(nb * 128, (nb + 1) * 128)
                for t in range(NTT):
                    ts_ = slice(t * 512, (t + 1) * 512)
                    par = (f * NTT + t) % 2
                    pg_, pu_ = PS[2 * par], PS[2 * par + 1]
                    kg_, ku_ = ("P", 2 * par), ("P", 2 * par + 1)
                    t0, t1 = T[:, 4 + 2 * par, :], T[:, 5 + 2 * par, :]
                    k0, k1 = ("T", 4 + 2 * par), ("T", 5 + 2 * par)
                    for k in range(16):
                        S.op("pe", lambda e, k=k, pg_=pg_, wfg=wfg, ns=ns, ts_=ts_: e.matmul(pg_[:, :], lhsT=wfg[:, k, ns], rhs=x1b[:, k, ts_], start=(k == 0), stop=(k == 15)),
                             reads=[wk + (0,)] + xk[t], writes=[kg_])
                    for k in range(16):
                        S.op("pe", lambda e, k=k, pu_=pu_, wfu=wfu, ns=ns, ts_=ts_: e.matmul(pu_[:, :], lhsT=wfu[:, k, ns], rhs=x1b[:, k, ts_], start=(k == 0), stop=(k == 15)),
                             reads=[wk + (1,)] + xk[t], writes=[ku_])
                    S.op("dve", lambda e, t0=t0, pg_=pg_, ts_=ts_: e.tensor_tensor(out=t0, in0=pg_[:, :], in1=R[:, 0, ts_], op=ALU.mult),
                         reads=[kg_, ("R2", t)], writes=[k0])
                    S.op("act", lambda e, t0=t0: e.activation(out=t0, in_=t0, func=AF.Silu), reads=[k0], writes=[k0])
                    S.op("dve", lambda e, t1=t1, pu_=pu_, ts_=ts_: e.tensor_tensor(out=t1, in0=pu_[:, :], in1=R[:, 0, ts_], op=ALU.mult),
                         reads=[ku_, ("R2", t)], writes=[k1])
                    S.op("dve", lambda e, t0=t0, t1=t1, f=f, ts_=ts_: e.tensor_tensor(out=hid[:, f, ts_], in0=t0, in1=t1, op=ALU.mult),
                         reads=[k0, k1], writes=[("hid", f, t)])

        S.barrier()
        for g in range(8):
            s_ = g % 2
            wd = _v3((W if s_ == 0 else B)[:, 0:11264], 44)
            wk = ("W", s_)
            gs = slice(g * 256, (g + 1) * 256)
            for hh in range(2):
                S.op("pool", lambda e, wd=wd, gs=gs, hh=hh: e.dma_start(out=wd[:, 22 * hh:22 * hh + 22, :], in_=wfdv[:, 22 * hh:22 * hh + 22, gs]),
                     writes=[wk + (hh,)], dma="w%d%s" % (s_, "ab"[hh]))
            for nb in range(2):
                n = g * 2 + nb
                ns = slice(nb * 128, (nb + 1) * 128)
                for t in range(NTT):
                    ts_ = slice(t * 512, (t + 1) * 512)
                    pb = (n * NTT + t) % 4
                    for k in range(44):
                        S.op("pe", lambda e, k=k, pb=pb, wd=wd, ns=ns, ts_=ts_: e.matmul(PS[pb][:, :], lhsT=wd[:, k, ns], rhs=hid[:, k, ts_], start=(k == 0), stop=(k == 43)),
                             reads=[wk + (0,), wk + (1,)] + [("hid", f, t) for f in range(44)], writes=[("P", pb)])
                    for f_ in pend:
                        f_()
                    pend.clear()
                    pend.append(evac_stats(PS[pb], ("P", pb), n, t, 6 + t, n == 0, n == 15, zdv, (n * NTT + t) % 2))
        for f_ in pend:
            f_()
        pend.clear()
        for t in range(NTT):
            rstd_from(PS[6 + t][:, :], R[:, 1, t * 512:(t + 1) * 512], ("P", 6 + t), ("R3", t))
        S.barrier()
        for t in range(NTT):
            ts_ = slice(t * 512, (t + 1) * 512)
            for n in range(16):
                sl4 = n % 8
                ta, tb = T[:, sl4, :], T[:, 8 + sl4, :]
                ka, kb = ("T", sl4), ("T", 8 + sl4)
                S.op("sp", lambda e, ta=ta, n=n, ts_=ts_: e.dma_start(out=ta, in_=x1dv[:, n, ts_]), reads=[("x1d", n, t)], writes=[ka], dma="ra%d" % sl4)
                S.op("act", lambda e, tb=tb, n=n, ts_=ts_: e.dma_start(out=tb, in_=zdv[:, n, ts_]), reads=[("zd", n, t)], writes=[kb], dma="rb%d" % sl4)
                S.op("dve", lambda e, tb=tb, n=n, ts_=ts_: e.scalar_tensor_tensor(out=tb, in0=tb, scalar=NV[:, 48 + n:49 + n], in1=R[:, 1, ts_],
                                                                                   op0=ALU.mult, op1=ALU.mult), reads=[kb, "NV", ("R3", t)], writes=[kb])
                S.op("dve", lambda e, ta=ta, tb=tb: e.tensor_tensor(out=ta, in0=ta, in1=tb, op=ALU.add), reads=[ka, kb], writes=[ka])
                S.op("pool", lambda e, ta=ta, n=n, ts_=ts_: e.dma_start(out=outTv[:, n, ts_], in_=ta), reads=[ka], writes=[("out", n, t)], dma="ost%d" % sl4)

        S.barrier()
        S.emit(nc, block, st)
    return nc


def _colvec(v):
    return np.ascontiguousarray(np.asarray(v, np.float32).reshape(16, 128).T)


def prep_l2(inp, oA, oB):
    x = np.asarray(inp["x"], np.float32)[0]
    w_in = np.asarray(inp["w_in"], np.float32)[0]
    wg = np.ascontiguousarray(w_in[:, 7176:7176 + 4096])
    nv = np.ascontiguousarray(np.concatenate([_colvec(inp["norm_mix_pre"][0]), _colvec(inp["norm_mix_post"][0]),
                                              _colvec(inp["norm_ffn_pre"][0]), _colvec(inp["norm_ffn_post"][0])], axis=1))
    ones = np.ones((128, 128), ml_dtypes.bfloat16)
    common = dict(wg=wg, wua=np.ascontiguousarray(inp["w_up_a"][0], dtype=np.float32), wub=np.ascontiguousarray(inp["w_up_b"][0], dtype=np.float32),
                  wo=np.ascontiguousarray(inp["w_o"][0], dtype=np.float32), wfi=np.ascontiguousarray(inp["w_ffn_in"][0], dtype=np.float32),
                  wfd=np.ascontiguousarray(inp["w_ffn_down"][0], dtype=np.float32), nv=nv, cst=ones)
    maps = []
    for c in range(NCORES):
        sl = slice(c * 1024, (c + 1) * 1024)
        oT = np.ascontiguousarray(np.concatenate([oA[sl].T, oB[sl].T], axis=0), dtype=np.float32)
        maps.append(dict(common, xT=np.ascontiguousarray(x[sl].T), oT=oT))
    return maps


def post_l2(results):
    return np.concatenate([np.ascontiguousarray(r["outT"].T) for r in results], axis=0)[None]


def build_l1(NTILES=16, stage=9):
    nc = bass.Bass("TRN2", target_bir_lowering=False)
    S = Sched()
    NTOK = NTILES * 512
    dt = nc.dram_tensor
    xT = dt("xT", [2048, NTOK], F32, kind="ExternalInput").ap()
    w1 = dt("w1", [2048, 896], F32, kind="ExternalInput").ap()
    wf = dt("wf", [2048, 2], F32, kind="ExternalInput").ap()
    nv1 = dt("nv1", [128, 16], F32, kind="ExternalInput").ap()
    lbl = dt("lbl", [128, 2], F32, kind="ExternalInput").ap()
    hnw = dt("hnw", [128, 128], F32, kind="ExternalInput").ap()
    bfx = dt("bfx", [128, 1], F32, kind="ExternalInput").ap()
    cb = dt("cb", [128, 640], BF16, kind="ExternalInput").ap()
    cf = dt("cf", [128, 1024], F32, kind="ExternalInput").ap()
    oA = dt("oA", [NTOK, 128], F32, kind="ExternalOutput").ap()
    oBT = dt("oBT", [128, NTOK], F32, kind="ExternalOutput").ap()
    xTv = xT.rearrange("(k p) n -> p k n", p=128)
    w1v = w1.rearrange("(k p) n -> p k n", p=128)
    wfv = wf.rearrange("(k p) n -> p k n", p=128)
    oAv = oA.rearrange("(b p) v -> p b v", p=128)

    with contextlib.ExitStack() as st:
        sb = lambda name, shape, d: st.enter_context(nc.sbuf_tensor(name, shape, d))
        XS = sb("XS", [128, 2, 8, 512], F32)
        XB = sb("XB", [128, 16, 512], BF16)
        SQ = sb("SQ", [128, 16, 512], BF16)
        W1 = sb("W1", [128, 16, 896], BF16)
        WF = sb("WF", [128, 16, 2], BF16)
        NV = sb("NV", [128, 16], F32)
        LBL = sb("LBL", [128, 2], F32)
        LB = sb("LB", [128, 2], F32)
        HNW = sb("HNW", [128, 128], F32)
        BFX = sb("BFX", [128, 1], F32)
        CB = sb("CB", [128, 640], BF16)
        CF = sb("CF", [128, 1024], F32)
        RT = sb("RT", [128, 512], F32)
        RC = sb("RC", [128, 4], F32)
        KT = sb("KT", [128, NTOK], BF16)
        V = sb("V", [128, NTILES * 4, 128], BF16)
        QT = sb("QT", [128, 2, 512], BF16)
        TQ = sb("TQ", [128, 512], F32)
        TZ = sb("TZ", [128, 512], F32)
        TE = sb("TE", [128, 512], F32)
        TS = sb("TS", [128, 512], F32)
        TF = sb("TF", [128, 512], F32)
        TK = sb("TK", [128, 512], F32)
        TB = sb("TB", [128, 512], F32)
        TD = sb("TD", [128, 512], F32)
        TE1 = sb("TE1", [128, 512], F32)
        TE2 = sb("TE2", [128, 512], F32)
        QTB = sb("QTB", [128, 512], BF16)
        KTB = sb("KTB", [128, 512], BF16)
        KTOK = sb("KTOK", [128, 4, 128], BF16)
        VTOK = sb("VTOK", [128, 2, 4, 128], BF16)
        GTOK = sb("GTOK", [128, 2, 4, 128], F32)
        EBM = sb("EBM", [128, 8], F32)
        EBL = sb("EBL", [128, 8], F32)
        EBLM = sb("EBLM", [128, 8], F32)
        BDM = sb("BDM", [128, 8], F32)
        ST = sb("ST", [128, 128], F32)
        SM = sb("SM", [128, 2, 128], BF16)
        SCB = sb("SCB", [128, 128], BF16)
        TKV = sb("TKV", [128, 128], F32)
        T1 = sb("T1", [128, 128], F32)
        TSC = sb("TSC", [128, 128], F32)
        EG = sb("EG", [128, 128], F32)
        GG = sb("GG", [128, 128], F32)
        OAT = sb("OAT", [128, 2, 128], F32)
        SSO = sb("SSO", [128, 1], F32)
        RO = sb("RO", [128, 1], F32)
        PT = sb("PT", [128, 2, 512], BF16)
        CROW = sb("CROW", [128, 2, 512], BF16)
        LROW = sb("LROW", [1, 512], F32)
        ZF = sb("ZF", [1, 512], F32)
        POSC = sb("POSC", [128, NTILES * 4], F32)
        RP = sb("RP", [128, NTILES + 1], F32)
        BIAS = sb("BIAS", [128, 2, NTILES * 4], F32)
        ODEN = sb("ODEN", [128, 512], F32)
        OOUT = sb("OOUT", [128, 512], F32)
        PS = [st.enter_context(nc.psum_tensor("ps%d" % i, [128, 512], F32)) for i in range(8)]
        block = st.enter_context(nc.Block())

        ONESB, IDENT, MASKNEG, BDMASK, E0 = CB[:, 0:128], CB[:, 128:256], CB[:, 256:384], CB[:, 384:512], CB[:, 512:640]
        RESET, ONESF = CF[:, 0:512], CF[:, 512:1024]

        def ld(eng, dst, src, key, dk):
            S.op(eng, lambda e: e.dma_start(out=dst, in_=src), writes=[key], dma=dk)

        ld("sp", NV[:, :], nv1[:, :], "NV", "c0")
        ld("sp", LBL[:, :], lbl[:, :], "LBL", "c1")
        ld("sp", HNW[:, :], hnw[:, :], "HNW", "c2")
        ld("sp", BFX[:, :], bfx[:, :], "BFX", "c3")
        ld("sp", CB[:, :], cb[:, :], "CB", "c4")
        ld("sp", CF[:, :], cf[:, :], "CF", "c5")
        ld("pool", W1[:, 0:8, :], w1v[:, 0:8, :], ("W1", 0), "w1a")
        ld("pool", W1[:, 8:16, :], w1v[:, 8:16, :], ("W1", 1), "w1b")
        ld("pool", WF[:, :, :], wfv[:, :, :], "WF", "wf")
        W1K = [("W1", 0), ("W1", 1)]

        S.op("dve", lambda e: e.tensor_tensor(out=LB[:, 0:1], in0=LBL[:, 0:1], in1=LBL[:, 1:2], op=ALU.subtract), reads=["LBL"], writes=["LB"])
        S.op("act", lambda e: e.activation(out=LB[:, 1:2], in_=LB[:, 0:1], func=AF.Exp, scale=-1.0), reads=["LB"], writes=["LB"])
        S.op("dve", lambda e: e.tensor_scalar(out=LB[:, 0:1], in0=LB[:, 1:2], scalar1=1.0, scalar2=None, op0=ALU.add), reads=["LB"], writes=["LB"])
        S.op("dve", lambda e: e.reciprocal(out=LB[:, 0:1], in_=LB[:, 0:1]), reads=["LB"], writes=["LB"])
        S.op("dve", lambda e: e.tensor_tensor(out=LB[:, 1:2], in0=LB[:, 1:2], in1=LB[:, 0:1], op=ALU.mult), reads=["LB"], writes=["LB"])
        S.op("dve", lambda e: e.memset(ST[:, :], 0.0), writes=["ST"])
        S.op("dve", lambda e: e.memset(RP[:, :], 0.0), writes=["RP"])
        S.op("dve", lambda e: e.memset(CROW[:, :, :], 0.0), writes=[("CROW", 0), ("CROW", 1)])

        def rstd(ps_ap, r_ap, pkey, rkey, n):
            S.op("dve", lambda e: e.tensor_scalar(out=r_ap, in0=ps_ap, scalar1=1.0 / n, scalar2=EPS, op0=ALU.mult, op1=ALU.add),
                 reads=[pkey], writes=[rkey])
            S.op("act", lambda e: e.activation(out=r_ap, in_=r_ap, func=AF.Ln), reads=[rkey], writes=[rkey])
            S.op("act", lambda e: e.activation(out=r_ap, in_=r_ap, func=AF.Exp, scale=-0.5), reads=[rkey], writes=[rkey])

        def gen_A(T):
            par = T % 2
            tsl = slice(T * 512, (T + 1) * 512)
            for h in range(2):
                S.op("sp", lambda e, h=h: e.dma_start(out=XS[:, h, :, :], in_=xTv[:, 8 * h:8 * h + 8, tsl]), writes=[("XS", h)], dma="xs%d" % h)
            yield
            for h in range(2):
                for k in range(8):
                    S.op("act", lambda e, h=h, k=k: e.activation(out=SQ[:, 8 * h + k, :], in_=XS[:, h, k, :], func=AF.Square),
                         reads=[("XS", h)], writes=[("SQ", h, k)])
                    if k % 2 == 0:
                        S.op("dve", lambda e, h=h, k=k: e.tensor_scalar(out=XB[:, 8 * h + k, :], in0=XS[:, h, k, :], scalar1=NV[:, 8 * h + k:8 * h + k + 1],
                                                                         scalar2=None, op0=ALU.mult), reads=[("XS", h), "NV"], writes=[("XB", h, k)])
                    else:
                        S.op("act", lambda e, h=h, k=k: e.activation(out=XB[:, 8 * h + k, :], in_=XS[:, h, k, :], func=AF.Copy, scale=NV[:, 8 * h + k:8 * h + k + 1]),
                             reads=[("XS", h), "NV"], writes=[("XB", h, k)])
                        yield
            SQK = [("SQ", h, k) for h in range(2) for k in range(8)]
            XBK = [("XB", h, k) for h in range(2) for k in range(8)]
            for k in range(16):
                S.op("pe", lambda e, k=k: e.matmul(PS[0][:, :], lhsT=ONESB, rhs=SQ[:, k, :], start=(k == 0), stop=(k == 15)),
                     reads=SQK + ["CB"], writes=[("P", 0)])
            rstd(PS[0][:, :], RT[:, :], ("P", 0), "RT", D_MODEL)
            yield
            for b in range(4):
                for k in range(16):
                    S.op("pe", lambda e, k=k, b=b: e.matmul(PS[3][:, 400 + b:401 + b], lhsT=SQ[:, k, b * 128:(b + 1) * 128], rhs=ONESB[:, 0:1],
                                                            start=(k == 0), stop=(k == 15)), reads=SQK + ["CB"], writes=[("P", 3)])
            rstd(PS[3][:, 400:404], RC[:, :], ("P", 3), "RC", D_MODEL)
            yield
            for j in range(4):
                pa = 0 if j % 2 == 0 else 3
                for k in range(16):
                    S.op("pe", lambda e, k=k, j=j, pa=pa: e.matmul(PS[pa][:, :], lhsT=W1[:, k, j * 128:(j + 1) * 128], rhs=XB[:, k, :],
                                                            start=(k == 0), stop=(k == 15)), reads=W1K + XBK, writes=[("P", pa)])
                if j == 0:
                    S.op("dve", lambda e, pa=pa: e.tensor_tensor(out=TQ[:, :], in0=PS[pa][:, :], in1=RT[:, :], op=ALU.mult), reads=[("P", pa), "RT"], writes=["TQ"])
                elif j == 1:
                    S.op("dve", lambda e, pa=pa: e.tensor_tensor(out=TZ[:, :], in0=PS[pa][:, :], in1=RT[:, :], op=ALU.mult), reads=[("P", pa), "RT"], writes=["TZ"])
                elif j == 2:
                    S.op("dve", lambda e, pa=pa: e.scalar_tensor_tensor(out=QT[:, par, :], in0=PS[pa][:, :], scalar=128.0 ** -0.5, in1=RT[:, :], op0=ALU.mult, op1=ALU.mult),
                         reads=[("P", pa), "RT"], writes=[("QT", par)])
                else:
                    S.op("dve", lambda e, pa=pa: e.tensor_tensor(out=KT[:, tsl], in0=PS[pa][:, :], in1=RT[:, :], op=ALU.mult), reads=[("P", pa), "RT"], writes=[("KT", T)])
                yield
            for b in range(4):
                pa = 0 if b % 2 == 0 else 3
                for k in range(16):
                    S.op("pe", lambda e, k=k, b=b, pa=pa: e.matmul(PS[pa][:, 0:384], lhsT=XB[:, k, b * 128:(b + 1) * 128], rhs=W1[:, k, 512:896],
                                                            start=(k == 0), stop=(k == 15)), reads=W1K + XBK, writes=[("P", pa)])
                S.op("act", lambda e, b=b, pa=pa: e.activation(out=VTOK[:, par, b, :], in_=PS[pa][:, 0:128], func=AF.Copy, scale=RC[:, b:b + 1]),
                     reads=[("P", pa), "RC"], writes=[("VTOK", par, b)])
                S.op("act", lambda e, b=b, pa=pa: e.activation(out=GTOK[:, par, b, :], in_=PS[pa][:, 128:256], func=AF.Copy, scale=RC[:, b:b + 1]),
                     reads=[("P", pa), "RC"], writes=[("GTOK", par, b)])
                S.op("act", lambda e, b=b, pa=pa: e.activation(out=V[:, 4 * T + b, :], in_=PS[pa][:, 256:384], func=AF.Copy, scale=RC[:, b:b + 1]),
                     reads=[("P", pa), "RC"], writes=[("V", 4 * T + b)])
                yield
            for k in range(16):
                S.op("pe", lambda e, k=k: e.matmul(PS[0][0:1, :], lhsT=WF[:, k, 0:1], rhs=XB[:, k, :], start=(k == 0), stop=(k == 15)),
                     reads=["WF"] + XBK, writes=[("P", 0)])
            S.op("dve", lambda e: e.tensor_tensor(out=ZF[0:1, :], in0=PS[0][0:1, :], in1=RT[0:1, :], op=ALU.mult), reads=[("P", 0), "RT"], writes=["ZF"])
            S.op("dve", lambda e: e.tensor_scalar(out=ZF[0:1, :], in0=ZF[0:1, :], scalar1=BFX[0:1, 0:1], scalar2=None, op0=ALU.add), reads=["ZF", "BFX"], writes=["ZF"])
            S.op("act", lambda e: e.activation(out=ZF[0:1, :], in_=ZF[0:1, :], func=AF.Exp, scale=-1.0), reads=["ZF"], writes=["ZF"])
            S.op("act", lambda e: e.activation(out=ZF[0:1, :], in_=ZF[0:1, :], func=AF.Ln, bias=1.0), reads=["ZF"], writes=["ZF"])
            S.op("dve", lambda e: e.tensor_tensor_scan(out=LROW[0:1, :], data0=ONESF[0:1, :], data1=ZF[0:1, :], initial=0.0, op0=ALU.mult, op1=ALU.add),
                 reads=["ZF", "CF"], writes=["LROW"])
            S.op("dve", lambda e: e.tensor_scalar(out=CROW[0:1, par, :], in0=LROW[0:1, :], scalar1=-1.0, scalar2=None, op0=ALU.mult), reads=["LROW"], writes=[("CROW", par)])
            yield
            for b in range(4):
                S.op("pe", lambda e, b=b: e.matmul(PS[3][:, 384 + b:385 + b], lhsT=LROW[0:1, b * 128:(b + 1) * 128], rhs=ONESF[0:1, 0:1], start=True, stop=True),
                     reads=["LROW", "CF"], writes=[("P", 3)])
            S.op("pe", lambda e: e.matmul(PS[3][:, 388:389], lhsT=ONESF[0:1, 0:128], rhs=LROW[0:1, 511:512], start=True, stop=True),
                 reads=["LROW", "CF"], writes=[("P", 3)])
            S.op("dve", lambda e: e.tensor_scalar(out=POSC[:, 4 * T:4 * T + 4], in0=PS[3][:, 384:388], scalar1=RP[:, T:T + 1], scalar2=None, op0=ALU.add),
                 reads=[("P", 3), "RP"], writes=["POSC"])
            S.op("dve", lambda e: e.tensor_tensor(out=RP[:, T + 1:T + 2], in0=PS[3][:, 388:389], in1=RP[:, T:T + 1], op=ALU.add),
                 reads=[("P", 3), "RP"], writes=["RP"])
            S.op("dve", lambda e: e.tensor_scalar(out=BIAS[:, par, 0:4 * T + 4], in0=POSC[:, 0:4 * T + 4], scalar1=RP[:, T:T + 1], scalar2=None, op0=ALU.subtract),
                 reads=["POSC", "RP"], writes=[("BIAS", par)])
            yield

        def gen_H(T):
            par = T % 2
            S.op("act", lambda e: e.activation(out=TE[:, :], in_=TZ[:, :], func=AF.Exp, scale=-1.0), reads=["TZ"], writes=["TE"])
            S.op("dve", lambda e: e.tensor_scalar(out=TS[:, :], in0=TE[:, :], scalar1=1.0, scalar2=None, op0=ALU.add), reads=["TE"], writes=["TS"])
            S.op("dve", lambda e: e.reciprocal(out=TS[:, :], in_=TS[:, :]), reads=["TS"], writes=["TS"])
            S.op("dve", lambda e: e.tensor_scalar(out=TF[:, :], in0=TS[:, :], scalar1=LB[:, 1:2], scalar2=LB[:, 0:1], op0=ALU.mult, op1=ALU.add),
                 reads=["TS", "LB"], writes=["TF"])
            yield
            S.op("act", lambda e: e.activation(out=TF[:, :], in_=TF[:, :], func=AF.Ln), reads=["TF"], writes=["TF"])
            S.op("dve", lambda e: e.scalar_tensor_tensor(out=TK[:, :], in0=TE[:, :], scalar=LB[:, 1:2], in1=TS[:, :], op0=ALU.mult, op1=ALU.mult),
                 reads=["TE", "TS", "LB"], writes=["TK"])
            S.op("dve", lambda e: e.tensor_tensor_scan(out=TB[:, :], data0=RESET, data1=TF[:, :], initial=0.0, op0=ALU.mult, op1=ALU.add),
                 reads=["TF", "CF"], writes=["TB"])
            yield
            TBv = TB[:, :].rearrange("p (c n) -> p c n", n=64)
            TDv = TD[:, :].rearrange("p (c n) -> p c n", n=64)
            S.op("dve", lambda e: e.tensor_tensor(out=TDv, in0=TBv, in1=TBv[:, :, 31:32].to_broadcast([128, 8, 64]), op=ALU.subtract), reads=["TB"], writes=["TD"])
            S.op("act", lambda e: e.activation(out=TE1[:, :], in_=TD[:, :], func=AF.Exp), reads=["TD"], writes=["TE1"])
            S.op("act", lambda e: e.activation(out=TE2[:, :], in_=TD[:, :], func=AF.Exp, scale=-1.0), reads=["TD"], writes=["TE2"])
            yield
            S.op("dve", lambda e: e.tensor_tensor(out=QTB[:, :], in0=TQ[:, :], in1=TE1[:, :], op=ALU.mult), reads=["TQ", "TE1"], writes=["QTB"])
            S.op("dve", lambda e: e.tensor_tensor(out=KTB[:, :], in0=TK[:, :], in1=TE2[:, :], op=ALU.mult), reads=["TK", "TE2"], writes=["KTB"])
            S.op("act", lambda e: e.activation(out=EBM[:, :], in_=TBv[:, :, 31], func=AF.Exp), reads=["TB"], writes=["EBM"])
            S.op("act", lambda e: e.activation(out=EBL[:, :], in_=TBv[:, :, 63], func=AF.Exp), reads=["TB"], writes=["EBL"])
            S.op("dve", lambda e: e.tensor_tensor(out=BDM[:, :], in0=TBv[:, :, 63], in1=TBv[:, :, 31], op=ALU.subtract), reads=["TB"], writes=["BDM"])
            S.op("act", lambda e: e.activation(out=EBLM[:, :], in_=BDM[:, :], func=AF.Exp), reads=["BDM"], writes=["EBLM"])
            ESC = ["EBM", "EBL", "EBLM"]
            yield
            for b in range(4):
                bs = slice(b * 128, (b + 1) * 128)
                trp = PS[1][:, 448:512].bitcast(BF16)
                S.op("pe", lambda e, bs=bs, trp=trp: e.transpose(out=trp, in_=KTB[:, bs], identity=IDENT), reads=["KTB", "CB"], writes=[("P", 1)])
                S.op("act", lambda e, b=b, trp=trp: e.activation(out=KTOK[:, b, :], in_=trp, func=AF.Copy), reads=[("P", 1)], writes=[("KTOK", b)])
                yield
                S.op("pe", lambda e, bs=bs: e.matmul(PS[1][:, 0:128], lhsT=KTB[:, bs], rhs=QTB[:, bs], start=True, stop=True),
                     reads=["KTB", "QTB"], writes=[("P", 1)])
                S.op("act", lambda e: e.activation(out=TSC[:, :], in_=PS[1][:, 0:128], func=AF.Copy), reads=[("P", 1)], writes=["TSC"])
                S.op("dve", lambda e: e.tensor_tensor(out=SCB[:, :], in0=TSC[:, :], in1=BDMASK, op=ALU.mult), reads=["TSC", "CB"], writes=["SCB"])
                yield
                for c2 in range(2):
                    rs = slice(c2 * 64, (c2 + 1) * 64)
                    S.op("pe", lambda e, b=b, c2=c2, rs=rs: e.matmul(PS[1 + c2][:, 128:256], lhsT=KTOK[rs, b, :], rhs=VTOK[rs, par, b, :], start=True, stop=True),
                         reads=[("KTOK", b), ("VTOK", par, b)], writes=[("P", 1 + c2)])
                yield
                for c2 in range(2):
                    c = 2 * b + c2
                    S.op("dve", lambda e, c=c, c2=c2: e.tensor_scalar(out=SM[:, c2, :], in0=ST[:, :], scalar1=EBM[:, c:c + 1], scalar2=None, op0=ALU.mult),
                         reads=["ST"] + ESC, writes=[("SM", c2)])
                    S.op("dve", lambda e, c=c, c2=c2: e.tensor_scalar(out=TKV[:, :], in0=PS[1 + c2][:, 128:256], scalar1=EBLM[:, c:c + 1], scalar2=None, op0=ALU.mult),
                         reads=[("P", 1 + c2)] + ESC, writes=["TKV"])
                    S.op("dve", lambda e, c=c: e.scalar_tensor_tensor(out=ST[:, :], in0=ST[:, :], scalar=EBL[:, c:c + 1], in1=TKV[:, :], op0=ALU.mult, op1=ALU.add),
                         reads=["ST", "TKV"] + ESC, writes=["ST"])
                yield
                S.op("pe", lambda e, b=b: e.matmul(PS[1][:, 0:128], lhsT=SCB[:, :], rhs=VTOK[:, par, b, :], start=True, stop=False),
                     reads=["SCB", ("VTOK", par, b)], writes=[("P", 1)])
                for c2 in range(2):
                    S.op("pe", lambda e, b=b, c2=c2: e.matmul(PS[1][c2 * 64:(c2 + 1) * 64, 0:128], lhsT=QTB[:, b * 128 + c2 * 64:b * 128 + (c2 + 1) * 64], rhs=SM[:, c2, :],
                                                              start=False, stop=True), reads=["QTB", ("SM", c2)], writes=[("P", 1)])
                ob_ = (4 * T + b) % 2
                S.op("act", lambda e: e.activation(out=T1[:, :], in_=PS[1][:, 0:128], func=AF.Square, accum_out=SSO[:, 0:1]), reads=[("P", 1)], writes=["T1", "SSO"])
                rstd(SSO[:, 0:1], RO[:, 0:1], "SSO", "RO", 128)
                S.op("dve", lambda e: e.scalar_tensor_tensor(out=T1[:, :], in0=PS[1][:, 0:128], scalar=RO[:, 0:1], in1=HNW[:, :], op0=ALU.mult, op1=ALU.mult),
                     reads=[("P", 1), "RO", "HNW"], writes=["T1"])
                yield
                S.op("act", lambda e, b=b: e.activation(out=EG[:, :], in_=GTOK[:, par, b, :], func=AF.Exp, scale=-1.0), reads=[("GTOK", par, b)], writes=["EG"])
                S.op("dve", lambda e: e.tensor_scalar(out=EG[:, :], in0=EG[:, :], scalar1=1.0, scalar2=None, op0=ALU.add), reads=["EG"], writes=["EG"])
                S.op("dve", lambda e: e.reciprocal(out=EG[:, :], in_=EG[:, :]), reads=["EG"], writes=["EG"])
                S.op("dve", lambda e, b=b: e.tensor_tensor(out=GG[:, :], in0=GTOK[:, par, b, :], in1=EG[:, :], op=ALU.mult), reads=["EG", ("GTOK", par, b)], writes=["GG"])
                S.op("dve", lambda e, ob_=ob_: e.tensor_tensor(out=OAT[:, ob_, :], in0=T1[:, :], in1=GG[:, :], op=ALU.mult), reads=["T1", "GG"], writes=[("OAT", ob_)])
                S.op("sp", lambda e, ob_=ob_, b=b: e.dma_start(out=oAv[:, 4 * T + b, :], in_=OAT[:, ob_, :]), reads=[("OAT", ob_)], dma="oa%d" % ob_)
                yield

        def gen_F(T):
            par = T % 2
            tsl = slice(T * 512, (T + 1) * 512)
            nkb = 4 * T + 4

            def s_stage(j):
                lo = 0 if j < 4 * T else 128 * (j - 4 * T)
                sp_ = j % 2
                sps = PS[4 + sp_]
                S.op("pe", lambda e: e.matmul(sps[:, lo:512], lhsT=KT[:, j * 128:(j + 1) * 128], rhs=QT[:, par, lo:512], start=True, stop=False),
                     reads=[("KT", j // 4), ("QT", par)], writes=[("P", 4 + sp_)])
                diag = j >= 4 * T
                S.op("pe", lambda e: e.matmul(sps[:, lo:512], lhsT=E0, rhs=CROW[:, par, lo:512], start=False, stop=(not diag)),
                     reads=[("CROW", par), "CB"], writes=[("P", 4 + sp_)])
                if diag:
                    S.op("pe", lambda e: e.matmul(sps[:, lo:lo + 128], lhsT=IDENT, rhs=MASKNEG, start=False, stop=True),
                         reads=["CB"], writes=[("P", 4 + sp_)])
                S.op("act", lambda e: e.activation(out=PT[:, sp_, lo:512], in_=sps[:, lo:512], func=AF.Exp, bias=BIAS[:, par, j:j + 1]),
                     reads=[("P", 4 + sp_), ("BIAS", par)], writes=[("PT", sp_)])

            def pv_stage(j):
                lo = 0 if j < 4 * T else 128 * (j - 4 * T)
                sp_ = j % 2
                S.op("pe", lambda e: e.matmul(PS[6][:, lo:512], lhsT=V[:, j, :], rhs=PT[:, sp_, lo:512], start=(j == 0), stop=(j == nkb - 1)),
                     reads=[("V", j), ("PT", sp_)], writes=[("P", 6)])
                S.op("pe", lambda e: e.matmul(PS[7][:, lo:512], lhsT=ONESB, rhs=PT[:, sp_, lo:512], start=(j == 0), stop=(j == nkb - 1)),
                     reads=["CB", ("PT", sp_)], writes=[("P", 7)])

            s_stage(0)
            yield
            for j in range(nkb):
                if j + 1 < nkb:
                    s_stage(j + 1)
                pv_stage(j)
                yield
            S.op("dve", lambda e: e.reciprocal(out=ODEN[:, :], in_=PS[7][:, :]), reads=[("P", 7)], writes=["ODEN"])
            S.op("dve", lambda e: e.tensor_tensor(out=OOUT[:, :], in0=PS[6][:, :], in1=ODEN[:, :], op=ALU.mult), reads=[("P", 6), "ODEN"], writes=["OOUT"])
            S.op("sp", lambda e: e.dma_start(out=oBT[:, tsl], in_=OOUT[:, :]), reads=["OOUT"], dma="ob")
            yield

        for _ in gen_A(0):
            pass
        for T in range(NTILES):
            gens = [gen_H(T), gen_F(T)]
            if T + 1 < NTILES:
                gens.append(gen_A(T + 1))
            while gens:
                for g in list(gens):
                    try:
                        next(g)
                    except StopIteration:
                        gens.remove(g)
        S.barrier()
        S.emit(nc, block, st)
    return nc


def prep_l1(inp, ntok=SEQ):
    x = np.asarray(inp["x"], np.float32)[0]
    xT = np.ascontiguousarray(x[:ntok].T)
    w_in = np.asarray(inp["w_in"], np.float32)[0]
    nv1 = _colvec(inp["norm_mix_pre"][0])
    p = np.arange(128)
    ones = np.ones((128, 128), np.float32)
    ident = np.eye(128, dtype=np.float32)
    maskneg = np.where(p[:, None] > p[None, :], -30000.0, 0.0).astype(np.float32)
    bdmask = ((p[:, None] // 64 == p[None, :] // 64) & (p[:, None] <= p[None, :])).astype(np.float32)
    e0 = np.zeros((128, 128), np.float32)
    e0[0, :] = 1.0
    cb = np.concatenate([ones, ident, maskneg, bdmask, e0], axis=1).astype(ml_dtypes.bfloat16)
    reset = np.tile((np.arange(512) % 64 != 0).astype(np.float32)[None, :], (128, 1))
    cf = np.ascontiguousarray(np.concatenate([reset, np.ones((128, 512), np.float32)], axis=1))
    lbl_all = np.asarray(inp["hgrn_lb_logits"], np.float32)
    hnw = np.ascontiguousarray(np.tile(np.asarray(inp["hgrn_norm_w"], np.float32)[0][None, :], (128, 1)))
    maps = []
    for c in range(NCORES):
        cs = lambda base: w_in[:, base + c * 128: base + (c + 1) * 128]
        w1 = np.ascontiguousarray(np.concatenate([cs(0), cs(1024), cs(4096), cs(5120), cs(2048), cs(3072), cs(6144)], axis=1))
        wf = np.ascontiguousarray(np.repeat(w_in[:, 7168 + c: 7169 + c], 2, axis=1))
        lbl = np.ascontiguousarray(lbl_all[:, c * 128:(c + 1) * 128].T)
        bfx = np.full((128, 1), np.asarray(inp["b_fox_f"], np.float32)[0, c], np.float32)
        maps.append(dict(xT=xT, w1=w1, wf=wf, nv1=nv1, lbl=lbl, hnw=hnw, bfx=bfx, cb=cb, cf=cf))
    return maps


def post_l1(results):
    oA = np.concatenate([r["oA"] for r in results], axis=1)
    oB = np.concatenate([np.ascontiguousarray(r["oBT"].T) for r in results], axis=1)
    return oA, oB


_CACHE = {}


def kernel(**inputs):
    inputs = {k: np.asarray(v) for k, v in inputs.items()}
    if "l1" not in _CACHE:
        _CACHE["l1"] = build_l1(SEQ // 512)
        _CACHE["l2"] = build_l2()
    cores = list(range(NCORES))
    r1 = run_bass_kernel_spmd(_CACHE["l1"], prep_l1(inputs), core_ids=cores)
    oA, oB = post_l1(r1.results)
    import os
    if os.environ.get("L1ONLY"):
        return np.zeros((1, SEQ, D_MODEL), np.float32)
    r2 = run_bass_kernel_spmd(_CACHE["l2"], prep_l2(inputs, oA, oB), core_ids=cores)
    return post_l2(r2.results).astype(np.float32)
```

```python
import contextlib
import numpy as np
import ml_dtypes
import concourse.bass as bass
import concourse.mybir as mybir
from concourse.bass_utils import run_bass_kernel_spmd

F32 = mybir.dt.float32
BF16 = mybir.dt.bfloat16
ALU = mybir.AluOpType
AF = mybir.ActivationFunctionType

D_MODEL = 2048
SEQ = 8192
D_FF = 5632
EPS = 1e-6
NCORES = 8

ENGS = ("pe", "act", "dve", "pool", "sp")


class Sched:
    def __init__(self):
        self.ops = {e: [] for e in ENGS}
        self.lastw = {}
        self.readers = {}
        self.dma_cnt = {}

    def _tok(self, eng, rec, idx):
        if rec["dma"] is not None:
            return ("dma", rec["dma"], rec["dma_val"])
        return ("eng", eng, idx)

    def op(self, eng, fn, reads=(), writes=(), dma=None):
        idx = len(self.ops[eng])
        rec = dict(fn=fn, dma=dma, needed=False, deps={})
        if dma is not None:
            self.dma_cnt[dma] = self.dma_cnt.get(dma, 0) + 1
            rec["dma_val"] = 16 * self.dma_cnt[dma]
        tok = self._tok(eng, rec, idx)
        deps = rec["deps"]

        def add(t):
            if t is None:
                return
            if t[0] == "dma":
                cur = self.dma_cnt[t[1]] - (1 if dma == t[1] else 0)
                t = ("dma", t[1], 16 * cur)
            if t[0] == "eng" and t[1] == "pe" and eng == "pe" and dma is None:
                return
            k = (t[0], t[1])
            if k not in deps or deps[k] < t[2]:
                deps[k] = t[2]

        for key in reads:
            add(self.lastw.get(key))
        for key in writes:
            add(self.lastw.get(key))
            for t in self.readers.get(key, {}).values():
                add(t)
        self.ops[eng].append(rec)
        for key in writes:
            self.lastw[key] = tok
            self.readers[key] = {}
        for key in reads:
            r = self.readers.setdefault(key, {})
            k = (tok[0], tok[1])
            if k not in r or r[k][2] < tok[2]:
                r[k] = tok
        return tok

    def barrier(self):
        toks = []
        for e in ENGS:
            for i in range(len(self.ops[e]) - 1, -1, -1):
                if self.ops[e][i]["dma"] is None and self.ops[e][i]["fn"] is not None:
                    toks.append(("eng", e, i))
                    break
        for k, c in self.dma_cnt.items():
            toks.append(("dma", k, 16 * c))
        for e in ENGS:
            rec = dict(fn=None, dma=None, needed=False, deps={})
            for t in toks:
                rec["deps"][(t[0], t[1])] = t[2]
            self.ops[e].append(rec)
        self.lastw = {}
        self.readers = {}

    def emit(self, nc, block, stack):
        sems = {}
        for e in ENGS:
            sems[("eng", e)] = stack.enter_context(nc.semaphore("s_" + e))
        for k in self.dma_cnt:
            sems[("dma", k)] = stack.enter_context(nc.semaphore("d_" + str(k)))
        for e in ENGS:
            for rec in self.ops[e]:
                for (kind, name), val in rec["deps"].items():
                    if kind == "eng":
                        self.ops[name][val]["needed"] = True
        vals = {}
        for e in ENGS:
            cnt = 0
            for i, rec in enumerate(self.ops[e]):
                if rec["needed"]:
                    cnt += 1
                    vals[(e, i)] = cnt
            assert cnt < 60000, (e, cnt)
        self.n_emitted = {e: len(self.ops[e]) for e in ENGS}

        def run(e, engine):
            waited = {}
            for i, rec in enumerate(self.ops[e]):
                for (kind, name), val in rec["deps"].items():
                    v = vals[(name, val)] if kind == "eng" else val
                    sk = (kind, name)
                    if waited.get(sk, 0) >= v:
                        continue
                    engine.wait_ge(sems[sk], v)
                    waited[sk] = v
                if rec["fn"] is None:
                    continue
                ins = rec["fn"](engine)
                if rec["dma"] is not None:
                    ins.then_inc(sems[("dma", rec["dma"])], 16)
                elif rec["needed"]:
                    ins.then_inc(sems[("eng", e)], 1)

        @block.tensor
        def _(eng):
            run("pe", eng)

        @block.scalar
        def _(eng):
            run("act", eng)

        @block.vector
        def _(eng):
            run("dve", eng)

        @block.gpsimd
        def _(eng):
            run("pool", eng)

        @block.sync
        def _(eng):
            run("sp", eng)


def _v3(ap, k):
    return ap.rearrange("p (k n) -> p k n", k=k)


def build_l2(NT=1024):
    nc = bass.Bass("TRN2", target_bir_lowering=False)
    S = Sched()
    NTT = NT // 512
    dt = nc.dram_tensor
    xT = dt("xT", [2048, NT], F32, kind="ExternalInput").ap()
    oT = dt("oT", [2048, NT], F32, kind="ExternalInput").ap()
    wg = dt("wg", [2048, 4096], F32, kind="ExternalInput").ap()
    wua = dt("wua", [1024, 2048], F32, kind="ExternalInput").ap()
    wub = dt("wub", [1024, 2048], F32, kind="ExternalInput").ap()
    wo = dt("wo", [2048, 2048], F32, kind="ExternalInput").ap()
    wfi = dt("wfi", [2048, 2 * D_FF], F32, kind="ExternalInput").ap()
    wfd = dt("wfd", [D_FF, 2048], F32, kind="ExternalInput").ap()
    nv = dt("nv", [128, 64], F32, kind="ExternalInput").ap()
    cst = dt("cst", [128, 128], BF16, kind="ExternalInput").ap()
    outT = dt("outT", [2048, NT], F32, kind="ExternalOutput").ap()
    zd = dt("zd", [2048, NT], F32).ap()
    x1d = dt("x1d", [2048, NT], F32).ap()

    def cm(ap):
        return ap.rearrange("(k p) n -> p k n", p=128)

    xTv, oTv, wgv, wuav, wubv, wov, wfiv, wfdv = map(cm, (xT, oT, wg, wua, wub, wo, wfi, wfd))
    outTv, zdv, x1dv = cm(outT), cm(zd), cm(x1d)

    with contextlib.ExitStack() as st:
        sb = lambda name, shape, d: st.enter_context(nc.sbuf_tensor(name, shape, d))
        A = sb("A", [128, 44 * NT], BF16)
        B = sb("B", [128, 16 * NT], BF16)
        W = sb("W", [128, 16384], BF16)
        T = sb("T", [128, 16, 512], F32)
        SQT = sb("SQT", [128, 4, 512], BF16)
        R = sb("R", [128, 2, NT], F32)
        NV = sb("NV", [128, 64], F32)
        ONES = sb("ONES", [128, 128], BF16)
        PS = [st.enter_context(nc.psum_tensor("ps%d" % i, [128, 512], F32)) for i in range(8)]
        block = st.enter_context(nc.Block())

        xb = _v3(A[:, 0:16 * NT], 16)
        ob = _v3(A[:, 16 * NT:32 * NT], 16)
        SP0 = 32 * NT
        SQ = _v3(A[:, SP0 + 8192:SP0 + 12288], 16)
        XS = _v3(W[:, 8192:16384].bitcast(F32), 16)
        hid = _v3(A[:, 0:44 * NT], 44)
        mb = _v3(B[:, :], 16)
        x1b = mb

        S.op("sp", lambda e: e.dma_start(out=NV[:, :], in_=nv[:, :]), writes=["NV"], dma="c0")
        S.op("sp", lambda e: e.dma_start(out=ONES[:, :], in_=cst[:, :]), writes=["ONES"], dma="c1")

        def rstd_from(ps_ap, r_ap, keyp, keyr):
            S.op("dve", lambda e: e.tensor_scalar(out=r_ap, in0=ps_ap, scalar1=1.0 / D_MODEL, scalar2=EPS,
                                                  op0=ALU.mult, op1=ALU.add), reads=[keyp], writes=[keyr])
            S.op("act", lambda e: e.activation(out=r_ap, in_=r_ap, func=AF.Sqrt), reads=[keyr], writes=[keyr])
            S.op("dve", lambda e: e.reciprocal(out=r_ap, in_=r_ap), reads=[keyr], writes=[keyr])

        XS2 = _v3(T[:, 0:8, :].rearrange("p a b -> p (a b)"), 16)
        for s in range(NT // 256):
            cs = slice(s * 256, (s + 1) * 256)
            XS_ = XS if s % 2 == 0 else XS2
            XSK = [("W", 1, 0), ("W", 1, 1)] if s % 2 == 0 else [("T", i) for i in range(8)]
            S.op("sp", lambda e, cs=cs, XS_=XS_: e.dma_start(out=XS_[:, :, :], in_=xTv[:, :, cs]), writes=XSK, dma="xs%d" % (s % 2))
            S.op("act", lambda e, XS_=XS_: e.activation(out=SQ[:, :, :], in_=XS_[:, :, :], func=AF.Square), reads=XSK, writes=["SQ"])
            for k in range(16):
                S.op("dve", lambda e, k=k, cs=cs, XS_=XS_: e.tensor_scalar(out=xb[:, k, cs], in0=XS_[:, k, :], scalar1=NV[:, k:k + 1],
                                                                           scalar2=None, op0=ALU.mult),
                     reads=XSK + ["NV"], writes=[("A0", k, s)])
            for k in range(16):
                S.op("pe", lambda e, k=k: e.matmul(PS[0][:, 0:256], lhsT=ONES[:, :], rhs=SQ[:, k, :], start=(k == 0), stop=(k == 15)),
                     reads=["SQ", "ONES"], writes=[("P", 0)])
            rstd_from(PS[0][:, 0:256], R[:, 0, cs], ("P", 0), ("R0", s))
        for h in range(2):
            S.op("pool", lambda e, h=h: e.dma_start(out=ob[:, 8 * h:8 * h + 8, :], in_=oTv[:, 8 * h:8 * h + 8, :]),
                 writes=[("A1", h)], dma="ob%d" % h)
        xb_keys = [("A0", k, s) for k in range(16) for s in range(NT // 256)]
        r0_keys = [("R0", s) for s in range(NT // 256)]

        for g in range(8):
            s_ = g % 2
            base = s_ * 8192
            wga = _v3(W[:, base:base + 4096], 16)
            wgb = _v3(W[:, base + 4096:base + 8192], 16)
            wa = _v3(A[:, SP0 + s_ * 4096:SP0 + s_ * 4096 + 2048], 8)
            wb = _v3(A[:, SP0 + s_ * 4096 + 2048:SP0 + s_ * 4096 + 4096], 8)
            gs = slice(g * 256, (g + 1) * 256)
            gs2 = slice(2048 + g * 256, 2048 + (g + 1) * 256)
            wk = ("W", s_)
            dk = "w%d" % s_
            S.op("pool", lambda e, wga=wga, gs=gs: e.dma_start(out=wga[:, :, :], in_=wgv[:, :, gs]), writes=[wk + (0,)], dma=dk + "a")
            S.op("pool", lambda e, wgb=wgb, gs2=gs2: e.dma_start(out=wgb[:, :, :], in_=wgv[:, :, gs2]), writes=[wk + (1,)], dma=dk + "b")
            S.op("pool", lambda e, wa=wa, gs=gs: e.dma_start(out=wa[:, :, :], in_=wuav[:, :, gs]), writes=[wk + (2,)], dma=dk + "c")
            S.op("pool", lambda e, wb=wb, gs=gs: e.dma_start(out=wb[:, :, :], in_=wubv[:, :, gs]), writes=[wk + (3,)], dma=dk + "d")
            for nb in range(2):
                n = g * 2 + nb
                ns = slice(nb * 128, (nb + 1) * 128)
                for t in range(NTT):
                    ts_ = slice(t * 512, (t + 1) * 512)
                    par = (n * NTT + t) % 2
                    pga, pgb, pya, pyb = (PS[4 * par + i] for i in range(4))
                    kga, kgb, kya, kyb = (("P", 4 * par + i) for i in range(4))
                    t0, t1 = T[:, 2 * par, :], T[:, 2 * par + 1, :]
                    k0, k1 = ("T", 2 * par), ("T", 2 * par + 1)
                    for k in range(16):
                        S.op("pe", lambda e, k=k, pga=pga, wga=wga, ns=ns, ts_=ts_: e.matmul(pga[:, :], lhsT=wga[:, k, ns], rhs=xb[:, k, ts_], start=(k == 0), stop=(k == 15)),
                             reads=[wk + (0,)] + xb_keys, writes=[kga])
                    for k in range(16):
                        S.op("pe", lambda e, k=k, pgb=pgb, wgb=wgb, ns=ns, ts_=ts_: e.matmul(pgb[:, :], lhsT=wgb[:, k, ns], rhs=xb[:, k, ts_], start=(k == 0), stop=(k == 15)),
                             reads=[wk + (1,)] + xb_keys, writes=[kgb])
                    for k in range(8):
                        S.op("pe", lambda e, k=k, pya=pya, wa=wa, ns=ns, ts_=ts_: e.matmul(pya[:, :], lhsT=wa[:, k, ns], rhs=ob[:, k, ts_], start=(k == 0), stop=(k == 7)),
                             reads=[wk + (2,), ("A1", 0)], writes=[kya])
                    for k in range(8):
                        S.op("pe", lambda e, k=k, pyb=pyb, wb=wb, ns=ns, ts_=ts_: e.matmul(pyb[:, :], lhsT=wb[:, k, ns], rhs=ob[:, 8 + k, ts_], start=(k == 0), stop=(k == 7)),
                             reads=[wk + (3,), ("A1", 1)], writes=[kyb])
                    S.op("dve", lambda e, t0=t0, pga=pga, ts_=ts_: e.tensor_tensor(out=t0, in0=pga[:, :], in1=R[:, 0, ts_], op=ALU.mult),
                         reads=[kga] + r0_keys, writes=[k0])
                    S.op("act", lambda e, t0=t0: e.activation(out=t0, in_=t0, func=AF.Sigmoid), reads=[k0], writes=[k0])
                    S.op("dve", lambda e, t1=t1, pgb=pgb, ts_=ts_: e.tensor_tensor(out=t1, in0=pgb[:, :], in1=R[:, 0, ts_], op=ALU.mult),
                         reads=[kgb] + r0_keys, writes=[k1])
                    S.op("act", lambda e, t1=t1: e.activation(out=t1, in_=t1, func=AF.Sigmoid), reads=[k1], writes=[k1])
                    S.op("dve", lambda e, t0=t0, pya=pya: e.tensor_tensor(out=t0, in0=pya[:, :], in1=t0, op=ALU.mult),
                         reads=[kya, k0], writes=[k0])
                    S.op("dve", lambda e, t1=t1, pyb=pyb: e.tensor_tensor(out=t1, in0=pyb[:, :], in1=t1, op=ALU.mult),
                         reads=[kyb, k1], writes=[k1])
                    S.op("dve", lambda e, t0=t0, t1=t1, n=n, ts_=ts_: e.tensor_tensor(out=mb[:, n, ts_], in0=t0, in1=t1, op=ALU.add),
                         reads=[k0, k1], writes=[("B", n, t)])

        S.barrier()
        ZS = _v3(A[:, 0:32 * NT].bitcast(F32), 16)

        def evac_stats(ps, pkey, nchunk, t, ssb, first, last, dram_v, par):
            zt = T[:, 4 + par, :]
            zk = ("T", 4 + par)
            sq = SQT[:, par, :]
            sk = ("SQT", par)
            S.op("act", lambda e: e.activation(out=zt, in_=ps[:, :], func=AF.Copy), reads=[pkey], writes=[zk])
            S.op("act", lambda e: e.activation(out=sq, in_=ps[:, :], func=AF.Square), reads=[pkey], writes=[sk])
            S.op("sp", lambda e: e.dma_start(out=dram_v[:, nchunk, t * 512:(t + 1) * 512], in_=zt), reads=[zk],
                 writes=[("zd", nchunk, t)], dma="zst%d" % par)
            return lambda: S.op("pe", lambda e: e.matmul(PS[ssb][:, :], lhsT=ONES[:, :], rhs=sq, start=first, stop=last),
                                reads=[sk, "ONES"], writes=[("P", ssb)])

        pend = []
        for g in range(8):
            s_ = g % 2
            base = s_ * 8192
            wos = _v3(W[:, base:base + 4096], 16)
            wk = ("W", s_)
            gs = slice(g * 256, (g + 1) * 256)
            S.op("pool", lambda e, wos=wos, gs=gs: e.dma_start(out=wos[:, :, :], in_=wov[:, :, gs]), writes=[wk], dma="w%d" % s_)
            for nb in range(2):
                n = g * 2 + nb
                ns = slice(nb * 128, (nb + 1) * 128)
                for t in range(NTT):
                    ts_ = slice(t * 512, (t + 1) * 512)
                    pb = (n * NTT + t) % 4
                    par = (n * NTT + t) % 2
                    for k in range(16):
                        S.op("pe", lambda e, k=k, pb=pb, wos=wos, ns=ns, ts_=ts_: e.matmul(PS[pb][:, :], lhsT=wos[:, k, ns], rhs=mb[:, k, ts_], start=(k == 0), stop=(k == 15)),
                             reads=[wk] + [("B", k, t) for k in range(16)], writes=[("P", pb)])
                    for f_ in pend:
                        f_()
                    pend.clear()
                    sq = SQT[:, par, :]
                    S.op("act", lambda e, pb=pb, n=n, ts_=ts_: e.activation(out=ZS[:, n, ts_], in_=PS[pb][:, :], func=AF.Copy), reads=[("P", pb)], writes=[("ZS", n, t)])
                    S.op("act", lambda e, pb=pb, sq=sq: e.activation(out=sq, in_=PS[pb][:, :], func=AF.Square), reads=[("P", pb)], writes=[("SQT", par)])
                    pend.append(lambda sq=sq, t=t, n=n, par=par: S.op("pe", lambda e: e.matmul(PS[6 + t][:, :], lhsT=ONES[:, :], rhs=sq, start=(n == 0), stop=(n == 15)),
                                                                       reads=[("SQT", par), "ONES"], writes=[("P", 6 + t)]))
        for f_ in pend:
            f_()
        pend.clear()
        for t in range(NTT):
            rstd_from(PS[6 + t][:, :], R[:, 1, t * 512:(t + 1) * 512], ("P", 6 + t), ("R1", t))

        for t in range(NTT):
            ts_ = slice(t * 512, (t + 1) * 512)
            for n in range(16):
                sl4 = n % 8
                par = n % 2
                ta = T[:, sl4, :]
                ka = ("T", sl4)
                S.op("sp", lambda e, ta=ta, n=n, ts_=ts_: e.dma_start(out=ta, in_=xTv[:, n, ts_]), writes=[ka], dma="ra%d" % sl4)
                S.op("dve", lambda e, n=n, ts_=ts_: e.scalar_tensor_tensor(out=ZS[:, n, ts_], in0=ZS[:, n, ts_], scalar=NV[:, 16 + n:17 + n], in1=R[:, 1, ts_],
                                                                            op0=ALU.mult, op1=ALU.mult), reads=[("ZS", n, t), "NV", ("R1", t)], writes=[("ZS", n, t)])
                S.op("dve", lambda e, ta=ta, n=n, ts_=ts_: e.tensor_tensor(out=ta, in0=ta, in1=ZS[:, n, ts_], op=ALU.add), reads=[ka, ("ZS", n, t)], writes=[ka])
                S.op("pool", lambda e, ta=ta, n=n, ts_=ts_: e.dma_start(out=x1dv[:, n, ts_], in_=ta), reads=[ka], writes=[("x1d", n, t)], dma="x1st%d" % sl4)
                sq = SQT[:, 2 + par, :]
                sk = ("SQT", 2 + par)
                S.op("dve", lambda e, ta=ta, n=n, ts_=ts_: e.tensor_scalar(out=x1b[:, n, ts_], in0=ta, scalar1=NV[:, 32 + n:33 + n], scalar2=None, op0=ALU.mult),
                     reads=[ka, "NV"], writes=[("B", n, t)])
                S.op("act", lambda e, ta=ta, sq=sq: e.activation(out=sq, in_=ta, func=AF.Square), reads=[ka], writes=[sk])
                S.op("pe", lambda e, sq=sq, n=n, t=t: e.matmul(PS[4 + t][:, :], lhsT=ONES[:, :], rhs=sq, start=(n == 0), stop=(n == 15)),
                     reads=[sk, "ONES"], writes=[("P", 4 + t)])
        for t in range(NTT):
            rstd_from(PS[4 + t][:, :], R[:, 0, t * 512:(t + 1) * 512], ("P", 4 + t), ("R2", t))

        S.barrier()
        xk = {t: [("B", k, t) for k in range(16)] for t in range(NTT)}
        for g in range(22):
            s_ = g % 2
            base = s_ * 8192
            wfg = _v3(W[:, base:base + 4096], 16)
            wfu = _v3(W[:, base + 4096:base + 8192], 16)
            wk = ("W", s_)
            gs = slice(g * 256, (g + 1) * 256)
            gs2 = slice(D_FF + g * 256, D_FF + (g + 1) * 256)
            S.op("pool", lambda e, wfg=wfg, gs=gs: e.dma_start(out=wfg[:, :, :], in_=wfiv[:, :, gs]), writes=[wk + (0,)], dma="w%da" % s_)
            S.op("pool", lambda e, wfu=wfu, gs2=gs2: e.dma_start(out=wfu[:, :, :], in_=wfiv[:, :, gs2]), writes=[wk + (1,)], dma="w%db" % s_)
            for nb in range(2):
                f = g * 2 + nb
                ns = slice(nb * 128, (nb + 1) * 128)
                for t in range(NTT):
                    ts_ = slice(t * 512, (t + 1) * 512)
                    par = (f * NTT + t) % 2
                    pg_, pu_ = PS[2 * par], PS[2 * par + 1]
                    kg_, ku_ = ("P", 2 * par), ("P", 2 * par + 1)
                    t0, t1 = T[:, 4 + 2 * par, :], T[:, 5 + 2 * par, :]
                    k0, k1 = ("T", 4 + 2 * par), ("T", 5 + 2 * par)
                    for k in range(16):
                        S.op("pe", lambda e, k=k, pg_=pg_, wfg=wfg, ns=ns, ts_=ts_: e.matmul(pg_[:, :], lhsT=wfg[:, k, ns], rhs=x1b[:, k, ts_], start=(k == 0), stop=(k == 15)),
                             reads=[wk + (0,)] + xk[t], writes=[kg_])
                    for k in range(16):
                        S.op("pe", lambda e, k=k, pu_=pu_, wfu=wfu, ns=ns, ts_=ts_: e.matmul(pu_[:, :], lhsT=wfu[:, k, ns], rhs=x1b[:, k, ts_], start=(k == 0), stop=(k == 15)),
                             reads=[wk + (1,)] + xk[t], writes=[ku_])
                    S.op("dve", lambda e, t0=t0, pg_=pg_, ts_=ts_: e.tensor_tensor(out=t0, in0=pg_[:, :], in1=R[:, 0, ts_], op=ALU.mult),
                         reads=[kg_, ("R2", t)], writes=[k0])
                    S.op("act", lambda e, t0=t0: e.activation(out=t0, in_=t0, func=AF.Silu), reads=[k0], writes=[k0])
                    S.op("dve", lambda e, t1=t1, pu_=pu_, ts_=ts_: e.tensor_tensor(out=t1, in0=pu_[:, :], in1=R[:, 0, ts_], op=ALU.mult),
                         reads=[ku_, ("R2", t)], writes=[k1])
                    S.op("dve", lambda e, t0=t0, t1=t1, f=f, ts_=ts_: e.tensor_tensor(out=hid[:, f, ts_], in0=t0, in1=t1, op=ALU.mult),
                         reads=[k0, k1], writes=[("hid", f, t)])

        S.barrier()
        for g in range(8):
            s_ = g % 2
            wd = _v3((W if s_ == 0 else B)[:, 0:11264], 44)
            wk = ("W", s_)
            gs = slice(g * 256, (g + 1) * 256)
            for hh in range(2):
                S.op("pool", lambda e, wd=wd, gs=gs, hh=hh: e.dma_start(out=wd[:, 22 * hh:22 * hh + 22, :], in_=wfdv[:, 22 * hh:22 * hh + 22, gs]),
                     writes=[wk + (hh,)], dma="w%d%s" % (s_, "ab"[hh]))
            for nb in range(2):
                n = g * 2 + nb
                ns = slice(nb * 128, (nb + 1) * 128)
                for t in range(NTT):
                    ts_ = slice(t * 512, (t + 1) * 512)
                    pb = (n * NTT + t) % 4
                    for k in range(44):
                        S.op("pe", lambda e, k=k, pb=pb, wd=wd, ns=ns, ts_=ts_: e.matmul(PS[pb][:, :], lhsT=wd[:, k, ns], rhs=hid[:, k, ts_], start=(k == 0), stop=(k == 43)),
                             reads=[wk + (0,), wk + (1,)] + [("hid", f, t) for f in range(44)], writes=[("P", pb)])
                    for f_ in pend:
                        f_()
                    pend.clear()
                    pend.append(evac_stats(PS[pb], ("P", pb), n, t, 6 + t, n == 0, n == 15, zdv, (n * NTT + t) % 2))
        for f_ in pend:
            f_()
        pend.clear()
        for t in range(NTT):
            rstd_from(PS[6 + t][:, :], R[:, 1, t * 512:(t + 1) * 512], ("P", 6 + t), ("R3", t))
        S.barrier()
        for t in range(NTT):
            ts_ = slice(t * 512, (t + 1) * 512)
            for n in range(16):
                sl4 = n % 8
                ta, tb = T[:, sl4, :], T[:, 8 + sl4, :]
                ka, kb = ("T", sl4), ("T", 8 + sl4)
                S.op("sp", lambda e, ta=ta, n=n, ts_=ts_: e.dma_start(out=ta, in_=x1dv[:, n, ts_]), reads=[("x1d", n, t)], writes=[ka], dma="ra%d" % sl4)
                S.op("act", lambda e, tb=tb, n=n, ts_=ts_: e.dma_start(out=tb, in_=zdv[:, n, ts_]), reads=[("zd", n, t)], writes=[kb], dma="rb%d" % sl4)
                S.op("dve", lambda e, tb=tb, n=n, ts_=ts_: e.scalar_tensor_tensor(out=tb, in0=tb, scalar=NV[:, 48 + n:49 + n], in1=R[:, 1, ts_],
                                                                                   op0=ALU.mult, op1=ALU.mult), reads=[kb, "NV", ("R3", t)], writes=[kb])
                S.op("dve", lambda e, ta=ta, tb=tb: e.tensor_tensor(out=ta, in0=ta, in1=tb, op=ALU.add), reads=[ka, kb], writes=[ka])
                S.op("pool", lambda e, ta=ta, n=n, ts_=ts_: e.dma_start(out=outTv[:, n, ts_], in_=ta), reads=[ka], writes=[("out", n, t)], dma="ost%d" % sl4)

        S.barrier()
        S.emit(nc, block, st)
    return nc


def _colvec(v):
    return np.ascontiguousarray(np.asarray(v, np.float32).reshape(16, 128).T)


def prep_l2(inp, oA, oB):
    x = np.asarray(inp["x"], np.float32)[0]
    w_in = np.asarray(inp["w_in"], np.float32)[0]
    wg = np.ascontiguousarray(w_in[:, 7176:7176 + 4096])
    nv = np.ascontiguousarray(np.concatenate([_colvec(inp["norm_mix_pre"][0]), _colvec(inp["norm_mix_post"][0]),
                                              _colvec(inp["norm_ffn_pre"][0]), _colvec(inp["norm_ffn_post"][0])], axis=1))
    ones = np.ones((128, 128), ml_dtypes.bfloat16)
    common = dict(wg=wg, wua=np.ascontiguousarray(inp["w_up_a"][0], dtype=np.float32), wub=np.ascontiguousarray(inp["w_up_b"][0], dtype=np.float32),
                  wo=np.ascontiguousarray(inp["w_o"][0], dtype=np.float32), wfi=np.ascontiguousarray(inp["w_ffn_in"][0], dtype=np.float32),
                  wfd=np.ascontiguousarray(inp["w_ffn_down"][0], dtype=np.float32), nv=nv, cst=ones)
    maps = []
    for c in range(NCORES):
        sl = slice(c * 1024, (c + 1) * 1024)
        oT = np.ascontiguousarray(np.concatenate([oA[sl].T, oB[sl].T], axis=0), dtype=np.float32)
        maps.append(dict(common, xT=np.ascontiguousarray(x[sl].T), oT=oT))
    return maps


def post_l2(results):
    return np.concatenate([np.ascontiguousarray(r["outT"].T) for r in results], axis=0)[None]


def build_l1(NTILES=16, stage=9):
    nc = bass.Bass("TRN2", target_bir_lowering=False)
    S = Sched()
    NTOK = NTILES * 512
    dt = nc.dram_tensor
    xT = dt("xT", [2048, NTOK], F32, kind="ExternalInput").ap()
    w1 = dt("w1", [2048, 896], F32, kind="ExternalInput").ap()
    wf = dt("wf", [2048, 2], F32, kind="ExternalInput").ap()
    nv1 = dt("nv1", [128, 16], F32, kind="ExternalInput").ap()
    lbl = dt("lbl", [128, 2], F32, kind="ExternalInput").ap()
    hnw = dt("hnw", [128, 128], F32, kind="ExternalInput").ap()
    bfx = dt("bfx", [128, 1], F32, kind="ExternalInput").ap()
    cb = dt("cb", [128, 640], BF16, kind="ExternalInput").ap()
    cf = dt("cf", [128, 1024], F32, kind="ExternalInput").ap()
    oA = dt("oA", [NTOK, 128], F32, kind="ExternalOutput").ap()
    oBT = dt("oBT", [128, NTOK], F32, kind="ExternalOutput").ap()
    xTv = xT.rearrange("(k p) n -> p k n", p=128)
    w1v = w1.rearrange("(k p) n -> p k n", p=128)
    wfv = wf.rearrange("(k p) n -> p k n", p=128)
    oAv = oA.rearrange("(b p) v -> p b v", p=128)

    with contextlib.ExitStack() as st:
        sb = lambda name, shape, d: st.enter_context(nc.sbuf_tensor(name, shape, d))
        XS = sb("XS", [128, 2, 8, 512], F32)
        XB = sb("XB", [128, 16, 512], BF16)
        SQ = sb("SQ", [128, 16, 512], BF16)
        W1 = sb("W1", [128, 16, 896], BF16)
        WF = sb("WF", [128, 16, 2], BF16)
        NV = sb("NV", [128, 16], F32)
        LBL = sb("LBL", [128, 2], F32)
        LB = sb("LB", [128, 2], F32)
        HNW = sb("HNW", [128, 128], F32)
        BFX = sb("BFX", [128, 1], F32)
        CB = sb("CB", [128, 640], BF16)
        CF = sb("CF", [128, 1024], F32)
        RT = sb("RT", [128, 512], F32)
        RC = sb("RC", [128, 4], F32)
        KT = sb("KT", [128, NTOK], BF16)
        V = sb("V", [128, NTILES * 4, 128], BF16)
        QT = sb("QT", [128, 2, 512], BF16)
        TQ = sb("TQ", [128, 512], F32)
        TZ = sb("TZ", [128, 512], F32)
        TE = sb("TE", [128, 512], F32)
        TS = sb("TS", [128, 512], F32)
        TF = sb("TF", [128, 512], F32)
        TK = sb("TK", [128, 512], F32)
        TB = sb("TB", [128, 512], F32)
        TD = sb("TD", [128, 512], F32)
        TE1 = sb("TE1", [128, 512], F32)
        TE2 = sb("TE2", [128, 512], F32)
        QTB = sb("QTB", [128, 512], BF16)
        KTB = sb("KTB", [128, 512], BF16)
        KTOK = sb("KTOK", [128, 4, 128], BF16)
        VTOK = sb("VTOK", [128, 2, 4, 128], BF16)
        GTOK = sb("GTOK", [128, 2, 4, 128], F32)
        EBM = sb("EBM", [128, 8], F32)
        EBL = sb("EBL", [128, 8], F32)
        EBLM = sb("EBLM", [128, 8], F32)
        BDM = sb("BDM", [128, 8], F32)
        ST = sb("ST", [128, 128], F32)
        SM = sb("SM", [128, 2, 128], BF16)
        SCB = sb("SCB", [128, 128], BF16)
        TKV = sb("TKV", [128, 128], F32)
        T1 = sb("T1", [128, 128], F32)
        TSC = sb("TSC", [128, 128], F32)
        EG = sb("EG", [128, 128], F32)
        GG = sb("GG", [128, 128], F32)
        OAT = sb("OAT", [128, 2, 128], F32)
        SSO = sb("SSO", [128, 1], F32)
        RO = sb("RO", [128, 1], F32)
        PT = sb("PT", [128, 2, 512], BF16)
        CROW = sb("CROW", [128, 2, 512], BF16)
        LROW = sb("LROW", [1, 512], F32)
        ZF = sb("ZF", [1, 512], F32)
        POSC = sb("POSC", [128, NTILES * 4], F32)
        RP = sb("RP", [128, NTILES + 1], F32)
        BIAS = sb("BIAS", [128, 2, NTILES * 4], F32)
        ODEN = sb("ODEN", [128, 512], F32)
        OOUT = sb("OOUT", [128, 512], F32)
        PS = [st.enter_context(nc.psum_tensor("ps%d" % i, [128, 512], F32)) for i in range(8)]
        block = st.enter_context(nc.Block())

        ONESB, IDENT, MASKNEG, BDMASK, E0 = CB[:, 0:128], CB[:, 128:256], CB[:, 256:384], CB[:, 384:512], CB[:, 512:640]
        RESET, ONESF = CF[:, 0:512], CF[:, 512:1024]

        def ld(eng, dst, src, key, dk):
            S.op(eng, lambda e: e.dma_start(out=dst, in_=src), writes=[key], dma=dk)

        ld("sp", NV[:, :], nv1[:, :], "NV", "c0")
        ld("sp", LBL[:, :], lbl[:, :], "LBL", "c1")
        ld("sp", HNW[:, :], hnw[:, :], "HNW", "c2")
        ld("sp", BFX[:, :], bfx[:, :], "BFX", "c3")
        ld("sp", CB[:, :], cb[:, :], "CB", "c4")
        ld("sp", CF[:, :], cf[:, :], "CF", "c5")
        ld("pool", W1[:, 0:8, :], w1v[:, 0:8, :], ("W1", 0), "w1a")
        ld("pool", W1[:, 8:16, :], w1v[:, 8:16, :], ("W1", 1), "w1b")
        ld("pool", WF[:, :, :], wfv[:, :, :], "WF", "wf")
        W1K = [("W1", 0), ("W1", 1)]

        S.op("dve", lambda e: e.tensor_tensor(out=LB[:, 0:1], in0=LBL[:, 0:1], in1=LBL[:, 1:2], op=ALU.subtract), reads=["LBL"], writes=["LB"])
        S.op("act", lambda e: e.activation(out=LB[:, 1:2], in_=LB[:, 0:1], func=AF.Exp, scale=-1.0), reads=["LB"], writes=["LB"])
        S.op("dve", lambda e: e.tensor_scalar(out=LB[:, 0:1], in0=LB[:, 1:2], scalar1=1.0, scalar2=None, op0=ALU.add), reads=["LB"], writes=["LB"])
        S.op("dve", lambda e: e.reciprocal(out=LB[:, 0:1], in_=LB[:, 0:1]), reads=["LB"], writes=["LB"])
        S.op("dve", lambda e: e.tensor_tensor(out=LB[:, 1:2], in0=LB[:, 1:2], in1=LB[:, 0:1], op=ALU.mult), reads=["LB"], writes=["LB"])
        S.op("dve", lambda e: e.memset(ST[:, :], 0.0), writes=["ST"])
        S.op("dve", lambda e: e.memset(RP[:, :], 0.0), writes=["RP"])
        S.op("dve", lambda e: e.memset(CROW[:, :, :], 0.0), writes=[("CROW", 0), ("CROW", 1)])

        def rstd(ps_ap, r_ap, pkey, rkey, n):
            S.op("dve", lambda e: e.tensor_scalar(out=r_ap, in0=ps_ap, scalar1=1.0 / n, scalar2=EPS, op0=ALU.mult, op1=ALU.add),
                 reads=[pkey], writes=[rkey])
            S.op("act", lambda e: e.activation(out=r_ap, in_=r_ap, func=AF.Ln), reads=[rkey], writes=[rkey])
            S.op("act", lambda e: e.activation(out=r_ap, in_=r_ap, func=AF.Exp, scale=-0.5), reads=[rkey], writes=[rkey])

        def gen_A(T):
            par = T % 2
            tsl = slice(T * 512, (T + 1) * 512)
            for h in range(2):
                S.op("sp", lambda e, h=h: e.dma_start(out=XS[:, h, :, :], in_=xTv[:, 8 * h:8 * h + 8, tsl]), writes=[("XS", h)], dma="xs%d" % h)
            yield
            for h in range(2):
                for k in range(8):
                    S.op("act", lambda e, h=h, k=k: e.activation(out=SQ[:, 8 * h + k, :], in_=XS[:, h, k, :], func=AF.Square),
                         reads=[("XS", h)], writes=[("SQ", h, k)])
                    if k % 2 == 0:
                        S.op("dve", lambda e, h=h, k=k: e.tensor_scalar(out=XB[:, 8 * h + k, :], in0=XS[:, h, k, :], scalar1=NV[:, 8 * h + k:8 * h + k + 1],
                                                                         scalar2=None, op0=ALU.mult), reads=[("XS", h), "NV"], writes=[("XB", h, k)])
                    else:
                        S.op("act", lambda e, h=h, k=k: e.activation(out=XB[:, 8 * h + k, :], in_=XS[:, h, k, :], func=AF.Copy, scale=NV[:, 8 * h + k:8 * h + k + 1]),
                             reads=[("XS", h), "NV"], writes=[("XB", h, k)])
                        yield
            SQK = [("SQ", h, k) for h in range(2) for k in range(8)]
            XBK = [("XB", h, k) for h in range(2) for k in range(8)]
            for k in range(16):
                S.op("pe", lambda e, k=k: e.matmul(PS[0][:, :], lhsT=ONESB, rhs=SQ[:, k, :], start=(k == 0), stop=(k == 15)),
                     reads=SQK + ["CB"], writes=[("P", 0)])
            rstd(PS[0][:, :], RT[:, :], ("P", 0), "RT", D_MODEL)
            yield
            for b in range(4):
                for k in range(16):
                    S.op("pe", lambda e, k=k, b=b: e.matmul(PS[3][:, 400 + b:401 + b], lhsT=SQ[:, k, b * 128:(b + 1) * 128], rhs=ONESB[:, 0:1],
                                                            start=(k == 0), stop=(k == 15)), reads=SQK + ["CB"], writes=[("P", 3)])
            rstd(PS[3][:, 400:404], RC[:, :], ("P", 3), "RC", D_MODEL)
            yield
            for j in range(4):
                pa = 0 if j % 2 == 0 else 3
                for k in range(16):
                    S.op("pe", lambda e, k=k, j=j, pa=pa: e.matmul(PS[pa][:, :], lhsT=W1[:, k, j * 128:(j + 1) * 128], rhs=XB[:, k, :],
                                                            start=(k == 0), stop=(k == 15)), reads=W1K + XBK, writes=[("P", pa)])
                if j == 0:
                    S.op("dve", lambda e, pa=pa: e.tensor_tensor(out=TQ[:, :], in0=PS[pa][:, :], in1=RT[:, :], op=ALU.mult), reads=[("P", pa), "RT"], writes=["TQ"])
                elif j == 1:
                    S.op("dve", lambda e, pa=pa: e.tensor_tensor(out=TZ[:, :], in0=PS[pa][:, :], in1=RT[:, :], op=ALU.mult), reads=[("P", pa), "RT"], writes=["TZ"])
                elif j == 2:
                    S.op("dve", lambda e, pa=pa: e.scalar_tensor_tensor(out=QT[:, par, :], in0=PS[pa][:, :], scalar=128.0 ** -0.5, in1=RT[:, :], op0=ALU.mult, op1=ALU.mult),
                         reads=[("P", pa), "RT"], writes=[("QT", par)])
                else:
                    S.op("dve", lambda e, pa=pa: e.tensor_tensor(out=KT[:, tsl], in0=PS[pa][:, :], in1=RT[:, :], op=ALU.mult), reads=[("P", pa), "RT"], writes=[("KT", T)])
                yield
            for b in range(4):
                pa = 0 if b % 2 == 0 else 3
                for k in range(16):
                    S.op("pe", lambda e, k=k, b=b, pa=pa: e.matmul(PS[pa][:, 0:384], lhsT=XB[:, k, b * 128:(b + 1) * 128], rhs=W1[:, k, 512:896],
                                                            start=(k == 0), stop=(k == 15)), reads=W1K + XBK, writes=[("P", pa)])
                S.op("act", lambda e, b=b, pa=pa: e.activation(out=VTOK[:, par, b, :], in_=PS[pa][:, 0:128], func=AF.Copy, scale=RC[:, b:b + 1]),
                     reads=[("P", pa), "RC"], writes=[("VTOK", par, b)])
                S.op("act", lambda e, b=b, pa=pa: e.activation(out=GTOK[:, par, b, :], in_=PS[pa][:, 128:256], func=AF.Copy, scale=RC[:, b:b + 1]),
                     reads=[("P", pa), "RC"], writes=[("GTOK", par, b)])
                S.op("act", lambda e, b=b, pa=pa: e.activation(out=V[:, 4 * T + b, :], in_=PS[pa][:, 256:384], func=AF.Copy, scale=RC[:, b:b + 1]),
                     reads=[("P", pa), "RC"], writes=[("V", 4 * T + b)])
                yield
            for k in range(16):
                S.op("pe", lambda e, k=k: e.matmul(PS[0][0:1, :], lhsT=WF[:, k, 0:1], rhs=XB[:, k, :], start=(k == 0), stop=(k == 15)),
                     reads=["WF"] + XBK, writes=[("P", 0)])
            S.op("dve", lambda e: e.tensor_tensor(out=ZF[0:1, :], in0=PS[0][0:1, :], in1=RT[0:1, :], op=ALU.mult), reads=[("P", 0), "RT"], writes=["ZF"])
            S.op("dve", lambda e: e.tensor_scalar(out=ZF[0:1, :], in0=ZF[0:1, :], scalar1=BFX[0:1, 0:1], scalar2=None, op0=ALU.add), reads=["ZF", "BFX"], writes=["ZF"])
            S.op("act", lambda e: e.activation(out=ZF[0:1, :], in_=ZF[0:1, :], func=AF.Exp, scale=-1.0), reads=["ZF"], writes=["ZF"])
            S.op("act", lambda e: e.activation(out=ZF[0:1, :], in_=ZF[0:1, :], func=AF.Ln, bias=1.0), reads=["ZF"], writes=["ZF"])
            S.op("dve", lambda e: e.tensor_tensor_scan(out=LROW[0:1, :], data0=ONESF[0:1, :], data1=ZF[0:1, :], initial=0.0, op0=ALU.mult, op1=ALU.add),
                 reads=["ZF", "CF"], writes=["LROW"])
            S.op("dve", lambda e: e.tensor_scalar(out=CROW[0:1, par, :], in0=LROW[0:1, :], scalar1=-1.0, scalar2=None, op0=ALU.mult), reads=["LROW"], writes=[("CROW", par)])
            yield
            for b in range(4):
                S.op("pe", lambda e, b=b: e.matmul(PS[3][:, 384 + b:385 + b], lhsT=LROW[0:1, b * 128:(b + 1) * 128], rhs=ONESF[0:1, 0:1], start=True, stop=True),
                     reads=["LROW", "CF"], writes=[("P", 3)])
            S.op("pe", lambda e: e.matmul(PS[3][:, 388:389], lhsT=ONESF[0:1, 0:128], rhs=LROW[0:1, 511:512], start=True, stop=True),
                 reads=["LROW", "CF"], writes=[("P", 3)])
            S.op("dve", lambda e: e.tensor_scalar(out=POSC[:, 4 * T:4 * T + 4], in0=PS[3][:, 384:388], scalar1=RP[:, T:T + 1], scalar2=None, op0=ALU.add),
                 reads=[("P", 3), "RP"], writes=["POSC"])
            S.op("dve", lambda e: e.tensor_tensor(out=RP[:, T + 1:T + 2], in0=PS[3][:, 388:389], in1=RP[:, T:T + 1], op=ALU.add),
                 reads=[("P", 3), "RP"], writes=["RP"])
            S.op("dve", lambda e: e.tensor_scalar(out=BIAS[:, par, 0:4 * T + 4], in0=POSC[:, 0:4 * T + 4], scalar1=RP[:, T:T + 1], scalar2=None, op0=ALU.subtract),
                 reads=["POSC", "RP"], writes=[("BIAS", par)])
            yield

        def gen_H(T):
            par = T % 2
            S.op("act", lambda e: e.activation(out=TE[:, :], in_=TZ[:, :], func=AF.Exp, scale=-1.0), reads=["TZ"], writes=["TE"])
            S.op("dve", lambda e: e.tensor_scalar(out=TS[:, :], in0=TE[:, :], scalar1=1.0, scalar2=None, op0=ALU.add), reads=["TE"], writes=["TS"])
            S.op("dve", lambda e: e.reciprocal(out=TS[:, :], in_=TS[:, :]), reads=["TS"], writes=["TS"])
            S.op("dve", lambda e: e.tensor_scalar(out=TF[:, :], in0=TS[:, :], scalar1=LB[:, 1:2], scalar2=LB[:, 0:1], op0=ALU.mult, op1=ALU.add),
                 reads=["TS", "LB"], writes=["TF"])
            yield
            S.op("act", lambda e: e.activation(out=TF[:, :], in_=TF[:, :], func=AF.Ln), reads=["TF"], writes=["TF"])
            S.op("dve", lambda e: e.scalar_tensor_tensor(out=TK[:, :], in0=TE[:, :], scalar=LB[:, 1:2], in1=TS[:, :], op0=ALU.mult, op1=ALU.mult),
                 reads=["TE", "TS", "LB"], writes=["TK"])
            S.op("dve", lambda e: e.tensor_tensor_scan(out=TB[:, :], data0=RESET, data1=TF[:, :], initial=0.0, op0=ALU.mult, op1=ALU.add),
                 reads=["TF", "CF"], writes=["TB"])
            yield
            TBv = TB[:, :].rearrange("p (c n) -> p c n", n=64)
            TDv = TD[:, :].rearrange("p (c n) -> p c n", n=64)
            S.op("dve", lambda e: e.tensor_tensor(out=TDv, in0=TBv, in1=TBv[:, :, 31:32].to_broadcast([128, 8, 64]), op=ALU.subtract), reads=["TB"], writes=["TD"])
            S.op("act", lambda e: e.activation(out=TE1[:, :], in_=TD[:, :], func=AF.Exp), reads=["TD"], writes=["TE1"])
            S.op("act", lambda e: e.activation(out=TE2[:, :], in_=TD[:, :], func=AF.Exp, scale=-1.0), reads=["TD"], writes=["TE2"])
            yield
            S.op("dve", lambda e: e.tensor_tensor(out=QTB[:, :], in0=TQ[:, :], in1=TE1[:, :], op=ALU.mult), reads=["TQ", "TE1"], writes=["QTB"])
            S.op("dve", lambda e: e.tensor_tensor(out=KTB[:, :], in0=TK[:, :], in1=TE2[:, :], op=ALU.mult), reads=["TK", "TE2"], writes=["KTB"])
            S.op("act", lambda e: e.activation(out=EBM[:, :], in_=TBv[:, :, 31], func=AF.Exp), reads=["TB"], writes=["EBM"])
            S.op("act", lambda e: e.activation(out=EBL[:, :], in_=TBv[:, :, 63], func=AF.Exp), reads=["TB"], writes=["EBL"])
            S.op("dve", lambda e: e.tensor_tensor(out=BDM[:, :], in0=TBv[:, :, 63], in1=TBv[:, :, 31], op=ALU.subtract), reads=["TB"], writes=["BDM"])
            S.op("act", lambda e: e.activation(out=EBLM[:, :], in_=BDM[:, :], func=AF.Exp), reads=["BDM"], writes=["EBLM"])
            ESC = ["EBM", "EBL", "EBLM"]
            yield
            for b in range(4):
                bs = slice(b * 128, (b + 1) * 128)
                trp = PS[1][:, 448:512].bitcast(BF16)
                S.op("pe", lambda e, bs=bs, trp=trp: e.transpose(out=trp, in_=KTB[:, bs], identity=IDENT), reads=["KTB", "CB"], writes=[("P", 1)])
                S.op("act", lambda e, b=b, trp=trp: e.activation(out=KTOK[:, b, :], in_=trp, func=AF.Copy), reads=[("P", 1)], writes=[("KTOK", b)])
                yield
                S.op("pe", lambda e, bs=bs: e.matmul(PS[1][:, 0:128], lhsT=KTB[:, bs], rhs=QTB[:, bs], start=True, stop=True),
                     reads=["KTB", "QTB"], writes=[("P", 1)])
                S.op("act", lambda e: e.activation(out=TSC[:, :], in_=PS[1][:, 0:128], func=AF.Copy), reads=[("P", 1)], writes=["TSC"])
                S.op("dve", lambda e: e.tensor_tensor(out=SCB[:, :], in0=TSC[:, :], in1=BDMASK, op=ALU.mult), reads=["TSC", "CB"], writes=["SCB"])
                yield
                for c2 in range(2):
                    rs = slice(c2 * 64, (c2 + 1) * 64)
                    S.op("pe", lambda e, b=b, c2=c2, rs=rs: e.matmul(PS[1 + c2][:, 128:256], lhsT=KTOK[rs, b, :], rhs=VTOK[rs, par, b, :], start=True, stop=True),
                         reads=[("KTOK", b), ("VTOK", par, b)], writes=[("P", 1 + c2)])
                yield
                for c2 in range(2):
                    c = 2 * b + c2
                    S.op("dve", lambda e, c=c, c2=c2: e.tensor_scalar(out=SM[:, c2, :], in0=ST[:, :], scalar1=EBM[:, c:c + 1], scalar2=None, op0=ALU.mult),
                         reads=["ST"] + ESC, writes=[("SM", c2)])
                    S.op("dve", lambda e, c=c, c2=c2: e.tensor_scalar(out=TKV[:, :], in0=PS[1 + c2][:, 128:256], scalar1=EBLM[:, c:c + 1], scalar2=None, op0=ALU.mult),
                         reads=[("P", 1 + c2)] + ESC, writes=["TKV"])
                    S.op("dve", lambda e, c=c: e.scalar_tensor_tensor(out=ST[:, :], in0=ST[:, :], scalar=EBL[:, c:c + 1], in1=TKV[:, :], op0=ALU.mult, op1=ALU.add),
                         reads=["ST", "TKV"] + ESC, writes=["ST"])
                yield
                S.op("pe", lambda e, b=b: e.matmul(PS[1][:, 0:128], lhsT=SCB[:, :], rhs=VTOK[:, par, b, :], start=True, stop=False),
                     reads=["SCB", ("VTOK", par, b)], writes=[("P", 1)])
                for c2 in range(2):
                    S.op("pe", lambda e, b=b, c2=c2: e.matmul(PS[1][c2 * 64:(c2 + 1) * 64, 0:128], lhsT=QTB[:, b * 128 + c2 * 64:b * 128 + (c2 + 1) * 64], rhs=SM[:, c2, :],
                                                              start=False, stop=True), reads=["QTB", ("SM", c2)], writes=[("P", 1)])
                ob_ = (4 * T + b) % 2
                S.op("act", lambda e: e.activation(out=T1[:, :], in_=PS[1][:, 0:128], func=AF.Square, accum_out=SSO[:, 0:1]), reads=[("P", 1)], writes=["T1", "SSO"])
                rstd(SSO[:, 0:1], RO[:, 0:1], "SSO", "RO", 128)
                S.op("dve", lambda e: e.scalar_tensor_tensor(out=T1[:, :], in0=PS[1][:, 0:128], scalar=RO[:, 0:1], in1=HNW[:, :], op0=ALU.mult, op1=ALU.mult),
                     reads=[("P", 1), "RO", "HNW"], writes=["T1"])
                yield
                S.op("act", lambda e, b=b: e.activation(out=EG[:, :], in_=GTOK[:, par, b, :], func=AF.Exp, scale=-1.0), reads=[("GTOK", par, b)], writes=["EG"])
                S.op("dve", lambda e: e.tensor_scalar(out=EG[:, :], in0=EG[:, :], scalar1=1.0, scalar2=None, op0=ALU.add), reads=["EG"], writes=["EG"])
                S.op("dve", lambda e: e.reciprocal(out=EG[:, :], in_=EG[:, :]), reads=["EG"], writes=["EG"])
                S.op("dve", lambda e, b=b: e.tensor_tensor(out=GG[:, :], in0=GTOK[:, par, b, :], in1=EG[:, :], op=ALU.mult), reads=["EG", ("GTOK", par, b)], writes=["GG"])
                S.op("dve", lambda e, ob_=ob_: e.tensor_tensor(out=OAT[:, ob_, :], in0=T1[:, :], in1=GG[:, :], op=ALU.mult), reads=["T1", "GG"], writes=[("OAT", ob_)])
                S.op("sp", lambda e, ob_=ob_, b=b: e.dma_start(out=oAv[:, 4 * T + b, :], in_=OAT[:, ob_, :]), reads=[("OAT", ob_)], dma="oa%d" % ob_)
                yield

        def gen_F(T):
            par = T % 2
            tsl = slice(T * 512, (T + 1) * 512)
            nkb = 4 * T + 4

            def s_stage(j):
                lo = 0 if j < 4 * T else 128 * (j - 4 * T)
                sp_ = j % 2
                sps = PS[4 + sp_]
                S.op("pe", lambda e: e.matmul(sps[:, lo:512], lhsT=KT[:, j * 128:(j + 1) * 128], rhs=QT[:, par, lo:512], start=True, stop=False),
                     reads=[("KT", j // 4), ("QT", par)], writes=[("P", 4 + sp_)])
                diag = j >= 4 * T
                S.op("pe", lambda e: e.matmul(sps[:, lo:512], lhsT=E0, rhs=CROW[:, par, lo:512], start=False, stop=(not diag)),
                     reads=[("CROW", par), "CB"], writes=[("P", 4 + sp_)])
                if diag:
                    S.op("pe", lambda e: e.matmul(sps[:, lo:lo + 128], lhsT=IDENT, rhs=MASKNEG, start=False, stop=True),
                         reads=["CB"], writes=[("P", 4 + sp_)])
                S.op("act", lambda e: e.activation(out=PT[:, sp_, lo:512], in_=sps[:, lo:512], func=AF.Exp, bias=BIAS[:, par, j:j + 1]),
                     reads=[("P", 4 + sp_), ("BIAS", par)], writes=[("PT", sp_)])

            def pv_stage(j):
                lo = 0 if j < 4 * T else 128 * (j - 4 * T)
                sp_ = j % 2
                S.op("pe", lambda e: e.matmul(PS[6][:, lo:512], lhsT=V[:, j, :], rhs=PT[:, sp_, lo:512], start=(j == 0), stop=(j == nkb - 1)),
                     reads=[("V", j), ("PT", sp_)], writes=[("P", 6)])
                S.op("pe", lambda e: e.matmul(PS[7][:, lo:512], lhsT=ONESB, rhs=PT[:, sp_, lo:512], start=(j == 0), stop=(j == nkb - 1)),
                     reads=["CB", ("PT", sp_)], writes=[("P", 7)])

            s_stage(0)
            yield
            for j in range(nkb):
                if j + 1 < nkb:
                    s_stage(j + 1)
                pv_stage(j)
                yield
            S.op("dve", lambda e: e.reciprocal(out=ODEN[:, :], in_=PS[7][:, :]), reads=[("P", 7)], writes=["ODEN"])
            S.op("dve", lambda e: e.tensor_tensor(out=OOUT[:, :], in0=PS[6][:, :], in1=ODEN[:, :], op=ALU.mult), reads=[("P", 6), "ODEN"], writes=["OOUT"])
            S.op("sp", lambda e: e.dma_start(out=oBT[:, tsl], in_=OOUT[:, :]), reads=["OOUT"], dma="ob")
            yield

        done = {"A": 0, "H": 0, "F": 0}

        def stream(name, gen_fn, can_start):
            for T in range(NTILES):
                while not can_start(T):
                    yield False
                for _ in gen_fn(T):
                    yield True
                done[name] = T + 1

        streams = [
            stream("A", gen_A, lambda T: done["H"] >= T - 1 and done["F"] >= T - 1),
            stream("H", gen_H, lambda T: done["A"] >= T + 1),
            stream("F", gen_F, lambda T: done["A"] >= T + 1),
        ]
        h_stream = streams[1]
        streams0 = list(streams)
        import os
        _RATES = [int(v) for v in os.environ.get("L1_RATES", "1,1,2").split(",")]
        while streams:
            progressed = False
            for g in list(streams):
                for _rep in range(_RATES[0] if g is streams0[0] else (_RATES[1] if g is h_stream else _RATES[2])):
                    try:
                        if next(g):
                            progressed = True
                    except StopIteration:
                        streams.remove(g)
                        progressed = True
                        break
            assert progressed or not streams
        S.barrier()
        S.emit(nc, block, st)
    return nc


def prep_l1(inp, ntok=SEQ):
    x = np.asarray(inp["x"], np.float32)[0]
    xT = np.ascontiguousarray(x[:ntok].T)
    w_in = np.asarray(inp["w_in"], np.float32)[0]
    nv1 = _colvec(inp["norm_mix_pre"][0])
    p = np.arange(128)
    ones = np.ones((128, 128), np.float32)
    ident = np.eye(128, dtype=np.float32)
    maskneg = np.where(p[:, None] > p[None, :], -30000.0, 0.0).astype(np.float32)
    bdmask = ((p[:, None] // 64 == p[None, :] // 64) & (p[:, None] <= p[None, :])).astype(np.float32)
    e0 = np.zeros((128, 128), np.float32)
    e0[0, :] = 1.0
    cb = np.concatenate([ones, ident, maskneg, bdmask, e0], axis=1).astype(ml_dtypes.bfloat16)
    reset = np.tile((np.arange(512) % 64 != 0).astype(np.float32)[None, :], (128, 1))
    cf = np.ascontiguousarray(np.concatenate([reset, np.ones((128, 512), np.float32)], axis=1))
    lbl_all = np.asarray(inp["hgrn_lb_logits"], np.float32)
    hnw = np.ascontiguousarray(np.tile(np.asarray(inp["hgrn_norm_w"], np.float32)[0][None, :], (128, 1)))
    maps = []
    for c in range(NCORES):
        cs = lambda base: w_in[:, base + c * 128: base + (c + 1) * 128]
        w1 = np.ascontiguousarray(np.concatenate([cs(0), cs(1024), cs(4096), cs(5120), cs(2048), cs(3072), cs(6144)], axis=1))
        wf = np.ascontiguousarray(np.repeat(w_in[:, 7168 + c: 7169 + c], 2, axis=1))
        lbl = np.ascontiguousarray(lbl_all[:, c * 128:(c + 1) * 128].T)
        bfx = np.full((128, 1), np.asarray(inp["b_fox_f"], np.float32)[0, c], np.float32)
        maps.append(dict(xT=xT, w1=w1, wf=wf, nv1=nv1, lbl=lbl, hnw=hnw, bfx=bfx, cb=cb, cf=cf))
    return maps


def post_l1(results):
    oA = np.concatenate([r["oA"] for r in results], axis=1)
    oB = np.concatenate([np.ascontiguousarray(r["oBT"].T) for r in results], axis=1)
    return oA, oB


_CACHE = {}


def kernel(**inputs):
    inputs = {k: np.asarray(v) for k, v in inputs.items()}
    if "l1" not in _CACHE:
        _CACHE["l1"] = build_l1(SEQ // 512)
        _CACHE["l2"] = build_l2()
    cores = list(range(NCORES))
    r1 = run_bass_kernel_spmd(_CACHE["l1"], prep_l1(inputs), core_ids=cores)
    oA, oB = post_l1(r1.results)
    import os
    if os.environ.get("L1ONLY"):
        return np.zeros((1, SEQ, D_MODEL), np.float32)
    r2 = run_bass_kernel_spmd(_CACHE["l2"], prep_l2(inputs, oA, oB), core_ids=cores)
    return post_l2(r2.results).astype(np.float32)
```

```python
import contextlib
import numpy as np
import ml_dtypes
import concourse.bass as bass
import concourse.mybir as mybir
from concourse.bass_utils import run_bass_kernel_spmd

F32 = mybir.dt.float32
BF16 = mybir.dt.bfloat16
ALU = mybir.AluOpType
AF = mybir.ActivationFunctionType

D_MODEL = 2048
SEQ = 8192
D_FF = 5632
EPS = 1e-6
NCORES = 8

ENGS = ("pe", "act", "dve", "pool", "sp")


class Sched:
    def __init__(self):
        self.ops = {e: [] for e in ENGS}
        self.lastw = {}
        self.readers = {}
        self.dma_cnt = {}

    def _tok(self, eng, rec, idx):
        if rec["dma"] is not None:
            return ("dma", rec["dma"], rec["dma_val"])
        return ("eng", eng, idx)

    def op(self, eng, fn, reads=(), writes=(), dma=None):
        idx = len(self.ops[eng])
        rec = dict(fn=fn, dma=dma, needed=False, deps={})
        if dma is not None:
            self.dma_cnt[dma] = self.dma_cnt.get(dma, 0) + 1
            rec["dma_val"] = 16 * self.dma_cnt[dma]
        tok = self._tok(eng, rec, idx)
        deps = rec["deps"]

        def add(t):
            if t is None:
                return
            if t[0] == "dma":
                cur = self.dma_cnt[t[1]] - (1 if dma == t[1] else 0)
                t = ("dma", t[1], 16 * cur)
            if t[0] == "eng" and t[1] == "pe" and eng == "pe" and dma is None:
                return
            k = (t[0], t[1])
            if k not in deps or deps[k] < t[2]:
                deps[k] = t[2]

        for key in reads:
            add(self.lastw.get(key))
        for key in writes:
            add(self.lastw.get(key))
            for t in self.readers.get(key, {}).values():
                add(t)
        self.ops[eng].append(rec)
        for key in writes:
            self.lastw[key] = tok
            self.readers[key] = {}
        for key in reads:
            r = self.readers.setdefault(key, {})
            k = (tok[0], tok[1])
            if k not in r or r[k][2] < tok[2]:
                r[k] = tok
        return tok

    def barrier(self):
        toks = []
        for e in ENGS:
            for i in range(len(self.ops[e]) - 1, -1, -1):
                if self.ops[e][i]["dma"] is None and self.ops[e][i]["fn"] is not None:
                    toks.append(("eng", e, i))
                    break
        for k, c in self.dma_cnt.items():
            toks.append(("dma", k, 16 * c))
        for e in ENGS:
            rec = dict(fn=None, dma=None, needed=False, deps={})
            for t in toks:
                rec["deps"][(t[0], t[1])] = t[2]
            self.ops[e].append(rec)
        self.lastw = {}
        self.readers = {}

    def emit(self, nc, block, stack):
        sems = {}
        for e in ENGS:
            sems[("eng", e)] = stack.enter_context(nc.semaphore("s_" + e))
        for k in self.dma_cnt:
            sems[("dma", k)] = stack.enter_context(nc.semaphore("d_" + str(k)))
        for e in ENGS:
            for rec in self.ops[e]:
                for (kind, name), val in rec["deps"].items():
                    if kind == "eng":
                        self.ops[name][val]["needed"] = True
        vals = {}
        for e in ENGS:
            cnt = 0
            for i, rec in enumerate(self.ops[e]):
                if rec["needed"]:
                    cnt += 1
                    vals[(e, i)] = cnt
            assert cnt < 60000, (e, cnt)
        self.n_emitted = {e: len(self.ops[e]) for e in ENGS}

        def run(e, engine):
            waited = {}
            for i, rec in enumerate(self.ops[e]):
                for (kind, name), val in rec["deps"].items():
                    v = vals[(name, val)] if kind == "eng" else val
                    sk = (kind, name)
                    if waited.get(sk, 0) >= v:
                        continue
                    engine.wait_ge(sems[sk], v)
                    waited[sk] = v
                if rec["fn"] is None:
                    continue
                ins = rec["fn"](engine)
                if rec["dma"] is not None:
                    ins.then_inc(sems[("dma", rec["dma"])], 16)
                elif rec["needed"]:
                    ins.then_inc(sems[("eng", e)], 1)

        @block.tensor
        def _(eng):
            run("pe", eng)

        @block.scalar
        def _(eng):
            run("act", eng)

        @block.vector
        def _(eng):
            run("dve", eng)

        @block.gpsimd
        def _(eng):
            run("pool", eng)

        @block.sync
        def _(eng):
            run("sp", eng)


def _v3(ap, k):
    return ap.rearrange("p (k n) -> p k n", k=k)


def build_l2(NT=1024):
    nc = bass.Bass("TRN2", target_bir_lowering=False)
    S = Sched()
    NTT = NT // 512
    dt = nc.dram_tensor
    xT = dt("xT", [2048, NT], F32, kind="ExternalInput").ap()
    oT = dt("oT", [2048, NT], F32, kind="ExternalInput").ap()
    wg = dt("wg", [2048, 4096], F32, kind="ExternalInput").ap()
    wua = dt("wua", [1024, 2048], F32, kind="ExternalInput").ap()
    wub = dt("wub", [1024, 2048], F32, kind="ExternalInput").ap()
    wo = dt("wo", [2048, 2048], F32, kind="ExternalInput").ap()
    wfi = dt("wfi", [2048, 2 * D_FF], F32, kind="ExternalInput").ap()
    wfd = dt("wfd", [D_FF, 2048], F32, kind="ExternalInput").ap()
    nv = dt("nv", [128, 64], F32, kind="ExternalInput").ap()
    cst = dt("cst", [128, 128], BF16, kind="ExternalInput").ap()
    outT = dt("outT", [2048, NT], F32, kind="ExternalOutput").ap()
    zd = dt("zd", [2048, NT], F32).ap()
    x1d = dt("x1d", [2048, NT], F32).ap()

    def cm(ap):
        return ap.rearrange("(k p) n -> p k n", p=128)

    xTv, oTv, wgv, wuav, wubv, wov, wfiv, wfdv = map(cm, (xT, oT, wg, wua, wub, wo, wfi, wfd))
    outTv, zdv, x1dv = cm(outT), cm(zd), cm(x1d)

    with contextlib.ExitStack() as st:
        sb = lambda name, shape, d: st.enter_context(nc.sbuf_tensor(name, shape, d))
        A = sb("A", [128, 44 * NT], BF16)
        B = sb("B", [128, 16 * NT], BF16)
        W = sb("W", [128, 16384], BF16)
        T = sb("T", [128, 16, 512], F32)
        SQT = sb("SQT", [128, 4, 512], BF16)
        R = sb("R", [128, 2, NT], F32)
        NV = sb("NV", [128, 64], F32)
        ONES = sb("ONES", [128, 128], BF16)
        PS = [st.enter_context(nc.psum_tensor("ps%d" % i, [128, 512], F32)) for i in range(8)]
        block = st.enter_context(nc.Block())

        xb = _v3(A[:, 0:16 * NT], 16)
        ob = _v3(A[:, 16 * NT:32 * NT], 16)
        SP0 = 32 * NT
        SQ = _v3(A[:, SP0 + 8192:SP0 + 12288], 16)
        XS = _v3(W[:, 8192:16384].bitcast(F32), 16)
        hid = _v3(A[:, 0:44 * NT], 44)
        mb = _v3(B[:, :], 16)
        x1b = mb

        S.op("sp", lambda e: e.dma_start(out=NV[:, :], in_=nv[:, :]), writes=["NV"], dma="c0")
        S.op("sp", lambda e: e.dma_start(out=ONES[:, :], in_=cst[:, :]), writes=["ONES"], dma="c1")

        def rstd_from(ps_ap, r_ap, keyp, keyr):
            S.op("dve", lambda e: e.tensor_scalar(out=r_ap, in0=ps_ap, scalar1=1.0 / D_MODEL, scalar2=EPS,
                                                  op0=ALU.mult, op1=ALU.add), reads=[keyp], writes=[keyr])
            S.op("act", lambda e: e.activation(out=r_ap, in_=r_ap, func=AF.Sqrt), reads=[keyr], writes=[keyr])
            S.op("dve", lambda e: e.reciprocal(out=r_ap, in_=r_ap), reads=[keyr], writes=[keyr])

        XS2 = _v3(T[:, 0:8, :].rearrange("p a b -> p (a b)"), 16)
        for s in range(NT // 256):
            cs = slice(s * 256, (s + 1) * 256)
            XS_ = XS if s % 2 == 0 else XS2
            XSK = [("W", 1, 0), ("W", 1, 1)] if s % 2 == 0 else [("T", i) for i in range(8)]
            S.op("sp", lambda e, cs=cs, XS_=XS_: e.dma_start(out=XS_[:, :, :], in_=xTv[:, :, cs]), writes=XSK, dma="xs%d" % (s % 2))
            S.op("act", lambda e, XS_=XS_: e.activation(out=SQ[:, :, :], in_=XS_[:, :, :], func=AF.Square), reads=XSK, writes=["SQ"])
            for k in range(16):
                S.op("dve", lambda e, k=k, cs=cs, XS_=XS_: e.tensor_scalar(out=xb[:, k, cs], in0=XS_[:, k, :], scalar1=NV[:, k:k + 1],
                                                                           scalar2=None, op0=ALU.mult),
                     reads=XSK + ["NV"], writes=[("A0", k, s)])
            for k in range(16):
                S.op("pe", lambda e, k=k: e.matmul(PS[0][:, 0:256], lhsT=ONES[:, :], rhs=SQ[:, k, :], start=(k == 0), stop=(k == 15)),
                     reads=["SQ", "ONES"], writes=[("P", 0)])
            rstd_from(PS[0][:, 0:256], R[:, 0, cs], ("P", 0), ("R0", s))
        for h in range(2):
            S.op("pool", lambda e, h=h: e.dma_start(out=ob[:, 8 * h:8 * h + 8, :], in_=oTv[:, 8 * h:8 * h + 8, :]),
                 writes=[("A1", h)], dma="ob%d" % h)
        xb_keys = [("A0", k, s) for k in range(16) for s in range(NT // 256)]
        r0_keys = [("R0", s) for s in range(NT // 256)]

        for g in range(8):
            s_ = g % 2
            base = s_ * 8192
            wga = _v3(W[:, base:base + 4096], 16)
            wgb = _v3(W[:, base + 4096:base + 8192], 16)
            wa = _v3(A[:, SP0 + s_ * 4096:SP0 + s_ * 4096 + 2048], 8)
            wb = _v3(A[:, SP0 + s_ * 4096 + 2048:SP0 + s_ * 4096 + 4096], 8)
            gs = slice(g * 256, (g + 1) * 256)
            gs2 = slice(2048 + g * 256, 2048 + (g + 1) * 256)
            wk = ("W", s_)
            dk = "w%d" % s_
            S.op("pool", lambda e, wga=wga, gs=gs: e.dma_start(out=wga[:, :, :], in_=wgv[:, :, gs]), writes=[wk + (0,)], dma=dk + "a")
            S.op("pool", lambda e, wgb=wgb, gs2=gs2: e.dma_start(out=wgb[:, :, :], in_=wgv[:, :, gs2]), writes=[wk + (1,)], dma=dk + "b")
            S.op("pool", lambda e, wa=wa, gs=gs: e.dma_start(out=wa[:, :, :], in_=wuav[:, :, gs]), writes=[wk + (2,)], dma=dk + "c")
            S.op("pool", lambda e, wb=wb, gs=gs: e.dma_start(out=wb[:, :, :], in_=wubv[:, :, gs]), writes=[wk + (3,)], dma=dk + "d")
            for nb in range(2):
                n = g * 2 + nb
                ns = slice(nb * 128, (nb + 1) * 128)
                for t in range(NTT):
                    ts_ = slice(t * 512, (t + 1) * 512)
                    par = (n * NTT + t) % 2
                    pga, pgb, pya, pyb = (PS[4 * par + i] for i in range(4))
                    kga, kgb, kya, kyb = (("P", 4 * par + i) for i in range(4))
                    t0, t1 = T[:, 2 * par, :], T[:, 2 * par + 1, :]
                    k0, k1 = ("T", 2 * par), ("T", 2 * par + 1)
                    for k in range(16):
                        S.op("pe", lambda e, k=k, pga=pga, wga=wga, ns=ns, ts_=ts_: e.matmul(pga[:, :], lhsT=wga[:, k, ns], rhs=xb[:, k, ts_], start=(k == 0), stop=(k == 15)),
                             reads=[wk + (0,)] + xb_keys, writes=[kga])
                    for k in range(16):
                        S.op("pe", lambda e, k=k, pgb=pgb, wgb=wgb, ns=ns, ts_=ts_: e.matmul(pgb[:, :], lhsT=wgb[:, k, ns], rhs=xb[:, k, ts_], start=(k == 0), stop=(k == 15)),
                             reads=[wk + (1,)] + xb_keys, writes=[kgb])
                    for k in range(8):
                        S.op("pe", lambda e, k=k, pya=pya, wa=wa, ns=ns, ts_=ts_: e.matmul(pya[:, :], lhsT=wa[:, k, ns], rhs=ob[:, k, ts_], start=(k == 0), stop=(k == 7)),
                             reads=[wk + (2,), ("A1", 0)], writes=[kya])
                    for k in range(8):
                        S.op("pe", lambda e, k=k, pyb=pyb, wb=wb, ns=ns, ts_=ts_: e.matmul(pyb[:, :], lhsT=wb[:, k, ns], rhs=ob[:, 8 + k, ts_], start=(k == 0), stop=(k == 7)),
                             reads=[wk + (3,), ("A1", 1)], writes=[kyb])
                    S.op("dve", lambda e, t0=t0, pga=pga, ts_=ts_: e.tensor_tensor(out=t0, in0=pga[:, :], in1=R[:, 0, ts_], op=ALU.mult),
                         reads=[kga] + r0_keys, writes=[k0])
                    S.op("act", lambda e, t0=t0: e.activation(out=t0, in_=t0, func=AF.Sigmoid), reads=[k0], writes=[k0])
                    S.op("dve", lambda e, t1=t1, pgb=pgb, ts_=ts_: e.tensor_tensor(out=t1, in0=pgb[:, :], in1=R[:, 0, ts_], op=ALU.mult),
                         reads=[kgb] + r0_keys, writes=[k1])
                    S.op("act", lambda e, t1=t1: e.activation(out=t1, in_=t1, func=AF.Sigmoid), reads=[k1], writes=[k1])
                    S.op("dve", lambda e, t0=t0, pya=pya: e.tensor_tensor(out=t0, in0=pya[:, :], in1=t0, op=ALU.mult),
                         reads=[kya, k0], writes=[k0])
                    S.op("dve", lambda e, t1=t1, pyb=pyb: e.tensor_tensor(out=t1, in0=pyb[:, :], in1=t1, op=ALU.mult),
                         reads=[kyb, k1], writes=[k1])
                    S.op("dve", lambda e, t0=t0, t1=t1, n=n, ts_=ts_: e.tensor_tensor(out=mb[:, n, ts_], in0=t0, in1=t1, op=ALU.add),
                         reads=[k0, k1], writes=[("B", n, t)])

        S.barrier()
        ZS = _v3(A[:, 0:32 * NT].bitcast(F32), 16)

        def evac_stats(ps, pkey, nchunk, t, ssb, first, last, dram_v, par):
            zt = T[:, 4 + par, :]
            zk = ("T", 4 + par)
            sq = SQT[:, par, :]
            sk = ("SQT", par)
            S.op("act", lambda e: e.activation(out=zt, in_=ps[:, :], func=AF.Copy), reads=[pkey], writes=[zk])
            S.op("act", lambda e: e.activation(out=sq, in_=ps[:, :], func=AF.Square), reads=[pkey], writes=[sk])
            S.op("sp", lambda e: e.dma_start(out=dram_v[:, nchunk, t * 512:(t + 1) * 512], in_=zt), reads=[zk],
                 writes=[("zd", nchunk, t)], dma="zst%d" % par)
            return lambda: S.op("pe", lambda e: e.matmul(PS[ssb][:, :], lhsT=ONES[:, :], rhs=sq, start=first, stop=last),
                                reads=[sk, "ONES"], writes=[("P", ssb)])

        pend = []
        for g in range(8):
            s_ = g % 2
            base = s_ * 8192
            wos = _v3(W[:, base:base + 4096], 16)
            wk = ("W", s_)
            gs = slice(g * 256, (g + 1) * 256)
            S.op("pool", lambda e, wos=wos, gs=gs: e.dma_start(out=wos[:, :, :], in_=wov[:, :, gs]), writes=[wk], dma="w%d" % s_)
            for nb in range(2):
                n = g * 2 + nb
                ns = slice(nb * 128, (nb + 1) * 128)
                for t in range(NTT):
                    ts_ = slice(t * 512, (t + 1) * 512)
                    pb = (n * NTT + t) % 4
                    par = (n * NTT + t) % 2
                    for k in range(16):
                        S.op("pe", lambda e, k=k, pb=pb, wos=wos, ns=ns, ts_=ts_: e.matmul(PS[pb][:, :], lhsT=wos[:, k, ns], rhs=mb[:, k, ts_], start=(k == 0), stop=(k == 15)),
                             reads=[wk] + [("B", k, t) for k in range(16)], writes=[("P", pb)])
                    for f_ in pend:
                        f_()
                    pend.clear()
                    sq = SQT[:, par, :]
                    S.op("act", lambda e, pb=pb, n=n, ts_=ts_: e.activation(out=ZS[:, n, ts_], in_=PS[pb][:, :], func=AF.Copy), reads=[("P", pb)], writes=[("ZS", n, t)])
                    S.op("act", lambda e, pb=pb, sq=sq: e.activation(out=sq, in_=PS[pb][:, :], func=AF.Square), reads=[("P", pb)], writes=[("SQT", par)])
                    pend.append(lambda sq=sq, t=t, n=n, par=par: S.op("pe", lambda e: e.matmul(PS[6 + t][:, :], lhsT=ONES[:, :], rhs=sq, start=(n == 0), stop=(n == 15)),
                                                                       reads=[("SQT", par), "ONES"], writes=[("P", 6 + t)]))
        for f_ in pend:
            f_()
        pend.clear()
        for t in range(NTT):
            rstd_from(PS[6 + t][:, :], R[:, 1, t * 512:(t + 1) * 512], ("P", 6 + t), ("R1", t))

        for t in range(NTT):
            ts_ = slice(t * 512, (t + 1) * 512)
            for n in range(16):
                sl4 = n % 8
                par = n % 2
                ta = T[:, sl4, :]
                ka = ("T", sl4)
                S.op("sp", lambda e, ta=ta, n=n, ts_=ts_: e.dma_start(out=ta, in_=xTv[:, n, ts_]), writes=[ka], dma="ra%d" % sl4)
                S.op("dve", lambda e, n=n, ts_=ts_: e.scalar_tensor_tensor(out=ZS[:, n, ts_], in0=ZS[:, n, ts_], scalar=NV[:, 16 + n:17 + n], in1=R[:, 1, ts_],
                                                                            op0=ALU.mult, op1=ALU.mult), reads=[("ZS", n, t), "NV", ("R1", t)], writes=[("ZS", n, t)])
                S.op("dve", lambda e, ta=ta, n=n, ts_=ts_: e.tensor_tensor(out=ta, in0=ta, in1=ZS[:, n, ts_], op=ALU.add), reads=[ka, ("ZS", n, t)], writes=[ka])
                S.op("pool", lambda e, ta=ta, n=n, ts_=ts_: e.dma_start(out=x1dv[:, n, ts_], in_=ta), reads=[ka], writes=[("x1d", n, t)], dma="x1st%d" % sl4)
                sq = SQT[:, 2 + par, :]
                sk = ("SQT", 2 + par)
                S.op("dve", lambda e, ta=ta, n=n, ts_=ts_: e.tensor_scalar(out=x1b[:, n, ts_], in0=ta, scalar1=NV[:, 32 + n:33 + n], scalar2=None, op0=ALU.mult),
                     reads=[ka, "NV"], writes=[("B", n, t)])
                S.op("act", lambda e, ta=ta, sq=sq: e.activation(out=sq, in_=ta, func=AF.Square), reads=[ka], writes=[sk])
                S.op("pe", lambda e, sq=sq, n=n, t=t: e.matmul(PS[4 + t][:, :], lhsT=ONES[:, :], rhs=sq, start=(n == 0), stop=(n == 15)),
                     reads=[sk, "ONES"], writes=[("P", 4 + t)])
        for t in range(NTT):
            rstd_from(PS[4 + t][:, :], R[:, 0, t * 512:(t + 1) * 512], ("P", 4 + t), ("R2", t))

        S.barrier()
        xk = {t: [("B", k, t) for k in range(16)] for t in range(NTT)}
        for g in range(22):
            s_ = g % 2
            base = s_ * 8192
            wfg = _v3(W[:, base:base + 4096], 16)
            wfu = _v3(W[:, base + 4096:base + 8192], 16)
            wk = ("W", s_)
            gs = slice(g * 256, (g + 1) * 256)
            gs2 = slice(D_FF + g * 256, D_FF + (g + 1) * 256)
            S.op("pool", lambda e, wfg=wfg, gs=gs: e.dma_start(out=wfg[:, :, :], in_=wfiv[:, :, gs]), writes=[wk + (0,)], dma="w%da" % s_)
            S.op("pool", lambda e, wfu=wfu, gs2=gs2: e.dma_start(out=wfu[:, :, :], in_=wfiv[:, :, gs2]), writes=[wk + (1,)], dma="w%db" % s_)
            for nb in range(2):
                f = g * 2 + nb
                ns = slice(nb * 128, (nb + 1) * 128)
                for t in range(NTT):
                    ts_ = slice(t * 512, (t + 1) * 512)
                    par = (f * NTT + t) % 2
                    pg_, pu_ = PS[2 * par], PS[2 * par + 1]
                    kg_, ku_ = ("P", 2 * par), ("P", 2 * par + 1)
                    t0, t1 = T[:, 4 + 2 * par, :], T[:, 5 + 2 * par, :]
                    k0, k1 = ("T", 4 + 2 * par), ("T", 5 + 2 * par)
                    for k in range(16):
                        S.op("pe", lambda e, k=k, pg_=pg_, wfg=wfg, ns=ns, ts_=ts_: e.matmul(pg_[:, :], lhsT=wfg[:, k, ns], rhs=x1b[:, k, ts_], start=(k == 0), stop=(k == 15)),
                             reads=[wk + (0,)] + xk[t], writes=[kg_])
                    for k in range(16):
                        S.op("pe", lambda e, k=k, pu_=pu_, wfu=wfu, ns=ns, ts_=ts_: e.matmul(pu_[:, :], lhsT=wfu[:, k, ns], rhs=x1b[:, k, ts_], start=(k == 0), stop=(k == 15)),
                             reads=[wk + (1,)] + xk[t], writes=[ku_])
                    S.op("dve", lambda e, t0=t0, pg_=pg_, ts_=ts_: e.tensor_tensor(out=t0, in0=pg_[:, :], in1=R[:, 0, ts_], op=ALU.mult),
                         reads=[kg_, ("R2", t)], writes=[k0])
                    S.op("act", lambda e, t0=t0: e.activation(out=t0, in_=t0, func=AF.Silu), reads=[k0], writes=[k0])
                    S.op("dve", lambda e, t1=t1, pu_=pu_, ts_=ts_: e.tensor_tensor(out=t1, in0=pu_[:, :], in1=R[:, 0, ts_], op=ALU.mult),
                         reads=[ku_, ("R2", t)], writes=[k1])
                    S.op("dve", lambda e, t0=t0, t1=t1, f=f, ts_=ts_: e.tensor_tensor(out=hid[:, f, ts_], in0=t0, in1=t1, op=ALU.mult),
                         reads=[k0, k1], writes=[("hid", f, t)])

        S.barrier()
        for g in range(8):
            s_ = g % 2
            wd = _v3((W if s_ == 0 else B)[:, 0:11264], 44)
            wk = ("W", s_)
            gs = slice(g * 256, (g + 1) * 256)
            for hh in range(2):
                S.op("pool", lambda e, wd=wd, gs=gs, hh=hh: e.dma_start(out=wd[:, 22 * hh:22 * hh + 22, :], in_=wfdv[:, 22 * hh:22 * hh + 22, gs]),
                     writes=[wk + (hh,)], dma="w%d%s" % (s_, "ab"[hh]))
            for nb in range(2):
                n = g * 2 + nb
                ns = slice(nb * 128, (nb + 1) * 128)
                for t in range(NTT):
                    ts_ = slice(t * 512, (t + 1) * 512)
                    pb = (n * NTT + t) % 4
                    for k in range(44):
                        S.op("pe", lambda e, k=k, pb=pb, wd=wd, ns=ns, ts_=ts_: e.matmul(PS[pb][:, :], lhsT=wd[:, k, ns], rhs=hid[:, k, ts_], start=(k == 0), stop=(k == 43)),
                             reads=[wk + (0,), wk + (1,)] + [("hid", f, t) for f in range(44)], writes=[("P", pb)])
                    for f_ in pend:
                        f_()
                    pend.clear()
                    pend.append(evac_stats(PS[pb], ("P", pb), n, t, 6 + t, n == 0, n == 15, zdv, (n * NTT + t) % 2))
        for f_ in pend:
            f_()
        pend.clear()
        for t in range(NTT):
            rstd_from(PS[6 + t][:, :], R[:, 1, t * 512:(t + 1) * 512], ("P", 6 + t), ("R3", t))
        S.barrier()
        for t in range(NTT):
            ts_ = slice(t * 512, (t + 1) * 512)
            for n in range(16):
                sl4 = n % 8
                ta, tb = T[:, sl4, :], T[:, 8 + sl4, :]
                ka, kb = ("T", sl4), ("T", 8 + sl4)
                S.op("sp", lambda e, ta=ta, n=n, ts_=ts_: e.dma_start(out=ta, in_=x1dv[:, n, ts_]), reads=[("x1d", n, t)], writes=[ka], dma="ra%d" % sl4)
                S.op("act", lambda e, tb=tb, n=n, ts_=ts_: e.dma_start(out=tb, in_=zdv[:, n, ts_]), reads=[("zd", n, t)], writes=[kb], dma="rb%d" % sl4)
                S.op("dve", lambda e, tb=tb, n=n, ts_=ts_: e.scalar_tensor_tensor(out=tb, in0=tb, scalar=NV[:, 48 + n:49 + n], in1=R[:, 1, ts_],
                                                                                   op0=ALU.mult, op1=ALU.mult), reads=[kb, "NV", ("R3", t)], writes=[kb])
                S.op("dve", lambda e, ta=ta, tb=tb: e.tensor_tensor(out=ta, in0=ta, in1=tb, op=ALU.add), reads=[ka, kb], writes=[ka])
                S.op("pool", lambda e, ta=ta, n=n, ts_=ts_: e.dma_start(out=outTv[:, n, ts_], in_=ta), reads=[ka], writes=[("out", n, t)], dma="ost%d" % sl4)

        S.barrier()
        S.emit(nc, block, st)
    return nc


def _colvec(v):
    return np.ascontiguousarray(np.asarray(v, np.float32).reshape(16, 128).T)


def prep_l2(inp, oA, oB):
    x = np.asarray(inp["x"], np.float32)[0]
    w_in = np.asarray(inp["w_in"], np.float32)[0]
    wg = np.ascontiguousarray(w_in[:, 7176:7176 + 4096])
    nv = np.ascontiguousarray(np.concatenate([_colvec(inp["norm_mix_pre"][0]), _colvec(inp["norm_mix_post"][0]),
                                              _colvec(inp["norm_ffn_pre"][0]), _colvec(inp["norm_ffn_post"][0])], axis=1))
    ones = np.ones((128, 128), ml_dtypes.bfloat16)
    common = dict(wg=wg, wua=np.ascontiguousarray(inp["w_up_a"][0], dtype=np.float32), wub=np.ascontiguousarray(inp["w_up_b"][0], dtype=np.float32),
                  wo=np.ascontiguousarray(inp["w_o"][0], dtype=np.float32), wfi=np.ascontiguousarray(inp["w_ffn_in"][0], dtype=np.float32),
                  wfd=np.ascontiguousarray(inp["w_ffn_down"][0], dtype=np.float32), nv=nv, cst=ones)
    maps = []
    for c in range(NCORES):
        sl = slice(c * 1024, (c + 1) * 1024)
        oT = np.ascontiguousarray(np.concatenate([oA[sl].T, oB[sl].T], axis=0), dtype=np.float32)
        maps.append(dict(common, xT=np.ascontiguousarray(x[sl].T), oT=oT))
    return maps


def post_l2(results):
    return np.concatenate([np.ascontiguousarray(r["outT"].T) for r in results], axis=0)[None]


def build_l1(NTILES=16, stage=9):
    nc = bass.Bass("TRN2", target_bir_lowering=False)
    S = Sched()
    NTOK = NTILES * 512
    dt = nc.dram_tensor
    xT = dt("xT", [2048, NTOK], F32, kind="ExternalInput").ap()
    w1 = dt("w1", [2048, 896], F32, kind="ExternalInput").ap()
    wf = dt("wf", [2048, 2], F32, kind="ExternalInput").ap()
    nv1 = dt("nv1", [128, 16], F32, kind="ExternalInput").ap()
    lbl = dt("lbl", [128, 2], F32, kind="ExternalInput").ap()
    hnw = dt("hnw", [128, 128], F32, kind="ExternalInput").ap()
    bfx = dt("bfx", [128, 1], F32, kind="ExternalInput").ap()
    cb = dt("cb", [128, 640], BF16, kind="ExternalInput").ap()
    cf = dt("cf", [128, 1024], F32, kind="ExternalInput").ap()
    oA = dt("oA", [NTOK, 128], F32, kind="ExternalOutput").ap()
    oBT = dt("oBT", [128, NTOK], F32, kind="ExternalOutput").ap()
    xTv = xT.rearrange("(k p) n -> p k n", p=128)
    w1v = w1.rearrange("(k p) n -> p k n", p=128)
    wfv = wf.rearrange("(k p) n -> p k n", p=128)
    oAv = oA.rearrange("(b p) v -> p b v", p=128)

    with contextlib.ExitStack() as st:
        sb = lambda name, shape, d: st.enter_context(nc.sbuf_tensor(name, shape, d))
        XS = sb("XS", [128, 2, 8, 512], F32)
        XB = sb("XB", [128, 16, 512], BF16)
        SQ = sb("SQ", [128, 16, 512], BF16)
        W1 = sb("W1", [128, 16, 896], BF16)
        WF = sb("WF", [128, 16, 2], BF16)
        NV = sb("NV", [128, 16], F32)
        LBL = sb("LBL", [128, 2], F32)
        LB = sb("LB", [128, 2], F32)
        HNW = sb("HNW", [128, 128], F32)
        BFX = sb("BFX", [128, 1], F32)
        CB = sb("CB", [128, 640], BF16)
        CF = sb("CF", [128, 1024], F32)
        RT = sb("RT", [128, 512], F32)
        RC = sb("RC", [128, 4], F32)
        KT = sb("KT", [128, NTOK], BF16)
        V = sb("V", [128, NTILES * 4, 128], BF16)
        QT = sb("QT", [128, 2, 512], BF16)
        TQ = sb("TQ", [128, 512], F32)
        TZ = sb("TZ", [128, 512], F32)
        TE = sb("TE", [128, 512], F32)
        TS = sb("TS", [128, 512], F32)
        TF = sb("TF", [128, 512], F32)
        TK = sb("TK", [128, 512], F32)
        TB = sb("TB", [128, 512], F32)
        TD = sb("TD", [128, 512], F32)
        TE1 = sb("TE1", [128, 512], F32)
        TE2 = sb("TE2", [128, 512], F32)
        QTB = sb("QTB", [128, 512], BF16)
        KTB = sb("KTB", [128, 512], BF16)
        KTOK = sb("KTOK", [128, 4, 128], BF16)
        VTOK = sb("VTOK", [128, 2, 4, 128], BF16)
        GTOK = sb("GTOK", [128, 2, 4, 128], F32)
        EBM = sb("EBM", [128, 8], F32)
        EBL = sb("EBL", [128, 8], F32)
        EBLM = sb("EBLM", [128, 8], F32)
        BDM = sb("BDM", [128, 8], F32)
        ST = sb("ST", [128, 128], F32)
        SM = sb("SM", [128, 2, 128], BF16)
        SCB = sb("SCB", [128, 128], BF16)
        TKV = sb("TKV", [128, 128], F32)
        T1 = sb("T1", [128, 128], F32)
        TSC = sb("TSC", [128, 128], F32)
        EG = sb("EG", [128, 128], F32)
        GG = sb("GG", [128, 128], F32)
        OAT = sb("OAT", [128, 2, 128], F32)
        SSO = sb("SSO", [128, 1], F32)
        RO = sb("RO", [128, 1], F32)
        PT = sb("PT", [128, 2, 512], BF16)
        CROW = sb("CROW", [128, 2, 512], BF16)
        LROW = sb("LROW", [1, 512], F32)
        ZF = sb("ZF", [1, 512], F32)
        POSC = sb("POSC", [128, NTILES * 4], F32)
        RP = sb("RP", [128, NTILES + 1], F32)
        BIAS = sb("BIAS", [128, 2, NTILES * 4], F32)
        ODEN = sb("ODEN", [128, 512], F32)
        OOUT = sb("OOUT", [128, 512], F32)
        PS = [st.enter_context(nc.psum_tensor("ps%d" % i, [128, 512], F32)) for i in range(8)]
        block = st.enter_context(nc.Block())

        ONESB, IDENT, MASKNEG, BDMASK, E0 = CB[:, 0:128], CB[:, 128:256], CB[:, 256:384], CB[:, 384:512], CB[:, 512:640]
        RESET, ONESF = CF[:, 0:512], CF[:, 512:1024]

        def ld(eng, dst, src, key, dk):
            S.op(eng, lambda e: e.dma_start(out=dst, in_=src), writes=[key], dma=dk)

        ld("sp", NV[:, :], nv1[:, :], "NV", "c0")
        ld("sp", LBL[:, :], lbl[:, :], "LBL", "c1")
        ld("sp", HNW[:, :], hnw[:, :], "HNW", "c2")
        ld("sp", BFX[:, :], bfx[:, :], "BFX", "c3")
        ld("sp", CB[:, :], cb[:, :], "CB", "c4")
        ld("sp", CF[:, :], cf[:, :], "CF", "c5")
        ld("pool", W1[:, 0:8, :], w1v[:, 0:8, :], ("W1", 0), "w1a")
        ld("pool", W1[:, 8:16, :], w1v[:, 8:16, :], ("W1", 1), "w1b")
        ld("pool", WF[:, :, :], wfv[:, :, :], "WF", "wf")
        W1K = [("W1", 0), ("W1", 1)]

        S.op("dve", lambda e: e.tensor_tensor(out=LB[:, 0:1], in0=LBL[:, 0:1], in1=LBL[:, 1:2], op=ALU.subtract), reads=["LBL"], writes=["LB"])
        S.op("act", lambda e: e.activation(out=LB[:, 1:2], in_=LB[:, 0:1], func=AF.Exp, scale=-1.0), reads=["LB"], writes=["LB"])
        S.op("dve", lambda e: e.tensor_scalar(out=LB[:, 0:1], in0=LB[:, 1:2], scalar1=1.0, scalar2=None, op0=ALU.add), reads=["LB"], writes=["LB"])
        S.op("dve", lambda e: e.reciprocal(out=LB[:, 0:1], in_=LB[:, 0:1]), reads=["LB"], writes=["LB"])
        S.op("dve", lambda e: e.tensor_tensor(out=LB[:, 1:2], in0=LB[:, 1:2], in1=LB[:, 0:1], op=ALU.mult), reads=["LB"], writes=["LB"])
        S.op("dve", lambda e: e.memset(ST[:, :], 0.0), writes=["ST"])
        S.op("dve", lambda e: e.memset(RP[:, :], 0.0), writes=["RP"])
        S.op("dve", lambda e: e.memset(CROW[:, :, :], 0.0), writes=[("CROW", 0), ("CROW", 1)])

        def rstd(ps_ap, r_ap, pkey, rkey, n):
            S.op("dve", lambda e: e.tensor_scalar(out=r_ap, in0=ps_ap, scalar1=1.0 / n, scalar2=EPS, op0=ALU.mult, op1=ALU.add),
                 reads=[pkey], writes=[rkey])
            S.op("act", lambda e: e.activation(out=r_ap, in_=r_ap, func=AF.Ln), reads=[rkey], writes=[rkey])
            S.op("act", lambda e: e.activation(out=r_ap, in_=r_ap, func=AF.Exp, scale=-0.5), reads=[rkey], writes=[rkey])

        def gen_A(T):
            par = T % 2
            tsl = slice(T * 512, (T + 1) * 512)
            for h in range(2):
                S.op("sp", lambda e, h=h: e.dma_start(out=XS[:, h, :, :], in_=xTv[:, 8 * h:8 * h + 8, tsl]), writes=[("XS", h)], dma="xs%d" % h)
            yield
            for h in range(2):
                for k in range(8):
                    S.op("act", lambda e, h=h, k=k: e.activation(out=SQ[:, 8 * h + k, :], in_=XS[:, h, k, :], func=AF.Square),
                         reads=[("XS", h)], writes=[("SQ", h, k)])
                    if k % 2 == 0:
                        S.op("dve", lambda e, h=h, k=k: e.tensor_scalar(out=XB[:, 8 * h + k, :], in0=XS[:, h, k, :], scalar1=NV[:, 8 * h + k:8 * h + k + 1],
                                                                         scalar2=None, op0=ALU.mult), reads=[("XS", h), "NV"], writes=[("XB", h, k)])
                    else:
                        S.op("act", lambda e, h=h, k=k: e.activation(out=XB[:, 8 * h + k, :], in_=XS[:, h, k, :], func=AF.Copy, scale=NV[:, 8 * h + k:8 * h + k + 1]),
                             reads=[("XS", h), "NV"], writes=[("XB", h, k)])
                        yield
            SQK = [("SQ", h, k) for h in range(2) for k in range(8)]
            XBK = [("XB", h, k) for h in range(2) for k in range(8)]
            for k in range(16):
                S.op("pe", lambda e, k=k: e.matmul(PS[0][:, :], lhsT=ONESB, rhs=SQ[:, k, :], start=(k == 0), stop=(k == 15)),
                     reads=SQK + ["CB"], writes=[("P", 0)])
            rstd(PS[0][:, :], RT[:, :], ("P", 0), "RT", D_MODEL)
            yield
            for b in range(4):
                for k in range(16):
                    S.op("pe", lambda e, k=k, b=b: e.matmul(PS[3][:, 400 + b:401 + b], lhsT=SQ[:, k, b * 128:(b + 1) * 128], rhs=ONESB[:, 0:1],
                                                            start=(k == 0), stop=(k == 15)), reads=SQK + ["CB"], writes=[("P", 3)])
            rstd(PS[3][:, 400:404], RC[:, :], ("P", 3), "RC", D_MODEL)
            yield
            for j in range(4):
                pa = 0 if j % 2 == 0 else 3
                for k in range(16):
                    S.op("pe", lambda e, k=k, j=j, pa=pa: e.matmul(PS[pa][:, :], lhsT=W1[:, k, j * 128:(j + 1) * 128], rhs=XB[:, k, :],
                                                            start=(k == 0), stop=(k == 15)), reads=W1K + XBK, writes=[("P", pa)])
                if j == 0:
                    S.op("dve", lambda e, pa=pa: e.tensor_tensor(out=TQ[:, :], in0=PS[pa][:, :], in1=RT[:, :], op=ALU.mult), reads=[("P", pa), "RT"], writes=["TQ"])
                elif j == 1:
                    S.op("dve", lambda e, pa=pa: e.tensor_tensor(out=TZ[:, :], in0=PS[pa][:, :], in1=RT[:, :], op=ALU.mult), reads=[("P", pa), "RT"], writes=["TZ"])
                elif j == 2:
                    S.op("dve", lambda e, pa=pa: e.scalar_tensor_tensor(out=QT[:, par, :], in0=PS[pa][:, :], scalar=128.0 ** -0.5, in1=RT[:, :], op0=ALU.mult, op1=ALU.mult),
                         reads=[("P", pa), "RT"], writes=[("QT", par)])
                else:
                    S.op("dve", lambda e, pa=pa: e.tensor_tensor(out=KT[:, tsl], in0=PS[pa][:, :], in1=RT[:, :], op=ALU.mult), reads=[("P", pa), "RT"], writes=[("KT", T)])
                yield
            for b in range(4):
                pa = 0 if b % 2 == 0 else 3
                for k in range(16):
                    S.op("pe", lambda e, k=k, b=b, pa=pa: e.matmul(PS[pa][:, 0:384], lhsT=XB[:, k, b * 128:(b + 1) * 128], rhs=W1[:, k, 512:896],
                                                            start=(k == 0), stop=(k == 15)), reads=W1K + XBK, writes=[("P", pa)])
                S.op("dve", lambda e, b=b, pa=pa: e.tensor_scalar(out=VTOK[:, par, b, :], in0=PS[pa][:, 0:128], scalar1=RC[:, b:b + 1], scalar2=None, op0=ALU.mult),
                     reads=[("P", pa), "RC"], writes=[("VTOK", par, b)])
                S.op("dve", lambda e, b=b, pa=pa: e.tensor_scalar(out=GTOK[:, par, b, :], in0=PS[pa][:, 128:256], scalar1=RC[:, b:b + 1], scalar2=None, op0=ALU.mult),
                     reads=[("P", pa), "RC"], writes=[("GTOK", par, b)])
                S.op("dve", lambda e, b=b, pa=pa: e.tensor_scalar(out=V[:, 4 * T + b, :], in0=PS[pa][:, 256:384], scalar1=RC[:, b:b + 1], scalar2=None, op0=ALU.mult),
                     reads=[("P", pa), "RC"], writes=[("V", 4 * T + b)])
                yield
            for k in range(16):
                S.op("pe", lambda e, k=k: e.matmul(PS[0][0:1, :], lhsT=WF[:, k, 0:1], rhs=XB[:, k, :], start=(k == 0), stop=(k == 15)),
                     reads=["WF"] + XBK, writes=[("P", 0)])
            S.op("dve", lambda e: e.tensor_tensor(out=ZF[0:1, :], in0=PS[0][0:1, :], in1=RT[0:1, :], op=ALU.mult), reads=[("P", 0), "RT"], writes=["ZF"])
            S.op("dve", lambda e: e.tensor_scalar(out=ZF[0:1, :], in0=ZF[0:1, :], scalar1=BFX[0:1, 0:1], scalar2=None, op0=ALU.add), reads=["ZF", "BFX"], writes=["ZF"])
            S.op("act", lambda e: e.activation(out=ZF[0:1, :], in_=ZF[0:1, :], func=AF.Exp, scale=-1.0), reads=["ZF"], writes=["ZF"])
            S.op("act", lambda e: e.activation(out=ZF[0:1, :], in_=ZF[0:1, :], func=AF.Ln, bias=1.0), reads=["ZF"], writes=["ZF"])
            S.op("dve", lambda e: e.tensor_tensor_scan(out=LROW[0:1, :], data0=ONESF[0:1, :], data1=ZF[0:1, :], initial=0.0, op0=ALU.mult, op1=ALU.add),
                 reads=["ZF", "CF"], writes=["LROW"])
            S.op("dve", lambda e: e.tensor_scalar(out=CROW[0:1, par, :], in0=LROW[0:1, :], scalar1=-1.0, scalar2=None, op0=ALU.mult), reads=["LROW"], writes=[("CROW", par)])
            yield
            for b in range(4):
                S.op("pe", lambda e, b=b: e.matmul(PS[3][:, 384 + b:385 + b], lhsT=LROW[0:1, b * 128:(b + 1) * 128], rhs=ONESF[0:1, 0:1], start=True, stop=True),
                     reads=["LROW", "CF"], writes=[("P", 3)])
            S.op("pe", lambda e: e.matmul(PS[3][:, 388:389], lhsT=ONESF[0:1, 0:128], rhs=LROW[0:1, 511:512], start=True, stop=True),
                 reads=["LROW", "CF"], writes=[("P", 3)])
            S.op("dve", lambda e: e.tensor_scalar(out=POSC[:, 4 * T:4 * T + 4], in0=PS[3][:, 384:388], scalar1=RP[:, T:T + 1], scalar2=None, op0=ALU.add),
                 reads=[("P", 3), "RP"], writes=["POSC"])
            S.op("dve", lambda e: e.tensor_tensor(out=RP[:, T + 1:T + 2], in0=PS[3][:, 388:389], in1=RP[:, T:T + 1], op=ALU.add),
                 reads=[("P", 3), "RP"], writes=["RP"])
            S.op("dve", lambda e: e.tensor_scalar(out=BIAS[:, par, 0:4 * T + 4], in0=POSC[:, 0:4 * T + 4], scalar1=RP[:, T:T + 1], scalar2=None, op0=ALU.subtract),
                 reads=["POSC", "RP"], writes=[("BIAS", par)])
            yield

        def gen_H(T):
            par = T % 2
            S.op("act", lambda e: e.activation(out=TE[:, :], in_=TZ[:, :], func=AF.Exp, scale=-1.0), reads=["TZ"], writes=["TE"])
            S.op("dve", lambda e: e.tensor_scalar(out=TS[:, :], in0=TE[:, :], scalar1=1.0, scalar2=None, op0=ALU.add), reads=["TE"], writes=["TS"])
            S.op("dve", lambda e: e.reciprocal(out=TS[:, :], in_=TS[:, :]), reads=["TS"], writes=["TS"])
            S.op("dve", lambda e: e.tensor_scalar(out=TF[:, :], in0=TS[:, :], scalar1=LB[:, 1:2], scalar2=LB[:, 0:1], op0=ALU.mult, op1=ALU.add),
                 reads=["TS", "LB"], writes=["TF"])
            yield
            S.op("act", lambda e: e.activation(out=TF[:, :], in_=TF[:, :], func=AF.Ln), reads=["TF"], writes=["TF"])
            S.op("dve", lambda e: e.scalar_tensor_tensor(out=TK[:, :], in0=TE[:, :], scalar=LB[:, 1:2], in1=TS[:, :], op0=ALU.mult, op1=ALU.mult),
                 reads=["TE", "TS", "LB"], writes=["TK"])
            S.op("dve", lambda e: e.tensor_tensor_scan(out=TB[:, :], data0=RESET, data1=TF[:, :], initial=0.0, op0=ALU.mult, op1=ALU.add),
                 reads=["TF", "CF"], writes=["TB"])
            yield
            TBv = TB[:, :].rearrange("p (c n) -> p c n", n=64)
            TDv = TD[:, :].rearrange("p (c n) -> p c n", n=64)
            S.op("dve", lambda e: e.tensor_tensor(out=TDv, in0=TBv, in1=TBv[:, :, 31:32].to_broadcast([128, 8, 64]), op=ALU.subtract), reads=["TB"], writes=["TD"])
            S.op("act", lambda e: e.activation(out=TE1[:, :], in_=TD[:, :], func=AF.Exp), reads=["TD"], writes=["TE1"])
            S.op("act", lambda e: e.activation(out=TE2[:, :], in_=TD[:, :], func=AF.Exp, scale=-1.0), reads=["TD"], writes=["TE2"])
            yield
            S.op("dve", lambda e: e.tensor_tensor(out=QTB[:, :], in0=TQ[:, :], in1=TE1[:, :], op=ALU.mult), reads=["TQ", "TE1"], writes=["QTB"])
            S.op("dve", lambda e: e.tensor_tensor(out=KTB[:, :], in0=TK[:, :], in1=TE2[:, :], op=ALU.mult), reads=["TK", "TE2"], writes=["KTB"])
            S.op("act", lambda e: e.activation(out=EBM[:, :], in_=TBv[:, :, 31], func=AF.Exp), reads=["TB"], writes=["EBM"])
            S.op("act", lambda e: e.activation(out=EBL[:, :], in_=TBv[:, :, 63], func=AF.Exp), reads=["TB"], writes=["EBL"])
            S.op("dve", lambda e: e.tensor_tensor(out=BDM[:, :], in0=TBv[:, :, 63], in1=TBv[:, :, 31], op=ALU.subtract), reads=["TB"], writes=["BDM"])
            S.op("act", lambda e: e.activation(out=EBLM[:, :], in_=BDM[:, :], func=AF.Exp), reads=["BDM"], writes=["EBLM"])
            ESC = ["EBM", "EBL", "EBLM"]
            yield
            for b in range(4):
                bs = slice(b * 128, (b + 1) * 128)
                trp = PS[1][:, 448:512].bitcast(BF16)
                S.op("pe", lambda e, bs=bs, trp=trp: e.transpose(out=trp, in_=KTB[:, bs], identity=IDENT), reads=["KTB", "CB"], writes=[("P", 1)])
                S.op("dve", lambda e, b=b, trp=trp: e.tensor_copy(out=KTOK[:, b, :], in_=trp), reads=[("P", 1)], writes=[("KTOK", b)])
                yield
                S.op("pe", lambda e, bs=bs: e.matmul(PS[1][:, 0:128], lhsT=KTB[:, bs], rhs=QTB[:, bs], start=True, stop=True),
                     reads=["KTB", "QTB"], writes=[("P", 1)])
                S.op("dve", lambda e: e.tensor_copy(out=TSC[:, :], in_=PS[1][:, 0:128]), reads=[("P", 1)], writes=["TSC"])
                S.op("dve", lambda e: e.tensor_tensor(out=SCB[:, :], in0=TSC[:, :], in1=BDMASK, op=ALU.mult), reads=["TSC", "CB"], writes=["SCB"])
                yield
                for c2 in range(2):
                    rs = slice(c2 * 64, (c2 + 1) * 64)
                    S.op("pe", lambda e, b=b, c2=c2, rs=rs: e.matmul(PS[1 + c2][:, 128:256], lhsT=KTOK[rs, b, :], rhs=VTOK[rs, par, b, :], start=True, stop=True),
                         reads=[("KTOK", b), ("VTOK", par, b)], writes=[("P", 1 + c2)])
                yield
                for c2 in range(2):
                    c = 2 * b + c2
                    S.op("dve", lambda e, c=c, c2=c2: e.tensor_scalar(out=SM[:, c2, :], in0=ST[:, :], scalar1=EBM[:, c:c + 1], scalar2=None, op0=ALU.mult),
                         reads=["ST"] + ESC, writes=[("SM", c2)])
                    S.op("dve", lambda e, c=c, c2=c2: e.tensor_scalar(out=TKV[:, :], in0=PS[1 + c2][:, 128:256], scalar1=EBLM[:, c:c + 1], scalar2=None, op0=ALU.mult),
                         reads=[("P", 1 + c2)] + ESC, writes=["TKV"])
                    S.op("dve", lambda e, c=c: e.scalar_tensor_tensor(out=ST[:, :], in0=ST[:, :], scalar=EBL[:, c:c + 1], in1=TKV[:, :], op0=ALU.mult, op1=ALU.add),
                         reads=["ST", "TKV"] + ESC, writes=["ST"])
                yield
                S.op("pe", lambda e, b=b: e.matmul(PS[1][:, 0:128], lhsT=SCB[:, :], rhs=VTOK[:, par, b, :], start=True, stop=False),
                     reads=["SCB", ("VTOK", par, b)], writes=[("P", 1)])
                for c2 in range(2):
                    S.op("pe", lambda e, b=b, c2=c2: e.matmul(PS[1][c2 * 64:(c2 + 1) * 64, 0:128], lhsT=QTB[:, b * 128 + c2 * 64:b * 128 + (c2 + 1) * 64], rhs=SM[:, c2, :],
                                                              start=False, stop=True), reads=["QTB", ("SM", c2)], writes=[("P", 1)])
                ob_ = (4 * T + b) % 2
                S.op("act", lambda e: e.activation(out=T1[:, :], in_=PS[1][:, 0:128], func=AF.Square, accum_out=SSO[:, 0:1]), reads=[("P", 1)], writes=["T1", "SSO"])
                rstd(SSO[:, 0:1], RO[:, 0:1], "SSO", "RO", 128)
                S.op("dve", lambda e: e.scalar_tensor_tensor(out=T1[:, :], in0=PS[1][:, 0:128], scalar=RO[:, 0:1], in1=HNW[:, :], op0=ALU.mult, op1=ALU.mult),
                     reads=[("P", 1), "RO", "HNW"], writes=["T1"])
                yield
                S.op("act", lambda e, b=b: e.activation(out=EG[:, :], in_=GTOK[:, par, b, :], func=AF.Exp, scale=-1.0), reads=[("GTOK", par, b)], writes=["EG"])
                S.op("dve", lambda e: e.tensor_scalar(out=EG[:, :], in0=EG[:, :], scalar1=1.0, scalar2=None, op0=ALU.add), reads=["EG"], writes=["EG"])
                S.op("dve", lambda e: e.reciprocal(out=EG[:, :], in_=EG[:, :]), reads=["EG"], writes=["EG"])
                S.op("dve", lambda e, b=b: e.tensor_tensor(out=GG[:, :], in0=GTOK[:, par, b, :], in1=EG[:, :], op=ALU.mult), reads=["EG", ("GTOK", par, b)], writes=["GG"])
                S.op("dve", lambda e, ob_=ob_: e.tensor_tensor(out=OAT[:, ob_, :], in0=T1[:, :], in1=GG[:, :], op=ALU.mult), reads=["T1", "GG"], writes=[("OAT", ob_)])
                S.op("sp", lambda e, ob_=ob_, b=b: e.dma_start(out=oAv[:, 4 * T + b, :], in_=OAT[:, ob_, :]), reads=[("OAT", ob_)], dma="oa%d" % ob_)
                yield

        def gen_F(T):
            par = T % 2
            tsl = slice(T * 512, (T + 1) * 512)
            nkb = 4 * T + 4

            def s_stage(j):
                lo = 0 if j < 4 * T else 128 * (j - 4 * T)
                sp_ = j % 2
                sps = PS[4 + sp_]
                S.op("pe", lambda e: e.matmul(sps[:, lo:512], lhsT=KT[:, j * 128:(j + 1) * 128], rhs=QT[:, par, lo:512], start=True, stop=False),
                     reads=[("KT", j // 4), ("QT", par)], writes=[("P", 4 + sp_)])
                diag = j >= 4 * T
                S.op("pe", lambda e: e.matmul(sps[:, lo:512], lhsT=E0, rhs=CROW[:, par, lo:512], start=False, stop=(not diag)),
                     reads=[("CROW", par), "CB"], writes=[("P", 4 + sp_)])
                if diag:
                    S.op("pe", lambda e: e.matmul(sps[:, lo:lo + 128], lhsT=IDENT, rhs=MASKNEG, start=False, stop=True),
                         reads=["CB"], writes=[("P", 4 + sp_)])
                S.op("act", lambda e: e.activation(out=PT[:, sp_, lo:512], in_=sps[:, lo:512], func=AF.Exp, bias=BIAS[:, par, j:j + 1]),
                     reads=[("P", 4 + sp_), ("BIAS", par)], writes=[("PT", sp_)])

            def pv_stage(j):
                lo = 0 if j < 4 * T else 128 * (j - 4 * T)
                sp_ = j % 2
                S.op("pe", lambda e: e.matmul(PS[6][:, lo:512], lhsT=V[:, j, :], rhs=PT[:, sp_, lo:512], start=(j == 0), stop=(j == nkb - 1)),
                     reads=[("V", j), ("PT", sp_)], writes=[("P", 6)])
                S.op("pe", lambda e: e.matmul(PS[7][:, lo:512], lhsT=ONESB, rhs=PT[:, sp_, lo:512], start=(j == 0), stop=(j == nkb - 1)),
                     reads=["CB", ("PT", sp_)], writes=[("P", 7)])

            s_stage(0)
            yield
            for j in range(nkb):
                if j + 1 < nkb:
                    s_stage(j + 1)
                pv_stage(j)
                yield
            S.op("dve", lambda e: e.reciprocal(out=ODEN[:, :], in_=PS[7][:, :]), reads=[("P", 7)], writes=["ODEN"])
            S.op("dve", lambda e: e.tensor_tensor(out=OOUT[:, :], in0=PS[6][:, :], in1=ODEN[:, :], op=ALU.mult), reads=[("P", 6), "ODEN"], writes=["OOUT"])
            S.op("sp", lambda e: e.dma_start(out=oBT[:, tsl], in_=OOUT[:, :]), reads=["OOUT"], dma="ob")
            yield

        done = {"A": 0, "H": 0, "F": 0}

        def stream(name, gen_fn, can_start):
            for T in range(NTILES):
                while not can_start(T):
                    yield False
                for _ in gen_fn(T):
                    yield True
                done[name] = T + 1

        streams = [
            stream("A", gen_A, lambda T: done["H"] >= T - 1 and done["F"] >= T - 1),
            stream("H", gen_H, lambda T: done["A"] >= T + 1),
            stream("F", gen_F, lambda T: done["A"] >= T + 1),
        ]
        h_stream = streams[1]
        streams0 = list(streams)
        _RATES = (1, 1, 2)
        while streams:
            progressed = False
            for g in list(streams):
                for _rep in range(_RATES[0] if g is streams0[0] else (_RATES[1] if g is h_stream else _RATES[2])):
                    try:
                        if next(g):
                            progressed = True
                    except StopIteration:
                        streams.remove(g)
                        progressed = True
                        break
            assert progressed or not streams
        S.barrier()
        S.emit(nc, block, st)
    return nc


def prep_l1(inp, ntok=SEQ):
    x = np.asarray(inp["x"], np.float32)[0]
    xT = np.ascontiguousarray(x[:ntok].T)
    w_in = np.asarray(inp["w_in"], np.float32)[0]
    nv1 = _colvec(inp["norm_mix_pre"][0])
    p = np.arange(128)
    ones = np.ones((128, 128), np.float32)
    ident = np.eye(128, dtype=np.float32)
    maskneg = np.where(p[:, None] > p[None, :], -30000.0, 0.0).astype(np.float32)
    bdmask = ((p[:, None] // 64 == p[None, :] // 64) & (p[:, None] <= p[None, :])).astype(np.float32)
    e0 = np.zeros((128, 128), np.float32)
    e0[0, :] = 1.0
    cb = np.concatenate([ones, ident, maskneg, bdmask, e0], axis=1).astype(ml_dtypes.bfloat16)
    reset = np.tile((np.arange(512) % 64 != 0).astype(np.float32)[None, :], (128, 1))
    cf = np.ascontiguousarray(np.concatenate([reset, np.ones((128, 512), np.float32)], axis=1))
    lbl_all = np.asarray(inp["hgrn_lb_logits"], np.float32)
    hnw = np.ascontiguousarray(np.tile(np.asarray(inp["hgrn_norm_w"], np.float32)[0][None, :], (128, 1)))
    maps = []
    for c in range(NCORES):
        cs = lambda base: w_in[:, base + c * 128: base + (c + 1) * 128]
        w1 = np.ascontiguousarray(np.concatenate([cs(0), cs(1024), cs(4096), cs(5120), cs(2048), cs(3072), cs(6144)], axis=1))
        wf = np.ascontiguousarray(np.repeat(w_in[:, 7168 + c: 7169 + c], 2, axis=1))
        lbl = np.ascontiguousarray(lbl_all[:, c * 128:(c + 1) * 128].T)
        bfx = np.full((128, 1), np.asarray(inp["b_fox_f"], np.float32)[0, c], np.float32)
        maps.append(dict(xT=xT, w1=w1, wf=wf, nv1=nv1, lbl=lbl, hnw=hnw, bfx=bfx, cb=cb, cf=cf))
    return maps


def post_l1(results):
    oA = np.concatenate([r["oA"] for r in results], axis=1)
    oB = np.concatenate([np.ascontiguousarray(r["oBT"].T) for r in results], axis=1)
    return oA, oB


_CACHE = {}


def kernel(**inputs):
    inputs = {k: np.asarray(v) for k, v in inputs.items()}
    if "l1" not in _CACHE:
        _CACHE["l1"] = build_l1(SEQ // 512)
        _CACHE["l2"] = build_l2()
    cores = list(range(NCORES))
    r1 = run_bass_kernel_spmd(_CACHE["l1"], prep_l1(inputs), core_ids=cores)
    oA, oB = post_l1(r1.results)
    r2 = run_bass_kernel_spmd(_CACHE["l2"], prep_l2(inputs, oA, oB), core_ids=cores)
    return post_l2(r2.results).astype(np.float32)
```
